# Optimizing a Trainium2 kernel written in Bass

```python
import jax, jax.numpy as jnp
from jax import lax
import numpy as np

D_MODEL = 1024
BATCH = 8
SEQ = 2048
DEPTH = 1
DEC_BATCH = 128
DEC_SEQ = 4
PAST_LEN = 16384
PAGE_SIZE = 128

D_MIX = D_MODEL
GLA_WIDTH = D_MIX // 2
GLA_HEADS = 4
GLA_DK = GLA_WIDTH // 2 // GLA_HEADS
GLA_DV = GLA_WIDTH // GLA_HEADS
GLA_RANK = 16
GLA_TAU = 16.0
GLA_CHUNK = 64
SGU_WIDTH = D_MIX - GLA_WIDTH
SGU_HEADS = 4
SGU_DH = SGU_WIDTH // SGU_HEADS
SGU_CHUNK = 128
N_MEM = 256
MEM_HEADS = 4
MEM_DH = D_MODEL // MEM_HEADS
D_FF = 2816
CONV_W = 3
EPS = 1e-6

QK_COLS = GLA_HEADS * GLA_DK
SPLIT_SIZES = (QK_COLS, QK_COLS, GLA_WIDTH, GLA_WIDTH, GLA_RANK, SGU_WIDTH, SGU_WIDTH)
D_IN = 2 * QK_COLS + 2 * GLA_WIDTH + GLA_RANK + 2 * SGU_WIDTH

kernel_name = "hymba_gla_sgu_mem_convffn_step"


def rmsnorm(x, g):
    xf = x.astype(jnp.float32)
    y = xf * lax.rsqrt(jnp.mean(xf * xf, axis=-1, keepdims=True) + EPS)
    return (y * g.astype(jnp.float32)).astype(x.dtype)


def split_cols(p):
    out, start = [], 0
    for s in SPLIT_SIZES:
        out.append(p[..., start:start + s])
        start += s
    return out


def gla_chunked(q, k, v, logg, s0):
    B, T, H, DK = q.shape
    DV = v.shape[-1]
    C = min(GLA_CHUNK, T)
    n = T // C
    f32 = jnp.float32

    def chunks(a):
        return jnp.moveaxis(a.astype(f32).reshape((B, n, C) + a.shape[2:]), 1, 0)

    causal = jnp.tril(jnp.ones((C, C), dtype=bool))[None, :, :, None, None]

    def step(S, inp):
        qc, kc, vc, gc = inp
        b = jnp.cumsum(gc, axis=1)
        decay = jnp.exp(jnp.where(causal, b[:, :, None] - b[:, None, :], -jnp.inf))
        scores = jnp.einsum('bihd,bjhd,bijhd->bijh', qc, kc, decay)
        o = (jnp.einsum('bijh,bjhv->bihv', scores, vc)
             + jnp.einsum('bihd,bhdv->bihv', qc * jnp.exp(b), S))
        b_last = b[:, -1]
        S = (jnp.exp(b_last)[..., None] * S
             + jnp.einsum('bjhd,bjhv->bhdv', kc * jnp.exp(b_last[:, None] - b), vc))
        return S, o

    S, o = lax.scan(step, s0.astype(f32), (chunks(q), chunks(k), chunks(v), chunks(logg)))
    o = jnp.moveaxis(o, 0, 1).reshape(B, T, H, DV)
    return o, S


def sgu_spatial(v, w_s, b_s):
    B, T, H, Dh = v.shape
    C = min(SGU_CHUNK, T)
    n = T // C
    w = jnp.tril(w_s[:, :C, :C])
    vc = v.reshape(B, n, C, H, Dh)
    z = jnp.einsum('hij,bnjhd->bnihd', w, vc) + b_s[:, :C].T[None, None, :, :, None]
    return z.reshape(B, T, H, Dh)


def mixer(xn, s0, lp):
    (g_mix, w_in, w_alpha, b_alpha, g_gla_out, g_sgu, w_s, b_s, w_out) = lp
    B, T, _ = xn.shape
    q, k, v, r, a, u, sv = split_cols(xn @ w_in)
    q = q.reshape(B, T, GLA_HEADS, GLA_DK) * (GLA_DK ** -0.5)
    k = k.reshape(B, T, GLA_HEADS, GLA_DK)
    v = v.reshape(B, T, GLA_HEADS, GLA_DV)
    logg = jax.nn.log_sigmoid((a @ w_alpha + b_alpha).astype(jnp.float32)) / GLA_TAU
    logg = logg.reshape(B, T, GLA_HEADS, GLA_DK)
    o, S = gla_chunked(q, k, v, logg, s0)
    o = rmsnorm(o, g_gla_out).astype(xn.dtype)
    o = o.reshape(B, T, GLA_WIDTH) * jax.nn.silu(r)
    u = jax.nn.gelu(u)
    sv = rmsnorm(jax.nn.gelu(sv).reshape(B, T, SGU_HEADS, SGU_DH), g_sgu)
    z = sgu_spatial(sv, w_s, b_s)
    s_out = u * z.reshape(B, T, SGU_WIDTH)
    y = jnp.concatenate([o, s_out], axis=-1) @ w_out
    return y, S.astype(s0.dtype), sv


def mem_kv(mem, g_mem, wk_x, wv_x):
    B = mem.shape[0]
    mn = rmsnorm(mem, g_mem)
    mk = (mn @ wk_x).reshape(B, N_MEM, MEM_HEADS, MEM_DH)
    mv = (mn @ wv_x).reshape(B, N_MEM, MEM_HEADS, MEM_DH)
    return mk, mv


def mem_attend(hn, mk, mv, wq_x, wo_x):
    B, T, _ = hn.shape
    q = (hn @ wq_x).reshape(B, T, MEM_HEADS, MEM_DH)
    s = jnp.einsum('bthd,bmhd->bhtm', q, mk).astype(jnp.float32) * (MEM_DH ** -0.5)
    p = jax.nn.softmax(s, axis=-1).astype(mv.dtype)
    o = jnp.einsum('bhtm,bmhd->bthd', p, mv).reshape(B, T, D_MODEL)
    return o @ wo_x


def conv_ffn(hn, buf, w_gate, w_up, conv_w, conv_b, w_down):
    T = hn.shape[1]
    g = hn @ w_gate
    gp = jnp.concatenate([buf.astype(g.dtype), g], axis=1)
    c = conv_b + sum(conv_w[j] * gp[:, j:j + T] for j in range(CONV_W))
    out = (jax.nn.gelu(c) * (hn @ w_up)) @ w_down
    return out, gp[:, -(CONV_W - 1):]


def layer(x, mk, mv, s0, buf, mix_p, g_x, wq_x, wo_x, g_ffn, w_gate, w_up, conv_w, conv_b, w_down):
    m, S, sv = mixer(rmsnorm(x, mix_p[0]), s0, mix_p)
    h = x + m
    h = h + mem_attend(rmsnorm(h, g_x), mk, mv, wq_x, wo_x)
    f, new_buf = conv_ffn(rmsnorm(h, g_ffn), buf, w_gate, w_up, conv_w, conv_b, w_down)
    return h + f, S, new_buf, sv


def setup_inputs(seed: int = 0) -> dict:
    key = jax.random.key(seed)
    ks = iter(jax.random.split(key, 40))
    nrm = lambda shape, s=1.0: jax.random.normal(next(ks), shape, jnp.float32) * s
    L = DEPTH
    return {
        "x_prompt": nrm((BATCH, SEQ, D_MODEL)),
        "x_sample": nrm((DEC_BATCH, DEC_SEQ, D_MODEL)),
        "mem_prompt": nrm((BATCH, N_MEM, D_MODEL)),
        "state_gla": nrm((L, DEC_BATCH, GLA_HEADS, GLA_DK, GLA_DV), 0.5),
        "state_conv": nrm((L, DEC_BATCH, CONV_W - 1, D_FF)),
        "cache_mem_k": nrm((L, DEC_BATCH, N_MEM, MEM_HEADS, MEM_DH)),
        "cache_mem_v": nrm((L, DEC_BATCH, N_MEM, MEM_HEADS, MEM_DH)),
        "g_mix": 1.0 + nrm((L, D_MODEL), 0.02),
        "w_in": nrm((L, D_MODEL, D_IN), D_MODEL ** -0.5),
        "w_alpha": nrm((L, GLA_RANK, QK_COLS), GLA_RANK ** -0.5),
        "b_alpha": nrm((L, QK_COLS), 0.02),
        "g_gla_out": 1.0 + nrm((L, GLA_HEADS, GLA_DV), 0.02),
        "g_sgu": 1.0 + nrm((L, SGU_HEADS, SGU_DH), 0.02),
        "w_s": nrm((L, SGU_HEADS, SGU_CHUNK, SGU_CHUNK), 0.5 * SGU_CHUNK ** -0.5),
        "b_s": 1.0 + nrm((L, SGU_HEADS, SGU_CHUNK), 0.02),
        "w_out": nrm((L, D_MIX, D_MODEL), D_MIX ** -0.5),
        "g_x": 1.0 + nrm((L, D_MODEL), 0.02),
        "g_mem": 1.0 + nrm((L, D_MODEL), 0.02),
        "wq_x": nrm((L, D_MODEL, D_MODEL), D_MODEL ** -0.5),
        "wk_x": nrm((L, D_MODEL, D_MODEL), D_MODEL ** -0.5),
        "wv_x": nrm((L, D_MODEL, D_MODEL), D_MODEL ** -0.5),
        "wo_x": nrm((L, D_MODEL, D_MODEL), D_MODEL ** -0.5),
        "g_ffn": 1.0 + nrm((L, D_MODEL), 0.02),
        "w_gate": nrm((L, D_MODEL, D_FF), D_MODEL ** -0.5),
        "w_up": nrm((L, D_MODEL, D_FF), D_MODEL ** -0.5),
        "conv_w": nrm((L, CONV_W, D_FF), CONV_W ** -0.5),
        "conv_b": nrm((L, D_FF), 0.02),
        "w_down": nrm((L, D_FF, D_MODEL), D_FF ** -0.5),
        "g_final": 1.0 + nrm((D_MODEL,), 0.02),
    }


def reference(x_prompt, x_sample, mem_prompt, state_gla, state_conv, cache_mem_k, cache_mem_v,
              g_mix, w_in, w_alpha, b_alpha, g_gla_out, g_sgu, w_s, b_s, w_out,
              g_x, g_mem, wq_x, wk_x, wv_x, wo_x,
              g_ffn, w_gate, w_up, conv_w, conv_b, w_down, g_final):
    hp, hs = x_prompt, x_sample
    sg_p, sc_p, mk_p, mv_p, sg_s, sc_s, sv_s = [], [], [], [], [], [], []
    for l in range(DEPTH):
        mix_p = (g_mix[l], w_in[l], w_alpha[l], b_alpha[l], g_gla_out[l], g_sgu[l], w_s[l], b_s[l], w_out[l])
        rest = (g_x[l], wq_x[l], wo_x[l], g_ffn[l], w_gate[l], w_up[l], conv_w[l], conv_b[l], w_down[l])
        mk, mv = mem_kv(mem_prompt, g_mem[l], wk_x[l], wv_x[l])
        s0 = jnp.zeros((hp.shape[0], GLA_HEADS, GLA_DK, GLA_DV), hp.dtype)
        b0 = jnp.zeros((hp.shape[0], CONV_W - 1, D_FF), hp.dtype)
        hp, S_p, buf_p, _ = layer(hp, mk, mv, s0, b0, mix_p, *rest)
        sg_p.append(S_p); sc_p.append(buf_p); mk_p.append(mk); mv_p.append(mv)
        hs, S_s, buf_s, sv = layer(hs, cache_mem_k[l], cache_mem_v[l], state_gla[l], state_conv[l],
                                   mix_p, *rest)
        sg_s.append(S_s); sc_s.append(buf_s); sv_s.append(sv)
    y_prompt = rmsnorm(hp, g_final)
    y_sample = rmsnorm(hs, g_final)
    return (y_prompt, y_sample,
            jnp.stack(sg_p), jnp.stack(sc_p), jnp.stack(mk_p), jnp.stack(mv_p),
            jnp.stack(sg_s), jnp.stack(sc_s), jnp.stack(sv_s))
```

```python
import numpy as np
from contextlib import ExitStack
import concourse.bass as bass
import concourse.mybir as mybir
from concourse.bass_utils import run_bass_kernel_spmd

F32 = mybir.dt.float32
BF16 = mybir.dt.bfloat16
U8 = mybir.dt.uint8
AF = mybir.ActivationFunctionType
ALU = mybir.AluOpType
AX = mybir.AxisListType

NCORES = 8
TP, TS = 2048, 64
T = TP + TS
D, KC = 1024, 8
DIN = 2576
CQ, CK, CV, CR, CA, CU, CSV = 0, 256, 512, 1024, 1536, 1552, 2064
DFF, NF = 2816, 22
UNITS = [(0, 4), (4, 4), (8, 4), (12, 4), (16, 3), (19, 3)]
EPS = 1e-6
NB = 256
NB3 = 512
DEBUG = False


class Sched:
    ENGS = ["tensor", "vector", "scalar", "gpsimd", "sync"]

    def __init__(self):
        self.ops = {e: [] for e in self.ENGS}
        self.cnt = {e: 0 for e in self.ENGS}
        self.pending = {e: False for e in self.ENGS}
        self.writers, self.readers = {}, {}
        self.seen = {e: {} for e in self.ENGS}
        self.dma_cnt = {}
        self.sem_names = set()
        self.tags = {e: [] for e in self.ENGS}
        self.ctx = ""
        self.dma_hist = {}

    def _deps(self, eng, reads, writes):
        toks = {}

        def add(d):
            for s, v in d.items():
                if v > toks.get(s, 0):
                    toks[s] = v
        for k in reads:
            add(self.writers.get(k, {}))
        for k in writes:
            add(self.writers.get(k, {}))
            add(self.readers.get(k, {}))
        waits = []
        for s, v in toks.items():
            if s == "E_" + eng and eng != "gpsimd":
                continue
            if self.seen[eng].get(s, 0) >= v:
                continue
            self.seen[eng][s] = v
            waits.append((s, v))
        return waits

    def _commit(self, tok, reads, writes):
        s, v = tok
        for k in reads:
            r = self.readers.setdefault(k, {})
            r[s] = max(r.get(s, 0), v)
        for k in writes:
            self.writers[k] = {s: v}
            self.readers[k] = {}

    def op(self, eng, fn, reads=(), writes=(), sig=True):
        isps = lambda k: isinstance(k, str) and k.startswith("ps") and k[2:].isdigit()
        writes = list(writes) + [k for k in reads if isps(k)]
        reads = [k for k in reads if not isps(k)]
        waits = self._deps(eng, reads, writes)
        s = "E_" + eng
        self.sem_names.add(s)
        if sig:
            self.cnt[eng] += 1
            self.pending[eng] = False
            tok = (s, self.cnt[eng])
        else:
            self.pending[eng] = True
            tok = (s, self.cnt[eng] + 1)
        self.ops[eng].append((fn, waits, tok, 1 if sig else 0))
        self.tags[eng].append(self.ctx)
        self._commit(tok, reads, writes)

    def dma(self, eng, fn, key, reads=(), writes=()):
        waits = self._deps(eng, reads, writes)
        s = "D_" + key
        self.sem_names.add(s)
        self.dma_cnt[s] = self.dma_cnt.get(s, 0) + 16
        tok = (s, self.dma_cnt[s])
        hist = self.dma_hist.setdefault(eng, [])
        lim = 16 if eng == "gpsimd" else 24
        if len(hist) >= lim:
            os_, ov_ = hist[-lim]
            if self.seen[eng].get(os_, 0) < ov_:
                self.seen[eng][os_] = ov_
                waits = list(waits) + [(os_, ov_)]
        hist.append(tok)
        self.ops[eng].append((fn, waits, tok, 2))
        self.tags[eng].append(self.ctx)
        self._commit(tok, reads, writes)

    def fence(self, keys):
        for e in self.ENGS:
            assert not self.pending[e], e
        d = {"E_" + e: self.cnt[e] for e in self.ENGS if self.cnt[e] > 0}
        d.update(self.dma_cnt)
        for k in keys:
            self.writers[k] = dict(d)
            self.readers[k] = {}

    def emit(self, block, sems, final_waits):
        def body(eng_name):
            def f(eng):
                for fn, waits, tok, kind in self.ops[eng_name]:
                    for s, v in waits:
                        eng.wait_ge(sems[s], v)
                    r = fn(eng)
                    if kind == 2:
                        r.then_inc(sems[tok[0]], 16)
                    elif kind == 1:
                        r.then_inc(sems[tok[0]], 1)
                for s, v in final_waits.get(eng_name, []):
                    eng.wait_ge(sems[s], v)
            return f
        block.tensor(body("tensor"))
        block.vector(body("vector"))
        block.scalar(body("scalar"))
        block.gpsimd(body("gpsimd"))
        block.sync(body("sync"))


def host_consts():
    c = {}
    c["ident"] = np.eye(128, dtype=np.float32)
    r = np.arange(128)
    c["mask128"] = ((r[:, None] <= r[None, :]) & ((r[:, None] // 64) == (r[None, :] // 64))).astype(np.float32)
    r64 = np.arange(64)
    c["mask_s"] = ((r64[:, None] <= r64[None, :]) & ((r64[:, None] // 4) == (r64[None, :] // 4))).astype(np.float32)
    rm = np.ones((128, NB), np.float32)
    rm[:, 0::64] = 0.0
    c["rm"] = rm
    rms = np.ones((128, TS), np.float32)
    rms[:, 0::4] = 0.0
    c["rm_s"] = rms
    c["trilT"] = (r[:, None] <= r[None, :]).astype(np.float32)
    bm = np.zeros((128, 16), np.float32)
    bm[r64, r64 // 4] = 1.0
    c["bm"] = bm
    sel = np.zeros((4, 64), np.float32)
    sel[r64 % 4, r64] = 1.0
    c["sel"] = sel
    return c


def build_nc():
    nc = bass.Bass("TRN2", target_bir_lowering=False)

    def din(name, shape):
        return nc.dram_tensor(name, list(shape), F32, kind="ExternalInput").ap()

    def dout(name, shape):
        return nc.dram_tensor(name, list(shape), F32, kind="ExternalOutput").ap()

    x_p = din("x_p", [TP, D]); x_s = din("x_s", [TS, D]); mem = din("mem", [256, D])
    sgla = din("sgla", [16, 4, 64, 128]); sconv = din("sconv", [32, DFF])
    ck = din("ck", [16, 256, D]); cv = din("cv", [16, 256, D])
    w_in = din("w_in", [D, DIN]); w_alpha = din("w_alpha", [16, 256])
    w_s = din("w_s", [4, 128, 128]); b_s = din("b_s", [1, 512]); g_sgu = din("g_sgu", [1, 512])
    w_out = din("w_out", [D, D]); wq = din("wq", [D, D]); wk = din("wk", [D, D]); wv = din("wv", [D, D]); wo = din("wo", [D, D])
    w_gate = din("w_gate", [D, DFF]); w_up = din("w_up", [D, DFF]); w_down = din("w_down", [DFF, D])
    vec1 = din("vec1", [68, 128]); vec2 = din("vec2", [66, 128]); g_fin = din("g_fin", [1, D])
    cst = {k: din("c_" + k, v.shape) for k, v in host_consts().items()}

    y_p = dout("y_p", [TP, D]); y_s = dout("y_s", [TS, D]); sg_p = dout("sg_p", [4, 64, 128]); sc_p = dout("sc_p", [2, DFF])
    mk_p = dout("mk_p", [256, D]); mv_p = dout("mv_p", [256, D]); sg_s = dout("sg_s", [16, 4, 64, 128])
    sc_s = dout("sc_s", [32, DFF]); sv_s = dout("sv_s", [TS, 512])

    if DEBUG:
        dbg1 = dout("dbg1", [128, KC, T]); dbg2 = dout("dbg2", [128, KC, T])
    S = Sched()
    es = ExitStack()
    with es:
        def sb(name, shape, dt):
            return es.enter_context(nc.sbuf_tensor(name, list(shape), dt))
        xT = sb("xT", [128, KC, T], F32)
        R1 = sb("R1", [128, 57600], U8)
        R2 = sb("R2", [128, 32768], U8)
        R3 = sb("R3", [128, 33792], U8)
        PS = [es.enter_context(nc.psum_tensor("ps%d" % i, [128, 512], F32)) for i in range(8)]

        def carve(arena, off, shape, dt):
            esz = 4 if dt == F32 else 2
            n = int(np.prod(shape[1:]))
            v = arena[:, off:off + n * esz].bitcast(dt)
            if len(shape) == 3:
                v = v.rearrange("p (a b) -> p a b", a=shape[1])
            elif len(shape) == 4:
                v = v.rearrange("p (a b c) -> p a b c", a=shape[1], b=shape[2])
            return v

        class Lay:
            def __init__(self, arena, base=0):
                self.arena, self.off = arena, base

            def get(self, shape, dt):
                esz = 4 if dt == F32 else 2
                v = carve(self.arena, self.off, shape, dt)
                self.off += int(np.prod(shape[1:])) * esz
                self.off = (self.off + 63) // 64 * 64
                assert self.off <= self.arena.shape[1], (self.off, self.arena.shape)
                return v

        ident_f = sb("ident_f", [128, 128], F32); ident_b = sb("ident_b", [128, 128], BF16)
        ones_b = sb("ones_b", [128, 128], BF16)
        mask128 = sb("mask128", [128, 128], F32); mask_s = sb("mask_s", [64, 64], F32)
        rm = sb("rm", [128, NB], F32); rm_s = sb("rm_s", [128, TS], F32)
        bm = sb("bm", [128, 16], F32); sel = sb("sel", [4, 64], F32)
        colv1 = sb("colv1", [128, 68], F32); colv2 = sb("colv2", [128, 66], F32)
        nba = sb("nba", [128, 2], F32); gg = sb("gg", [128, 4], F32)
        gsgu_bc = sb("gsgu_bc", [128, 512], F32)
        Wm = sb("Wm", [128, 4, 128], BF16); Wm_s = sb("Wm_s", [64, 4, 64], BF16)
        bs_hi = sb("bs_hi", [1, 512], BF16); bs_lo = sb("bs_lo", [1, 512], BF16)
        walpha_b = sb("walpha_b", [16, 256], BF16)
        ghist = sb("ghist", [128, NF, 2], F32)
        KT = sb("KT", [128, 8, 256], BF16); vbf = sb("vbf", [128, 2, 1024], BF16)
        smx = sb("smx", [128, 4], F32)
        Sst = sb("Sst", [128, 2, 128], F32); S_bf = sb("S_bf", [128, 2, 128], BF16); S_bfB = sb("S_bfB", [128, 2, 128], BF16)
        S_bfs = [S_bf, S_bfB]
        gch = [0]

        G_MIX, G_X, G_MEM, G_FFN, G_FIN = 0, 8, 16, 24, 32

        def ld(eng, out, in_, key, reads=(), writes=(), **kw):
            S.dma(eng, lambda e: e.dma_start(out=out, in_=in_, **kw), key, reads=reads, writes=writes)

        def V(name, kw, reads=(), writes=()):
            S.op("vector", lambda e: getattr(e, name)(**kw), reads, writes)

        def A(fn, reads=(), writes=()):
            S.op("scalar", fn, reads, writes)

        def G(name, kw, reads=(), writes=()):
            S.op("gpsimd", lambda e: getattr(e, name)(**kw), reads, writes)

        def MM(out, lhsT, rhs, start, stop, reads, writes, sig=None):
            if sig is None:
                sig = stop
            S.op("tensor", lambda e: e.matmul(out, lhsT=lhsT, rhs=rhs, start=start, stop=stop), reads, writes, sig=sig)

        def TR(out, in_, ident, reads, writes, sig=True):
            S.op("tensor", lambda e: e.transpose(out=out, in_=in_, identity=ident), reads, writes, sig=sig)

        def act(out, in_, func, reads, writes, **kw):
            A(lambda e: e.activation(out=out, in_=in_, func=func, **kw), reads, writes)

        for nm, t in [("ident", ident_f), ("mask128", mask128), ("mask_s", mask_s), ("rm", rm), ("rm_s", rm_s),
                      ("bm", bm), ("sel", sel)]:
            ld("sync", t[:], cst[nm], "c_" + nm, writes=[nm])
        ld("sync", gsgu_bc[:], g_sgu.broadcast_to([128, 512]), "gsgu", writes=["gsgu_bc"])
        ld("gpsimd", walpha_b[:], w_alpha, "walpha", writes=["walpha_b"])
        V("tensor_copy", dict(out=ident_b[:], in_=ident_f[:]), ["ident"], ["ident_b"])
        G("memset", dict(ap=ones_b[:], constant=1.0), [], ["ones_b"])
        G("memset", dict(ap=ghist[:], constant=0.0), [], ["ghist"])
        G("memset", dict(ap=Sst[:], constant=0.0), [], ["Sst"])
        G("memset", dict(ap=S_bf[:], constant=0.0), [], [("S_bf", 0)])

        wk_b = carve(R2, 0, [128, 8, 1024], BF16); wv_b = carve(R2, 16384, [128, 8, 1024], BF16)
        w_in_b = carve(R1, 0, [128, 8, DIN], BF16); w_out_b = carve(R1, 41216, [128, 8, 1024], BF16)

        def load_w(dst, src, key, ncols, reads=(), writes=()):
            srcv = src.rearrange("(c p) n -> p c n", p=128)
            nck = srcv.shape[1]
            step = 8 if ncols <= 1024 else 4
            for c0 in range(0, nck, step):
                ld("gpsimd", dst[:, c0:c0 + step, :], srcv[:, c0:c0 + step, :], key, reads=reads, writes=[key] + list(writes))

        load_w(wk_b, wk, "wk", 1024)
        load_w(wv_b, wv, "wv", 1024)
        load_w(w_in_b, w_in, "w_in", DIN)
        load_w(w_out_b, w_out, "w_out", 1024)

        S.ctx = "p0"
        L = Lay(R3)
        stg = [L.get([128, 1024], F32) for _ in range(2)]
        memT = L.get([128, 8, 256], F32)
        mnT = L.get([128, 8, 256], BF16)
        sqb0 = L.get([128, 8, 256], BF16)
        rstd0 = L.get([128, 256], F32); tmpn0 = L.get([128, 256], F32)
        Wm32 = L.get([128, 4, 128], F32); wsl = L.get([128, 4, 128], F32)
        vst = L.get([128, 128], F32)
        bsf = L.get([128, 512], F32)
        trilT = L.get([128, 128], F32)
        ld("sync", trilT[:], cst["trilT"], "c_trilT", writes=["trilT"])
        ld("sync", bsf[0:1, :], b_s, "bsf", writes=["bsf"])
        act(bs_hi[:], bsf[0:1, :], AF.Copy, ["bsf"], ["bs_hi"])
        V("tensor_tensor", dict(out=bs_lo[:], in0=bsf[0:1, :], in1=bs_hi[:], op=ALU.subtract), ["bsf", "bs_hi"], ["bs_lo"])

        ld("sync", vst[0:68, :], vec1, "vst", writes=["vst"])
        TR(PS[0][:, 0:68], vst[0:68, :], ident_f[0:68, 0:68], ["vst", "ident"], ["ps0"])
        V("tensor_copy", dict(out=colv1[:], in_=PS[0][:, 0:68]), ["ps0"], ["colv1"])
        ld("sync", vst[0:66, :], vec2, "vst", writes=["vst"])
        TR(PS[0][:, 0:66], vst[0:66, :], ident_f[0:66, 0:66], ["vst", "ident"], ["ps0"])
        V("tensor_copy", dict(out=colv2[:], in_=PS[0][:, 0:66]), ["ps0"], ["colv2"])
        act(nba[:], colv1[:, 40:42], AF.Copy, ["colv1"], ["nba"], scale=-1.0)
        act(gg[:], colv1[:, 42:46], AF.Copy, ["colv1"], ["gg"], scale=0.5)

        ld("sync", wsl[:], w_s.rearrange("h i j -> i h j"), "wsl", writes=["wsl"])
        for h in range(4):
            TR(PS[1][:, h * 128:(h + 1) * 128], wsl[:, h, :], ident_f[:], ["wsl", "ident"], ["ps1"], sig=(h == 3))
        V("tensor_tensor", dict(out=Wm32[:], in0=PS[1][:].rearrange("p (h i) -> p h i", h=4),
                                    in1=trilT[:].unsqueeze(1).broadcast_to([128, 4, 128]), op=ALU.mult),
          ["ps1", "trilT"], ["Wm32"])
        V("tensor_copy", dict(out=Wm[:], in_=Wm32[:]), ["Wm32"], ["Wm"])
        for h in range(4):
            MM(PS[1][0:64, h * 64:(h + 1) * 64], sel[0:4, :], Wm32[0:4, h, 0:4].unsqueeze(1).broadcast_to([4, 16, 4]),
               True, True, ["sel", "Wm32"], ["ps1"], sig=(h == 3))
        V("tensor_tensor", dict(out=Wm_s[:], in0=PS[1][0:64, 0:256].rearrange("p (h i) -> p h i", h=4),
                                    in1=mask_s[:].unsqueeze(1).broadcast_to([64, 4, 64]), op=ALU.mult),
          ["ps1", "mask_s"], ["Wm_s"])

        def fm_norm(src_fn, n, gcol, out_fn, sqb, rstd, tmpn, psb, pskey, rkeys, wkeys, tag, do_stats=True, do_apply=True):
            if do_stats:
                fm_stats(src_fn, n, sqb, rstd, tmpn, psb, pskey, rkeys, tag)
            if do_apply:
                for c in range(KC):
                    V("scalar_tensor_tensor", dict(out=out_fn(c), in0=src_fn(c), scalar=colv1[:, gcol + c:gcol + c + 1],
                                                   in1=rstd[:, 0:n], op0=ALU.mult, op1=ALU.mult),
                      list(rkeys) + [tag + "rstd", "colv1"], wkeys)

        def fm_squares(src_fn, n, sqb, rkeys, tag):
            for c in range(KC):
                act(sqb[:, c, 0:n], src_fn(c), AF.Square, rkeys, [tag + "sqb"])

        def fm_stats(src_fn, n, sqb, rstd, tmpn, psb, pskey, rkeys, tag, squares=True):
            if squares:
                fm_squares(src_fn, n, sqb, rkeys, tag)
            for c in range(KC):
                MM(psb[:, 0:n], ones_b[:], sqb[:, c, 0:n], c == 0, c == KC - 1, ["ones_b", tag + "sqb"], [pskey])
            act(tmpn[:, 0:n], psb[:, 0:n], AF.Ln, [pskey], [tag + "tmpn"], scale=1.0 / D, bias=EPS)
            V("tensor_scalar", dict(out=tmpn[:, 0:n], in0=tmpn[:, 0:n], scalar1=-0.5, scalar2=None, op0=ALU.mult),
              [tag + "tmpn"], [tag + "tmpn"])
            act(rstd[:, 0:n], tmpn[:, 0:n], AF.Exp, [tag + "tmpn"], [tag + "rstd"])

        ntile = TP // 128
        xs_slots = [carve(R2, 16384 + i_ * 4096, [128, 1024], F32) for i_ in range(2)]

        def xtile(t, buf, bkey, dkey, banks, extra_w=()):
            rows = 128 if t < ntile else TS
            src = x_p[t * 128:(t + 1) * 128, :] if t < ntile else x_s
            ld("sync", buf[0:rows, :], src, dkey, writes=[bkey] + list(extra_w))
            for half in range(2):
                pb, pk = PS[banks[half]], "ps%d" % banks[half]
                for c4 in range(4):
                    c = half * 4 + c4
                    TR(pb[:, c4 * 128:c4 * 128 + rows], buf[0:rows, c * 128:(c + 1) * 128], ident_f[0:rows, 0:rows],
                       [bkey, "ident"], [pk], sig=(c4 == 3))
                dst = xT[:, half * 4:half * 4 + 4, t * 128:t * 128 + rows]
                srcp = pb[:].rearrange("p (c n) -> p c n", c=4)[:, :, 0:rows]
                if half == 0:
                    V("tensor_copy", dict(out=dst, in_=srcp), [pk], [("xT", t // 2)])
                else:
                    act(dst, srcp, AF.Copy, [pk], [("xT", t // 2)])
        for t in range(2):
            xtile(t, stg[t % 2], ("stg", t % 2), "stg%d" % (t % 2), (2, 3))

        for t in range(2):
            ld("sync", stg[t][:], mem[t * 128:(t + 1) * 128, :], "stg%d" % t, writes=[("stg", t)])
            for half in range(2):
                pb = PS[2 + half]
                for c4 in range(4):
                    c = half * 4 + c4
                    TR(pb[:, c4 * 128:(c4 + 1) * 128], stg[t][:, c * 128:(c + 1) * 128], ident_f[:],
                       [("stg", t), "ident"], ["ps%d" % (2 + half)], sig=(c4 == 3))
                V("tensor_copy", dict(out=memT[:, half * 4:half * 4 + 4, t * 128:(t + 1) * 128],
                                                                  in_=pb[:].rearrange("p (c n) -> p c n", c=4)),
                  ["ps%d" % (2 + half)], ["memT"])
        fm_norm(lambda c: memT[:, c, :], 256, G_MEM, lambda c: mnT[:, c, :], sqb0, rstd0, tmpn0, PS[4], "ps4",
                ["memT"], ["mnT"], "p0")
        for t in range(2):
            for (wb, wkey, outd, isv) in [(wk_b, "wk", mk_p, False), (wv_b, "wv", mv_p, True)]:
                for half in range(2):
                    pb, pk = PS[5 + half], "ps%d" % (5 + half)
                    for c in range(KC):
                        MM(pb[:], mnT[:, c, t * 128:(t + 1) * 128], wb[:, c, half * 512:(half + 1) * 512], c == 0, c == KC - 1,
                           ["mnT", wkey], [pk])
                    act(stg[t][:, half * 512:(half + 1) * 512], pb[:], AF.Copy, [pk], [("stg", t)])
                    if isv:
                        V("tensor_copy", dict(out=vbf[:, t, half * 512:(half + 1) * 512], in_=pb[:]),
                          [pk], ["vbf"])
                ld("sync", outd[t * 128:(t + 1) * 128, :], stg[t][:], "o_mkv", reads=[("stg", t)])
        for oc in range(8):
            pb, pk = PS[5 + oc % 2], "ps%d" % (5 + oc % 2)
            for c in range(KC):
                MM(pb[:, 0:256], wk_b[:, c, oc * 128:(oc + 1) * 128], mnT[:, c, :], c == 0, c == KC - 1, ["mnT", "wk"], [pk])
            act(KT[:, oc, :], pb[:, 0:256], AF.Copy, [pk], ["KT"])

        wq_b = carve(R2, 16384, [128, 8, 1024], BF16); wo_b = carve(R2, 0, [128, 8, 1024], BF16)
        S.fence([("xs", 0), ("xs", 1)])
        L = Lay(R3)
        xn = L.get([128, 8, NB], BF16); sqb = L.get([128, 8, NB], BF16)
        rstd = L.get([128, NB], F32); tmpn = L.get([128, NB], F32)
        aT = L.get([128, NB], BF16)
        e1 = L.get([128, NB], F32); Bc = L.get([128, NB], F32)
        eb = L.get([128, 2, NB], F32); enb = L.get([128, 2, NB], F32)
        qm = L.get([128, 4, NB], BF16); ktT = L.get([128, 2, NB], BF16); khT = L.get([128, 2, NB], BF16)
        cat = L.get([128, 8, NB], BF16); th = L.get([128, NB], F32)
        v_tm = L.get([128, 2, 512], BF16)
        p1keys = ["xn", "m1sqb", "m1rstd", "m1tmpn", "aT", "e1", "Bc", ("eb", 0), ("eb", 1), ("enb", 0), ("enb", 1), "qm", ("ktT", 0), ("ktT", 1), ("khT", 0), ("khT", 1), "cat", "th", "v_tm"]
        L2 = Lay(R2)
        svg = L2.get([128, 512], F32); svsq = L2.get([128, 512], F32); svn = L2.get([128, 2, 512], BF16)
        svo = svsq
        sc_bd = L2.get([128, 4, 128], BF16); khm = L2.get([128, 2, 256], BF16)
        TR2_OFF = L2.off
        rstd_o = L2.get([128, NB], F32); tmp_o = L2.get([128, NB], F32); osq = L2.get([128, 4, NB], BF16)
        sst = L2.get([128, 8], F32); srs = L2.get([128, 8], F32)
        S0 = [L2.get([128, 2, 128], F32) for _ in range(2)]
        S0b = [L2.get([128, 2, 128], BF16) for _ in range(2)]
        S0 = S0 + [carve(R3, 4096 + i_ * 1024, [128, 2, 128], F32) for i_ in range(2)]
        S0b = S0b + [carve(R3, 4096 + 2048 + i_ * 512, [128, 2, 128], BF16) for i_ in range(2)]
        p1keys += ["svg", "svsq", "svn", "sc_bd", "khm", "rstd_o", "tmp_o", ("osq", 0), ("osq", 1), ("osq", 2), ("osq", 3), "sst", "srs",
                   ("S0", 0), ("S0", 1), ("S0b", 0), ("S0b", 1)]
        S.fence(p1keys)
        G("memset", dict(ap=qm[:], constant=0.0), [], ["qm"])
        G("memset", dict(ap=khm[:], constant=0.0), [], ["khm"])

        rr = [0]

        def rot():
            i = rr[0] % 2
            rr[0] += 1
            return PS[i], "ps%d" % i

        def mixer_block(tok0, n, xkeys, sample, pre=False, nxt=None, prev_wout=None, xfill=()):
            ntl = (n + 127) // 128
            blk = slice(tok0, tok0 + n)
            S.ctx = "p1.norm"
            fm_norm(lambda c: xT[:, c, blk], n, G_MIX, lambda c: xn[:, c, 0:n], sqb, rstd, tmpn, PS[0], "ps0",
                    xkeys, ["xn"], "m1", do_stats=not pre)
            xfill = list(xfill)
            if prev_wout is not None:
                prev_wout()
            if xfill:
                xfill.pop(0)()
            rr[0] = 1

            def proj(col0, m):
                pb, pk = rot()
                for c in range(KC):
                    MM(pb[0:m, 0:n], w_in_b[:, c, col0:col0 + m], xn[:, c, 0:n], c == 0, c == KC - 1, ["w_in", "xn"], [pk])
                return pb, pk
            S.ctx = "p1.gates"
            pb, pk = proj(CA, 16)
            act(aT[0:16, 0:n], pb[0:16, 0:n], AF.Copy, [pk], ["aT"])
            rmask = rm_s if sample else rm
            for c in range(2):
                pb, pk = rot()
                MM(pb[:, 0:n], walpha_b[:, c * 128:(c + 1) * 128], aT[0:16, 0:n], True, True, ["walpha_b", "aT"], [pk])
                eX, eXk = (e1, "e1") if c == 0 else (th, "th")
                act(eX[:, 0:n], pb[:, 0:n], AF.Exp, [pk, "nba"], [eXk], scale=-1.0, bias=nba[:, c:c + 1])
                V("tensor_scalar", dict(out=eX[:, 0:n], in0=eX[:, 0:n], scalar1=1.0, scalar2=None, op0=ALU.add), [eXk], [eXk])
                act(eX[:, 0:n], eX[:, 0:n], AF.Ln, [eXk], [eXk])
                bX, bXk = (Bc, "Bc") if c == 0 else (tmpn, "m1tmpn")
                V("tensor_tensor_scan", dict(out=bX[:, 0:n], data0=rmask[:, 0:n], data1=eX[:, 0:n], initial=0.0,
                                                 op0=ALU.mult, op1=ALU.add), [eXk, "rm", "rm_s"], [bXk])
                act(eb[:, c, 0:n], bX[:, 0:n], AF.Exp, [bXk], [("eb", c)], scale=-1.0 / 16)
                act(enb[:, c, 0:n], bX[:, 0:n], AF.Exp, [bXk], [("enb", c)], scale=1.0 / 16)
            S.ctx = "p1.qk"
            def proj4(col0, m, j):
                bi_ = (2, 3, 0, 1)[j % 4]
                pb, pk = PS[bi_], "ps%d" % bi_
                for c in range(KC):
                    MM(pb[0:m, 0:n], w_in_b[:, c, col0:col0 + m], xn[:, c, 0:n], c == 0, c == KC - 1, ["w_in", "xn"], [pk])
                return pb, pk
            for c in range(2):
                pb, pk = proj4(CQ + c * 128, 128, c)
                for h2 in range(2):
                    rs_ = slice(h2 * 64, (h2 + 1) * 64)
                    V("scalar_tensor_tensor", dict(
                        out=qm[rs_, 2 * c + h2, 0:n], in0=pb[rs_, 0:n], scalar=0.125, in1=eb[rs_, c, 0:n],
                        op0=ALU.mult, op1=ALU.mult), [pk, ("eb", c)], ["qm"])
            if xfill:
                S.ctx = "p1.x"
                xfill.pop(0)()
                S.ctx = "p1.qk"
            cl = 4 if sample else 64
            for c in range(2):
                pb, pk = proj4(CK + c * 128, 128, 2 + c)
                V("tensor_tensor", dict(out=ktT[:, c, 0:n], in0=pb[:, 0:n], in1=enb[:, c, 0:n], op=ALU.mult),
                  [pk, ("enb", c)], [("ktT", c)])
                G("tensor_tensor", dict(
                    out=khT[:, c, 0:n].rearrange("p (a b) -> p a b", b=cl),
                    in0=ktT[:, c, 0:n].rearrange("p (a b) -> p a b", b=cl),
                    in1=eb[:, c, cl - 1:n:cl].unsqueeze(2).broadcast_to([128, n // cl, cl]), op=ALU.mult),
                  [("ktT", c), ("eb", c)], [("khT", c)])
            S.ctx = "p1.ru"
            def ru_piece(j):
                S.ctx = "p1.ru"
                if j < 4:
                    pb, pk = proj(CR + j * 128, 128)
                    act(th[:, 0:n], pb[:, 0:n], AF.Tanh, [pk], ["th"], scale=0.5)
                    V("scalar_tensor_tensor", dict(out=cat[:, j, 0:n], in0=th[:, 0:n], scalar=1.0, in1=pb[:, 0:n],
                                                   op0=ALU.add, op1=ALU.mult), [pk, "th"], [("cat", j)])
                else:
                    pb, pk = proj(CU + (j - 4) * 128, 128)
                    act(cat[:, j, 0:n], pb[:, 0:n], AF.Gelu_apprx_tanh, [pk], [("cat", j)])
                S.ctx = "p1.gla"
            ru_q = list(range(8))
            S.ctx = "p1.tm"
            if sample:
                G("memset", dict(ap=v_tm[:], constant=0.0), [], ["v_tm"])
                G("memset", dict(ap=svn[:], constant=0.0), [], ["svn"])
                G("memset", dict(ap=sc_bd[:], constant=0.0), [], ["sc_bd"])
            for tl in range(ntl):
                rows = min(128, n - tl * 128)
                tsl = slice(tl * 128, tl * 128 + rows)
                for c in range(KC):
                    MM(PS[2][0:rows, :], xn[:, c, tsl], w_in_b[:, c, CV:CV + 512], c == 0, c == KC - 1, ["xn", "w_in"], ["ps2"])
                act(v_tm[0:rows, tl, :], PS[2][0:rows, :], AF.Copy, ["ps2"], ["v_tm"])
                for c in range(KC):
                    MM(PS[3][0:rows, :], xn[:, c, tsl], w_in_b[:, c, CSV:CSV + 512], c == 0, c == KC - 1, ["xn", "w_in"], ["ps3"])
                act(svg[0:rows, :], PS[3][0:rows, :], AF.Gelu_apprx_tanh, ["ps3"], ["svg"])
                V("tensor_tensor", dict(out=svsq[0:rows, :], in0=svg[0:rows, :], in1=svg[0:rows, :], op=ALU.mult),
                  ["svg"], ["svsq"])
                V("tensor_reduce", dict(out=sst[0:rows, 0:4], in_=svsq[0:rows, :].rearrange("p (h d) -> p h d", h=4),
                                                       axis=AX.X, op=ALU.add), ["svsq"], ["sst"])
                act(sst[0:rows, 4:8], sst[0:rows, 0:4], AF.Ln, ["sst"], ["sst"], scale=1.0 / 128, bias=EPS)
                V("tensor_scalar", dict(out=sst[0:rows, 4:8], in0=sst[0:rows, 4:8], scalar1=-0.5, scalar2=None, op0=ALU.mult),
                  ["sst"], ["sst"])
                act(srs[0:rows, 0:4], sst[0:rows, 4:8], AF.Exp, ["sst"], ["srs"])
                for h in range(4):
                    hs = slice(h * 128, (h + 1) * 128)
                    dst = svo[0:rows, hs] if sample else svn[0:rows, tl, hs]
                    V("scalar_tensor_tensor", dict(
                        out=dst, in0=svg[0:rows, hs], scalar=srs[0:rows, h:h + 1], in1=gsgu_bc[0:rows, hs],
                        op0=ALU.mult, op1=ALU.mult), ["svg", "srs", "gsgu_bc"], ["svsq" if sample else "svn"])
                if sample:
                    V("tensor_copy", dict(out=svn[0:rows, 0, :], in_=svo[0:rows, :]), ["svsq", "svn"], ["svn"])
                    ld("sync", sv_s, svo[0:rows, :], "o_svs", reads=["svsq"])
            if nxt is not None:
                S.ctx = "p1.norm"
                ntok0, nn_, nxk = nxt
                nblk = slice(ntok0, ntok0 + nn_)
                fm_squares(lambda c: xT[:, c, nblk], nn_, sqb, nxk, "m1")
            S.ctx = "p1.gla"
            for tl in range(ntl):
                rows = min(128, n - tl * 128)
                tsl = slice(tl * 128, tl * 128 + rows)
                for h in range(4):
                    MM(PS[4][0:rows, h * 128:h * 128 + rows], ktT[:, h // 2, tsl], qm[:, h, tsl], True, True,
                       [("ktT", h // 2), "qm"], ["ps4"], sig=(h == 3))
                msk = mask_s[:] if sample else mask128[:]
                V("tensor_tensor", dict(
                    out=sc_bd[0:rows, :, 0:rows], in0=PS[4][0:rows, :].rearrange("p (h i) -> p h i", h=4)[:, :, 0:rows],
                    in1=msk.unsqueeze(1).broadcast_to([rows, 4, rows]), op=ALU.mult), ["ps4", "mask128", "mask_s"], ["sc_bd"])
                p5b = PS[5][:].bitcast(BF16)
                for c in range(2):
                    TR(p5b[0:rows, c * 128:(c + 1) * 128], khT[:, c, tsl], ident_b[:], [("khT", c), "ident_b"], ["ps5"], sig=(c == 1))
                if not sample:
                    for p in range(2):
                        prs = slice(p * 64, (p + 1) * 64)
                        act(khm[prs, p, :], p5b[prs, 0:256], AF.Copy, ["ps5"], ["khm"])
                else:
                    act(tmp_o[0:64, 0:128].bitcast(BF16), p5b[0:64, 0:256], AF.Copy, ["ps5"], ["tmp_o"])
                groups = [(p, slice(p * 64, (p + 1) * 64), None) for p in range(rows // 64)] if not sample else \
                    [(b, slice(b * 4, (b + 1) * 4), b) for b in range(16)]
                for gi, (p, csl, bidx) in enumerate(groups):
                    if sample:
                        s = bidx % 4
                        ld("sync", S0[s][:], sgla[bidx].rearrange("(c h2) d v -> (h2 d) c v", c=2), "S0_%d" % s,
                           writes=[("S0", s)] + (["m1sqb"] if s >= 2 else []))
                        act(S0b[s][:], S0[s][:], AF.Copy, [("S0", s)], [("S0b", s)] + (["m1sqb"] if s >= 2 else []))
                        V("tensor_scalar", dict(out=khm[0:64, bidx % 2, :], in0=tmp_o[0:64, 0:128].bitcast(BF16),
                                                                    scalar1=bm[0:64, bidx:bidx + 1], scalar2=None, op0=ALU.mult),
                          ["tmp_o", "bm"], ["khm"])
                        Sb_cur, Sb_key = S0b[s], ("S0b", s)
                        khm_cur = khm[:, bidx % 2, :]
                    else:
                        rslot, wslot = gch[0] % 2, (gch[0] + 1) % 2
                        gch[0] += 1
                        Sb_cur, Sb_key = S_bfs[rslot], ("S_bf", rslot)
                        khm_cur = khm[:, p, :]
                    ncol = csl.stop - csl.start
                    gsl = slice(tl * 128 + csl.start, tl * 128 + csl.stop)
                    ubank = (2 + rslot) if not sample else (4 + (gi % 2))
                    pu, puk = PS[ubank], "ps%d" % ubank

                    def u_mms():
                        for h in range(4):
                            c = h // 2
                            MM(pu[:, h * 128:(h + 1) * 128], khm_cur[:, c * 128:(c + 1) * 128], v_tm[:, tl, h * 128:(h + 1) * 128],
                               True, True, ["khm", "v_tm"], [puk], sig=(h == 3))

                    def o_mms():
                        for h in range(4):
                            c = h // 2
                            pob = PS[6 + h // 2]
                            ocol = (h % 2) * NB + tl * 128
                            outap = pob[:, ocol + csl.start:ocol + csl.stop]
                            MM(outap, Sb_cur[:, c, :], qm[:, h, gsl], True, False, [Sb_key, "qm"], ["ps%d" % (6 + h // 2)], sig=False)
                            MM(outap, v_tm[:, tl, h * 128:(h + 1) * 128], sc_bd[:, h, csl], False, True, ["v_tm", "sc_bd"],
                               ["ps%d" % (6 + h // 2)], sig=(h % 2 == 1))
                    if sample or gi > 0:
                        u_mms()
                        o_mms()
                    else:
                        o_mms()
                        u_mms()
                    ecol = tl * 128 + csl.stop - 1
                    for h in range(4):
                        c, h2 = h // 2, h % 2
                        rs_ = slice(h2 * 64, (h2 + 1) * 64)
                        if sample:
                            V("scalar_tensor_tensor", dict(
                                out=S0[s][rs_, c, :], in0=S0[s][rs_, c, :], scalar=eb[rs_, c, ecol:ecol + 1],
                                in1=pu[rs_, h * 128:(h + 1) * 128], op0=ALU.mult, op1=ALU.add), [puk, ("eb", c), ("S0", s)], [("S0", s)])
                        else:
                            V("scalar_tensor_tensor", dict(
                                out=S_bfs[wslot][rs_, c, :], in0=Sst[rs_, c, :], scalar=eb[rs_, c, ecol:ecol + 1],
                                in1=pu[rs_, h * 128:(h + 1) * 128], op0=ALU.mult, op1=ALU.add), [puk, ("eb", c), "Sst"], [("S_bf", wslot)])
                    if not sample:
                        for h in range(4):
                            c, h2 = h // 2, h % 2
                            rs_ = slice(h2 * 64, (h2 + 1) * 64)
                            V("scalar_tensor_tensor", dict(
                                out=Sst[rs_, c, :], in0=Sst[rs_, c, :], scalar=eb[rs_, c, ecol:ecol + 1],
                                in1=pu[rs_, h * 128:(h + 1) * 128], op0=ALU.mult, op1=ALU.add), [puk, ("eb", c), "Sst"], ["Sst"])
                    if sample:
                        ld("sync", sg_s[bidx].rearrange("(c h2) d v -> (h2 d) c v", c=2), S0[s][:], "o_sgs", reads=[("S0", s)])
                        if ru_q:
                            ru_piece(ru_q.pop(0))
                    else:
                        for _ in range(2):
                            if ru_q:
                                ru_piece(ru_q.pop(0))
            S.ctx = "p1.onorm"
            while ru_q:
                ru_piece(ru_q.pop(0))
            if nxt is not None:
                S.ctx = "p1.norm"
                fm_stats(lambda c: xT[:, c, nblk], nn_, sqb, rstd, tmpn, PS[0], "ps0", nxk, "m1", squares=False)
            S.ctx = "p1.onorm"
            for h in range(4):
                pob, pok = PS[6 + h // 2], "ps%d" % (6 + h // 2)
                osl = slice((h % 2) * NB, (h % 2) * NB + n)
                act(osq[:, h, 0:n], pob[:, osl], AF.Square, [pok], [("osq", h)])
            for h in range(4):
                pb, pk = rot()
                for tl in range(ntl):
                    rows = min(128, n - tl * 128)
                    zo = pb[:, tl * 128:tl * 128 + rows]
                    if sample:
                        MM(zo, svn[0:64, 0, h * 128:(h + 1) * 128], Wm_s[:, h, :], True, False, ["svn", "Wm_s"], [pk], sig=False)
                        bh = bs_hi[0:1, h * 128:h * 128 + 4].unsqueeze(1).broadcast_to([1, 16, 4])
                        bl = bs_lo[0:1, h * 128:h * 128 + 4].unsqueeze(1).broadcast_to([1, 16, 4])
                    else:
                        MM(zo, svn[:, tl, h * 128:(h + 1) * 128], Wm[:, h, :], True, False, ["svn", "Wm"], [pk], sig=False)
                        bh = bs_hi[0:1, h * 128:(h + 1) * 128]
                        bl = bs_lo[0:1, h * 128:(h + 1) * 128]
                    MM(zo, ones_b[0:1, :], bh, False, False, ["ones_b", "bs_hi"], [pk], sig=False)
                    MM(zo, ones_b[0:1, :], bl, False, True, ["ones_b", "bs_lo"], [pk], sig=True)
                V("tensor_tensor", dict(out=cat[:, 4 + h, 0:n], in0=pb[:, 0:n], in1=cat[:, 4 + h, 0:n], op=ALU.mult),
                  [pk, ("cat", 4 + h)], [("cat", 4 + h)])
            S.ctx = "p1.wout"
            S.ctx = "p1.onorm"
            tr2 = carve(R2, TR2_OFF, [128, 2, NB], F32)
            for hp in range(2):
                pob, pok = PS[6 + hp], "ps%d" % (6 + hp)
                pb, pk = rot()
                for h2_ in range(2):
                    h = 2 * hp + h2_
                    MM(pb[:, h2_ * NB:h2_ * NB + n], ones_b[:], osq[:, h, 0:n], True, True, ["ones_b", ("osq", h)], [pk], sig=(h2_ == 1))
                pv_ = pb[:].rearrange("p (a b) -> p a b", a=2)[:, :, 0:n]
                act(tr2[:, :, 0:n], pv_, AF.Ln, [pk], ["tmp_o", "rstd_o"], scale=1.0 / 128, bias=EPS)
                V("tensor_scalar", dict(out=tr2[:, :, 0:n], in0=tr2[:, :, 0:n], scalar1=-0.5, scalar2=None, op0=ALU.mult),
                  ["tmp_o", "rstd_o"], ["tmp_o", "rstd_o"])
                act(tr2[:, :, 0:n], tr2[:, :, 0:n], AF.Exp, ["tmp_o", "rstd_o"], ["tmp_o", "rstd_o"])
                for h2_ in range(2):
                    h = 2 * hp + h2_
                    osl = slice(h2_ * NB, h2_ * NB + n)
                    V("scalar_tensor_tensor", dict(out=tr2[:, h2_, 0:n], in0=pob[:, osl], scalar=gg[:, h:h + 1],
                                                   in1=tr2[:, h2_, 0:n], op0=ALU.mult, op1=ALU.mult),
                      [pok, "gg", "tmp_o", "rstd_o"], ["tmp_o", "rstd_o"])
                for h2_ in range(2):
                    h = 2 * hp + h2_
                    V("tensor_tensor", dict(out=cat[:, h, 0:n], in0=tr2[:, h2_, 0:n], in1=cat[:, h, 0:n], op=ALU.mult),
                      ["tmp_o", "rstd_o", ("cat", h)], [("cat", h)])
            catk = [("cat", j) for j in range(8)]
            def wout_part():
                S.ctx = "p1.wout"
                for oc in range(8):
                    bi_ = (2, 3, 0, 1)[oc % 4]
                    pb, pk = PS[bi_], "ps%d" % bi_
                    for c in range(KC):
                        MM(pb[:, 0:n], w_out_b[:, c, oc * 128:(oc + 1) * 128], cat[:, c, 0:n], c == 0, c == KC - 1, ["w_out"] + catk, [pk])
                    V("tensor_tensor", dict(out=xT[:, oc, blk], in0=pb[:, 0:n], in1=xT[:, oc, blk], op=ALU.add),
                      [pk] + list(xkeys), list(xkeys))
            return wout_part

        mblocks = [(b * NB, NB, [("xT", b)]) for b in range(TP // NB)] + [(TP, TS, [("xT", 8)])]
        for i_, (tk, nn, xk) in enumerate(mblocks):
            smp = (i_ == len(mblocks) - 1)
            if smp:
                ld("sync", sg_p.rearrange("(c h2) d v -> (h2 d) c v", c=2), Sst[:], "o_sgp", reads=["Sst"])
            if i_ <= 6:
                tl_ = [2 * i_ + 2, 2 * i_ + 3]
            elif i_ == 7:
                tl_ = [16]
            else:
                tl_ = []
            xf_ = [(lambda t=t: xtile(t, xs_slots[t % 2], ("xs", t % 2), "xs%d" % (t % 2), (0, 1), extra_w=["wv"])) for t in tl_]
            pw_ = mixer_block(tk, nn, xk, smp, pre=(i_ > 0), nxt=(mblocks[i_ + 1] if i_ + 1 < len(mblocks) else None),
                              prev_wout=(pw_ if i_ > 0 else None), xfill=xf_)
            if i_ == 7:
                load_w(wq_b, wq, "wq", 1024, writes=["wv", ("xs", 0), ("xs", 1)])
        pw_()

        if DEBUG:
            ld("sync", dbg1, xT[:], "o_dbg1", reads=[("xT", i) for i in range(9)])
        S.fence(["wo"])
        load_w(wo_b, wo, "wo", 1024)
        L = Lay(R3)
        hn = L.get([128, 8, NB], BF16); sqb2 = L.get([128, 8, NB], BF16)
        rstd2 = L.get([128, NB], F32); tmpn2 = L.get([128, NB], F32)
        qT = L.get([128, 8, NB], BF16)
        rinv = L.get([128, 256], F32); pn = L.get([128, 4, 256], BF16)
        pT = L.get([128, 4, 2, NB], BF16); oT = L.get([128, 8, NB], BF16)
        qTm = [L.get([128, 8, TS], BF16) for _ in range(2)]
        KTs = L.get([128, 8, 256], BF16)
        qT_s = L.get([128, 8, TS], BF16); pT_s = L.get([128, 4, 2, TS], BF16)
        kslot = [carve(R1, 49152 + i * 4096, [128, 2, 1024], BF16) for i in range(2)]
        p2keys = ["hn", "m2sqb", "m2rstd", "m2tmpn"] + [("qT", o_) for o_ in range(8)] + [("pn", 0), ("pn", 1), ("pn", 2), ("pn", 3), "pT", "oT", "qT_s", "pT_s", "rinv", ("KTs", 0), ("KTs", 1)] + [("sm8", h) for h in range(4)] + [("sm12", h) for h in range(4)] + [("smx", 0), ("smx", 1), ("qTm", 0), ("qTm", 1), "KTs",
                  ("kslot", 0), ("kslot", 1)]
        S.fence(p2keys)
        def slot_views(si):
            base = si * 24576
            return (carve(R1, base, [128, 8, 512], BF16), carve(R1, base + 8192, [128, 8, 512], BF16),
                    carve(R1, base + 16384, [128, 4, 1024], BF16))
        S.fence([("fslot", 0), ("fslot", 1)])

        def unit_pieces(u):
            f0, nf = UNITS[u]
            si = u % 2
            wg_v, wu_v, wd_v = slot_views(si)
            key = "fs%d" % si
            gsrc = w_gate.rearrange("(c p) n -> p c n", p=128)
            usrc = w_up.rearrange("(c p) n -> p c n", p=128)
            pcs = []
            pcs.append(lambda: ld("gpsimd", wg_v[:, :, 0:nf * 128], gsrc[:, :, f0 * 128:(f0 + nf) * 128], key, writes=[("fslot", si)]))
            pcs.append(lambda: ld("gpsimd", wu_v[:, :, 0:nf * 128], usrc[:, :, f0 * 128:(f0 + nf) * 128], key, writes=[("fslot", si)]))
            pcs.append(lambda: ld("gpsimd", wd_v[:, 0:nf, :],
                                  w_down[f0 * 128:(f0 + nf) * 128, :].rearrange("(l p) n -> p l n", p=128), key, writes=[("fslot", si)]))
            return pcs

        def load_unit(u):
            for p_ in unit_pieces(u):
                p_()

        def softmax_tile(rows, psA, psB, pkeys):
            for hh, pb in enumerate([psA, psB]):
                V("tensor_reduce", dict(out=smx[0:rows, 2 * hh:2 * hh + 2], in_=pb[0:rows, :].rearrange("p (h m) -> p h m", h=2),
                                        axis=AX.X, op=ALU.max, negate=True), [pkeys[hh]], [("smx", hh)])
            for h in range(4):
                pb = [psA, psB][h // 2]
                act(pn[0:rows, h, :], pb[0:rows, (h % 2) * 256:(h % 2 + 1) * 256], AF.Exp, [pkeys[h // 2], ("smx", h // 2)],
                    [("pn", h)], bias=smx[0:rows, h:h + 1])

        def attn_block(tok0, n, xkeys, prev_wo, sfill, pre=False, nxt=None):
            blk = slice(tok0, tok0 + n)
            sfill = list(sfill)

            def fill():
                if sfill:
                    sfill.pop(0)()
                if wpieces:
                    wpieces.pop(0)()
            S.ctx = "p2.norm"
            fm_norm(lambda c: xT[:, c, blk], n, G_X, lambda c: hn[:, c, 0:n], sqb2, rstd2, tmpn2, PS[0], "ps0",
                    xkeys, ["hn"], "m2", do_stats=not pre)
            rr[0] = 1
            S.ctx = "p2.q"
            for oc in range(8):
                bi_ = (0, 1, 2, 3)[oc % 4]
                pb, pk = PS[bi_], "ps%d" % bi_
                for c in range(KC):
                    MM(pb[:, 0:n], wq_b[:, c, oc * 128:(oc + 1) * 128], hn[:, c, 0:n], c == 0, c == KC - 1, ["wq", "hn"], [pk])
                act(qT[:, oc, 0:n], pb[:, 0:n], AF.Copy, [pk], [("qT", oc)], scale=1.0 / 16)
            fill()
            ntl = n // 128

            def scores(tl):
                S.ctx = "p2.sc"
                tsl = slice(tl * 128, (tl + 1) * 128)
                for h in range(4):
                    pb, pk = PS[2 + h // 2], "ps%d" % (2 + h // 2)
                    for dc in range(2):
                        MM(pb[:, (h % 2) * 256:(h % 2 + 1) * 256], qT[:, 2 * h + dc, tsl], KT[:, 2 * h + dc, :], dc == 0, dc == 1,
                           [("qT", 2 * h + dc), "KT"], [pk])

            def transposes(tl):
                S.ctx = "p2.pT"
                tsl = slice(tl * 128, (tl + 1) * 128)
                p4b = PS[4][:].bitcast(BF16)
                for h in range(4):
                    for mc in range(2):
                        TR(p4b[:, (h * 2 + mc) * 128:(h * 2 + mc + 1) * 128], pn[:, h, mc * 128:(mc + 1) * 128], ident_b[:],
                           [("pn", h), "ident_b"], ["ps4"], sig=(mc == 1))
                V("tensor_copy", dict(out=pT[:, :, :, tsl], in_=p4b.rearrange("p (h m t) -> p h m t", h=4, m=2)), ["ps4"], ["pT"])
            scores(0)
            for tl in range(ntl):
                S.ctx = "p2.smax"
                softmax_tile(128, PS[2], PS[3], ["ps2", "ps3"])
                if tl + 1 < ntl:
                    scores(tl + 1)
                if tl == 0 and prev_wo is not None:
                    prev_wo()
                transposes(tl)
                fill()
            if nxt is not None:
                S.ctx = "p2.norm"
                ntok0, nn_, nxk = nxt
                nblk = slice(ntok0, ntok0 + nn_)
                fm_squares(lambda c: xT[:, c, nblk], nn_, sqb2, nxk, "m2")
            S.ctx = "p2.pv"
            pvb = [0, 1, 4]
            pvi = [0]

            def rot3():
                i_ = pvb[pvi[0] % 3]
                pvi[0] += 1
                return PS[i_], "ps%d" % i_
            for h in range(4):
                pbs, pks = rot3()
                for mc in range(2):
                    MM(pbs[:, 0:n], ones_b[:], pT[:, h, mc, 0:n], mc == 0, mc == 1, ["ones_b", "pT"], [pks])
                V("reciprocal", dict(out=rinv[:, 0:n], in_=pbs[:, 0:n]), [pks], ["rinv"])
                for dc in range(2):
                    oc = 2 * h + dc
                    pb, pk = rot3()
                    for mc in range(2):
                        MM(pb[:, 0:n], vbf[:, mc, oc * 128:(oc + 1) * 128], pT[:, h, mc, 0:n], mc == 0, mc == 1,
                           ["vbf", "pT"], [pk], sig=(mc == 1))
                    V("tensor_tensor", dict(out=oT[:, oc, 0:n], in0=pb[:, 0:n], in1=rinv[:, 0:n], op=ALU.mult),
                      [pk, "rinv"], ["oT"])
            if nxt is not None:
                S.ctx = "p2.norm"
                fm_stats(lambda c: xT[:, c, nblk], nn_, sqb2, rstd2, tmpn2, PS[0], "ps0", nxk, "m2", squares=False)
            fill()
            for fl in sfill:
                fl()

            def wo_part():
                S.ctx = "p2.wo"
                for oc in range(8):
                    pb, pk = rot()
                    for c in range(KC):
                        MM(pb[:, 0:n], wo_b[:, c, oc * 128:(oc + 1) * 128], oT[:, c, 0:n], c == 0, c == KC - 1, ["wo", "oT"], [pk])
                    V("tensor_tensor", dict(out=xT[:, oc, blk], in0=pb[:, 0:n], in1=xT[:, oc, blk], op=ALU.add),
                      [pk] + list(xkeys), list(xkeys))
            return wo_part

        sblk = slice(TP, TP + TS)
        skeys = [("xT", 8)]

        def s_init():
            S.ctx = "p2.s_init"
            fm_norm(lambda c: xT[:, c, sblk], TS, G_X, lambda c: hn[:, c, 0:TS], sqb2, rstd2, tmpn2, PS[0], "ps0",
                    skeys, ["hn"], "m2")
            rr[0] = 1
            for oc in range(8):
                pb, pk = rot()
                for c in range(KC):
                    MM(pb[:, 0:TS], wq_b[:, c, oc * 128:(oc + 1) * 128], hn[:, c, 0:TS], c == 0, c == KC - 1, ["wq", "hn"], [pk])
                act(qT_s[:, oc, :], pb[:, 0:TS], AF.Copy, [pk], ["qT_s"], scale=1.0 / 16)
            G("memset", dict(ap=qTm[0][:], constant=0.0), [], [("qTm", 0)])
            G("memset", dict(ap=qTm[1][:], constant=0.0), [], [("qTm", 1)])

        def s_kpass(b):
            S.ctx = "p2.s_k"
            s = b % 2
            ld("gpsimd", kslot[s][:], ck[b].rearrange("(mt p) d -> p mt d", p=128), "ks%d" % s, writes=[("kslot", s)])
            if b >= 2:
                V("memset", dict(ap=qTm[s][:, :, (b - 2) * 4:(b - 2) * 4 + 4], constant=0.0), [], [("qTm", s)])
            V("tensor_copy", dict(out=qTm[s][:, :, b * 4:b * 4 + 4], in_=qT_s[:, :, b * 4:b * 4 + 4]), ["qT_s"], [("qTm", s)])
            for half in range(2):
                p45 = PS[4 + half][:].bitcast(BF16)
                for oc4 in range(4):
                    oc = half * 4 + oc4
                    for mt in range(2):
                        TR(p45[:, oc4 * 256 + mt * 128:oc4 * 256 + (mt + 1) * 128], kslot[s][:, mt, oc * 128:(oc + 1) * 128],
                           ident_b[:], [("kslot", s), "ident_b"], ["ps%d" % (4 + half)], sig=(oc4 == 3 and mt == 1))
                src = p45.rearrange("p (o m) -> p o m", o=4)
                if half == 0:
                    V("tensor_copy", dict(out=KTs[:, 0:4, :], in_=src), ["ps4"], [("KTs", 0)])
                else:
                    act(KTs[:, 4:8, :], src, AF.Copy, ["ps5"], [("KTs", 1)])
            for h in range(4):
                pb, pk = PS[6 + h // 2], "ps%d" % (6 + h // 2)
                for dc in range(2):
                    first = (b == 0 and h % 2 == 0 and dc == 0)
                    last = (b == 15 and dc == 1)
                    MM(pb[0:64, (h % 2) * 256:(h % 2 + 1) * 256], qTm[s][:, 2 * h + dc, :], KTs[:, 2 * h + dc, :], first, last,
                       [("qTm", s), ("KTs", h // 2)], [pk], sig=(dc == 1))

        def s_mid():
            S.ctx = "p2.s_mid"
            softmax_tile(64, PS[6], PS[7], ["ps6", "ps7"])
            p4b = PS[4][:].bitcast(BF16)
            for h in range(4):
                for mc in range(2):
                    TR(p4b[:, (h * 2 + mc) * 64:(h * 2 + mc + 1) * 64], pn[0:64, h, mc * 128:(mc + 1) * 128], ident_b[0:64, 0:64],
                       [("pn", h), "ident_b"], ["ps4"], sig=(mc == 1))
            V("tensor_copy", dict(out=pT_s[:], in_=p4b[:, 0:512].rearrange("p (h m t) -> p h m t", h=4, m=2)), ["ps4"], ["pT_s"])

        def s_vpass(b):
            S.ctx = "p2.s_v"
            s = b % 2
            ld("gpsimd", kslot[s][:], cv[b].rearrange("(mt p) d -> p mt d", p=128), "ks%d" % s, writes=[("kslot", s)])
            for oc in range(8):
                for mc in range(2):
                    MM(PS[5][:, oc * 64 + b * 4:oc * 64 + b * 4 + 4], kslot[s][:, mc, oc * 128:(oc + 1) * 128],
                       pT_s[:, oc // 2, mc, b * 4:b * 4 + 4], mc == 0, mc == 1, [("kslot", s), "pT_s"], ["ps5"],
                       sig=(oc == 7 and mc == 1))

        def s_fin():
            S.ctx = "p2.s_fin"
            pbs, pks = rot()
            for h in range(4):
                for mc in range(2):
                    MM(pbs[:, h * TS:(h + 1) * TS], ones_b[:], pT_s[:, h, mc, :], mc == 0, mc == 1, ["ones_b", "pT_s"], [pks], sig=(mc == 1))
            V("reciprocal", dict(out=rinv[:, 0:4 * TS], in_=pbs[:, 0:4 * TS]), [pks], ["rinv"])
            V("tensor_tensor", dict(out=oT[:, :, 0:TS].rearrange("p (h d) t -> p h d t", h=4),
                                    in0=PS[5][:].rearrange("p (h d t) -> p h d t", h=4, d=2),
                                    in1=rinv[:, 0:4 * TS].rearrange("p (h t) -> p h t", h=4).unsqueeze(2).broadcast_to([128, 4, 2, TS]),
                                    op=ALU.mult), ["ps5", "rinv"], ["oT"])
            for oc in range(8):
                pb, pk = rot()
                for c in range(KC):
                    MM(pb[:, 0:TS], wo_b[:, c, oc * 128:(oc + 1) * 128], oT[:, c, 0:TS], c == 0, c == KC - 1, ["wo", "oT"], [pk])
                V("tensor_tensor", dict(out=xT[:, oc, sblk], in0=pb[:, 0:TS], in1=xT[:, oc, sblk], op=ALU.add),
                  [pk] + skeys, skeys)

        s_init()
        prev_wo = None
        wpieces = unit_pieces(0) + unit_pieces(1)
        for b in range(TP // NB):
            if b < 4:
                sf = [(lambda bb=bb: s_kpass(bb)) for bb in range(4 * b, 4 * b + 4)]
            else:
                sf = [(lambda bb=bb: s_vpass(bb)) for bb in range(4 * (b - 4), 4 * (b - 4) + 4)]
            nxt_ = ((b + 1) * NB, NB, [("xT", b + 1)]) if b + 1 < TP // NB else None
            prev_wo = attn_block(b * NB, NB, [("xT", b)], prev_wo, sf, pre=(b > 0), nxt=nxt_)
            if b == 3:
                s_mid()
        prev_wo()
        for p_ in wpieces:
            p_()
        s_fin()

        if DEBUG:
            ld("sync", dbg2, xT[:], "o_dbg2", reads=[("xT", i) for i in range(9)])
        hnA = carve(R3, 0, [128, 8, T], BF16)
        L = Lay(R2)
        sqb3 = L.get([128, 8, 256], BF16)
        rstd3 = L.get([128, 256], F32); tmpn3 = L.get([128, 256], F32)
        gsbs = [L.get([128, NB3 + 2], F32) for _ in range(2)]
        cbs = [L.get([128, NB3], F32) for _ in range(2)]
        actbs = [L.get([128, 4, NB3], BF16) for _ in range(2)]
        sconvT = L.get([128, NF, 32], F32)
        gkeep = sconvT
        gsss = [L.get([128, 16, 6], F32) for _ in range(2)]
        ubs = [L.get([128, NB3], BF16) for _ in range(2)]
        stgs = [carve(R1, 49152 + i_ * 4096, [128, 1024], F32) for i_ in range(2)]
        stg3 = stgs[0]
        gfin_bc = L.get([128, 1024], F32)
        fst = L.get([128, 16], F32)
        p3keys = [("hnA", i_) for i_ in range(5)] + [("gsbh", 0), ("gsbh", 1), ("gssh", 0), ("gssh", 1), "m3sqb", "m3rstd", "m3tmpn", ("gsb", 0), ("gsb", 1), ("cb", 0), ("cb", 1), "sconvT", ("gss", 0), ("gss", 1), ("ub", 0), ("ub", 1), "stg3", ("stg", 1), "gfin_bc", ("fst", 0), ("fst", 1)] + \
                 [("actb", a_, l_) for a_ in range(2) for l_ in range(4)]
        S.fence(p3keys)
        ld("sync", gfin_bc[:], g_fin.broadcast_to([128, D]), "gfin", writes=["gfin_bc"])
        for half in range(3):
            c0, c1 = half * 1024, min(DFF, (half + 1) * 1024)
            ld("sync", stg3[0:32, 0:c1 - c0], sconv[:, c0:c1], "stg3", writes=["stg3"])
            nfh = (c1 - c0) // 128
            for j in range(nfh):
                TR(PS[7][:, j * 32:(j + 1) * 32], stg3[0:32, j * 128:(j + 1) * 128], ident_f[0:32, 0:32], ["stg3", "ident"], ["ps7"],
                   sig=(j == nfh - 1))
            V("tensor_copy", dict(out=sconvT[:, half * 8:half * 8 + nfh, :],
                                  in_=PS[7][:, 0:nfh * 32].rearrange("p (f t) -> p f t", t=32)), ["ps7"], ["sconvT"])
        blocks3 = [(b * NB3, NB3, [("xT", 2 * b), ("xT", 2 * b + 1)], False) for b in range(TP // NB3)] + [(TP, TS, [("xT", 8)], True)]
        gcnt = [0]

        def ffn_front(u, bi, ab, fillers=()):
            S.ctx = "p3.front"
            tails = {}
            f0, nf = UNITS[u]
            si = u % 2
            wg_v, wu_v, wd_v = slot_views(si)
            fk = ("fslot", si)
            tok0, n, xkeys, sample = blocks3[bi]
            blk = slice(tok0, tok0 + n)
            if u == 0:
                for s0 in range(0, n, 256):
                    nn = min(256, n - s0)
                    sub = slice(tok0 + s0, tok0 + s0 + nn)
                    fm_norm(lambda c: xT[:, c, sub], nn, G_FFN, lambda c: hnA[:, c, sub], sqb3, rstd3, tmpn3, PS[0], "ps0",
                            xkeys, [("hnA", bi)], "m3")
            hk = ("hnA", bi)
            for lf in range(nf):
                f = f0 + lf
                gi = gcnt[0] % 2
                gcnt[0] += 1
                gsb, cb = gsbs[gi], cbs[gi]
                gk, ck_ = ("gsb", gi), ("cb", gi)
                pg, pgk = PS[1 + (lf % 2)], "ps%d" % (1 + lf % 2)
                pub_ = 3 if u == len(UNITS) - 1 else 3 + (lf % 2)
                pu, puk = PS[pub_], "ps%d" % pub_
                for c in range(KC):
                    MM(pg[:, 0:n], wg_v[:, c, lf * 128:(lf + 1) * 128], hnA[:, c, blk], c == 0, c == KC - 1, [fk, hk], [pgk])
                for c in range(KC):
                    MM(pu[:, 0:n], wu_v[:, c, lf * 128:(lf + 1) * 128], hnA[:, c, blk], c == 0, c == KC - 1, [fk, hk], [puk])
                cw = lambda j, f=f: colv2[:, j * NF + f:j * NF + f + 1]
                cbias = colv1[:, 46 + f:47 + f]
                ghk = ("gsbh", gi)
                if not sample:
                    V("tensor_copy", dict(out=gsb[:, 0:2], in_=ghist[:, f, :]), ["ghist"], [ghk])
                    act(gsb[:, 2:2 + n], pg[:, 0:n], AF.Copy, [pgk], [gk])
                    V("tensor_copy", dict(out=ghist[:, f, :], in_=gsb[:, n:n + 2]), [gk], ["ghist"])
                    g0, g1, g2 = gsb[:, 0:n], gsb[:, 1:1 + n], gsb[:, 2:2 + n]
                    cbo = cb[:, 0:n]
                else:
                    gk = ("gss", gi)
                    ghk = ("gssh", gi)
                    gss = gsss[gi]
                    V("tensor_copy", dict(out=gss[:, :, 0:2], in_=sconvT[:, f, :].rearrange("p (b j) -> p b j", j=2)),
                      ["sconvT"], [ghk])
                    act(gss[:, :, 2:6], pg[:, 0:n].rearrange("p (b t) -> p b t", t=4), AF.Copy, [pgk], [gk])
                    V("tensor_copy", dict(out=gkeep[:, f, :].rearrange("p (b j) -> p b j", j=2), in_=gss[:, :, 4:6]),
                      [gk], ["sconvT"])
                    g0, g1, g2 = gss[:, :, 0:4], gss[:, :, 1:5], gss[:, :, 2:6]
                    cbo = cb[:, 0:n].rearrange("p (b t) -> p b t", t=4)
                ub, ubk = ubs[gi], ("ub", gi)
                act(ub[:, 0:n], pu[:, 0:n], AF.Copy, [puk], [ubk])

                def ident(cbo=cbo, g2=g2, gk=gk, ck_=ck_, cw=cw, cbias=cbias):
                    act(cbo, g2, AF.Identity, [gk, "colv1", "colv2"], [ck_], scale=cw(2), bias=cbias)

                S.ctx = "p3.front"

                def tail(cbo=cbo, g1=g1, g0=g0, cw=cw, gk=gk, ghk=ghk, ck_=ck_, cb=cb, ub=ub, ubk=ubk, lf=lf):
                    S.ctx = "p3.chain"
                    V("scalar_tensor_tensor", dict(out=cbo, in0=g1, scalar=cw(1), in1=cbo, op0=ALU.mult, op1=ALU.add), [gk, ghk, ck_], [ck_])
                    V("scalar_tensor_tensor", dict(out=cbo, in0=g0, scalar=cw(0), in1=cbo, op0=ALU.mult, op1=ALU.add), [gk, ghk, ck_], [ck_])
                    act(cb[:, 0:n], cb[:, 0:n], AF.Gelu_apprx_tanh, [ck_], [ck_])
                    V("tensor_tensor", dict(out=actbs[ab][:, lf, 0:n], in0=cb[:, 0:n], in1=ub[:, 0:n], op=ALU.mult),
                      [ck_, ubk], [("actb", ab, lf)])
                tails[lf] = tail
                if lf >= 1:
                    tails[lf - 1]()
                ident()
                if fillers:
                    fillers.pop(0)()
                S.ctx = "p3.front"
            return tails[nf - 1]

        fcnt = [0]

        def final_out(bi):
            tok0, n, xkeys, sample = blocks3[bi]
            tiles = list(range(0, n, 128))
            As, Bs = [], []
            for s0 in tiles:
                par = fcnt[0] % 2
                fcnt[0] += 1
                As.append(lambda s0=s0, par=par: final_A(tok0, n, xkeys, s0, par))
                Bs.append([(lambda s0=s0, par=par, half=half: final_B(tok0, n, xkeys, sample, s0, par, half)) for half in range(2)])
            seq = []
            nt = len(tiles)
            seq.append(As[0])
            if nt > 1:
                seq.append(As[1])
            for i_ in range(nt):
                seq += Bs[i_]
                if i_ + 2 < nt:
                    seq.append(As[i_ + 2])
            return seq

        def final_A(tok0, n, xkeys, s0, par):
            S.ctx = "p3.final"
            rows = min(128, n - s0)
            tsl = slice(tok0 + s0, tok0 + s0 + rows)
            fo = par * 8
            act(sqb3[:, :, 0:rows], xT[:, :, tsl], AF.Square, list(xkeys), ["m3sqb"])
            for c in range(KC):
                MM(PS[4][0:rows, 0:1], sqb3[:, c, 0:rows], ones_b[:, 0:1], c == 0, c == KC - 1, ["m3sqb", "ones_b"], ["ps4"])
            act(fst[0:rows, fo + 3:fo + 4], PS[4][0:rows, 0:1], AF.Ln, ["ps4"], [("fst", par)], scale=1.0 / D, bias=EPS)
            V("tensor_scalar", dict(out=fst[0:rows, fo + 4:fo + 5], in0=fst[0:rows, fo + 3:fo + 4], scalar1=-0.5, scalar2=None, op0=ALU.mult),
              [("fst", par)], [("fst", par)])
            act(fst[0:rows, fo + 5:fo + 6], fst[0:rows, fo + 4:fo + 5], AF.Exp, [("fst", par)], [("fst", par)])

        def final_B(tok0, n, xkeys, sample, s0, par, half):
            S.ctx = "p3.final"
            rows = min(128, n - s0)
            tsl = slice(tok0 + s0, tok0 + s0 + rows)
            fo = par * 8
            stg_, sk_ = stgs[par], ("stg3" if par == 0 else ("stg", 1))
            bank_ = 7 if half == 0 else 0
            pb, pk = PS[bank_], "ps%d" % bank_
            for c4 in range(4):
                c = half * 4 + c4
                TR(pb[0:rows, c4 * 128:(c4 + 1) * 128], xT[:, c, tsl], ident_f[:], list(xkeys) + ["ident"], [pk], sig=(c4 == 3))
            V("scalar_tensor_tensor", dict(out=stg_[0:rows, half * 512:(half + 1) * 512], in0=pb[0:rows, :], scalar=fst[0:rows, fo + 5:fo + 6],
                                           in1=gfin_bc[0:rows, half * 512:(half + 1) * 512], op0=ALU.mult, op1=ALU.mult),
              [pk, ("fst", par), "gfin_bc"], [sk_])
            if half == 1:
                dst = y_s if sample else y_p[tok0 + s0:tok0 + s0 + rows, :]
                ld("sync", dst, stg_[0:rows, :], "o_y%d" % par, reads=[sk_])

        def ffn_back(u, bi, ab, fillers=None):
            S.ctx = "p3.back"
            f0, nf = UNITS[u]
            si = u % 2
            wg_v, wu_v, wd_v = slot_views(si)
            fk = ("fslot", si)
            tok0, n, xkeys, sample = blocks3[bi]
            blk = slice(tok0, tok0 + n)
            dbanks = [5, 6]
            if u != len(UNITS) - 1:
                dbanks.append(7)
                if u > 0:
                    dbanks.append(0)
            for oc in range(8):
                bi_ = dbanks[oc % len(dbanks)]
                pb, pk = PS[bi_], "ps%d" % bi_
                for lf in range(nf):
                    MM(pb[:, 0:n], wd_v[:, lf, oc * 128:(oc + 1) * 128], actbs[ab][:, lf, 0:n], lf == 0, lf == nf - 1,
                       [fk] + [("actb", ab, l) for l in range(nf)], [pk])
                V("tensor_tensor", dict(out=xT[:, oc, blk], in0=pb[:, 0:n], in1=xT[:, oc, blk], op=ALU.add),
                  [pk] + list(xkeys), list(xkeys))
                if fillers:
                    fillers.pop(0)()
                    S.ctx = "p3.back"

        def conv_state_out(f0, nf):
            S.ctx = "p3.cso"
            stgp = stgs[1]
            for j in range(nf):
                TR(PS[0][0:2, j * 128:(j + 1) * 128], ghist[:, f0 + j, :], ident_f[:], ["ghist", "ident"], ["ps0"], sig=(j == nf - 1))
            V("tensor_copy", dict(out=stgp[0:2, 0:nf * 128], in_=PS[0][0:2, 0:nf * 128]), ["ps0"], [("stg", 1)])
            ld("sync", sc_p[:, f0 * 128:(f0 + nf) * 128], stgp[0:2, 0:nf * 128], "o_scp", reads=[("stg", 1)])
            for j in range(nf):
                TR(PS[7][0:32, j * 128:(j + 1) * 128], gkeep[:, f0 + j, :], ident_f[:], ["sconvT", "ident"], ["ps7"], sig=(j == nf - 1))
            V("tensor_copy", dict(out=stg3[0:32, 0:nf * 128], in_=PS[7][0:32, 0:nf * 128]), ["ps7"], ["stg3"])
            ld("sync", sc_s[:, f0 * 128:(f0 + nf) * 128], stg3[0:32, 0:nf * 128], "o_scs", reads=["stg3"])

        abc = 0
        for u in range(len(UNITS)):
            prev = None
            lastu = (u == len(UNITS) - 1)
            pend = []
            for bi in range(len(blocks3)):
                ab = abc % 2
                abc += 1
                deferred = ffn_front(u, bi, ab, fillers=pend)
                if prev is not None:
                    ffn_back(u, *prev, fillers=pend)
                    if lastu:
                        pend += final_out(prev[0])
                deferred()
                prev = (bi, ab)
            ffn_back(u, *prev, fillers=pend)
            if lastu:
                pend += final_out(prev[0])
            for fl in pend:
                fl()
            conv_state_out(*UNITS[u])
            if u + 2 < len(UNITS):
                load_unit(u + 2)
        _NC_CACHE["sched"] = S
        sems = {n: es.enter_context(nc.semaphore(n)) for n in sorted(S.sem_names)}
        final = {"sync": [(s, v) for s, v in S.dma_cnt.items() if s.startswith("D_o_")]}
        with nc.Block() as block:
            S.emit(block, sems, final)
    return nc


_NC_CACHE = {}


def kernel(**inp):
    f = lambda a: np.ascontiguousarray(np.asarray(a, dtype=np.float32))
    x_prompt, x_sample, mem_prompt = f(inp["x_prompt"]), f(inp["x_sample"]), f(inp["mem_prompt"])
    state_gla, state_conv = f(inp["state_gla"])[0], f(inp["state_conv"])[0]
    ckk, cvv = f(inp["cache_mem_k"])[0], f(inp["cache_mem_v"])[0]
    vec1 = np.concatenate([f(inp["g_mix"]).reshape(8, 128), f(inp["g_x"]).reshape(8, 128), f(inp["g_mem"]).reshape(8, 128),
                           f(inp["g_ffn"]).reshape(8, 128), f(inp["g_final"]).reshape(8, 128), f(inp["b_alpha"]).reshape(2, 128),
                           f(inp["g_gla_out"]).reshape(4, 128), f(inp["conv_b"]).reshape(22, 128)], axis=0)
    vec2 = f(inp["conv_w"]).reshape(66, 128)
    shared = {
        "w_in": f(inp["w_in"])[0], "w_alpha": f(inp["w_alpha"])[0], "w_s": f(inp["w_s"])[0],
        "b_s": f(inp["b_s"]).reshape(1, 512), "g_sgu": f(inp["g_sgu"]).reshape(1, 512),
        "w_out": f(inp["w_out"])[0], "wq": f(inp["wq_x"])[0], "wk": f(inp["wk_x"])[0], "wv": f(inp["wv_x"])[0], "wo": f(inp["wo_x"])[0],
        "w_gate": f(inp["w_gate"])[0], "w_up": f(inp["w_up"])[0], "w_down": f(inp["w_down"])[0],
        "vec1": np.ascontiguousarray(vec1), "vec2": np.ascontiguousarray(vec2), "g_fin": f(inp["g_final"]).reshape(1, D),
    }
    for k, v in host_consts().items():
        shared["c_" + k] = v
    in_maps = []
    for c in range(NCORES):
        m = dict(shared)
        sl = slice(c * 16, (c + 1) * 16)
        m["x_p"] = x_prompt[c]
        m["x_s"] = np.ascontiguousarray(x_sample[sl].reshape(TS, D))
        m["mem"] = mem_prompt[c]
        m["sgla"] = np.ascontiguousarray(state_gla[sl])
        m["sconv"] = np.ascontiguousarray(state_conv[sl].reshape(32, DFF))
        m["ck"] = np.ascontiguousarray(ckk[sl].reshape(16, 256, D))
        m["cv"] = np.ascontiguousarray(cvv[sl].reshape(16, 256, D))
        in_maps.append(m)
    if "nc" not in _NC_CACHE:
        _NC_CACHE["nc"] = build_nc()
    res = run_bass_kernel_spmd(_NC_CACHE["nc"], in_maps, core_ids=list(range(NCORES)))
    R = res.results
    cat = lambda k: np.stack([np.asarray(r[k], dtype=np.float32) for r in R], axis=0)
    y_prompt = cat("y_p")
    y_sample = cat("y_s").reshape(128, 4, D)
    sg_p = cat("sg_p")[None]
    sc_p = cat("sc_p")[None]
    mk_p = cat("mk_p").reshape(1, 8, 256, 4, 256)
    mv_p = cat("mv_p").reshape(1, 8, 256, 4, 256)
    sg_s = cat("sg_s").reshape(1, 128, 4, 64, 128)
    sc_s = cat("sc_s").reshape(1, 128, 2, DFF)
    sv_s = cat("sv_s").reshape(1, 128, 4, 4, 128)
    if DEBUG:
        _NC_CACHE["dbg"] = (R[0]["dbg1"], R[0]["dbg2"])
    return (y_prompt, y_sample, sg_p, sc_p, mk_p, mv_p, sg_s, sc_s, sv_s)
```

```python
import numpy as np
from contextlib import ExitStack
import concourse.bass as bass
import concourse.mybir as mybir
from concourse.bass_utils import run_bass_kernel_spmd

F32 = mybir.dt.float32
BF16 = mybir.dt.bfloat16
U8 = mybir.dt.uint8
AF = mybir.ActivationFunctionType
ALU = mybir.AluOpType
AX = mybir.AxisListType

NCORES = 8
TP, TS = 2048, 64
T = TP + TS
D, KC = 1024, 8
DIN = 2576
CQ, CK, CV, CR, CA, CU, CSV = 0, 256, 512, 1024, 1536, 1552, 2064
DFF, NF = 2816, 22
UNITS = [(0, 4), (4, 4), (8, 4), (12, 4), (16, 3), (19, 3)]
EPS = 1e-6
NB = 256
NB3 = 512
DEBUG = False


class Sched:
    ENGS = ["tensor", "vector", "scalar", "gpsimd", "sync"]

    def __init__(self):
        self.ops = {e: [] for e in self.ENGS}
        self.cnt = {e: 0 for e in self.ENGS}
        self.pending = {e: False for e in self.ENGS}
        self.writers, self.readers = {}, {}
        self.seen = {e: {} for e in self.ENGS}
        self.dma_cnt = {}
        self.sem_names = set()
        self.tags = {e: [] for e in self.ENGS}
        self.ctx = ""
        self.dma_hist = {}

    def _deps(self, eng, reads, writes):
        toks = {}

        def add(d):
            for s, v in d.items():
                if v > toks.get(s, 0):
                    toks[s] = v
        for k in reads:
            add(self.writers.get(k, {}))
        for k in writes:
            add(self.writers.get(k, {}))
            add(self.readers.get(k, {}))
        waits = []
        for s, v in toks.items():
            if s == "E_" + eng and eng != "gpsimd":
                continue
            if self.seen[eng].get(s, 0) >= v:
                continue
            self.seen[eng][s] = v
            waits.append((s, v))
        return waits

    def _commit(self, tok, reads, writes):
        s, v = tok
        for k in reads:
            r = self.readers.setdefault(k, {})
            r[s] = max(r.get(s, 0), v)
        for k in writes:
            self.writers[k] = {s: v}
            self.readers[k] = {}

    def op(self, eng, fn, reads=(), writes=(), sig=True):
        isps = lambda k: isinstance(k, str) and k.startswith("ps") and k[2:].isdigit()
        writes = list(writes) + [k for k in reads if isps(k)]
        reads = [k for k in reads if not isps(k)]
        waits = self._deps(eng, reads, writes)
        s = "E_" + eng
        self.sem_names.add(s)
        if sig:
            self.cnt[eng] += 1
            self.pending[eng] = False
            tok = (s, self.cnt[eng])
        else:
            self.pending[eng] = True
            tok = (s, self.cnt[eng] + 1)
        self.ops[eng].append((fn, waits, tok, 1 if sig else 0))
        self.tags[eng].append(self.ctx)
        self._commit(tok, reads, writes)

    def dma(self, eng, fn, key, reads=(), writes=()):
        waits = self._deps(eng, reads, writes)
        s = "D_" + key
        self.sem_names.add(s)
        self.dma_cnt[s] = self.dma_cnt.get(s, 0) + 16
        tok = (s, self.dma_cnt[s])
        hist = self.dma_hist.setdefault(eng, [])
        lim = 16 if eng == "gpsimd" else 24
        if len(hist) >= lim:
            os_, ov_ = hist[-lim]
            if self.seen[eng].get(os_, 0) < ov_:
                self.seen[eng][os_] = ov_
                waits = list(waits) + [(os_, ov_)]
        hist.append(tok)
        self.ops[eng].append((fn, waits, tok, 2))
        self.tags[eng].append(self.ctx)
        self._commit(tok, reads, writes)

    def fence(self, keys):
        for e in self.ENGS:
            assert not self.pending[e], e
        d = {"E_" + e: self.cnt[e] for e in self.ENGS if self.cnt[e] > 0}
        d.update(self.dma_cnt)
        for k in keys:
            self.writers[k] = dict(d)
            self.readers[k] = {}

    def emit(self, block, sems, final_waits):
        def body(eng_name):
            def f(eng):
                for fn, waits, tok, kind in self.ops[eng_name]:
                    for s, v in waits:
                        eng.wait_ge(sems[s], v)
                    r = fn(eng)
                    if kind == 2:
                        r.then_inc(sems[tok[0]], 16)
                    elif kind == 1:
                        r.then_inc(sems[tok[0]], 1)
                for s, v in final_waits.get(eng_name, []):
                    eng.wait_ge(sems[s], v)
            return f
        block.tensor(body("tensor"))
        block.vector(body("vector"))
        block.scalar(body("scalar"))
        block.gpsimd(body("gpsimd"))
        block.sync(body("sync"))


def host_consts():
    c = {}
    c["ident"] = np.eye(128, dtype=np.float32)
    r = np.arange(128)
    c["mask128"] = ((r[:, None] <= r[None, :]) & ((r[:, None] // 64) == (r[None, :] // 64))).astype(np.float32)
    r64 = np.arange(64)
    c["mask_s"] = ((r64[:, None] <= r64[None, :]) & ((r64[:, None] // 4) == (r64[None, :] // 4))).astype(np.float32)
    rm = np.ones((128, NB), np.float32)
    rm[:, 0::64] = 0.0
    c["rm"] = rm
    rms = np.ones((128, TS), np.float32)
    rms[:, 0::4] = 0.0
    c["rm_s"] = rms
    c["trilT"] = (r[:, None] <= r[None, :]).astype(np.float32)
    bm = np.zeros((128, 16), np.float32)
    bm[r64, r64 // 4] = 1.0
    c["bm"] = bm
    sel = np.zeros((4, 64), np.float32)
    sel[r64 % 4, r64] = 1.0
    c["sel"] = sel
    return c


def build_nc():
    nc = bass.Bass("TRN2", target_bir_lowering=False)

    def din(name, shape):
        return nc.dram_tensor(name, list(shape), F32, kind="ExternalInput").ap()

    def dout(name, shape):
        return nc.dram_tensor(name, list(shape), F32, kind="ExternalOutput").ap()

    x_p = din("x_p", [TP, D]); x_s = din("x_s", [TS, D]); mem = din("mem", [256, D])
    sgla = din("sgla", [16, 4, 64, 128]); sconv = din("sconv", [32, DFF])
    ck = din("ck", [16, 256, D]); cv = din("cv", [16, 256, D])
    w_in = din("w_in", [D, DIN]); w_alpha = din("w_alpha", [16, 256])
    w_s = din("w_s", [4, 128, 128]); b_s = din("b_s", [1, 512]); g_sgu = din("g_sgu", [1, 512])
    w_out = din("w_out", [D, D]); wq = din("wq", [D, D]); wk = din("wk", [D, D]); wv = din("wv", [D, D]); wo = din("wo", [D, D])
    w_gate = din("w_gate", [D, DFF]); w_up = din("w_up", [D, DFF]); w_down = din("w_down", [DFF, D])
    vec1 = din("vec1", [68, 128]); vec2 = din("vec2", [66, 128]); g_fin = din("g_fin", [1, D])
    cst = {k: din("c_" + k, v.shape) for k, v in host_consts().items()}

    y_p = dout("y_p", [TP, D]); y_s = dout("y_s", [TS, D]); sg_p = dout("sg_p", [4, 64, 128]); sc_p = dout("sc_p", [2, DFF])
    mk_p = dout("mk_p", [256, D]); mv_p = dout("mv_p", [256, D]); sg_s = dout("sg_s", [16, 4, 64, 128])
    sc_s = dout("sc_s", [32, DFF]); sv_s = dout("sv_s", [TS, 512])

    if DEBUG:
        dbg1 = dout("dbg1", [128, KC, T]); dbg2 = dout("dbg2", [128, KC, T])
    S = Sched()
    es = ExitStack()
    with es:
        def sb(name, shape, dt):
            return es.enter_context(nc.sbuf_tensor(name, list(shape), dt))
        xT = sb("xT", [128, KC, T], F32)
        R1 = sb("R1", [128, 57600], U8)
        R2 = sb("R2", [128, 32768], U8)
        R3 = sb("R3", [128, 33792], U8)
        PS = [es.enter_context(nc.psum_tensor("ps%d" % i, [128, 512], F32)) for i in range(8)]

        def carve(arena, off, shape, dt):
            esz = 4 if dt == F32 else 2
            n = int(np.prod(shape[1:]))
            v = arena[:, off:off + n * esz].bitcast(dt)
            if len(shape) == 3:
                v = v.rearrange("p (a b) -> p a b", a=shape[1])
            elif len(shape) == 4:
                v = v.rearrange("p (a b c) -> p a b c", a=shape[1], b=shape[2])
            return v

        class Lay:
            def __init__(self, arena, base=0):
                self.arena, self.off = arena, base

            def get(self, shape, dt):
                esz = 4 if dt == F32 else 2
                v = carve(self.arena, self.off, shape, dt)
                self.off += int(np.prod(shape[1:])) * esz
                self.off = (self.off + 63) // 64 * 64
                assert self.off <= self.arena.shape[1], (self.off, self.arena.shape)
                return v

        ident_f = sb("ident_f", [128, 128], F32); ident_b = sb("ident_b", [128, 128], BF16)
        ones_b = sb("ones_b", [128, 128], BF16)
        mask128 = sb("mask128", [128, 128], F32); mask_s = sb("mask_s", [64, 64], F32)
        rm = sb("rm", [128, NB], F32); rm_s = sb("rm_s", [128, TS], F32)
        bm = sb("bm", [128, 16], F32); sel = sb("sel", [4, 64], F32)
        colv1 = sb("colv1", [128, 68], F32); colv2 = sb("colv2", [128, 66], F32)
        nba = sb("nba", [128, 2], F32); gg = sb("gg", [128, 4], F32)
        gsgu_bc = sb("gsgu_bc", [128, 512], F32)
        Wm = sb("Wm", [128, 4, 128], BF16); Wm_s = sb("Wm_s", [64, 4, 64], BF16)
        bs_hi = sb("bs_hi", [1, 512], BF16); bs_lo = sb("bs_lo", [1, 512], BF16)
        walpha_b = sb("walpha_b", [16, 256], BF16)
        ghist = sb("ghist", [128, NF, 2], F32)
        KT = sb("KT", [128, 8, 256], BF16); vbf = sb("vbf", [128, 2, 1024], BF16)
        smx = sb("smx", [128, 4], F32)
        Sst = sb("Sst", [128, 2, 128], F32); S_bf = sb("S_bf", [128, 2, 128], BF16); S_bfB = sb("S_bfB", [128, 2, 128], BF16)
        S_bfs = [S_bf, S_bfB]
        gch = [0]

        G_MIX, G_X, G_MEM, G_FFN, G_FIN = 0, 8, 16, 24, 32

        def ld(eng, out, in_, key, reads=(), writes=(), **kw):
            S.dma(eng, lambda e: e.dma_start(out=out, in_=in_, **kw), key, reads=reads, writes=writes)

        def V(name, kw, reads=(), writes=()):
            S.op("vector", lambda e: getattr(e, name)(**kw), reads, writes)

        def A(fn, reads=(), writes=()):
            S.op("scalar", fn, reads, writes)

        def G(name, kw, reads=(), writes=()):
            S.op("gpsimd", lambda e: getattr(e, name)(**kw), reads, writes)

        def MM(out, lhsT, rhs, start, stop, reads, writes, sig=None):
            if sig is None:
                sig = stop
            S.op("tensor", lambda e: e.matmul(out, lhsT=lhsT, rhs=rhs, start=start, stop=stop), reads, writes, sig=sig)

        def TR(out, in_, ident, reads, writes, sig=True):
            S.op("tensor", lambda e: e.transpose(out=out, in_=in_, identity=ident), reads, writes, sig=sig)

        def act(out, in_, func, reads, writes, **kw):
            A(lambda e: e.activation(out=out, in_=in_, func=func, **kw), reads, writes)

        for nm, t in [("ident", ident_f), ("mask128", mask128), ("mask_s", mask_s), ("rm", rm), ("rm_s", rm_s),
                      ("bm", bm), ("sel", sel)]:
            ld("sync", t[:], cst[nm], "c_" + nm, writes=[nm])
        ld("sync", gsgu_bc[:], g_sgu.broadcast_to([128, 512]), "gsgu", writes=["gsgu_bc"])
        ld("gpsimd", walpha_b[:], w_alpha, "walpha", writes=["walpha_b"])
        V("tensor_copy", dict(out=ident_b[:], in_=ident_f[:]), ["ident"], ["ident_b"])
        G("memset", dict(ap=ones_b[:], constant=1.0), [], ["ones_b"])
        G("memset", dict(ap=ghist[:], constant=0.0), [], ["ghist"])
        G("memset", dict(ap=Sst[:], constant=0.0), [], ["Sst"])
        G("memset", dict(ap=S_bf[:], constant=0.0), [], [("S_bf", 0)])

        wk_b = carve(R2, 0, [128, 8, 1024], BF16); wv_b = carve(R2, 16384, [128, 8, 1024], BF16)
        w_in_b = carve(R1, 0, [128, 8, DIN], BF16); w_out_b = carve(R1, 41216, [128, 8, 1024], BF16)

        def load_w(dst, src, key, ncols, reads=(), writes=()):
            srcv = src.rearrange("(c p) n -> p c n", p=128)
            nck = srcv.shape[1]
            step = 8 if ncols <= 1024 else 4
            for c0 in range(0, nck, step):
                ld("gpsimd", dst[:, c0:c0 + step, :], srcv[:, c0:c0 + step, :], key, reads=reads, writes=[key] + list(writes))

        load_w(wk_b, wk, "wk", 1024)
        load_w(wv_b, wv, "wv", 1024)
        load_w(w_in_b, w_in, "w_in", DIN)
        load_w(w_out_b, w_out, "w_out", 1024)

        S.ctx = "p0"
        L = Lay(R3)
        stg = [L.get([128, 1024], F32) for _ in range(2)]
        memT = L.get([128, 8, 256], F32)
        mnT = L.get([128, 8, 256], BF16)
        sqb0 = L.get([128, 8, 256], BF16)
        rstd0 = L.get([128, 256], F32); tmpn0 = L.get([128, 256], F32)
        Wm32 = L.get([128, 4, 128], F32); wsl = L.get([128, 4, 128], F32)
        vst = L.get([128, 128], F32)
        bsf = L.get([128, 512], F32)
        trilT = L.get([128, 128], F32)
        ld("sync", trilT[:], cst["trilT"], "c_trilT", writes=["trilT"])
        ld("sync", bsf[0:1, :], b_s, "bsf", writes=["bsf"])
        act(bs_hi[:], bsf[0:1, :], AF.Copy, ["bsf"], ["bs_hi"])
        V("tensor_tensor", dict(out=bs_lo[:], in0=bsf[0:1, :], in1=bs_hi[:], op=ALU.subtract), ["bsf", "bs_hi"], ["bs_lo"])

        ld("sync", vst[0:68, :], vec1, "vst", writes=["vst"])
        TR(PS[0][:, 0:68], vst[0:68, :], ident_f[0:68, 0:68], ["vst", "ident"], ["ps0"])
        V("tensor_copy", dict(out=colv1[:], in_=PS[0][:, 0:68]), ["ps0"], ["colv1"])
        ld("sync", vst[0:66, :], vec2, "vst", writes=["vst"])
        TR(PS[0][:, 0:66], vst[0:66, :], ident_f[0:66, 0:66], ["vst", "ident"], ["ps0"])
        V("tensor_copy", dict(out=colv2[:], in_=PS[0][:, 0:66]), ["ps0"], ["colv2"])
        act(nba[:], colv1[:, 40:42], AF.Copy, ["colv1"], ["nba"], scale=-1.0)
        act(gg[:], colv1[:, 42:46], AF.Copy, ["colv1"], ["gg"], scale=0.5)

        ld("sync", wsl[:], w_s.rearrange("h i j -> i h j"), "wsl", writes=["wsl"])
        for h in range(4):
            TR(PS[1][:, h * 128:(h + 1) * 128], wsl[:, h, :], ident_f[:], ["wsl", "ident"], ["ps1"], sig=(h == 3))
        V("tensor_tensor", dict(out=Wm32[:], in0=PS[1][:].rearrange("p (h i) -> p h i", h=4),
                                    in1=trilT[:].unsqueeze(1).broadcast_to([128, 4, 128]), op=ALU.mult),
          ["ps1", "trilT"], ["Wm32"])
        V("tensor_copy", dict(out=Wm[:], in_=Wm32[:]), ["Wm32"], ["Wm"])
        for h in range(4):
            MM(PS[1][0:64, h * 64:(h + 1) * 64], sel[0:4, :], Wm32[0:4, h, 0:4].unsqueeze(1).broadcast_to([4, 16, 4]),
               True, True, ["sel", "Wm32"], ["ps1"], sig=(h == 3))
        V("tensor_tensor", dict(out=Wm_s[:], in0=PS[1][0:64, 0:256].rearrange("p (h i) -> p h i", h=4),
                                    in1=mask_s[:].unsqueeze(1).broadcast_to([64, 4, 64]), op=ALU.mult),
          ["ps1", "mask_s"], ["Wm_s"])

        def fm_norm(src_fn, n, gcol, out_fn, sqb, rstd, tmpn, psb, pskey, rkeys, wkeys, tag, do_stats=True, do_apply=True):
            if do_stats:
                fm_stats(src_fn, n, sqb, rstd, tmpn, psb, pskey, rkeys, tag)
            if do_apply:
                for c in range(KC):
                    V("scalar_tensor_tensor", dict(out=out_fn(c), in0=src_fn(c), scalar=colv1[:, gcol + c:gcol + c + 1],
                                                   in1=rstd[:, 0:n], op0=ALU.mult, op1=ALU.mult),
                      list(rkeys) + [tag + "rstd", "colv1"], [(wkeys[0], c)] if wkeys[0] in ("xn", "hn") else wkeys)

        def fm_squares(src_fn, n, sqb, rkeys, tag):
            for c in range(KC):
                act(sqb[:, c, 0:n], src_fn(c), AF.Square, rkeys, [tag + "sqb"])

        def fm_stats(src_fn, n, sqb, rstd, tmpn, psb, pskey, rkeys, tag, squares=True):
            if squares:
                fm_squares(src_fn, n, sqb, rkeys, tag)
            for c in range(KC):
                MM(psb[:, 0:n], ones_b[:], sqb[:, c, 0:n], c == 0, c == KC - 1, ["ones_b", tag + "sqb"], [pskey])
            act(tmpn[:, 0:n], psb[:, 0:n], AF.Ln, [pskey], [tag + "tmpn"], scale=1.0 / D, bias=EPS)
            V("tensor_scalar", dict(out=tmpn[:, 0:n], in0=tmpn[:, 0:n], scalar1=-0.5, scalar2=None, op0=ALU.mult),
              [tag + "tmpn"], [tag + "tmpn"])
            act(rstd[:, 0:n], tmpn[:, 0:n], AF.Exp, [tag + "tmpn"], [tag + "rstd"])

        ntile = TP // 128
        xs_slots = [carve(R2, 16384 + i_ * 4096, [128, 1024], F32) for i_ in range(2)]

        def xtile(t, buf, bkey, dkey, banks, extra_w=()):
            rows = 128 if t < ntile else TS
            src = x_p[t * 128:(t + 1) * 128, :] if t < ntile else x_s
            ld("sync", buf[0:rows, :], src, dkey, writes=[bkey] + list(extra_w))
            for half in range(2):
                pb, pk = PS[banks[half]], "ps%d" % banks[half]
                for c4 in range(4):
                    c = half * 4 + c4
                    TR(pb[:, c4 * 128:c4 * 128 + rows], buf[0:rows, c * 128:(c + 1) * 128], ident_f[0:rows, 0:rows],
                       [bkey, "ident"], [pk], sig=(c4 == 3))
                dst = xT[:, half * 4:half * 4 + 4, t * 128:t * 128 + rows]
                srcp = pb[:].rearrange("p (c n) -> p c n", c=4)[:, :, 0:rows]
                if half == 0:
                    V("tensor_copy", dict(out=dst, in_=srcp), [pk], [("xT", t // 2)])
                else:
                    act(dst, srcp, AF.Copy, [pk], [("xT", t // 2)])
        for t in range(2):
            xtile(t, stg[t % 2], ("stg", t % 2), "stg%d" % (t % 2), (2, 3))

        for t in range(2):
            ld("sync", stg[t][:], mem[t * 128:(t + 1) * 128, :], "stg%d" % t, writes=[("stg", t)])
            for half in range(2):
                pb = PS[2 + half]
                for c4 in range(4):
                    c = half * 4 + c4
                    TR(pb[:, c4 * 128:(c4 + 1) * 128], stg[t][:, c * 128:(c + 1) * 128], ident_f[:],
                       [("stg", t), "ident"], ["ps%d" % (2 + half)], sig=(c4 == 3))
                V("tensor_copy", dict(out=memT[:, half * 4:half * 4 + 4, t * 128:(t + 1) * 128],
                                                                  in_=pb[:].rearrange("p (c n) -> p c n", c=4)),
                  ["ps%d" % (2 + half)], ["memT"])
        fm_norm(lambda c: memT[:, c, :], 256, G_MEM, lambda c: mnT[:, c, :], sqb0, rstd0, tmpn0, PS[4], "ps4",
                ["memT"], ["mnT"], "p0")
        for t in range(2):
            for (wb, wkey, outd, isv) in [(wk_b, "wk", mk_p, False), (wv_b, "wv", mv_p, True)]:
                for half in range(2):
                    pb, pk = PS[5 + half], "ps%d" % (5 + half)
                    for c in range(KC):
                        MM(pb[:], mnT[:, c, t * 128:(t + 1) * 128], wb[:, c, half * 512:(half + 1) * 512], c == 0, c == KC - 1,
                           ["mnT", wkey], [pk])
                    act(stg[t][:, half * 512:(half + 1) * 512], pb[:], AF.Copy, [pk], [("stg", t)])
                    if isv:
                        V("tensor_copy", dict(out=vbf[:, t, half * 512:(half + 1) * 512], in_=pb[:]),
                          [pk], ["vbf"])
                ld("sync", outd[t * 128:(t + 1) * 128, :], stg[t][:], "o_mkv", reads=[("stg", t)])
        for oc in range(8):
            pb, pk = PS[5 + oc % 2], "ps%d" % (5 + oc % 2)
            for c in range(KC):
                MM(pb[:, 0:256], wk_b[:, c, oc * 128:(oc + 1) * 128], mnT[:, c, :], c == 0, c == KC - 1, ["mnT", "wk"], [pk])
            act(KT[:, oc, :], pb[:, 0:256], AF.Copy, [pk], ["KT"])

        wq_b = carve(R2, 16384, [128, 8, 1024], BF16); wo_b = carve(R2, 0, [128, 8, 1024], BF16)
        S.fence([("xs", 0), ("xs", 1)])
        L = Lay(R3)
        xn = L.get([128, 8, NB], BF16); sqb = L.get([128, 8, NB], BF16)
        rstd = L.get([128, NB], F32); tmpn = L.get([128, NB], F32)
        aT = L.get([128, NB], BF16)
        e1 = L.get([128, NB], F32); Bc = L.get([128, NB], F32)
        eb = L.get([128, 2, NB], F32); enb = L.get([128, 2, NB], F32)
        qm = L.get([128, 4, NB], BF16); ktT = L.get([128, 2, NB], BF16); khT = L.get([128, 2, NB], BF16)
        cat = L.get([128, 8, NB], BF16); th = L.get([128, NB], F32)
        v_tm = L.get([128, 2, 512], BF16)
        p1keys = [("xn", c_) for c_ in range(KC)] + ["m1sqb", "m1rstd", "m1tmpn", "aT", "e1", "Bc", ("eb", 0), ("eb", 1), ("enb", 0), ("enb", 1), "qm", ("ktT", 0), ("ktT", 1), ("khT", 0), ("khT", 1), "cat", "th", "v_tm"]
        L2 = Lay(R2)
        svg = L2.get([128, 512], F32); svsq = L2.get([128, 512], F32); svn = L2.get([128, 2, 512], BF16)
        svo = svsq
        sc_bd = L2.get([128, 4, 128], BF16); khm = L2.get([128, 2, 256], BF16)
        TR2_OFF = L2.off
        rstd_o = L2.get([128, NB], F32); tmp_o = L2.get([128, NB], F32); osq = L2.get([128, 4, NB], BF16)
        sst = L2.get([128, 8], F32); srs = L2.get([128, 8], F32)
        S0 = [L2.get([128, 2, 128], F32) for _ in range(2)]
        S0b = [L2.get([128, 2, 128], BF16) for _ in range(2)]
        S0 = S0 + [carve(R3, 4096 + i_ * 1024, [128, 2, 128], F32) for i_ in range(2)]
        S0b = S0b + [carve(R3, 4096 + 2048 + i_ * 512, [128, 2, 128], BF16) for i_ in range(2)]
        p1keys += ["svg", "svsq", "svn", "sc_bd", "khm", "rstd_o", "tmp_o", ("osq", 0), ("osq", 1), ("osq", 2), ("osq", 3), "sst", "srs",
                   ("S0", 0), ("S0", 1), ("S0b", 0), ("S0b", 1)]
        S.fence(p1keys)
        G("memset", dict(ap=qm[:], constant=0.0), [], ["qm"])
        G("memset", dict(ap=khm[:], constant=0.0), [], ["khm"])

        rr = [0]

        def rot():
            i = rr[0] % 2
            rr[0] += 1
            return PS[i], "ps%d" % i

        def mixer_block(tok0, n, xkeys, sample, pre=False, nxt=None, prev_wout=None, xfill=()):
            ntl = (n + 127) // 128
            blk = slice(tok0, tok0 + n)
            S.ctx = "p1.norm"
            fm_norm(lambda c: xT[:, c, blk], n, G_MIX, lambda c: xn[:, c, 0:n], sqb, rstd, tmpn, PS[0], "ps0",
                    xkeys, ["xn"], "m1", do_stats=not pre)
            xfill = list(xfill)
            if prev_wout is not None:
                prev_wout()
            if xfill:
                xfill.pop(0)()
            rr[0] = 1

            def proj(col0, m):
                pb, pk = rot()
                for c in range(KC):
                    MM(pb[0:m, 0:n], w_in_b[:, c, col0:col0 + m], xn[:, c, 0:n], c == 0, c == KC - 1, ["w_in", ("xn", c)], [pk])
                return pb, pk
            S.ctx = "p1.gates"
            pb, pk = proj(CA, 16)
            act(aT[0:16, 0:n], pb[0:16, 0:n], AF.Copy, [pk], ["aT"])
            rmask = rm_s if sample else rm
            for c in range(2):
                pb, pk = rot()
                MM(pb[:, 0:n], walpha_b[:, c * 128:(c + 1) * 128], aT[0:16, 0:n], True, True, ["walpha_b", "aT"], [pk])
                eX, eXk = (e1, "e1") if c == 0 else (th, "th")
                act(eX[:, 0:n], pb[:, 0:n], AF.Exp, [pk, "nba"], [eXk], scale=-1.0, bias=nba[:, c:c + 1])
                V("tensor_scalar", dict(out=eX[:, 0:n], in0=eX[:, 0:n], scalar1=1.0, scalar2=None, op0=ALU.add), [eXk], [eXk])
                act(eX[:, 0:n], eX[:, 0:n], AF.Ln, [eXk], [eXk])
                V("tensor_tensor_scan", dict(out=Bc[:, 0:n], data0=rmask[:, 0:n], data1=eX[:, 0:n], initial=0.0,
                                                 op0=ALU.mult, op1=ALU.add), [eXk, "rm", "rm_s"], ["Bc"])
                act(eb[:, c, 0:n], Bc[:, 0:n], AF.Exp, ["Bc"], [("eb", c)], scale=-1.0 / 16)
                act(enb[:, c, 0:n], Bc[:, 0:n], AF.Exp, ["Bc"], [("enb", c)], scale=1.0 / 16)
            S.ctx = "p1.qk"
            def proj4(col0, m, j):
                bi_ = (2, 3, 0, 1)[j % 4]
                pb, pk = PS[bi_], "ps%d" % bi_
                for c in range(KC):
                    MM(pb[0:m, 0:n], w_in_b[:, c, col0:col0 + m], xn[:, c, 0:n], c == 0, c == KC - 1, ["w_in", ("xn", c)], [pk])
                return pb, pk
            for c in range(2):
                pb, pk = proj4(CQ + c * 128, 128, c)
                for h2 in range(2):
                    rs_ = slice(h2 * 64, (h2 + 1) * 64)
                    V("scalar_tensor_tensor", dict(
                        out=qm[rs_, 2 * c + h2, 0:n], in0=pb[rs_, 0:n], scalar=0.125, in1=eb[rs_, c, 0:n],
                        op0=ALU.mult, op1=ALU.mult), [pk, ("eb", c)], ["qm"])
            if xfill:
                S.ctx = "p1.x"
                xfill.pop(0)()
                S.ctx = "p1.qk"
            cl = 4 if sample else 64
            for c in range(2):
                pb, pk = proj4(CK + c * 128, 128, 2 + c)
                V("tensor_tensor", dict(out=ktT[:, c, 0:n], in0=pb[:, 0:n], in1=enb[:, c, 0:n], op=ALU.mult),
                  [pk, ("enb", c)], [("ktT", c)])
                G("tensor_tensor", dict(
                    out=khT[:, c, 0:n].rearrange("p (a b) -> p a b", b=cl),
                    in0=ktT[:, c, 0:n].rearrange("p (a b) -> p a b", b=cl),
                    in1=eb[:, c, cl - 1:n:cl].unsqueeze(2).broadcast_to([128, n // cl, cl]), op=ALU.mult),
                  [("ktT", c), ("eb", c)], [("khT", c)])
            S.ctx = "p1.ru"
            def ru_piece(j):
                S.ctx = "p1.ru"
                if j < 4:
                    pb, pk = proj(CR + j * 128, 128)
                    act(th[:, 0:n], pb[:, 0:n], AF.Tanh, [pk], ["th"], scale=0.5)
                    V("scalar_tensor_tensor", dict(out=cat[:, j, 0:n], in0=th[:, 0:n], scalar=1.0, in1=pb[:, 0:n],
                                                   op0=ALU.add, op1=ALU.mult), [pk, "th"], [("cat", j)])
                else:
                    pb, pk = proj(CU + (j - 4) * 128, 128)
                    act(cat[:, j, 0:n], pb[:, 0:n], AF.Gelu_apprx_tanh, [pk], [("cat", j)])
                S.ctx = "p1.gla"
            ru_q = list(range(8))
            S.ctx = "p1.tm"
            if sample:
                G("memset", dict(ap=v_tm[:], constant=0.0), [], ["v_tm"])
                G("memset", dict(ap=svn[:], constant=0.0), [], ["svn"])
                G("memset", dict(ap=sc_bd[:], constant=0.0), [], ["sc_bd"])
            for tl in range(ntl):
                rows = min(128, n - tl * 128)
                tsl = slice(tl * 128, tl * 128 + rows)
                for c in range(KC):
                    MM(PS[2][0:rows, :], xn[:, c, tsl], w_in_b[:, c, CV:CV + 512], c == 0, c == KC - 1, [("xn", c), "w_in"], ["ps2"])
                act(v_tm[0:rows, tl, :], PS[2][0:rows, :], AF.Copy, ["ps2"], ["v_tm"])
                for c in range(KC):
                    MM(PS[3][0:rows, :], xn[:, c, tsl], w_in_b[:, c, CSV:CSV + 512], c == 0, c == KC - 1, [("xn", c), "w_in"], ["ps3"])
                act(svg[0:rows, :], PS[3][0:rows, :], AF.Gelu_apprx_tanh, ["ps3"], ["svg"])
                V("tensor_tensor", dict(out=svsq[0:rows, :], in0=svg[0:rows, :], in1=svg[0:rows, :], op=ALU.mult),
                  ["svg"], ["svsq"])
                V("tensor_reduce", dict(out=sst[0:rows, 0:4], in_=svsq[0:rows, :].rearrange("p (h d) -> p h d", h=4),
                                                       axis=AX.X, op=ALU.add), ["svsq"], ["sst"])
                act(sst[0:rows, 4:8], sst[0:rows, 0:4], AF.Ln, ["sst"], ["sst"], scale=1.0 / 128, bias=EPS)
                V("tensor_scalar", dict(out=sst[0:rows, 4:8], in0=sst[0:rows, 4:8], scalar1=-0.5, scalar2=None, op0=ALU.mult),
                  ["sst"], ["sst"])
                act(srs[0:rows, 0:4], sst[0:rows, 4:8], AF.Exp, ["sst"], ["srs"])
                for h in range(4):
                    hs = slice(h * 128, (h + 1) * 128)
                    dst = svo[0:rows, hs] if sample else svn[0:rows, tl, hs]
                    V("scalar_tensor_tensor", dict(
                        out=dst, in0=svg[0:rows, hs], scalar=srs[0:rows, h:h + 1], in1=gsgu_bc[0:rows, hs],
                        op0=ALU.mult, op1=ALU.mult), ["svg", "srs", "gsgu_bc"], ["svsq" if sample else "svn"])
                if sample:
                    V("tensor_copy", dict(out=svn[0:rows, 0, :], in_=svo[0:rows, :]), ["svsq", "svn"], ["svn"])
                    ld("sync", sv_s, svo[0:rows, :], "o_svs", reads=["svsq"])
            if nxt is not None:
                S.ctx = "p1.norm"
                ntok0, nn_, nxk = nxt
                nblk = slice(ntok0, ntok0 + nn_)
                fm_squares(lambda c: xT[:, c, nblk], nn_, sqb, nxk, "m1")
            S.ctx = "p1.gla"
            for tl in range(ntl):
                rows = min(128, n - tl * 128)
                tsl = slice(tl * 128, tl * 128 + rows)
                for h in range(4):
                    MM(PS[4][0:rows, h * 128:h * 128 + rows], ktT[:, h // 2, tsl], qm[:, h, tsl], True, True,
                       [("ktT", h // 2), "qm"], ["ps4"], sig=(h == 3))
                msk = mask_s[:] if sample else mask128[:]
                V("tensor_tensor", dict(
                    out=sc_bd[0:rows, :, 0:rows], in0=PS[4][0:rows, :].rearrange("p (h i) -> p h i", h=4)[:, :, 0:rows],
                    in1=msk.unsqueeze(1).broadcast_to([rows, 4, rows]), op=ALU.mult), ["ps4", "mask128", "mask_s"], ["sc_bd"])
                p5b = PS[5][:].bitcast(BF16)
                for c in range(2):
                    TR(p5b[0:rows, c * 128:(c + 1) * 128], khT[:, c, tsl], ident_b[:], [("khT", c), "ident_b"], ["ps5"], sig=(c == 1))
                if not sample:
                    for p in range(2):
                        prs = slice(p * 64, (p + 1) * 64)
                        act(khm[prs, p, :], p5b[prs, 0:256], AF.Copy, ["ps5"], ["khm"])
                else:
                    act(tmp_o[0:64, 0:128].bitcast(BF16), p5b[0:64, 0:256], AF.Copy, ["ps5"], ["tmp_o"])
                groups = [(p, slice(p * 64, (p + 1) * 64), None) for p in range(rows // 64)] if not sample else \
                    [(b, slice(b * 4, (b + 1) * 4), b) for b in range(16)]
                for gi, (p, csl, bidx) in enumerate(groups):
                    if sample:
                        s = bidx % 4
                        ld("sync", S0[s][:], sgla[bidx].rearrange("(c h2) d v -> (h2 d) c v", c=2), "S0_%d" % s,
                           writes=[("S0", s)] + (["m1sqb"] if s >= 2 else []))
                        act(S0b[s][:], S0[s][:], AF.Copy, [("S0", s)], [("S0b", s)] + (["m1sqb"] if s >= 2 else []))
                        V("tensor_scalar", dict(out=khm[0:64, bidx % 2, :], in0=tmp_o[0:64, 0:128].bitcast(BF16),
                                                                    scalar1=bm[0:64, bidx:bidx + 1], scalar2=None, op0=ALU.mult),
                          ["tmp_o", "bm"], ["khm"])
                        Sb_cur, Sb_key = S0b[s], ("S0b", s)
                        khm_cur = khm[:, bidx % 2, :]
                    else:
                        rslot, wslot = gch[0] % 2, (gch[0] + 1) % 2
                        gch[0] += 1
                        Sb_cur, Sb_key = S_bfs[rslot], ("S_bf", rslot)
                        khm_cur = khm[:, p, :]
                    ncol = csl.stop - csl.start
                    gsl = slice(tl * 128 + csl.start, tl * 128 + csl.stop)
                    ubank = (2 + rslot) if not sample else (4 + (gi % 2))
                    pu, puk = PS[ubank], "ps%d" % ubank

                    def u_mms():
                        for h in range(4):
                            c = h // 2
                            MM(pu[:, h * 128:(h + 1) * 128], khm_cur[:, c * 128:(c + 1) * 128], v_tm[:, tl, h * 128:(h + 1) * 128],
                               True, True, ["khm", "v_tm"], [puk], sig=(h == 3))

                    def o_mms():
                        for h in range(4):
                            c = h // 2
                            pob = PS[6 + h // 2]
                            ocol = (h % 2) * NB + tl * 128
                            outap = pob[:, ocol + csl.start:ocol + csl.stop]
                            MM(outap, Sb_cur[:, c, :], qm[:, h, gsl], True, False, [Sb_key, "qm"], ["ps%d" % (6 + h // 2)], sig=False)
                            MM(outap, v_tm[:, tl, h * 128:(h + 1) * 128], sc_bd[:, h, csl], False, True, ["v_tm", "sc_bd"],
                               ["ps%d" % (6 + h // 2)], sig=(h % 2 == 1))
                    if sample or gi > 0:
                        u_mms()
                        o_mms()
                    else:
                        o_mms()
                        u_mms()
                    ecol = tl * 128 + csl.stop - 1
                    for h in range(4):
                        c, h2 = h // 2, h % 2
                        rs_ = slice(h2 * 64, (h2 + 1) * 64)
                        if sample:
                            V("scalar_tensor_tensor", dict(
                                out=S0[s][rs_, c, :], in0=S0[s][rs_, c, :], scalar=eb[rs_, c, ecol:ecol + 1],
                                in1=pu[rs_, h * 128:(h + 1) * 128], op0=ALU.mult, op1=ALU.add), [puk, ("eb", c), ("S0", s)], [("S0", s)])
                        else:
                            V("scalar_tensor_tensor", dict(
                                out=S_bfs[wslot][rs_, c, :], in0=Sst[rs_, c, :], scalar=eb[rs_, c, ecol:ecol + 1],
                                in1=pu[rs_, h * 128:(h + 1) * 128], op0=ALU.mult, op1=ALU.add), [puk, ("eb", c), "Sst"], [("S_bf", wslot)])
                    if not sample:
                        for h in range(4):
                            c, h2 = h // 2, h % 2
                            rs_ = slice(h2 * 64, (h2 + 1) * 64)
                            V("scalar_tensor_tensor", dict(
                                out=Sst[rs_, c, :], in0=Sst[rs_, c, :], scalar=eb[rs_, c, ecol:ecol + 1],
                                in1=pu[rs_, h * 128:(h + 1) * 128], op0=ALU.mult, op1=ALU.add), [puk, ("eb", c), "Sst"], ["Sst"])
                    if sample:
                        ld("sync", sg_s[bidx].rearrange("(c h2) d v -> (h2 d) c v", c=2), S0[s][:], "o_sgs", reads=[("S0", s)])
                        if ru_q:
                            ru_piece(ru_q.pop(0))
                    else:
                        for _ in range(2):
                            if ru_q:
                                ru_piece(ru_q.pop(0))
            S.ctx = "p1.onorm"
            while ru_q:
                ru_piece(ru_q.pop(0))
            if nxt is not None:
                S.ctx = "p1.norm"
                fm_stats(lambda c: xT[:, c, nblk], nn_, sqb, rstd, tmpn, PS[0], "ps0", nxk, "m1", squares=False)
            S.ctx = "p1.onorm"
            for h in range(4):
                pob, pok = PS[6 + h // 2], "ps%d" % (6 + h // 2)
                osl = slice((h % 2) * NB, (h % 2) * NB + n)
                act(osq[:, h, 0:n], pob[:, osl], AF.Square, [pok], [("osq", h)])
            for h in range(4):
                pb, pk = rot()
                for tl in range(ntl):
                    rows = min(128, n - tl * 128)
                    zo = pb[:, tl * 128:tl * 128 + rows]
                    if sample:
                        MM(zo, svn[0:64, 0, h * 128:(h + 1) * 128], Wm_s[:, h, :], True, False, ["svn", "Wm_s"], [pk], sig=False)
                        bh = bs_hi[0:1, h * 128:h * 128 + 4].unsqueeze(1).broadcast_to([1, 16, 4])
                        bl = bs_lo[0:1, h * 128:h * 128 + 4].unsqueeze(1).broadcast_to([1, 16, 4])
                    else:
                        MM(zo, svn[:, tl, h * 128:(h + 1) * 128], Wm[:, h, :], True, False, ["svn", "Wm"], [pk], sig=False)
                        bh = bs_hi[0:1, h * 128:(h + 1) * 128]
                        bl = bs_lo[0:1, h * 128:(h + 1) * 128]
                    MM(zo, ones_b[0:1, :], bh, False, False, ["ones_b", "bs_hi"], [pk], sig=False)
                    MM(zo, ones_b[0:1, :], bl, False, True, ["ones_b", "bs_lo"], [pk], sig=True)
                V("tensor_tensor", dict(out=cat[:, 4 + h, 0:n], in0=pb[:, 0:n], in1=cat[:, 4 + h, 0:n], op=ALU.mult),
                  [pk, ("cat", 4 + h)], [("cat", 4 + h)])
            S.ctx = "p1.wout"
            S.ctx = "p1.onorm"
            tr2 = carve(R2, TR2_OFF, [128, 2, NB], F32)
            for hp in range(2):
                pob, pok = PS[6 + hp], "ps%d" % (6 + hp)
                pb, pk = rot()
                for h2_ in range(2):
                    h = 2 * hp + h2_
                    MM(pb[:, h2_ * NB:h2_ * NB + n], ones_b[:], osq[:, h, 0:n], True, True, ["ones_b", ("osq", h)], [pk], sig=(h2_ == 1))
                pv_ = pb[:].rearrange("p (a b) -> p a b", a=2)[:, :, 0:n]
                act(tr2[:, :, 0:n], pv_, AF.Ln, [pk], ["tmp_o", "rstd_o"], scale=1.0 / 128, bias=EPS)
                V("tensor_scalar", dict(out=tr2[:, :, 0:n], in0=tr2[:, :, 0:n], scalar1=-0.5, scalar2=None, op0=ALU.mult),
                  ["tmp_o", "rstd_o"], ["tmp_o", "rstd_o"])
                act(tr2[:, :, 0:n], tr2[:, :, 0:n], AF.Exp, ["tmp_o", "rstd_o"], ["tmp_o", "rstd_o"])
                for h2_ in range(2):
                    h = 2 * hp + h2_
                    osl = slice(h2_ * NB, h2_ * NB + n)
                    V("scalar_tensor_tensor", dict(out=tr2[:, h2_, 0:n], in0=pob[:, osl], scalar=gg[:, h:h + 1],
                                                   in1=tr2[:, h2_, 0:n], op0=ALU.mult, op1=ALU.mult),
                      [pok, "gg", "tmp_o", "rstd_o"], ["tmp_o", "rstd_o"])
                for h2_ in range(2):
                    h = 2 * hp + h2_
                    V("tensor_tensor", dict(out=cat[:, h, 0:n], in0=tr2[:, h2_, 0:n], in1=cat[:, h, 0:n], op=ALU.mult),
                      ["tmp_o", "rstd_o", ("cat", h)], [("cat", h)])
            catk = [("cat", j) for j in range(8)]
            def wout_part():
                S.ctx = "p1.wout"
                for oc in range(8):
                    bi_ = (2, 3, 0, 1)[oc % 4]
                    pb, pk = PS[bi_], "ps%d" % bi_
                    for c in range(KC):
                        MM(pb[:, 0:n], w_out_b[:, c, oc * 128:(oc + 1) * 128], cat[:, c, 0:n], c == 0, c == KC - 1, ["w_out"] + catk, [pk])
                    V("tensor_tensor", dict(out=xT[:, oc, blk], in0=pb[:, 0:n], in1=xT[:, oc, blk], op=ALU.add),
                      [pk] + list(xkeys), list(xkeys))
            return wout_part

        mblocks = [(b * NB, NB, [("xT", b)]) for b in range(TP // NB)] + [(TP, TS, [("xT", 8)])]
        for i_, (tk, nn, xk) in enumerate(mblocks):
            smp = (i_ == len(mblocks) - 1)
            if smp:
                ld("sync", sg_p.rearrange("(c h2) d v -> (h2 d) c v", c=2), Sst[:], "o_sgp", reads=["Sst"])
            if i_ <= 6:
                tl_ = [2 * i_ + 2, 2 * i_ + 3]
            elif i_ == 7:
                tl_ = [16]
            else:
                tl_ = []
            xf_ = [(lambda t=t: xtile(t, xs_slots[t % 2], ("xs", t % 2), "xs%d" % (t % 2), (0, 1), extra_w=["wv"])) for t in tl_]
            pw_ = mixer_block(tk, nn, xk, smp, pre=(i_ > 0), nxt=(mblocks[i_ + 1] if i_ + 1 < len(mblocks) else None),
                              prev_wout=(pw_ if i_ > 0 else None), xfill=xf_)
            if i_ == 7:
                load_w(wq_b, wq, "wq", 1024, writes=["wv", ("xs", 0), ("xs", 1)])
        pw_()

        if DEBUG:
            ld("sync", dbg1, xT[:], "o_dbg1", reads=[("xT", i) for i in range(9)])
        S.fence(["wo"])
        load_w(wo_b, wo, "wo", 1024)
        L = Lay(R3)
        hn = L.get([128, 8, NB], BF16); sqb2 = L.get([128, 8, NB], BF16)
        rstd2 = L.get([128, NB], F32); tmpn2 = L.get([128, NB], F32)
        qT = L.get([128, 8, NB], BF16)
        rinv = L.get([128, 256], F32); pn = L.get([128, 4, 256], BF16)
        pT = L.get([128, 4, 2, NB], BF16); oT = L.get([128, 8, NB], BF16)
        qTm = [L.get([128, 8, TS], BF16) for _ in range(2)]
        KTs = L.get([128, 8, 256], BF16)
        qT_s = L.get([128, 8, TS], BF16); pT_s = L.get([128, 4, 2, TS], BF16)
        kslot = [carve(R1, 49152 + i * 4096, [128, 2, 1024], BF16) for i in range(2)]
        p2keys = [("hn", c_) for c_ in range(KC)] + ["m2sqb", "m2rstd", "m2tmpn"] + [("qT", o_) for o_ in range(8)] + [("pn", 0), ("pn", 1), ("pn", 2), ("pn", 3), "pT", "oT", "qT_s", "pT_s", "rinv", ("KTs", 0), ("KTs", 1)] + [("sm8", h) for h in range(4)] + [("sm12", h) for h in range(4)] + [("smx", 0), ("smx", 1), ("qTm", 0), ("qTm", 1), "KTs",
                  ("kslot", 0), ("kslot", 1)]
        S.fence(p2keys)
        def slot_views(si):
            base = si * 24576
            return (carve(R1, base, [128, 8, 512], BF16), carve(R1, base + 8192, [128, 8, 512], BF16),
                    carve(R1, base + 16384, [128, 4, 1024], BF16))
        S.fence([("fslot", 0), ("fslot", 1)])

        def unit_pieces(u):
            f0, nf = UNITS[u]
            si = u % 2
            wg_v, wu_v, wd_v = slot_views(si)
            key = "fs%d" % si
            gsrc = w_gate.rearrange("(c p) n -> p c n", p=128)
            usrc = w_up.rearrange("(c p) n -> p c n", p=128)
            pcs = []
            pcs.append(lambda: ld("gpsimd", wg_v[:, :, 0:nf * 128], gsrc[:, :, f0 * 128:(f0 + nf) * 128], key, writes=[("fslot", si)]))
            pcs.append(lambda: ld("gpsimd", wu_v[:, :, 0:nf * 128], usrc[:, :, f0 * 128:(f0 + nf) * 128], key, writes=[("fslot", si)]))
            pcs.append(lambda: ld("gpsimd", wd_v[:, 0:nf, :],
                                  w_down[f0 * 128:(f0 + nf) * 128, :].rearrange("(l p) n -> p l n", p=128), key, writes=[("fslot", si)]))
            return pcs

        def load_unit(u):
            for p_ in unit_pieces(u):
                p_()

        def softmax_tile(rows, psA, psB, pkeys):
            for hh, pb in enumerate([psA, psB]):
                V("tensor_reduce", dict(out=smx[0:rows, 2 * hh:2 * hh + 2], in_=pb[0:rows, :].rearrange("p (h m) -> p h m", h=2),
                                        axis=AX.X, op=ALU.max, negate=True), [pkeys[hh]], [("smx", hh)])
            for h in range(4):
                pb = [psA, psB][h // 2]
                act(pn[0:rows, h, :], pb[0:rows, (h % 2) * 256:(h % 2 + 1) * 256], AF.Exp, [pkeys[h // 2], ("smx", h // 2)],
                    [("pn", h)], bias=smx[0:rows, h:h + 1])

        def attn_block(tok0, n, xkeys, prev_wo, sfill, pre=False, nxt=None):
            blk = slice(tok0, tok0 + n)
            sfill = list(sfill)

            def fill():
                if sfill:
                    sfill.pop(0)()
                if wpieces:
                    wpieces.pop(0)()
            S.ctx = "p2.norm"
            fm_norm(lambda c: xT[:, c, blk], n, G_X, lambda c: hn[:, c, 0:n], sqb2, rstd2, tmpn2, PS[0], "ps0",
                    xkeys, ["hn"], "m2", do_stats=not pre)
            rr[0] = 1
            S.ctx = "p2.q"
            for oc in range(8):
                bi_ = (0, 1, 2, 3)[oc % 4]
                pb, pk = PS[bi_], "ps%d" % bi_
                for c in range(KC):
                    MM(pb[:, 0:n], wq_b[:, c, oc * 128:(oc + 1) * 128], hn[:, c, 0:n], c == 0, c == KC - 1, ["wq", ("hn", c)], [pk])
                act(qT[:, oc, 0:n], pb[:, 0:n], AF.Copy, [pk], [("qT", oc)], scale=1.0 / 16)
            fill()
            ntl = n // 128

            def scores(tl):
                S.ctx = "p2.sc"
                tsl = slice(tl * 128, (tl + 1) * 128)
                for h in range(4):
                    pb, pk = PS[2 + h // 2], "ps%d" % (2 + h // 2)
                    for dc in range(2):
                        MM(pb[:, (h % 2) * 256:(h % 2 + 1) * 256], qT[:, 2 * h + dc, tsl], KT[:, 2 * h + dc, :], dc == 0, dc == 1,
                           [("qT", 2 * h + dc), "KT"], [pk])

            def transposes(tl):
                S.ctx = "p2.pT"
                tsl = slice(tl * 128, (tl + 1) * 128)
                p4b = PS[4][:].bitcast(BF16)
                for h in range(4):
                    for mc in range(2):
                        TR(p4b[:, (h * 2 + mc) * 128:(h * 2 + mc + 1) * 128], pn[:, h, mc * 128:(mc + 1) * 128], ident_b[:],
                           [("pn", h), "ident_b"], ["ps4"], sig=(mc == 1))
                V("tensor_copy", dict(out=pT[:, :, :, tsl], in_=p4b.rearrange("p (h m t) -> p h m t", h=4, m=2)), ["ps4"], ["pT"])
            scores(0)
            for tl in range(ntl):
                S.ctx = "p2.smax"
                softmax_tile(128, PS[2], PS[3], ["ps2", "ps3"])
                if tl + 1 < ntl:
                    scores(tl + 1)
                if tl == 0 and prev_wo is not None:
                    prev_wo()
                transposes(tl)
                fill()
            if nxt is not None:
                S.ctx = "p2.norm"
                ntok0, nn_, nxk = nxt
                nblk = slice(ntok0, ntok0 + nn_)
                fm_squares(lambda c: xT[:, c, nblk], nn_, sqb2, nxk, "m2")
            S.ctx = "p2.pv"
            pvb = [0, 1, 4]
            pvi = [0]

            def rot3():
                i_ = pvb[pvi[0] % 3]
                pvi[0] += 1
                return PS[i_], "ps%d" % i_
            for h in range(4):
                pbs, pks = rot3()
                for mc in range(2):
                    MM(pbs[:, 0:n], ones_b[:], pT[:, h, mc, 0:n], mc == 0, mc == 1, ["ones_b", "pT"], [pks])
                V("reciprocal", dict(out=rinv[:, 0:n], in_=pbs[:, 0:n]), [pks], ["rinv"])
                for dc in range(2):
                    oc = 2 * h + dc
                    pb, pk = rot3()
                    for mc in range(2):
                        MM(pb[:, 0:n], vbf[:, mc, oc * 128:(oc + 1) * 128], pT[:, h, mc, 0:n], mc == 0, mc == 1,
                           ["vbf", "pT"], [pk], sig=(mc == 1))
                    V("tensor_tensor", dict(out=oT[:, oc, 0:n], in0=pb[:, 0:n], in1=rinv[:, 0:n], op=ALU.mult),
                      [pk, "rinv"], ["oT"])
            if nxt is not None:
                S.ctx = "p2.norm"
                fm_stats(lambda c: xT[:, c, nblk], nn_, sqb2, rstd2, tmpn2, PS[0], "ps0", nxk, "m2", squares=False)
            fill()
            for fl in sfill:
                fl()

            def wo_part():
                S.ctx = "p2.wo"
                for oc in range(8):
                    pb, pk = rot()
                    for c in range(KC):
                        MM(pb[:, 0:n], wo_b[:, c, oc * 128:(oc + 1) * 128], oT[:, c, 0:n], c == 0, c == KC - 1, ["wo", "oT"], [pk])
                    V("tensor_tensor", dict(out=xT[:, oc, blk], in0=pb[:, 0:n], in1=xT[:, oc, blk], op=ALU.add),
                      [pk] + list(xkeys), list(xkeys))
            return wo_part

        sblk = slice(TP, TP + TS)
        skeys = [("xT", 8)]

        def s_init():
            S.ctx = "p2.s_init"
            fm_norm(lambda c: xT[:, c, sblk], TS, G_X, lambda c: hn[:, c, 0:TS], sqb2, rstd2, tmpn2, PS[0], "ps0",
                    skeys, ["hn"], "m2")
            rr[0] = 1
            for oc in range(8):
                pb, pk = rot()
                for c in range(KC):
                    MM(pb[:, 0:TS], wq_b[:, c, oc * 128:(oc + 1) * 128], hn[:, c, 0:TS], c == 0, c == KC - 1, ["wq", ("hn", c)], [pk])
                act(qT_s[:, oc, :], pb[:, 0:TS], AF.Copy, [pk], ["qT_s"], scale=1.0 / 16)
            G("memset", dict(ap=qTm[0][:], constant=0.0), [], [("qTm", 0)])
            G("memset", dict(ap=qTm[1][:], constant=0.0), [], [("qTm", 1)])

        def s_kpass(b):
            S.ctx = "p2.s_k"
            s = b % 2
            ld("gpsimd", kslot[s][:], ck[b].rearrange("(mt p) d -> p mt d", p=128), "ks%d" % s, writes=[("kslot", s)])
            if b >= 2:
                V("memset", dict(ap=qTm[s][:, :, (b - 2) * 4:(b - 2) * 4 + 4], constant=0.0), [], [("qTm", s)])
            V("tensor_copy", dict(out=qTm[s][:, :, b * 4:b * 4 + 4], in_=qT_s[:, :, b * 4:b * 4 + 4]), ["qT_s"], [("qTm", s)])
            for half in range(2):
                p45 = PS[4 + half][:].bitcast(BF16)
                for oc4 in range(4):
                    oc = half * 4 + oc4
                    for mt in range(2):
                        TR(p45[:, oc4 * 256 + mt * 128:oc4 * 256 + (mt + 1) * 128], kslot[s][:, mt, oc * 128:(oc + 1) * 128],
                           ident_b[:], [("kslot", s), "ident_b"], ["ps%d" % (4 + half)], sig=(oc4 == 3 and mt == 1))
                src = p45.rearrange("p (o m) -> p o m", o=4)
                if half == 0:
                    V("tensor_copy", dict(out=KTs[:, 0:4, :], in_=src), ["ps4"], [("KTs", 0)])
                else:
                    act(KTs[:, 4:8, :], src, AF.Copy, ["ps5"], [("KTs", 1)])
            for h in range(4):
                pb, pk = PS[6 + h // 2], "ps%d" % (6 + h // 2)
                for dc in range(2):
                    first = (b == 0 and h % 2 == 0 and dc == 0)
                    last = (b == 15 and dc == 1)
                    MM(pb[0:64, (h % 2) * 256:(h % 2 + 1) * 256], qTm[s][:, 2 * h + dc, :], KTs[:, 2 * h + dc, :], first, last,
                       [("qTm", s), ("KTs", h // 2)], [pk], sig=(dc == 1))

        def s_mid():
            S.ctx = "p2.s_mid"
            softmax_tile(64, PS[6], PS[7], ["ps6", "ps7"])
            p4b = PS[4][:].bitcast(BF16)
            for h in range(4):
                for mc in range(2):
                    TR(p4b[:, (h * 2 + mc) * 64:(h * 2 + mc + 1) * 64], pn[0:64, h, mc * 128:(mc + 1) * 128], ident_b[0:64, 0:64],
                       [("pn", h), "ident_b"], ["ps4"], sig=(mc == 1))
            V("tensor_copy", dict(out=pT_s[:], in_=p4b[:, 0:512].rearrange("p (h m t) -> p h m t", h=4, m=2)), ["ps4"], ["pT_s"])

        def s_vpass(b):
            S.ctx = "p2.s_v"
            s = b % 2
            ld("gpsimd", kslot[s][:], cv[b].rearrange("(mt p) d -> p mt d", p=128), "ks%d" % s, writes=[("kslot", s)])
            for oc in range(8):
                for mc in range(2):
                    MM(PS[5][:, oc * 64 + b * 4:oc * 64 + b * 4 + 4], kslot[s][:, mc, oc * 128:(oc + 1) * 128],
                       pT_s[:, oc // 2, mc, b * 4:b * 4 + 4], mc == 0, mc == 1, [("kslot", s), "pT_s"], ["ps5"],
                       sig=(oc == 7 and mc == 1))

        def s_fin():
            S.ctx = "p2.s_fin"
            pbs, pks = rot()
            for h in range(4):
                for mc in range(2):
                    MM(pbs[:, h * TS:(h + 1) * TS], ones_b[:], pT_s[:, h, mc, :], mc == 0, mc == 1, ["ones_b", "pT_s"], [pks], sig=(mc == 1))
            V("reciprocal", dict(out=rinv[:, 0:4 * TS], in_=pbs[:, 0:4 * TS]), [pks], ["rinv"])
            V("tensor_tensor", dict(out=oT[:, :, 0:TS].rearrange("p (h d) t -> p h d t", h=4),
                                    in0=PS[5][:].rearrange("p (h d t) -> p h d t", h=4, d=2),
                                    in1=rinv[:, 0:4 * TS].rearrange("p (h t) -> p h t", h=4).unsqueeze(2).broadcast_to([128, 4, 2, TS]),
                                    op=ALU.mult), ["ps5", "rinv"], ["oT"])
            for oc in range(8):
                pb, pk = rot()
                for c in range(KC):
                    MM(pb[:, 0:TS], wo_b[:, c, oc * 128:(oc + 1) * 128], oT[:, c, 0:TS], c == 0, c == KC - 1, ["wo", "oT"], [pk])
                V("tensor_tensor", dict(out=xT[:, oc, sblk], in0=pb[:, 0:TS], in1=xT[:, oc, sblk], op=ALU.add),
                  [pk] + skeys, skeys)

        s_init()
        prev_wo = None
        wpieces = unit_pieces(0) + unit_pieces(1)
        for b in range(TP // NB):
            if b < 4:
                sf = [(lambda bb=bb: s_kpass(bb)) for bb in range(4 * b, 4 * b + 4)]
            else:
                sf = [(lambda bb=bb: s_vpass(bb)) for bb in range(4 * (b - 4), 4 * (b - 4) + 4)]
            nxt_ = ((b + 1) * NB, NB, [("xT", b + 1)]) if b + 1 < TP // NB else None
            prev_wo = attn_block(b * NB, NB, [("xT", b)], prev_wo, sf, pre=(b > 0), nxt=nxt_)
            if b == 3:
                s_mid()
        prev_wo()
        for p_ in wpieces:
            p_()
        s_fin()

        if DEBUG:
            ld("sync", dbg2, xT[:], "o_dbg2", reads=[("xT", i) for i in range(9)])
        hnA = carve(R3, 0, [128, 8, T], BF16)
        L = Lay(R2)
        sqb3 = L.get([128, 8, 256], BF16)
        rstd3 = L.get([128, 256], F32); tmpn3 = L.get([128, 256], F32)
        gsbs = [L.get([128, NB3 + 2], F32) for _ in range(2)]
        cbs = [L.get([128, NB3], F32) for _ in range(2)]
        actbs = [L.get([128, 4, NB3], BF16) for _ in range(2)]
        sconvT = L.get([128, NF, 32], F32)
        gkeep = sconvT
        gsss = [L.get([128, 16, 6], F32) for _ in range(2)]
        ubs = [L.get([128, NB3], BF16) for _ in range(2)]
        stgs = [carve(R1, 49152 + i_ * 4096, [128, 1024], F32) for i_ in range(2)]
        stg3 = stgs[0]
        gfin_bc = L.get([128, 1024], F32)
        fst = L.get([128, 16], F32)
        p3keys = [("hnA", i_) for i_ in range(5)] + [("gsbh", 0), ("gsbh", 1), ("gssh", 0), ("gssh", 1), "m3sqb", "m3rstd", "m3tmpn", ("gsb", 0), ("gsb", 1), ("cb", 0), ("cb", 1), "sconvT", ("gss", 0), ("gss", 1), ("ub", 0), ("ub", 1), "stg3", ("stg", 1), "gfin_bc", ("fst", 0), ("fst", 1)] + \
                 [("actb", a_, l_) for a_ in range(2) for l_ in range(4)]
        S.fence(p3keys)
        ld("sync", gfin_bc[:], g_fin.broadcast_to([128, D]), "gfin", writes=["gfin_bc"])
        for half in range(3):
            c0, c1 = half * 1024, min(DFF, (half + 1) * 1024)
            ld("sync", stg3[0:32, 0:c1 - c0], sconv[:, c0:c1], "stg3", writes=["stg3"])
            nfh = (c1 - c0) // 128
            for j in range(nfh):
                TR(PS[7][:, j * 32:(j + 1) * 32], stg3[0:32, j * 128:(j + 1) * 128], ident_f[0:32, 0:32], ["stg3", "ident"], ["ps7"],
                   sig=(j == nfh - 1))
            V("tensor_copy", dict(out=sconvT[:, half * 8:half * 8 + nfh, :],
                                  in_=PS[7][:, 0:nfh * 32].rearrange("p (f t) -> p f t", t=32)), ["ps7"], ["sconvT"])
        blocks3 = [(b * NB3, NB3, [("xT", 2 * b), ("xT", 2 * b + 1)], False) for b in range(TP // NB3)] + [(TP, TS, [("xT", 8)], True)]
        gcnt = [0]

        def ffn_front(u, bi, ab, fillers=()):
            S.ctx = "p3.front"
            tails = {}
            f0, nf = UNITS[u]
            si = u % 2
            wg_v, wu_v, wd_v = slot_views(si)
            fk = ("fslot", si)
            tok0, n, xkeys, sample = blocks3[bi]
            blk = slice(tok0, tok0 + n)
            if u == 0:
                for s0 in range(0, n, 256):
                    nn = min(256, n - s0)
                    sub = slice(tok0 + s0, tok0 + s0 + nn)
                    fm_norm(lambda c: xT[:, c, sub], nn, G_FFN, lambda c: hnA[:, c, sub], sqb3, rstd3, tmpn3, PS[0], "ps0",
                            xkeys, [("hnA", bi)], "m3")
            hk = ("hnA", bi)
            for lf in range(nf):
                f = f0 + lf
                gi = gcnt[0] % 2
                gcnt[0] += 1
                gsb, cb = gsbs[gi], cbs[gi]
                gk, ck_ = ("gsb", gi), ("cb", gi)
                pg, pgk = PS[1 + (lf % 2)], "ps%d" % (1 + lf % 2)
                pub_ = 3 if u == len(UNITS) - 1 else 3 + (lf % 2)
                pu, puk = PS[pub_], "ps%d" % pub_
                for c in range(KC):
                    MM(pg[:, 0:n], wg_v[:, c, lf * 128:(lf + 1) * 128], hnA[:, c, blk], c == 0, c == KC - 1, [fk, hk], [pgk])
                for c in range(KC):
                    MM(pu[:, 0:n], wu_v[:, c, lf * 128:(lf + 1) * 128], hnA[:, c, blk], c == 0, c == KC - 1, [fk, hk], [puk])
                cw = lambda j, f=f: colv2[:, j * NF + f:j * NF + f + 1]
                cbias = colv1[:, 46 + f:47 + f]
                ghk = ("gsbh", gi)
                if not sample:
                    V("tensor_copy", dict(out=gsb[:, 0:2], in_=ghist[:, f, :]), ["ghist"], [ghk])
                    act(gsb[:, 2:2 + n], pg[:, 0:n], AF.Copy, [pgk], [gk])
                    V("tensor_copy", dict(out=ghist[:, f, :], in_=gsb[:, n:n + 2]), [gk], ["ghist"])
                    g0, g1, g2 = gsb[:, 0:n], gsb[:, 1:1 + n], gsb[:, 2:2 + n]
                    cbo = cb[:, 0:n]
                else:
                    gk = ("gss", gi)
                    ghk = ("gssh", gi)
                    gss = gsss[gi]
                    V("tensor_copy", dict(out=gss[:, :, 0:2], in_=sconvT[:, f, :].rearrange("p (b j) -> p b j", j=2)),
                      ["sconvT"], [ghk])
                    act(gss[:, :, 2:6], pg[:, 0:n].rearrange("p (b t) -> p b t", t=4), AF.Copy, [pgk], [gk])
                    V("tensor_copy", dict(out=gkeep[:, f, :].rearrange("p (b j) -> p b j", j=2), in_=gss[:, :, 4:6]),
                      [gk], ["sconvT"])
                    g0, g1, g2 = gss[:, :, 0:4], gss[:, :, 1:5], gss[:, :, 2:6]
                    cbo = cb[:, 0:n].rearrange("p (b t) -> p b t", t=4)
                ub, ubk = ubs[gi], ("ub", gi)
                act(ub[:, 0:n], pu[:, 0:n], AF.Copy, [puk], [ubk])

                def ident(cbo=cbo, g2=g2, gk=gk, ck_=ck_, cw=cw, cbias=cbias):
                    act(cbo, g2, AF.Identity, [gk, "colv1", "colv2"], [ck_], scale=cw(2), bias=cbias)

                S.ctx = "p3.front"

                def tail(cbo=cbo, g1=g1, g0=g0, cw=cw, gk=gk, ghk=ghk, ck_=ck_, cb=cb, ub=ub, ubk=ubk, lf=lf):
                    S.ctx = "p3.chain"
                    V("scalar_tensor_tensor", dict(out=cbo, in0=g1, scalar=cw(1), in1=cbo, op0=ALU.mult, op1=ALU.add), [gk, ghk, ck_], [ck_])
                    V("scalar_tensor_tensor", dict(out=cbo, in0=g0, scalar=cw(0), in1=cbo, op0=ALU.mult, op1=ALU.add), [gk, ghk, ck_], [ck_])
                    act(cb[:, 0:n], cb[:, 0:n], AF.Gelu_apprx_tanh, [ck_], [ck_])
                    V("tensor_tensor", dict(out=actbs[ab][:, lf, 0:n], in0=cb[:, 0:n], in1=ub[:, 0:n], op=ALU.mult),
                      [ck_, ubk], [("actb", ab, lf)])
                tails[lf] = tail
                if lf >= 1:
                    tails[lf - 1]()
                ident()
                if fillers:
                    fillers.pop(0)()
                S.ctx = "p3.front"
            return tails[nf - 1]

        fcnt = [0]

        def final_out(bi):
            tok0, n, xkeys, sample = blocks3[bi]
            tiles = list(range(0, n, 128))
            As, Bs = [], []
            for s0 in tiles:
                par = fcnt[0] % 2
                fcnt[0] += 1
                As.append(lambda s0=s0, par=par: final_A(tok0, n, xkeys, s0, par))
                Bs.append([(lambda s0=s0, par=par, half=half: final_B(tok0, n, xkeys, sample, s0, par, half)) for half in range(2)])
            seq = []
            nt = len(tiles)
            seq.append(As[0])
            if nt > 1:
                seq.append(As[1])
            for i_ in range(nt):
                seq += Bs[i_]
                if i_ + 2 < nt:
                    seq.append(As[i_ + 2])
            return seq

        def final_A(tok0, n, xkeys, s0, par):
            S.ctx = "p3.final"
            rows = min(128, n - s0)
            tsl = slice(tok0 + s0, tok0 + s0 + rows)
            fo = par * 8
            act(sqb3[:, :, 0:rows], xT[:, :, tsl], AF.Square, list(xkeys), ["m3sqb"])
            for c in range(KC):
                MM(PS[4][0:rows, 0:1], sqb3[:, c, 0:rows], ones_b[:, 0:1], c == 0, c == KC - 1, ["m3sqb", "ones_b"], ["ps4"])
            act(fst[0:rows, fo + 3:fo + 4], PS[4][0:rows, 0:1], AF.Ln, ["ps4"], [("fst", par)], scale=1.0 / D, bias=EPS)
            V("tensor_scalar", dict(out=fst[0:rows, fo + 4:fo + 5], in0=fst[0:rows, fo + 3:fo + 4], scalar1=-0.5, scalar2=None, op0=ALU.mult),
              [("fst", par)], [("fst", par)])
            act(fst[0:rows, fo + 5:fo + 6], fst[0:rows, fo + 4:fo + 5], AF.Exp, [("fst", par)], [("fst", par)])

        def final_B(tok0, n, xkeys, sample, s0, par, half):
            S.ctx = "p3.final"
            rows = min(128, n - s0)
            tsl = slice(tok0 + s0, tok0 + s0 + rows)
            fo = par * 8
            stg_, sk_ = stgs[par], ("stg3" if par == 0 else ("stg", 1))
            bank_ = 7 if half == 0 else 0
            pb, pk = PS[bank_], "ps%d" % bank_
            for c4 in range(4):
                c = half * 4 + c4
                TR(pb[0:rows, c4 * 128:(c4 + 1) * 128], xT[:, c, tsl], ident_f[:], list(xkeys) + ["ident"], [pk], sig=(c4 == 3))
            V("scalar_tensor_tensor", dict(out=stg_[0:rows, half * 512:(half + 1) * 512], in0=pb[0:rows, :], scalar=fst[0:rows, fo + 5:fo + 6],
                                           in1=gfin_bc[0:rows, half * 512:(half + 1) * 512], op0=ALU.mult, op1=ALU.mult),
              [pk, ("fst", par), "gfin_bc"], [sk_])
            if half == 1:
                dst = y_s if sample else y_p[tok0 + s0:tok0 + s0 + rows, :]
                ld("sync", dst, stg_[0:rows, :], "o_y%d" % par, reads=[sk_])

        def ffn_back(u, bi, ab, fillers=None):
            S.ctx = "p3.back"
            f0, nf = UNITS[u]
            si = u % 2
            wg_v, wu_v, wd_v = slot_views(si)
            fk = ("fslot", si)
            tok0, n, xkeys, sample = blocks3[bi]
            blk = slice(tok0, tok0 + n)
            dbanks = [5, 6]
            if u != len(UNITS) - 1:
                dbanks.append(7)
                if u > 0:
                    dbanks.append(0)
            for oc in range(8):
                bi_ = dbanks[oc % len(dbanks)]
                pb, pk = PS[bi_], "ps%d" % bi_
                for lf in range(nf):
                    MM(pb[:, 0:n], wd_v[:, lf, oc * 128:(oc + 1) * 128], actbs[ab][:, lf, 0:n], lf == 0, lf == nf - 1,
                       [fk] + [("actb", ab, l) for l in range(nf)], [pk])
                V("tensor_tensor", dict(out=xT[:, oc, blk], in0=pb[:, 0:n], in1=xT[:, oc, blk], op=ALU.add),
                  [pk] + list(xkeys), list(xkeys))
                if fillers:
                    fillers.pop(0)()
                    S.ctx = "p3.back"

        def conv_state_out(f0, nf):
            S.ctx = "p3.cso"
            stgp = stgs[1]
            for j in range(nf):
                TR(PS[0][0:2, j * 128:(j + 1) * 128], ghist[:, f0 + j, :], ident_f[:], ["ghist", "ident"], ["ps0"], sig=(j == nf - 1))
            V("tensor_copy", dict(out=stgp[0:2, 0:nf * 128], in_=PS[0][0:2, 0:nf * 128]), ["ps0"], [("stg", 1)])
            ld("sync", sc_p[:, f0 * 128:(f0 + nf) * 128], stgp[0:2, 0:nf * 128], "o_scp", reads=[("stg", 1)])
            for j in range(nf):
                TR(PS[7][0:32, j * 128:(j + 1) * 128], gkeep[:, f0 + j, :], ident_f[:], ["sconvT", "ident"], ["ps7"], sig=(j == nf - 1))
            V("tensor_copy", dict(out=stg3[0:32, 0:nf * 128], in_=PS[7][0:32, 0:nf * 128]), ["ps7"], ["stg3"])
            ld("sync", sc_s[:, f0 * 128:(f0 + nf) * 128], stg3[0:32, 0:nf * 128], "o_scs", reads=["stg3"])

        abc = 0
        for u in range(len(UNITS)):
            prev = None
            lastu = (u == len(UNITS) - 1)
            pend = []
            for bi in range(len(blocks3)):
                ab = abc % 2
                abc += 1
                deferred = ffn_front(u, bi, ab, fillers=pend)
                if prev is not None:
                    ffn_back(u, *prev, fillers=pend)
                    if lastu:
                        pend += final_out(prev[0])
                deferred()
                prev = (bi, ab)
            ffn_back(u, *prev, fillers=pend)
            if lastu:
                pend += final_out(prev[0])
            for fl in pend:
                fl()
            conv_state_out(*UNITS[u])
            if u + 2 < len(UNITS):
                load_unit(u + 2)
        _NC_CACHE["sched"] = S
        sems = {n: es.enter_context(nc.semaphore(n)) for n in sorted(S.sem_names)}
        final = {"sync": [(s, v) for s, v in S.dma_cnt.items() if s.startswith("D_o_")]}
        with nc.Block() as block:
            S.emit(block, sems, final)
    return nc


_NC_CACHE = {}


def kernel(**inp):
    f = lambda a: np.ascontiguousarray(np.asarray(a, dtype=np.float32))
    x_prompt, x_sample, mem_prompt = f(inp["x_prompt"]), f(inp["x_sample"]), f(inp["mem_prompt"])
    state_gla, state_conv = f(inp["state_gla"])[0], f(inp["state_conv"])[0]
    ckk, cvv = f(inp["cache_mem_k"])[0], f(inp["cache_mem_v"])[0]
    vec1 = np.concatenate([f(inp["g_mix"]).reshape(8, 128), f(inp["g_x"]).reshape(8, 128), f(inp["g_mem"]).reshape(8, 128),
                           f(inp["g_ffn"]).reshape(8, 128), f(inp["g_final"]).reshape(8, 128), f(inp["b_alpha"]).reshape(2, 128),
                           f(inp["g_gla_out"]).reshape(4, 128), f(inp["conv_b"]).reshape(22, 128)], axis=0)
    vec2 = f(inp["conv_w"]).reshape(66, 128)
    shared = {
        "w_in": f(inp["w_in"])[0], "w_alpha": f(inp["w_alpha"])[0], "w_s": f(inp["w_s"])[0],
        "b_s": f(inp["b_s"]).reshape(1, 512), "g_sgu": f(inp["g_sgu"]).reshape(1, 512),
        "w_out": f(inp["w_out"])[0], "wq": f(inp["wq_x"])[0], "wk": f(inp["wk_x"])[0], "wv": f(inp["wv_x"])[0], "wo": f(inp["wo_x"])[0],
        "w_gate": f(inp["w_gate"])[0], "w_up": f(inp["w_up"])[0], "w_down": f(inp["w_down"])[0],
        "vec1": np.ascontiguousarray(vec1), "vec2": np.ascontiguousarray(vec2), "g_fin": f(inp["g_final"]).reshape(1, D),
    }
    for k, v in host_consts().items():
        shared["c_" + k] = v
    in_maps = []
    for c in range(NCORES):
        m = dict(shared)
        sl = slice(c * 16, (c + 1) * 16)
        m["x_p"] = x_prompt[c]
        m["x_s"] = np.ascontiguousarray(x_sample[sl].reshape(TS, D))
        m["mem"] = mem_prompt[c]
        m["sgla"] = np.ascontiguousarray(state_gla[sl])
        m["sconv"] = np.ascontiguousarray(state_conv[sl].reshape(32, DFF))
        m["ck"] = np.ascontiguousarray(ckk[sl].reshape(16, 256, D))
        m["cv"] = np.ascontiguousarray(cvv[sl].reshape(16, 256, D))
        in_maps.append(m)
    if "nc" not in _NC_CACHE:
        _NC_CACHE["nc"] = build_nc()
    res = run_bass_kernel_spmd(_NC_CACHE["nc"], in_maps, core_ids=list(range(NCORES)))
    R = res.results
    cat = lambda k: np.stack([np.asarray(r[k], dtype=np.float32) for r in R], axis=0)
    y_prompt = cat("y_p")
    y_sample = cat("y_s").reshape(128, 4, D)
    sg_p = cat("sg_p")[None]
    sc_p = cat("sc_p")[None]
    mk_p = cat("mk_p").reshape(1, 8, 256, 4, 256)
    mv_p = cat("mv_p").reshape(1, 8, 256, 4, 256)
    sg_s = cat("sg_s").reshape(1, 128, 4, 64, 128)
    sc_s = cat("sc_s").reshape(1, 128, 2, DFF)
    sv_s = cat("sv_s").reshape(1, 128, 4, 4, 128)
    if DEBUG:
        _NC_CACHE["dbg"] = (R[0]["dbg1"], R[0]["dbg2"])
    return (y_prompt, y_sample, sg_p, sc_p, mk_p, mv_p, sg_s, sc_s, sv_s)
```

```python
import numpy as np
from contextlib import ExitStack
import concourse.bass as bass
import concourse.mybir as mybir
from concourse.bass_utils import run_bass_kernel_spmd

F32 = mybir.dt.float32
BF16 = mybir.dt.bfloat16
U8 = mybir.dt.uint8
AF = mybir.ActivationFunctionType
ALU = mybir.AluOpType
AX = mybir.AxisListType

NCORES = 8
TP, TS = 2048, 64
T = TP + TS
D, KC = 1024, 8
DIN = 2576
CQ, CK, CV, CR, CA, CU, CSV = 0, 256, 512, 1024, 1536, 1552, 2064
DFF, NF = 2816, 22
UNITS = [(0, 4), (4, 4), (8, 4), (12, 4), (16, 3), (19, 3)]
EPS = 1e-6
NB = 256
NB3 = 512
DEBUG = False


class Sched:
    ENGS = ["tensor", "vector", "scalar", "gpsimd", "sync"]

    def __init__(self):
        self.ops = {e: [] for e in self.ENGS}
        self.cnt = {e: 0 for e in self.ENGS}
        self.pending = {e: False for e in self.ENGS}
        self.writers, self.readers = {}, {}
        self.seen = {e: {} for e in self.ENGS}
        self.dma_cnt = {}
        self.sem_names = set()
        self.tags = {e: [] for e in self.ENGS}
        self.ctx = ""
        self.dma_hist = {}

    def _deps(self, eng, reads, writes):
        toks = {}

        def add(d):
            for s, v in d.items():
                if v > toks.get(s, 0):
                    toks[s] = v
        for k in reads:
            add(self.writers.get(k, {}))
        for k in writes:
            add(self.writers.get(k, {}))
            add(self.readers.get(k, {}))
        waits = []
        for s, v in toks.items():
            if s == "E_" + eng and eng != "gpsimd":
                continue
            if self.seen[eng].get(s, 0) >= v:
                continue
            self.seen[eng][s] = v
            waits.append((s, v))
        return waits

    def _commit(self, tok, reads, writes):
        s, v = tok
        for k in reads:
            r = self.readers.setdefault(k, {})
            r[s] = max(r.get(s, 0), v)
        for k in writes:
            self.writers[k] = {s: v}
            self.readers[k] = {}

    def op(self, eng, fn, reads=(), writes=(), sig=True):
        isps = lambda k: isinstance(k, str) and k.startswith("ps") and k[2:].isdigit()
        writes = list(writes) + [k for k in reads if isps(k)]
        reads = [k for k in reads if not isps(k)]
        waits = self._deps(eng, reads, writes)
        s = "E_" + eng
        self.sem_names.add(s)
        if sig:
            self.cnt[eng] += 1
            self.pending[eng] = False
            tok = (s, self.cnt[eng])
        else:
            self.pending[eng] = True
            tok = (s, self.cnt[eng] + 1)
        self.ops[eng].append((fn, waits, tok, 1 if sig else 0))
        self.tags[eng].append(self.ctx)
        self._commit(tok, reads, writes)

    def dma(self, eng, fn, key, reads=(), writes=()):
        waits = self._deps(eng, reads, writes)
        s = "D_" + key
        self.sem_names.add(s)
        self.dma_cnt[s] = self.dma_cnt.get(s, 0) + 16
        tok = (s, self.dma_cnt[s])
        hist = self.dma_hist.setdefault(eng, [])
        lim = 16 if eng == "gpsimd" else 24
        if len(hist) >= lim:
            os_, ov_ = hist[-lim]
            if self.seen[eng].get(os_, 0) < ov_:
                self.seen[eng][os_] = ov_
                waits = list(waits) + [(os_, ov_)]
        hist.append(tok)
        self.ops[eng].append((fn, waits, tok, 2))
        self.tags[eng].append(self.ctx)
        self._commit(tok, reads, writes)

    def fence(self, keys):
        for e in self.ENGS:
            assert not self.pending[e], e
        d = {"E_" + e: self.cnt[e] for e in self.ENGS if self.cnt[e] > 0}
        d.update(self.dma_cnt)
        for k in keys:
            self.writers[k] = dict(d)
            self.readers[k] = {}

    def emit(self, block, sems, final_waits):
        def body(eng_name):
            def f(eng):
                for fn, waits, tok, kind in self.ops[eng_name]:
                    for s, v in waits:
                        eng.wait_ge(sems[s], v)
                    r = fn(eng)
                    if kind == 2:
                        r.then_inc(sems[tok[0]], 16)
                    elif kind == 1:
                        r.then_inc(sems[tok[0]], 1)
                for s, v in final_waits.get(eng_name, []):
                    eng.wait_ge(sems[s], v)
            return f
        block.tensor(body("tensor"))
        block.vector(body("vector"))
        block.scalar(body("scalar"))
        block.gpsimd(body("gpsimd"))
        block.sync(body("sync"))


def host_consts():
    c = {}
    c["ident"] = np.eye(128, dtype=np.float32)
    r = np.arange(128)
    c["mask128"] = ((r[:, None] <= r[None, :]) & ((r[:, None] // 64) == (r[None, :] // 64))).astype(np.float32)
    r64 = np.arange(64)
    c["mask_s"] = ((r64[:, None] <= r64[None, :]) & ((r64[:, None] // 4) == (r64[None, :] // 4))).astype(np.float32)
    rm = np.ones((128, NB), np.float32)
    rm[:, 0::64] = 0.0
    c["rm"] = rm
    rms = np.ones((128, TS), np.float32)
    rms[:, 0::4] = 0.0
    c["rm_s"] = rms
    c["trilT"] = (r[:, None] <= r[None, :]).astype(np.float32)
    bm = np.zeros((128, 16), np.float32)
    bm[r64, r64 // 4] = 1.0
    c["bm"] = bm
    sel = np.zeros((4, 64), np.float32)
    sel[r64 % 4, r64] = 1.0
    c["sel"] = sel
    return c


def build_nc():
    nc = bass.Bass("TRN2", target_bir_lowering=False)

    def din(name, shape):
        return nc.dram_tensor(name, list(shape), F32, kind="ExternalInput").ap()

    def dout(name, shape):
        return nc.dram_tensor(name, list(shape), F32, kind="ExternalOutput").ap()

    x_p = din("x_p", [TP, D]); x_s = din("x_s", [TS, D]); mem = din("mem", [256, D])
    sgla = din("sgla", [16, 4, 64, 128]); sconv = din("sconv", [32, DFF])
    ck = din("ck", [16, 256, D]); cv = din("cv", [16, 256, D])
    w_in = din("w_in", [D, DIN]); w_alpha = din("w_alpha", [16, 256])
    w_s = din("w_s", [4, 128, 128]); b_s = din("b_s", [1, 512]); g_sgu = din("g_sgu", [1, 512])
    w_out = din("w_out", [D, D]); wq = din("wq", [D, D]); wk = din("wk", [D, D]); wv = din("wv", [D, D]); wo = din("wo", [D, D])
    w_gate = din("w_gate", [D, DFF]); w_up = din("w_up", [D, DFF]); w_down = din("w_down", [DFF, D])
    vec1 = din("vec1", [68, 128]); vec2 = din("vec2", [66, 128]); g_fin = din("g_fin", [1, D])
    cst = {k: din("c_" + k, v.shape) for k, v in host_consts().items()}

    y_p = dout("y_p", [TP, D]); y_s = dout("y_s", [TS, D]); sg_p = dout("sg_p", [4, 64, 128]); sc_p = dout("sc_p", [2, DFF])
    mk_p = dout("mk_p", [256, D]); mv_p = dout("mv_p", [256, D]); sg_s = dout("sg_s", [16, 4, 64, 128])
    sc_s = dout("sc_s", [32, DFF]); sv_s = dout("sv_s", [TS, 512])

    if DEBUG:
        dbg1 = dout("dbg1", [128, KC, T]); dbg2 = dout("dbg2", [128, KC, T])
    S = Sched()
    es = ExitStack()
    with es:
        def sb(name, shape, dt):
            return es.enter_context(nc.sbuf_tensor(name, list(shape), dt))
        xT = sb("xT", [128, KC, T], F32)
        R1 = sb("R1", [128, 57600], U8)
        R2 = sb("R2", [128, 32768], U8)
        R3 = sb("R3", [128, 33792], U8)
        PS = [es.enter_context(nc.psum_tensor("ps%d" % i, [128, 512], F32)) for i in range(8)]

        def carve(arena, off, shape, dt):
            esz = 4 if dt == F32 else 2
            n = int(np.prod(shape[1:]))
            v = arena[:, off:off + n * esz].bitcast(dt)
            if len(shape) == 3:
                v = v.rearrange("p (a b) -> p a b", a=shape[1])
            elif len(shape) == 4:
                v = v.rearrange("p (a b c) -> p a b c", a=shape[1], b=shape[2])
            return v

        class Lay:
            def __init__(self, arena, base=0):
                self.arena, self.off = arena, base

            def get(self, shape, dt):
                esz = 4 if dt == F32 else 2
                v = carve(self.arena, self.off, shape, dt)
                self.off += int(np.prod(shape[1:])) * esz
                self.off = (self.off + 63) // 64 * 64
                assert self.off <= self.arena.shape[1], (self.off, self.arena.shape)
                return v

        ident_f = sb("ident_f", [128, 128], F32); ident_b = sb("ident_b", [128, 128], BF16)
        ones_b = sb("ones_b", [128, 128], BF16)
        mask128 = sb("mask128", [128, 128], F32); mask_s = sb("mask_s", [64, 64], F32)
        rm = sb("rm", [128, NB], F32); rm_s = sb("rm_s", [128, TS], F32)
        bm = sb("bm", [128, 16], F32); sel = sb("sel", [4, 64], F32)
        colv1 = sb("colv1", [128, 68], F32); colv2 = sb("colv2", [128, 66], F32)
        nba = sb("nba", [128, 2], F32); gg = sb("gg", [128, 4], F32)
        gsgu_bc = sb("gsgu_bc", [128, 512], F32)
        Wm = sb("Wm", [128, 4, 128], BF16); Wm_s = sb("Wm_s", [64, 4, 64], BF16)
        bs_hi = sb("bs_hi", [1, 512], BF16); bs_lo = sb("bs_lo", [1, 512], BF16)
        walpha_b = sb("walpha_b", [16, 256], BF16)
        ghist = sb("ghist", [128, NF, 2], F32)
        KT = sb("KT", [128, 8, 256], BF16); vbf = sb("vbf", [128, 2, 1024], BF16)
        smx = sb("smx", [128, 4], F32)
        Sst = sb("Sst", [128, 2, 128], F32); S_bf = sb("S_bf", [128, 2, 128], BF16); S_bfB = sb("S_bfB", [128, 2, 128], BF16)
        S_bfs = [S_bf, S_bfB]
        gch = [0]

        G_MIX, G_X, G_MEM, G_FFN, G_FIN = 0, 8, 16, 24, 32

        def ld(eng, out, in_, key, reads=(), writes=(), **kw):
            S.dma(eng, lambda e: e.dma_start(out=out, in_=in_, **kw), key, reads=reads, writes=writes)

        def V(name, kw, reads=(), writes=()):
            S.op("vector", lambda e: getattr(e, name)(**kw), reads, writes)

        def A(fn, reads=(), writes=()):
            S.op("scalar", fn, reads, writes)

        def G(name, kw, reads=(), writes=()):
            S.op("gpsimd", lambda e: getattr(e, name)(**kw), reads, writes)

        def MM(out, lhsT, rhs, start, stop, reads, writes, sig=None):
            if sig is None:
                sig = stop
            S.op("tensor", lambda e: e.matmul(out, lhsT=lhsT, rhs=rhs, start=start, stop=stop), reads, writes, sig=sig)

        def TR(out, in_, ident, reads, writes, sig=True):
            S.op("tensor", lambda e: e.transpose(out=out, in_=in_, identity=ident), reads, writes, sig=sig)

        def act(out, in_, func, reads, writes, **kw):
            A(lambda e: e.activation(out=out, in_=in_, func=func, **kw), reads, writes)

        for nm, t in [("ident", ident_f), ("mask128", mask128), ("mask_s", mask_s), ("rm", rm), ("rm_s", rm_s),
                      ("bm", bm), ("sel", sel)]:
            ld("sync", t[:], cst[nm], "c_" + nm, writes=[nm])
        ld("sync", gsgu_bc[:], g_sgu.broadcast_to([128, 512]), "gsgu", writes=["gsgu_bc"])
        ld("gpsimd", walpha_b[:], w_alpha, "walpha", writes=["walpha_b"])
        V("tensor_copy", dict(out=ident_b[:], in_=ident_f[:]), ["ident"], ["ident_b"])
        G("memset", dict(ap=ones_b[:], constant=1.0), [], ["ones_b"])
        G("memset", dict(ap=ghist[:], constant=0.0), [], ["ghist"])
        G("memset", dict(ap=Sst[:], constant=0.0), [], ["Sst"])
        G("memset", dict(ap=S_bf[:], constant=0.0), [], [("S_bf", 0)])

        wk_b = carve(R2, 0, [128, 8, 1024], BF16); wv_b = carve(R2, 16384, [128, 8, 1024], BF16)
        w_in_b = carve(R1, 0, [128, 8, DIN], BF16); w_out_b = carve(R1, 41216, [128, 8, 1024], BF16)

        def load_w(dst, src, key, ncols, reads=(), writes=()):
            srcv = src.rearrange("(c p) n -> p c n", p=128)
            nck = srcv.shape[1]
            step = 8 if ncols <= 1024 else 4
            for c0 in range(0, nck, step):
                ld("gpsimd", dst[:, c0:c0 + step, :], srcv[:, c0:c0 + step, :], key, reads=reads, writes=[key] + list(writes))

        load_w(wk_b, wk, "wk", 1024)
        load_w(wv_b, wv, "wv", 1024)
        load_w(w_in_b, w_in, "w_in", DIN)
        load_w(w_out_b, w_out, "w_out", 1024)

        S.ctx = "p0"
        L = Lay(R3)
        stg = [L.get([128, 1024], F32) for _ in range(2)]
        memT = L.get([128, 8, 256], F32)
        mnT = L.get([128, 8, 256], BF16)
        sqb0 = L.get([128, 8, 256], BF16)
        rstd0 = L.get([128, 256], F32); tmpn0 = L.get([128, 256], F32)
        Wm32 = L.get([128, 4, 128], F32); wsl = L.get([128, 4, 128], F32)
        vst = L.get([128, 128], F32)
        bsf = L.get([128, 512], F32)
        trilT = L.get([128, 128], F32)
        ld("sync", trilT[:], cst["trilT"], "c_trilT", writes=["trilT"])
        ld("sync", bsf[0:1, :], b_s, "bsf", writes=["bsf"])
        act(bs_hi[:], bsf[0:1, :], AF.Copy, ["bsf"], ["bs_hi"])
        V("tensor_tensor", dict(out=bs_lo[:], in0=bsf[0:1, :], in1=bs_hi[:], op=ALU.subtract), ["bsf", "bs_hi"], ["bs_lo"])

        ld("sync", vst[0:68, :], vec1, "vst", writes=["vst"])
        TR(PS[0][:, 0:68], vst[0:68, :], ident_f[0:68, 0:68], ["vst", "ident"], ["ps0"])
        V("tensor_copy", dict(out=colv1[:], in_=PS[0][:, 0:68]), ["ps0"], ["colv1"])
        ld("sync", vst[0:66, :], vec2, "vst", writes=["vst"])
        TR(PS[0][:, 0:66], vst[0:66, :], ident_f[0:66, 0:66], ["vst", "ident"], ["ps0"])
        V("tensor_copy", dict(out=colv2[:], in_=PS[0][:, 0:66]), ["ps0"], ["colv2"])
        act(nba[:], colv1[:, 40:42], AF.Copy, ["colv1"], ["nba"], scale=-1.0)
        act(gg[:], colv1[:, 42:46], AF.Copy, ["colv1"], ["gg"], scale=0.5)

        ld("sync", wsl[:], w_s.rearrange("h i j -> i h j"), "wsl", writes=["wsl"])
        for h in range(4):
            TR(PS[1][:, h * 128:(h + 1) * 128], wsl[:, h, :], ident_f[:], ["wsl", "ident"], ["ps1"], sig=(h == 3))
        V("tensor_tensor", dict(out=Wm32[:], in0=PS[1][:].rearrange("p (h i) -> p h i", h=4),
                                    in1=trilT[:].unsqueeze(1).broadcast_to([128, 4, 128]), op=ALU.mult),
          ["ps1", "trilT"], ["Wm32"])
        V("tensor_copy", dict(out=Wm[:], in_=Wm32[:]), ["Wm32"], ["Wm"])
        for h in range(4):
            MM(PS[1][0:64, h * 64:(h + 1) * 64], sel[0:4, :], Wm32[0:4, h, 0:4].unsqueeze(1).broadcast_to([4, 16, 4]),
               True, True, ["sel", "Wm32"], ["ps1"], sig=(h == 3))
        V("tensor_tensor", dict(out=Wm_s[:], in0=PS[1][0:64, 0:256].rearrange("p (h i) -> p h i", h=4),
                                    in1=mask_s[:].unsqueeze(1).broadcast_to([64, 4, 64]), op=ALU.mult),
          ["ps1", "mask_s"], ["Wm_s"])

        def fm_norm(src_fn, n, gcol, out_fn, sqb, rstd, tmpn, psb, pskey, rkeys, wkeys, tag, do_stats=True, do_apply=True):
            if do_stats:
                fm_stats(src_fn, n, sqb, rstd, tmpn, psb, pskey, rkeys, tag)
            if do_apply:
                for c in range(KC):
                    V("scalar_tensor_tensor", dict(out=out_fn(c), in0=src_fn(c), scalar=colv1[:, gcol + c:gcol + c + 1],
                                                   in1=rstd[:, 0:n], op0=ALU.mult, op1=ALU.mult),
                      list(rkeys) + [tag + "rstd", "colv1"], [(wkeys[0], c)] if wkeys[0] in ("xn", "hn") else wkeys)

        def fm_squares(src_fn, n, sqb, rkeys, tag):
            for c in range(KC):
                act(sqb[:, c, 0:n], src_fn(c), AF.Square, rkeys, [tag + "sqb"])

        def fm_stats(src_fn, n, sqb, rstd, tmpn, psb, pskey, rkeys, tag, squares=True):
            if squares:
                fm_squares(src_fn, n, sqb, rkeys, tag)
            for c in range(KC):
                MM(psb[:, 0:n], ones_b[:], sqb[:, c, 0:n], c == 0, c == KC - 1, ["ones_b", tag + "sqb"], [pskey])
            act(tmpn[:, 0:n], psb[:, 0:n], AF.Ln, [pskey], [tag + "tmpn"], scale=1.0 / D, bias=EPS)
            V("tensor_scalar", dict(out=tmpn[:, 0:n], in0=tmpn[:, 0:n], scalar1=-0.5, scalar2=None, op0=ALU.mult),
              [tag + "tmpn"], [tag + "tmpn"])
            act(rstd[:, 0:n], tmpn[:, 0:n], AF.Exp, [tag + "tmpn"], [tag + "rstd"])

        ntile = TP // 128
        xs_slots = [carve(R2, 16384 + i_ * 4096, [128, 1024], F32) for i_ in range(2)]

        def xtile(t, buf, bkey, dkey, banks, extra_w=()):
            rows = 128 if t < ntile else TS
            src = x_p[t * 128:(t + 1) * 128, :] if t < ntile else x_s
            ld("sync", buf[0:rows, :], src, dkey, writes=[bkey] + list(extra_w))
            for half in range(2):
                pb, pk = PS[banks[half]], "ps%d" % banks[half]
                for c4 in range(4):
                    c = half * 4 + c4
                    TR(pb[:, c4 * 128:c4 * 128 + rows], buf[0:rows, c * 128:(c + 1) * 128], ident_f[0:rows, 0:rows],
                       [bkey, "ident"], [pk], sig=(c4 == 3))
                dst = xT[:, half * 4:half * 4 + 4, t * 128:t * 128 + rows]
                srcp = pb[:].rearrange("p (c n) -> p c n", c=4)[:, :, 0:rows]
                if half == 0:
                    V("tensor_copy", dict(out=dst, in_=srcp), [pk], [("xT", t // 2)])
                else:
                    act(dst, srcp, AF.Copy, [pk], [("xT", t // 2)])
        for t in range(2):
            xtile(t, stg[t % 2], ("stg", t % 2), "stg%d" % (t % 2), (2, 3))

        for t in range(2):
            ld("sync", stg[t][:], mem[t * 128:(t + 1) * 128, :], "stg%d" % t, writes=[("stg", t)])
            for half in range(2):
                pb = PS[2 + half]
                for c4 in range(4):
                    c = half * 4 + c4
                    TR(pb[:, c4 * 128:(c4 + 1) * 128], stg[t][:, c * 128:(c + 1) * 128], ident_f[:],
                       [("stg", t), "ident"], ["ps%d" % (2 + half)], sig=(c4 == 3))
                V("tensor_copy", dict(out=memT[:, half * 4:half * 4 + 4, t * 128:(t + 1) * 128],
                                                                  in_=pb[:].rearrange("p (c n) -> p c n", c=4)),
                  ["ps%d" % (2 + half)], ["memT"])
        fm_norm(lambda c: memT[:, c, :], 256, G_MEM, lambda c: mnT[:, c, :], sqb0, rstd0, tmpn0, PS[4], "ps4",
                ["memT"], ["mnT"], "p0")
        for t in range(2):
            for (wb, wkey, outd, isv) in [(wk_b, "wk", mk_p, False), (wv_b, "wv", mv_p, True)]:
                for half in range(2):
                    pb, pk = PS[5 + half], "ps%d" % (5 + half)
                    for c in range(KC):
                        MM(pb[:], mnT[:, c, t * 128:(t + 1) * 128], wb[:, c, half * 512:(half + 1) * 512], c == 0, c == KC - 1,
                           ["mnT", wkey], [pk])
                    act(stg[t][:, half * 512:(half + 1) * 512], pb[:], AF.Copy, [pk], [("stg", t)])
                    if isv:
                        V("tensor_copy", dict(out=vbf[:, t, half * 512:(half + 1) * 512], in_=pb[:]),
                          [pk], ["vbf"])
                ld("sync", outd[t * 128:(t + 1) * 128, :], stg[t][:], "o_mkv", reads=[("stg", t)])
        for oc in range(8):
            pb, pk = PS[5 + oc % 2], "ps%d" % (5 + oc % 2)
            for c in range(KC):
                MM(pb[:, 0:256], wk_b[:, c, oc * 128:(oc + 1) * 128], mnT[:, c, :], c == 0, c == KC - 1, ["mnT", "wk"], [pk])
            act(KT[:, oc, :], pb[:, 0:256], AF.Copy, [pk], ["KT"])

        wq_b = carve(R2, 16384, [128, 8, 1024], BF16); wo_b = carve(R2, 0, [128, 8, 1024], BF16)
        S.fence([("xs", 0), ("xs", 1)])
        L = Lay(R3)
        xn = L.get([128, 8, NB], BF16); sqb = L.get([128, 8, NB], BF16)
        rstd = L.get([128, NB], F32); tmpn = L.get([128, NB], F32)
        aT = L.get([128, NB], BF16)
        e1 = L.get([128, NB], F32); Bc = L.get([128, NB], F32)
        eb = L.get([128, 2, NB], F32); enb = L.get([128, 2, NB], F32)
        qm = L.get([128, 4, NB], BF16); ktT = L.get([128, 2, NB], BF16); khT = L.get([128, 2, NB], BF16)
        cat = L.get([128, 8, NB], BF16); th = L.get([128, NB], F32)
        v_tm = L.get([128, 2, 512], BF16)
        p1keys = [("xn", c_) for c_ in range(KC)] + ["m1sqb", "m1rstd", "m1tmpn", "aT", "e1", "Bc", ("eb", 0), ("eb", 1), ("enb", 0), ("enb", 1), "qm", ("ktT", 0), ("ktT", 1), ("khT", 0), ("khT", 1), "cat", "th", "v_tm"]
        L2 = Lay(R2)
        svg = L2.get([128, 512], F32); svsq = L2.get([128, 512], F32); svn = L2.get([128, 2, 512], BF16)
        svo = svsq
        sc_bd = L2.get([128, 4, 128], BF16); khm = L2.get([128, 2, 256], BF16)
        TR2_OFF = L2.off
        rstd_o = L2.get([128, NB], F32); tmp_o = L2.get([128, NB], F32); osq = L2.get([128, 4, NB], BF16)
        sst = L2.get([128, 8], F32); srs = L2.get([128, 8], F32)
        S0 = [L2.get([128, 2, 128], F32) for _ in range(2)]
        S0b = [L2.get([128, 2, 128], BF16) for _ in range(2)]
        S0 = S0 + [carve(R3, 4096 + i_ * 1024, [128, 2, 128], F32) for i_ in range(2)]
        S0b = S0b + [carve(R3, 4096 + 2048 + i_ * 512, [128, 2, 128], BF16) for i_ in range(2)]
        p1keys += ["svg", "svsq", "svn", "sc_bd", "khm", "rstd_o", "tmp_o", ("osq", 0), ("osq", 1), ("osq", 2), ("osq", 3), "sst", "srs",
                   ("S0", 0), ("S0", 1), ("S0b", 0), ("S0b", 1)]
        S.fence(p1keys)
        G("memset", dict(ap=qm[:], constant=0.0), [], ["qm"])
        G("memset", dict(ap=khm[:], constant=0.0), [], ["khm"])

        rr = [0]

        def rot():
            i = rr[0] % 2
            rr[0] += 1
            return PS[i], "ps%d" % i

        def mixer_block(tok0, n, xkeys, sample, pre=False, nxt=None, prev_wout=None, xfill=()):
            ntl = (n + 127) // 128
            blk = slice(tok0, tok0 + n)
            S.ctx = "p1.norm"
            fm_norm(lambda c: xT[:, c, blk], n, G_MIX, lambda c: xn[:, c, 0:n], sqb, rstd, tmpn, PS[0], "ps0",
                    xkeys, ["xn"], "m1", do_stats=not pre)
            xfill = list(xfill)
            if prev_wout is not None:
                prev_wout()
            if xfill:
                xfill.pop(0)()
            rr[0] = 1

            def proj(col0, m):
                pb, pk = rot()
                for c in range(KC):
                    MM(pb[0:m, 0:n], w_in_b[:, c, col0:col0 + m], xn[:, c, 0:n], c == 0, c == KC - 1, ["w_in", ("xn", c)], [pk])
                return pb, pk
            S.ctx = "p1.gates"
            pb, pk = proj(CA, 16)
            act(aT[0:16, 0:n], pb[0:16, 0:n], AF.Copy, [pk], ["aT"])
            rmask = rm_s if sample else rm
            for c in range(2):
                pb, pk = rot()
                MM(pb[:, 0:n], walpha_b[:, c * 128:(c + 1) * 128], aT[0:16, 0:n], True, True, ["walpha_b", "aT"], [pk])
                eX, eXk = (e1, "e1") if c == 0 else (th, "th")
                act(eX[:, 0:n], pb[:, 0:n], AF.Exp, [pk, "nba"], [eXk], scale=-1.0, bias=nba[:, c:c + 1])
                V("tensor_scalar", dict(out=eX[:, 0:n], in0=eX[:, 0:n], scalar1=1.0, scalar2=None, op0=ALU.add), [eXk], [eXk])
                act(eX[:, 0:n], eX[:, 0:n], AF.Ln, [eXk], [eXk])
                V("tensor_tensor_scan", dict(out=Bc[:, 0:n], data0=rmask[:, 0:n], data1=eX[:, 0:n], initial=0.0,
                                                 op0=ALU.mult, op1=ALU.add), [eXk, "rm", "rm_s"], ["Bc"])
                act(eb[:, c, 0:n], Bc[:, 0:n], AF.Exp, ["Bc"], [("eb", c)], scale=-1.0 / 16)
                act(enb[:, c, 0:n], Bc[:, 0:n], AF.Exp, ["Bc"], [("enb", c)], scale=1.0 / 16)
            S.ctx = "p1.qk"
            def proj4(col0, m, j):
                bi_ = (2, 3, 0, 1)[j % 4]
                pb, pk = PS[bi_], "ps%d" % bi_
                for c in range(KC):
                    MM(pb[0:m, 0:n], w_in_b[:, c, col0:col0 + m], xn[:, c, 0:n], c == 0, c == KC - 1, ["w_in", ("xn", c)], [pk])
                return pb, pk
            for c in range(2):
                pb, pk = proj4(CQ + c * 128, 128, c)
                for h2 in range(2):
                    rs_ = slice(h2 * 64, (h2 + 1) * 64)
                    V("scalar_tensor_tensor", dict(
                        out=qm[rs_, 2 * c + h2, 0:n], in0=pb[rs_, 0:n], scalar=0.125, in1=eb[rs_, c, 0:n],
                        op0=ALU.mult, op1=ALU.mult), [pk, ("eb", c)], ["qm"])
            if xfill:
                S.ctx = "p1.x"
                xfill.pop(0)()
                S.ctx = "p1.qk"
            cl = 4 if sample else 64
            for c in range(2):
                pb, pk = proj4(CK + c * 128, 128, 2 + c)
                V("tensor_tensor", dict(out=ktT[:, c, 0:n], in0=pb[:, 0:n], in1=enb[:, c, 0:n], op=ALU.mult),
                  [pk, ("enb", c)], [("ktT", c)])
                G("tensor_tensor", dict(
                    out=khT[:, c, 0:n].rearrange("p (a b) -> p a b", b=cl),
                    in0=ktT[:, c, 0:n].rearrange("p (a b) -> p a b", b=cl),
                    in1=eb[:, c, cl - 1:n:cl].unsqueeze(2).broadcast_to([128, n // cl, cl]), op=ALU.mult),
                  [("ktT", c), ("eb", c)], [("khT", c)])
            S.ctx = "p1.ru"
            def ru_piece(j):
                S.ctx = "p1.ru"
                if j < 4:
                    pb, pk = proj(CR + j * 128, 128)
                    act(th[:, 0:n], pb[:, 0:n], AF.Tanh, [pk], ["th"], scale=0.5)
                    V("scalar_tensor_tensor", dict(out=cat[:, j, 0:n], in0=th[:, 0:n], scalar=1.0, in1=pb[:, 0:n],
                                                   op0=ALU.add, op1=ALU.mult), [pk, "th"], [("cat", j)])
                else:
                    pb, pk = proj(CU + (j - 4) * 128, 128)
                    act(cat[:, j, 0:n], pb[:, 0:n], AF.Gelu_apprx_tanh, [pk], [("cat", j)])
                S.ctx = "p1.gla"
            ru_q = list(range(8))
            S.ctx = "p1.tm"
            if sample:
                G("memset", dict(ap=v_tm[:], constant=0.0), [], ["v_tm"])
                G("memset", dict(ap=svn[:], constant=0.0), [], ["svn"])
                G("memset", dict(ap=sc_bd[:], constant=0.0), [], ["sc_bd"])
            for tl in range(ntl):
                rows = min(128, n - tl * 128)
                tsl = slice(tl * 128, tl * 128 + rows)
                for c in range(KC):
                    MM(PS[2][0:rows, :], xn[:, c, tsl], w_in_b[:, c, CV:CV + 512], c == 0, c == KC - 1, [("xn", c), "w_in"], ["ps2"])
                act(v_tm[0:rows, tl, :], PS[2][0:rows, :], AF.Copy, ["ps2"], ["v_tm"])
                for c in range(KC):
                    MM(PS[3][0:rows, :], xn[:, c, tsl], w_in_b[:, c, CSV:CSV + 512], c == 0, c == KC - 1, [("xn", c), "w_in"], ["ps3"])
                act(svg[0:rows, :], PS[3][0:rows, :], AF.Gelu_apprx_tanh, ["ps3"], ["svg"])
                V("tensor_tensor", dict(out=svsq[0:rows, :], in0=svg[0:rows, :], in1=svg[0:rows, :], op=ALU.mult),
                  ["svg"], ["svsq"])
                V("tensor_reduce", dict(out=sst[0:rows, 0:4], in_=svsq[0:rows, :].rearrange("p (h d) -> p h d", h=4),
                                                       axis=AX.X, op=ALU.add), ["svsq"], ["sst"])
                act(sst[0:rows, 4:8], sst[0:rows, 0:4], AF.Ln, ["sst"], ["sst"], scale=1.0 / 128, bias=EPS)
                V("tensor_scalar", dict(out=sst[0:rows, 4:8], in0=sst[0:rows, 4:8], scalar1=-0.5, scalar2=None, op0=ALU.mult),
                  ["sst"], ["sst"])
                act(srs[0:rows, 0:4], sst[0:rows, 4:8], AF.Exp, ["sst"], ["srs"])
                for h in range(4):
                    hs = slice(h * 128, (h + 1) * 128)
                    dst = svo[0:rows, hs] if sample else svn[0:rows, tl, hs]
                    V("scalar_tensor_tensor", dict(
                        out=dst, in0=svg[0:rows, hs], scalar=srs[0:rows, h:h + 1], in1=gsgu_bc[0:rows, hs],
                        op0=ALU.mult, op1=ALU.mult), ["svg", "srs", "gsgu_bc"], ["svsq" if sample else "svn"])
                if sample:
                    V("tensor_copy", dict(out=svn[0:rows, 0, :], in_=svo[0:rows, :]), ["svsq", "svn"], ["svn"])
                    ld("sync", sv_s, svo[0:rows, :], "o_svs", reads=["svsq"])
            if nxt is not None:
                S.ctx = "p1.norm"
                ntok0, nn_, nxk = nxt
                nblk = slice(ntok0, ntok0 + nn_)
                fm_squares(lambda c: xT[:, c, nblk], nn_, sqb, nxk, "m1")
            S.ctx = "p1.gla"
            for tl in range(ntl):
                rows = min(128, n - tl * 128)
                tsl = slice(tl * 128, tl * 128 + rows)
                for h in range(4):
                    MM(PS[4][0:rows, h * 128:h * 128 + rows], ktT[:, h // 2, tsl], qm[:, h, tsl], True, True,
                       [("ktT", h // 2), "qm"], ["ps4"], sig=(h == 3))
                msk = mask_s[:] if sample else mask128[:]
                V("tensor_tensor", dict(
                    out=sc_bd[0:rows, :, 0:rows], in0=PS[4][0:rows, :].rearrange("p (h i) -> p h i", h=4)[:, :, 0:rows],
                    in1=msk.unsqueeze(1).broadcast_to([rows, 4, rows]), op=ALU.mult), ["ps4", "mask128", "mask_s"], ["sc_bd"])
                p5b = PS[5][:].bitcast(BF16)
                for c in range(2):
                    TR(p5b[0:rows, c * 128:(c + 1) * 128], khT[:, c, tsl], ident_b[:], [("khT", c), "ident_b"], ["ps5"], sig=(c == 1))
                if not sample:
                    for p in range(2):
                        prs = slice(p * 64, (p + 1) * 64)
                        act(khm[prs, p, :], p5b[prs, 0:256], AF.Copy, ["ps5"], ["khm"])
                else:
                    act(tmp_o[0:64, 0:128].bitcast(BF16), p5b[0:64, 0:256], AF.Copy, ["ps5"], ["tmp_o"])
                groups = [(p, slice(p * 64, (p + 1) * 64), None) for p in range(rows // 64)] if not sample else \
                    [(b, slice(b * 4, (b + 1) * 4), b) for b in range(16)]
                for gi, (p, csl, bidx) in enumerate(groups):
                    if sample:
                        s = bidx % 4
                        ld("sync", S0[s][:], sgla[bidx].rearrange("(c h2) d v -> (h2 d) c v", c=2), "S0_%d" % s,
                           writes=[("S0", s)] + (["m1sqb"] if s >= 2 else []))
                        act(S0b[s][:], S0[s][:], AF.Copy, [("S0", s)], [("S0b", s)] + (["m1sqb"] if s >= 2 else []))
                        V("tensor_scalar", dict(out=khm[0:64, bidx % 2, :], in0=tmp_o[0:64, 0:128].bitcast(BF16),
                                                                    scalar1=bm[0:64, bidx:bidx + 1], scalar2=None, op0=ALU.mult),
                          ["tmp_o", "bm"], ["khm"])
                        Sb_cur, Sb_key = S0b[s], ("S0b", s)
                        khm_cur = khm[:, bidx % 2, :]
                    else:
                        rslot, wslot = gch[0] % 2, (gch[0] + 1) % 2
                        gch[0] += 1
                        Sb_cur, Sb_key = S_bfs[rslot], ("S_bf", rslot)
                        khm_cur = khm[:, p, :]
                    ncol = csl.stop - csl.start
                    gsl = slice(tl * 128 + csl.start, tl * 128 + csl.stop)
                    ubank = (2 + rslot) if not sample else (4 + (gi % 2))
                    pu, puk = PS[ubank], "ps%d" % ubank

                    def u_mms():
                        for h in range(4):
                            c = h // 2
                            MM(pu[:, h * 128:(h + 1) * 128], khm_cur[:, c * 128:(c + 1) * 128], v_tm[:, tl, h * 128:(h + 1) * 128],
                               True, True, ["khm", "v_tm"], [puk], sig=(h == 3))

                    def o_mms():
                        for h in range(4):
                            c = h // 2
                            pob = PS[6 + h // 2]
                            ocol = (h % 2) * NB + tl * 128
                            outap = pob[:, ocol + csl.start:ocol + csl.stop]
                            MM(outap, Sb_cur[:, c, :], qm[:, h, gsl], True, False, [Sb_key, "qm"], ["ps%d" % (6 + h // 2)], sig=False)
                            MM(outap, v_tm[:, tl, h * 128:(h + 1) * 128], sc_bd[:, h, csl], False, True, ["v_tm", "sc_bd"],
                               ["ps%d" % (6 + h // 2)], sig=(h % 2 == 1))
                    if sample or gi > 0:
                        u_mms()
                        o_mms()
                    else:
                        o_mms()
                        u_mms()
                    ecol = tl * 128 + csl.stop - 1
                    for h in range(4):
                        c, h2 = h // 2, h % 2
                        rs_ = slice(h2 * 64, (h2 + 1) * 64)
                        if sample:
                            V("scalar_tensor_tensor", dict(
                                out=S0[s][rs_, c, :], in0=S0[s][rs_, c, :], scalar=eb[rs_, c, ecol:ecol + 1],
                                in1=pu[rs_, h * 128:(h + 1) * 128], op0=ALU.mult, op1=ALU.add), [puk, ("eb", c), ("S0", s)], [("S0", s)])
                        else:
                            V("scalar_tensor_tensor", dict(
                                out=S_bfs[wslot][rs_, c, :], in0=Sst[rs_, c, :], scalar=eb[rs_, c, ecol:ecol + 1],
                                in1=pu[rs_, h * 128:(h + 1) * 128], op0=ALU.mult, op1=ALU.add), [puk, ("eb", c), "Sst"], [("S_bf", wslot)])
                    if not sample:
                        for h in range(4):
                            c, h2 = h // 2, h % 2
                            rs_ = slice(h2 * 64, (h2 + 1) * 64)
                            V("scalar_tensor_tensor", dict(
                                out=Sst[rs_, c, :], in0=Sst[rs_, c, :], scalar=eb[rs_, c, ecol:ecol + 1],
                                in1=pu[rs_, h * 128:(h + 1) * 128], op0=ALU.mult, op1=ALU.add), [puk, ("eb", c), "Sst"], ["Sst"])
                    if sample:
                        ld("sync", sg_s[bidx].rearrange("(c h2) d v -> (h2 d) c v", c=2), S0[s][:], "o_sgs", reads=[("S0", s)])
                        if ru_q:
                            ru_piece(ru_q.pop(0))
                    else:
                        for _ in range(2):
                            if ru_q:
                                ru_piece(ru_q.pop(0))
            S.ctx = "p1.onorm"
            while ru_q:
                ru_piece(ru_q.pop(0))
            if nxt is not None:
                S.ctx = "p1.norm"
                fm_stats(lambda c: xT[:, c, nblk], nn_, sqb, rstd, tmpn, PS[0], "ps0", nxk, "m1", squares=False)
            S.ctx = "p1.onorm"
            for h in range(4):
                pob, pok = PS[6 + h // 2], "ps%d" % (6 + h // 2)
                osl = slice((h % 2) * NB, (h % 2) * NB + n)
                act(osq[:, h, 0:n], pob[:, osl], AF.Square, [pok], [("osq", h)])
            for h in range(4):
                pb, pk = rot()
                for tl in range(ntl):
                    rows = min(128, n - tl * 128)
                    zo = pb[:, tl * 128:tl * 128 + rows]
                    if sample:
                        MM(zo, svn[0:64, 0, h * 128:(h + 1) * 128], Wm_s[:, h, :], True, False, ["svn", "Wm_s"], [pk], sig=False)
                        bh = bs_hi[0:1, h * 128:h * 128 + 4].unsqueeze(1).broadcast_to([1, 16, 4])
                        bl = bs_lo[0:1, h * 128:h * 128 + 4].unsqueeze(1).broadcast_to([1, 16, 4])
                    else:
                        MM(zo, svn[:, tl, h * 128:(h + 1) * 128], Wm[:, h, :], True, False, ["svn", "Wm"], [pk], sig=False)
                        bh = bs_hi[0:1, h * 128:(h + 1) * 128]
                        bl = bs_lo[0:1, h * 128:(h + 1) * 128]
                    MM(zo, ones_b[0:1, :], bh, False, False, ["ones_b", "bs_hi"], [pk], sig=False)
                    MM(zo, ones_b[0:1, :], bl, False, True, ["ones_b", "bs_lo"], [pk], sig=True)
                V("tensor_tensor", dict(out=cat[:, 4 + h, 0:n], in0=pb[:, 0:n], in1=cat[:, 4 + h, 0:n], op=ALU.mult),
                  [pk, ("cat", 4 + h)], [("cat", 4 + h)])
            S.ctx = "p1.wout"
            S.ctx = "p1.onorm"
            tr2 = carve(R2, TR2_OFF, [128, 2, NB], F32)
            for hp in range(2):
                pob, pok = PS[6 + hp], "ps%d" % (6 + hp)
                pb, pk = rot()
                for h2_ in range(2):
                    h = 2 * hp + h2_
                    MM(pb[:, h2_ * NB:h2_ * NB + n], ones_b[:], osq[:, h, 0:n], True, True, ["ones_b", ("osq", h)], [pk], sig=(h2_ == 1))
                pv_ = pb[:].rearrange("p (a b) -> p a b", a=2)[:, :, 0:n]
                act(tr2[:, :, 0:n], pv_, AF.Ln, [pk], ["tmp_o", "rstd_o"], scale=1.0 / 128, bias=EPS)
                V("tensor_scalar", dict(out=tr2[:, :, 0:n], in0=tr2[:, :, 0:n], scalar1=-0.5, scalar2=None, op0=ALU.mult),
                  ["tmp_o", "rstd_o"], ["tmp_o", "rstd_o"])
                act(tr2[:, :, 0:n], tr2[:, :, 0:n], AF.Exp, ["tmp_o", "rstd_o"], ["tmp_o", "rstd_o"])
                for h2_ in range(2):
                    h = 2 * hp + h2_
                    osl = slice(h2_ * NB, h2_ * NB + n)
                    V("scalar_tensor_tensor", dict(out=tr2[:, h2_, 0:n], in0=pob[:, osl], scalar=gg[:, h:h + 1],
                                                   in1=tr2[:, h2_, 0:n], op0=ALU.mult, op1=ALU.mult),
                      [pok, "gg", "tmp_o", "rstd_o"], ["tmp_o", "rstd_o"])
                for h2_ in range(2):
                    h = 2 * hp + h2_
                    V("tensor_tensor", dict(out=cat[:, h, 0:n], in0=tr2[:, h2_, 0:n], in1=cat[:, h, 0:n], op=ALU.mult),
                      ["tmp_o", "rstd_o", ("cat", h)], [("cat", h)])
            catk = [("cat", j) for j in range(8)]
            def wout_part():
                S.ctx = "p1.wout"
                corder = [4, 5, 6, 7, 0, 1, 2, 3]
                for g0 in (0, 4):
                    for ci, c in enumerate(corder):
                        for oc in range(g0, g0 + 4):
                            bi_ = (2, 3, 0, 1)[oc % 4]
                            MM(PS[bi_][:, 0:n], w_out_b[:, c, oc * 128:(oc + 1) * 128], cat[:, c, 0:n], ci == 0, ci == 7,
                               ["w_out", ("cat", c)], ["ps%d" % bi_])
                    for oc in range(g0, g0 + 4):
                        bi_ = (2, 3, 0, 1)[oc % 4]
                        V("tensor_tensor", dict(out=xT[:, oc, blk], in0=PS[bi_][:, 0:n], in1=xT[:, oc, blk], op=ALU.add),
                          ["ps%d" % bi_] + list(xkeys), list(xkeys))
            return wout_part

        mblocks = [(b * NB, NB, [("xT", b)]) for b in range(TP // NB)] + [(TP, TS, [("xT", 8)])]
        for i_, (tk, nn, xk) in enumerate(mblocks):
            smp = (i_ == len(mblocks) - 1)
            if smp:
                ld("sync", sg_p.rearrange("(c h2) d v -> (h2 d) c v", c=2), Sst[:], "o_sgp", reads=["Sst"])
            if i_ <= 6:
                tl_ = [2 * i_ + 2, 2 * i_ + 3]
            elif i_ == 7:
                tl_ = [16]
            else:
                tl_ = []
            xf_ = [(lambda t=t: xtile(t, xs_slots[t % 2], ("xs", t % 2), "xs%d" % (t % 2), (0, 1), extra_w=["wv"])) for t in tl_]
            pw_ = mixer_block(tk, nn, xk, smp, pre=(i_ > 0), nxt=(mblocks[i_ + 1] if i_ + 1 < len(mblocks) else None),
                              prev_wout=(pw_ if i_ > 0 else None), xfill=xf_)
            if i_ == 7:
                load_w(wq_b, wq, "wq", 1024, writes=["wv", ("xs", 0), ("xs", 1)])
        pw_()

        if DEBUG:
            ld("sync", dbg1, xT[:], "o_dbg1", reads=[("xT", i) for i in range(9)])
        S.fence(["wo"])
        load_w(wo_b, wo, "wo", 1024)
        L = Lay(R3)
        hn = L.get([128, 8, NB], BF16); sqb2 = L.get([128, 8, NB], BF16)
        rstd2 = L.get([128, NB], F32); tmpn2 = L.get([128, NB], F32)
        qT = L.get([128, 8, NB], BF16)
        rinv = L.get([128, 256], F32); pn = L.get([128, 4, 256], BF16)
        pT = L.get([128, 4, 2, NB], BF16); oT = L.get([128, 8, NB], BF16)
        qTm = [L.get([128, 8, TS], BF16) for _ in range(2)]
        KTs = L.get([128, 8, 256], BF16)
        qT_s = L.get([128, 8, TS], BF16); pT_s = L.get([128, 4, 2, TS], BF16)
        kslot = [carve(R1, 49152 + i * 4096, [128, 2, 1024], BF16) for i in range(2)]
        p2keys = [("hn", c_) for c_ in range(KC)] + ["m2sqb", "m2rstd", "m2tmpn"] + [("qT", o_) for o_ in range(8)] + [("pn", 0), ("pn", 1), ("pn", 2), ("pn", 3), "pT", "oT", "qT_s", "pT_s", "rinv", ("KTs", 0), ("KTs", 1)] + [("sm8", h) for h in range(4)] + [("sm12", h) for h in range(4)] + [("smx", 0), ("smx", 1), ("qTm", 0), ("qTm", 1), "KTs",
                  ("kslot", 0), ("kslot", 1)]
        S.fence(p2keys)
        def slot_views(si):
            base = si * 24576
            return (carve(R1, base, [128, 8, 512], BF16), carve(R1, base + 8192, [128, 8, 512], BF16),
                    carve(R1, base + 16384, [128, 4, 1024], BF16))
        S.fence([("fslot", 0), ("fslot", 1)])

        def unit_pieces(u):
            f0, nf = UNITS[u]
            si = u % 2
            wg_v, wu_v, wd_v = slot_views(si)
            key = "fs%d" % si
            gsrc = w_gate.rearrange("(c p) n -> p c n", p=128)
            usrc = w_up.rearrange("(c p) n -> p c n", p=128)
            pcs = []
            pcs.append(lambda: ld("gpsimd", wg_v[:, :, 0:nf * 128], gsrc[:, :, f0 * 128:(f0 + nf) * 128], key, writes=[("fslot", si)]))
            pcs.append(lambda: ld("gpsimd", wu_v[:, :, 0:nf * 128], usrc[:, :, f0 * 128:(f0 + nf) * 128], key, writes=[("fslot", si)]))
            pcs.append(lambda: ld("gpsimd", wd_v[:, 0:nf, :],
                                  w_down[f0 * 128:(f0 + nf) * 128, :].rearrange("(l p) n -> p l n", p=128), key, writes=[("fslot", si)]))
            return pcs

        def load_unit(u):
            for p_ in unit_pieces(u):
                p_()

        def softmax_tile(rows, psA, psB, pkeys):
            for hh, pb in enumerate([psA, psB]):
                V("tensor_reduce", dict(out=smx[0:rows, 2 * hh:2 * hh + 2], in_=pb[0:rows, :].rearrange("p (h m) -> p h m", h=2),
                                        axis=AX.X, op=ALU.max, negate=True), [pkeys[hh]], [("smx", hh)])
            for h in range(4):
                pb = [psA, psB][h // 2]
                act(pn[0:rows, h, :], pb[0:rows, (h % 2) * 256:(h % 2 + 1) * 256], AF.Exp, [pkeys[h // 2], ("smx", h // 2)],
                    [("pn", h)], bias=smx[0:rows, h:h + 1])

        def attn_block(tok0, n, xkeys, prev_wo, sfill, pre=False, nxt=None):
            blk = slice(tok0, tok0 + n)
            sfill = list(sfill)

            def fill():
                if sfill:
                    sfill.pop(0)()
                if wpieces:
                    wpieces.pop(0)()
            S.ctx = "p2.norm"
            fm_norm(lambda c: xT[:, c, blk], n, G_X, lambda c: hn[:, c, 0:n], sqb2, rstd2, tmpn2, PS[0], "ps0",
                    xkeys, ["hn"], "m2", do_stats=not pre)
            rr[0] = 1
            S.ctx = "p2.q"
            for oc in range(8):
                bi_ = (0, 1, 2, 3)[oc % 4]
                pb, pk = PS[bi_], "ps%d" % bi_
                for c in range(KC):
                    MM(pb[:, 0:n], wq_b[:, c, oc * 128:(oc + 1) * 128], hn[:, c, 0:n], c == 0, c == KC - 1, ["wq", ("hn", c)], [pk])
                act(qT[:, oc, 0:n], pb[:, 0:n], AF.Copy, [pk], [("qT", oc)], scale=1.0 / 16)
            fill()
            ntl = n // 128

            def scores(tl):
                S.ctx = "p2.sc"
                tsl = slice(tl * 128, (tl + 1) * 128)
                for h in range(4):
                    pb, pk = PS[2 + h // 2], "ps%d" % (2 + h // 2)
                    for dc in range(2):
                        MM(pb[:, (h % 2) * 256:(h % 2 + 1) * 256], qT[:, 2 * h + dc, tsl], KT[:, 2 * h + dc, :], dc == 0, dc == 1,
                           [("qT", 2 * h + dc), "KT"], [pk])

            def transposes(tl):
                S.ctx = "p2.pT"
                tsl = slice(tl * 128, (tl + 1) * 128)
                p4b = PS[4][:].bitcast(BF16)
                for h in range(4):
                    for mc in range(2):
                        TR(p4b[:, (h * 2 + mc) * 128:(h * 2 + mc + 1) * 128], pn[:, h, mc * 128:(mc + 1) * 128], ident_b[:],
                           [("pn", h), "ident_b"], ["ps4"], sig=(mc == 1))
                V("tensor_copy", dict(out=pT[:, :, :, tsl], in_=p4b.rearrange("p (h m t) -> p h m t", h=4, m=2)), ["ps4"], ["pT"])
            scores(0)
            for tl in range(ntl):
                S.ctx = "p2.smax"
                softmax_tile(128, PS[2], PS[3], ["ps2", "ps3"])
                if tl + 1 < ntl:
                    scores(tl + 1)
                if tl == 0 and prev_wo is not None:
                    prev_wo()
                transposes(tl)
                fill()
            if nxt is not None:
                S.ctx = "p2.norm"
                ntok0, nn_, nxk = nxt
                nblk = slice(ntok0, ntok0 + nn_)
                fm_squares(lambda c: xT[:, c, nblk], nn_, sqb2, nxk, "m2")
            S.ctx = "p2.pv"
            pvb = [0, 1, 4]
            pvi = [0]

            def rot3():
                i_ = pvb[pvi[0] % 3]
                pvi[0] += 1
                return PS[i_], "ps%d" % i_
            for h in range(4):
                pbs, pks = rot3()
                for mc in range(2):
                    MM(pbs[:, 0:n], ones_b[:], pT[:, h, mc, 0:n], mc == 0, mc == 1, ["ones_b", "pT"], [pks])
                V("reciprocal", dict(out=rinv[:, 0:n], in_=pbs[:, 0:n]), [pks], ["rinv"])
                for dc in range(2):
                    oc = 2 * h + dc
                    pb, pk = rot3()
                    for mc in range(2):
                        MM(pb[:, 0:n], vbf[:, mc, oc * 128:(oc + 1) * 128], pT[:, h, mc, 0:n], mc == 0, mc == 1,
                           ["vbf", "pT"], [pk], sig=(mc == 1))
                    V("tensor_tensor", dict(out=oT[:, oc, 0:n], in0=pb[:, 0:n], in1=rinv[:, 0:n], op=ALU.mult),
                      [pk, "rinv"], ["oT"])
            if nxt is not None:
                S.ctx = "p2.norm"
                fm_stats(lambda c: xT[:, c, nblk], nn_, sqb2, rstd2, tmpn2, PS[0], "ps0", nxk, "m2", squares=False)
            fill()
            for fl in sfill:
                fl()

            def wo_part():
                S.ctx = "p2.wo"
                for oc in range(8):
                    pb, pk = rot()
                    for c in range(KC):
                        MM(pb[:, 0:n], wo_b[:, c, oc * 128:(oc + 1) * 128], oT[:, c, 0:n], c == 0, c == KC - 1, ["wo", "oT"], [pk])
                    V("tensor_tensor", dict(out=xT[:, oc, blk], in0=pb[:, 0:n], in1=xT[:, oc, blk], op=ALU.add),
                      [pk] + list(xkeys), list(xkeys))
            return wo_part

        sblk = slice(TP, TP + TS)
        skeys = [("xT", 8)]

        def s_init():
            S.ctx = "p2.s_init"
            fm_norm(lambda c: xT[:, c, sblk], TS, G_X, lambda c: hn[:, c, 0:TS], sqb2, rstd2, tmpn2, PS[0], "ps0",
                    skeys, ["hn"], "m2")
            rr[0] = 1
            for oc in range(8):
                pb, pk = rot()
                for c in range(KC):
                    MM(pb[:, 0:TS], wq_b[:, c, oc * 128:(oc + 1) * 128], hn[:, c, 0:TS], c == 0, c == KC - 1, ["wq", ("hn", c)], [pk])
                act(qT_s[:, oc, :], pb[:, 0:TS], AF.Copy, [pk], ["qT_s"], scale=1.0 / 16)
            G("memset", dict(ap=qTm[0][:], constant=0.0), [], [("qTm", 0)])
            G("memset", dict(ap=qTm[1][:], constant=0.0), [], [("qTm", 1)])

        def s_kpass(b):
            S.ctx = "p2.s_k"
            s = b % 2
            ld("gpsimd", kslot[s][:], ck[b].rearrange("(mt p) d -> p mt d", p=128), "ks%d" % s, writes=[("kslot", s)])
            if b >= 2:
                V("memset", dict(ap=qTm[s][:, :, (b - 2) * 4:(b - 2) * 4 + 4], constant=0.0), [], [("qTm", s)])
            V("tensor_copy", dict(out=qTm[s][:, :, b * 4:b * 4 + 4], in_=qT_s[:, :, b * 4:b * 4 + 4]), ["qT_s"], [("qTm", s)])
            for half in range(2):
                p45 = PS[4 + half][:].bitcast(BF16)
                for oc4 in range(4):
                    oc = half * 4 + oc4
                    for mt in range(2):
                        TR(p45[:, oc4 * 256 + mt * 128:oc4 * 256 + (mt + 1) * 128], kslot[s][:, mt, oc * 128:(oc + 1) * 128],
                           ident_b[:], [("kslot", s), "ident_b"], ["ps%d" % (4 + half)], sig=(oc4 == 3 and mt == 1))
                src = p45.rearrange("p (o m) -> p o m", o=4)
                if half == 0:
                    V("tensor_copy", dict(out=KTs[:, 0:4, :], in_=src), ["ps4"], [("KTs", 0)])
                else:
                    act(KTs[:, 4:8, :], src, AF.Copy, ["ps5"], [("KTs", 1)])
            for h in range(4):
                pb, pk = PS[6 + h // 2], "ps%d" % (6 + h // 2)
                for dc in range(2):
                    first = (b == 0 and h % 2 == 0 and dc == 0)
                    last = (b == 15 and dc == 1)
                    MM(pb[0:64, (h % 2) * 256:(h % 2 + 1) * 256], qTm[s][:, 2 * h + dc, :], KTs[:, 2 * h + dc, :], first, last,
                       [("qTm", s), ("KTs", h // 2)], [pk], sig=(dc == 1))

        def s_mid():
            S.ctx = "p2.s_mid"
            softmax_tile(64, PS[6], PS[7], ["ps6", "ps7"])
            p4b = PS[4][:].bitcast(BF16)
            for h in range(4):
                for mc in range(2):
                    TR(p4b[:, (h * 2 + mc) * 64:(h * 2 + mc + 1) * 64], pn[0:64, h, mc * 128:(mc + 1) * 128], ident_b[0:64, 0:64],
                       [("pn", h), "ident_b"], ["ps4"], sig=(mc == 1))
            V("tensor_copy", dict(out=pT_s[:], in_=p4b[:, 0:512].rearrange("p (h m t) -> p h m t", h=4, m=2)), ["ps4"], ["pT_s"])

        def s_vpass(b):
            S.ctx = "p2.s_v"
            s = b % 2
            ld("gpsimd", kslot[s][:], cv[b].rearrange("(mt p) d -> p mt d", p=128), "ks%d" % s, writes=[("kslot", s)])
            for oc in range(8):
                for mc in range(2):
                    MM(PS[5][:, oc * 64 + b * 4:oc * 64 + b * 4 + 4], kslot[s][:, mc, oc * 128:(oc + 1) * 128],
                       pT_s[:, oc // 2, mc, b * 4:b * 4 + 4], mc == 0, mc == 1, [("kslot", s), "pT_s"], ["ps5"],
                       sig=(oc == 7 and mc == 1))

        def s_fin():
            S.ctx = "p2.s_fin"
            pbs, pks = rot()
            for h in range(4):
                for mc in range(2):
                    MM(pbs[:, h * TS:(h + 1) * TS], ones_b[:], pT_s[:, h, mc, :], mc == 0, mc == 1, ["ones_b", "pT_s"], [pks], sig=(mc == 1))
            V("reciprocal", dict(out=rinv[:, 0:4 * TS], in_=pbs[:, 0:4 * TS]), [pks], ["rinv"])
            V("tensor_tensor", dict(out=oT[:, :, 0:TS].rearrange("p (h d) t -> p h d t", h=4),
                                    in0=PS[5][:].rearrange("p (h d t) -> p h d t", h=4, d=2),
                                    in1=rinv[:, 0:4 * TS].rearrange("p (h t) -> p h t", h=4).unsqueeze(2).broadcast_to([128, 4, 2, TS]),
                                    op=ALU.mult), ["ps5", "rinv"], ["oT"])
            for oc in range(8):
                pb, pk = rot()
                for c in range(KC):
                    MM(pb[:, 0:TS], wo_b[:, c, oc * 128:(oc + 1) * 128], oT[:, c, 0:TS], c == 0, c == KC - 1, ["wo", "oT"], [pk])
                V("tensor_tensor", dict(out=xT[:, oc, sblk], in0=pb[:, 0:TS], in1=xT[:, oc, sblk], op=ALU.add),
                  [pk] + skeys, skeys)

        s_init()
        prev_wo = None
        wpieces = unit_pieces(0) + unit_pieces(1)
        for b in range(TP // NB):
            if b < 4:
                sf = [(lambda bb=bb: s_kpass(bb)) for bb in range(4 * b, 4 * b + 4)]
            else:
                sf = [(lambda bb=bb: s_vpass(bb)) for bb in range(4 * (b - 4), 4 * (b - 4) + 4)]
            nxt_ = ((b + 1) * NB, NB, [("xT", b + 1)]) if b + 1 < TP // NB else None
            prev_wo = attn_block(b * NB, NB, [("xT", b)], prev_wo, sf, pre=(b > 0), nxt=nxt_)
            if b == 3:
                s_mid()
        prev_wo()
        for p_ in wpieces:
            p_()
        s_fin()

        if DEBUG:
            ld("sync", dbg2, xT[:], "o_dbg2", reads=[("xT", i) for i in range(9)])
        hnA = carve(R3, 0, [128, 8, T], BF16)
        L = Lay(R2)
        sqb3 = L.get([128, 8, 256], BF16)
        rstd3 = L.get([128, 256], F32); tmpn3 = L.get([128, 256], F32)
        gsbs = [L.get([128, NB3 + 2], F32) for _ in range(2)]
        cbs = [L.get([128, NB3], F32) for _ in range(2)]
        actbs = [L.get([128, 4, NB3], BF16) for _ in range(2)]
        sconvT = L.get([128, NF, 32], F32)
        gkeep = sconvT
        gsss = [L.get([128, 16, 6], F32) for _ in range(2)]
        ubs = [L.get([128, NB3], BF16) for _ in range(2)]
        stgs = [carve(R1, 49152 + i_ * 4096, [128, 1024], F32) for i_ in range(2)]
        stg3 = stgs[0]
        gfin_bc = L.get([128, 1024], F32)
        fst = L.get([128, 16], F32)
        p3keys = [("hnA", i_) for i_ in range(5)] + [("gsbh", 0), ("gsbh", 1), ("gssh", 0), ("gssh", 1), "m3sqb", "m3rstd", "m3tmpn", ("gsb", 0), ("gsb", 1), ("cb", 0), ("cb", 1), "sconvT", ("gss", 0), ("gss", 1), ("ub", 0), ("ub", 1), "stg3", ("stg", 1), "gfin_bc", ("fst", 0), ("fst", 1)] + \
                 [("actb", a_, l_) for a_ in range(2) for l_ in range(4)]
        S.fence(p3keys)
        ld("sync", gfin_bc[:], g_fin.broadcast_to([128, D]), "gfin", writes=["gfin_bc"])
        for half in range(3):
            c0, c1 = half * 1024, min(DFF, (half + 1) * 1024)
            ld("sync", stg3[0:32, 0:c1 - c0], sconv[:, c0:c1], "stg3", writes=["stg3"])
            nfh = (c1 - c0) // 128
            for j in range(nfh):
                TR(PS[7][:, j * 32:(j + 1) * 32], stg3[0:32, j * 128:(j + 1) * 128], ident_f[0:32, 0:32], ["stg3", "ident"], ["ps7"],
                   sig=(j == nfh - 1))
            V("tensor_copy", dict(out=sconvT[:, half * 8:half * 8 + nfh, :],
                                  in_=PS[7][:, 0:nfh * 32].rearrange("p (f t) -> p f t", t=32)), ["ps7"], ["sconvT"])
        blocks3 = [(b * NB3, NB3, [("xT", 2 * b), ("xT", 2 * b + 1)], False) for b in range(TP // NB3)] + [(TP, TS, [("xT", 8)], True)]
        gcnt = [0]

        def ffn_front(u, bi, ab, fillers=()):
            S.ctx = "p3.front"
            tails = {}
            f0, nf = UNITS[u]
            si = u % 2
            wg_v, wu_v, wd_v = slot_views(si)
            fk = ("fslot", si)
            tok0, n, xkeys, sample = blocks3[bi]
            blk = slice(tok0, tok0 + n)
            if u == 0:
                for s0 in range(0, n, 256):
                    nn = min(256, n - s0)
                    sub = slice(tok0 + s0, tok0 + s0 + nn)
                    fm_norm(lambda c: xT[:, c, sub], nn, G_FFN, lambda c: hnA[:, c, sub], sqb3, rstd3, tmpn3, PS[0], "ps0",
                            xkeys, [("hnA", bi)], "m3")
            hk = ("hnA", bi)
            for lf in range(nf):
                f = f0 + lf
                gi = gcnt[0] % 2
                gcnt[0] += 1
                gsb, cb = gsbs[gi], cbs[gi]
                gk, ck_ = ("gsb", gi), ("cb", gi)
                pg, pgk = PS[1 + (lf % 2)], "ps%d" % (1 + lf % 2)
                pub_ = 3 if u == len(UNITS) - 1 else 3 + (lf % 2)
                pu, puk = PS[pub_], "ps%d" % pub_
                for c in range(KC):
                    MM(pg[:, 0:n], wg_v[:, c, lf * 128:(lf + 1) * 128], hnA[:, c, blk], c == 0, c == KC - 1, [fk, hk], [pgk])
                for c in range(KC):
                    MM(pu[:, 0:n], wu_v[:, c, lf * 128:(lf + 1) * 128], hnA[:, c, blk], c == 0, c == KC - 1, [fk, hk], [puk])
                cw = lambda j, f=f: colv2[:, j * NF + f:j * NF + f + 1]
                cbias = colv1[:, 46 + f:47 + f]
                ghk = ("gsbh", gi)
                if not sample:
                    V("tensor_copy", dict(out=gsb[:, 0:2], in_=ghist[:, f, :]), ["ghist"], [ghk])
                    act(gsb[:, 2:2 + n], pg[:, 0:n], AF.Copy, [pgk], [gk])
                    V("tensor_copy", dict(out=ghist[:, f, :], in_=gsb[:, n:n + 2]), [gk], ["ghist"])
                    g0, g1, g2 = gsb[:, 0:n], gsb[:, 1:1 + n], gsb[:, 2:2 + n]
                    cbo = cb[:, 0:n]
                else:
                    gk = ("gss", gi)
                    ghk = ("gssh", gi)
                    gss = gsss[gi]
                    V("tensor_copy", dict(out=gss[:, :, 0:2], in_=sconvT[:, f, :].rearrange("p (b j) -> p b j", j=2)),
                      ["sconvT"], [ghk])
                    act(gss[:, :, 2:6], pg[:, 0:n].rearrange("p (b t) -> p b t", t=4), AF.Copy, [pgk], [gk])
                    V("tensor_copy", dict(out=gkeep[:, f, :].rearrange("p (b j) -> p b j", j=2), in_=gss[:, :, 4:6]),
                      [gk], ["sconvT"])
                    g0, g1, g2 = gss[:, :, 0:4], gss[:, :, 1:5], gss[:, :, 2:6]
                    cbo = cb[:, 0:n].rearrange("p (b t) -> p b t", t=4)
                ub, ubk = ubs[gi], ("ub", gi)
                act(ub[:, 0:n], pu[:, 0:n], AF.Copy, [puk], [ubk])

                def ident(cbo=cbo, g2=g2, gk=gk, ck_=ck_, cw=cw, cbias=cbias):
                    act(cbo, g2, AF.Identity, [gk, "colv1", "colv2"], [ck_], scale=cw(2), bias=cbias)

                S.ctx = "p3.front"

                def tail(cbo=cbo, g1=g1, g0=g0, cw=cw, gk=gk, ghk=ghk, ck_=ck_, cb=cb, ub=ub, ubk=ubk, lf=lf):
                    S.ctx = "p3.chain"
                    V("scalar_tensor_tensor", dict(out=cbo, in0=g1, scalar=cw(1), in1=cbo, op0=ALU.mult, op1=ALU.add), [gk, ghk, ck_], [ck_])
                    V("scalar_tensor_tensor", dict(out=cbo, in0=g0, scalar=cw(0), in1=cbo, op0=ALU.mult, op1=ALU.add), [gk, ghk, ck_], [ck_])
                    act(cb[:, 0:n], cb[:, 0:n], AF.Gelu_apprx_tanh, [ck_], [ck_])
                    V("tensor_tensor", dict(out=actbs[ab][:, lf, 0:n], in0=cb[:, 0:n], in1=ub[:, 0:n], op=ALU.mult),
                      [ck_, ubk], [("actb", ab, lf)])
                tails[lf] = tail
                if lf >= 1:
                    tails[lf - 1]()
                ident()
                if fillers:
                    fillers.pop(0)()
                S.ctx = "p3.front"
            return tails[nf - 1]

        fcnt = [0]

        def final_out(bi):
            tok0, n, xkeys, sample = blocks3[bi]
            tiles = list(range(0, n, 128))
            As, Bs = [], []
            for s0 in tiles:
                par = fcnt[0] % 2
                fcnt[0] += 1
                As.append(lambda s0=s0, par=par: final_A(tok0, n, xkeys, s0, par))
                Bs.append([(lambda s0=s0, par=par, half=half: final_B(tok0, n, xkeys, sample, s0, par, half)) for half in range(2)])
            seq = []
            nt = len(tiles)
            seq.append(As[0])
            if nt > 1:
                seq.append(As[1])
            for i_ in range(nt):
                seq += Bs[i_]
                if i_ + 2 < nt:
                    seq.append(As[i_ + 2])
            return seq

        def final_A(tok0, n, xkeys, s0, par):
            S.ctx = "p3.final"
            rows = min(128, n - s0)
            tsl = slice(tok0 + s0, tok0 + s0 + rows)
            fo = par * 8
            act(sqb3[:, :, 0:rows], xT[:, :, tsl], AF.Square, list(xkeys), ["m3sqb"])
            for c in range(KC):
                MM(PS[4][0:rows, 0:1], sqb3[:, c, 0:rows], ones_b[:, 0:1], c == 0, c == KC - 1, ["m3sqb", "ones_b"], ["ps4"])
            act(fst[0:rows, fo + 3:fo + 4], PS[4][0:rows, 0:1], AF.Ln, ["ps4"], [("fst", par)], scale=1.0 / D, bias=EPS)
            V("tensor_scalar", dict(out=fst[0:rows, fo + 4:fo + 5], in0=fst[0:rows, fo + 3:fo + 4], scalar1=-0.5, scalar2=None, op0=ALU.mult),
              [("fst", par)], [("fst", par)])
            act(fst[0:rows, fo + 5:fo + 6], fst[0:rows, fo + 4:fo + 5], AF.Exp, [("fst", par)], [("fst", par)])

        def final_B(tok0, n, xkeys, sample, s0, par, half):
            S.ctx = "p3.final"
            rows = min(128, n - s0)
            tsl = slice(tok0 + s0, tok0 + s0 + rows)
            fo = par * 8
            stg_, sk_ = stgs[par], ("stg3" if par == 0 else ("stg", 1))
            bank_ = 7 if half == 0 else 0
            pb, pk = PS[bank_], "ps%d" % bank_
            for c4 in range(4):
                c = half * 4 + c4
                TR(pb[0:rows, c4 * 128:(c4 + 1) * 128], xT[:, c, tsl], ident_f[:], list(xkeys) + ["ident"], [pk], sig=(c4 == 3))
            V("scalar_tensor_tensor", dict(out=stg_[0:rows, half * 512:(half + 1) * 512], in0=pb[0:rows, :], scalar=fst[0:rows, fo + 5:fo + 6],
                                           in1=gfin_bc[0:rows, half * 512:(half + 1) * 512], op0=ALU.mult, op1=ALU.mult),
              [pk, ("fst", par), "gfin_bc"], [sk_])
            if half == 1:
                dst = y_s if sample else y_p[tok0 + s0:tok0 + s0 + rows, :]
                ld("sync", dst, stg_[0:rows, :], "o_y%d" % par, reads=[sk_])

        def ffn_back(u, bi, ab, fillers=None):
            S.ctx = "p3.back"
            f0, nf = UNITS[u]
            si = u % 2
            wg_v, wu_v, wd_v = slot_views(si)
            fk = ("fslot", si)
            tok0, n, xkeys, sample = blocks3[bi]
            blk = slice(tok0, tok0 + n)
            dbanks = [5, 6]
            if u != len(UNITS) - 1:
                dbanks.append(7)
                if u > 0:
                    dbanks.append(0)
            for oc in range(8):
                bi_ = dbanks[oc % len(dbanks)]
                pb, pk = PS[bi_], "ps%d" % bi_
                for lf in range(nf):
                    MM(pb[:, 0:n], wd_v[:, lf, oc * 128:(oc + 1) * 128], actbs[ab][:, lf, 0:n], lf == 0, lf == nf - 1,
                       [fk] + [("actb", ab, l) for l in range(nf)], [pk])
                V("tensor_tensor", dict(out=xT[:, oc, blk], in0=pb[:, 0:n], in1=xT[:, oc, blk], op=ALU.add),
                  [pk] + list(xkeys), list(xkeys))
                if fillers:
                    fillers.pop(0)()
                    S.ctx = "p3.back"

        def conv_state_out(f0, nf):
            S.ctx = "p3.cso"
            stgp = stgs[1]
            for j in range(nf):
                TR(PS[0][0:2, j * 128:(j + 1) * 128], ghist[:, f0 + j, :], ident_f[:], ["ghist", "ident"], ["ps0"], sig=(j == nf - 1))
            V("tensor_copy", dict(out=stgp[0:2, 0:nf * 128], in_=PS[0][0:2, 0:nf * 128]), ["ps0"], [("stg", 1)])
            ld("sync", sc_p[:, f0 * 128:(f0 + nf) * 128], stgp[0:2, 0:nf * 128], "o_scp", reads=[("stg", 1)])
            for j in range(nf):
                TR(PS[7][0:32, j * 128:(j + 1) * 128], gkeep[:, f0 + j, :], ident_f[:], ["sconvT", "ident"], ["ps7"], sig=(j == nf - 1))
            V("tensor_copy", dict(out=stg3[0:32, 0:nf * 128], in_=PS[7][0:32, 0:nf * 128]), ["ps7"], ["stg3"])
            ld("sync", sc_s[:, f0 * 128:(f0 + nf) * 128], stg3[0:32, 0:nf * 128], "o_scs", reads=["stg3"])

        abc = 0
        for u in range(len(UNITS)):
            prev = None
            lastu = (u == len(UNITS) - 1)
            pend = []
            for bi in range(len(blocks3)):
                ab = abc % 2
                abc += 1
                deferred = ffn_front(u, bi, ab, fillers=pend)
                if prev is not None:
                    ffn_back(u, *prev, fillers=pend)
                    if lastu:
                        pend += final_out(prev[0])
                deferred()
                prev = (bi, ab)
            ffn_back(u, *prev, fillers=pend)
            if lastu:
                pend += final_out(prev[0])
            for fl in pend:
                fl()
            conv_state_out(*UNITS[u])
            if u + 2 < len(UNITS):
                load_unit(u + 2)
        _NC_CACHE["sched"] = S
        sems = {n: es.enter_context(nc.semaphore(n)) for n in sorted(S.sem_names)}
        final = {"sync": [(s, v) for s, v in S.dma_cnt.items() if s.startswith("D_o_")]}
        with nc.Block() as block:
            S.emit(block, sems, final)
    return nc


_NC_CACHE = {}


def kernel(**inp):
    f = lambda a: np.ascontiguousarray(np.asarray(a, dtype=np.float32))
    x_prompt, x_sample, mem_prompt = f(inp["x_prompt"]), f(inp["x_sample"]), f(inp["mem_prompt"])
    state_gla, state_conv = f(inp["state_gla"])[0], f(inp["state_conv"])[0]
    ckk, cvv = f(inp["cache_mem_k"])[0], f(inp["cache_mem_v"])[0]
    vec1 = np.concatenate([f(inp["g_mix"]).reshape(8, 128), f(inp["g_x"]).reshape(8, 128), f(inp["g_mem"]).reshape(8, 128),
                           f(inp["g_ffn"]).reshape(8, 128), f(inp["g_final"]).reshape(8, 128), f(inp["b_alpha"]).reshape(2, 128),
                           f(inp["g_gla_out"]).reshape(4, 128), f(inp["conv_b"]).reshape(22, 128)], axis=0)
    vec2 = f(inp["conv_w"]).reshape(66, 128)
    shared = {
        "w_in": f(inp["w_in"])[0], "w_alpha": f(inp["w_alpha"])[0], "w_s": f(inp["w_s"])[0],
        "b_s": f(inp["b_s"]).reshape(1, 512), "g_sgu": f(inp["g_sgu"]).reshape(1, 512),
        "w_out": f(inp["w_out"])[0], "wq": f(inp["wq_x"])[0], "wk": f(inp["wk_x"])[0], "wv": f(inp["wv_x"])[0], "wo": f(inp["wo_x"])[0],
        "w_gate": f(inp["w_gate"])[0], "w_up": f(inp["w_up"])[0], "w_down": f(inp["w_down"])[0],
        "vec1": np.ascontiguousarray(vec1), "vec2": np.ascontiguousarray(vec2), "g_fin": f(inp["g_final"]).reshape(1, D),
    }
    for k, v in host_consts().items():
        shared["c_" + k] = v
    in_maps = []
    for c in range(NCORES):
        m = dict(shared)
        sl = slice(c * 16, (c + 1) * 16)
        m["x_p"] = x_prompt[c]
        m["x_s"] = np.ascontiguousarray(x_sample[sl].reshape(TS, D))
        m["mem"] = mem_prompt[c]
        m["sgla"] = np.ascontiguousarray(state_gla[sl])
        m["sconv"] = np.ascontiguousarray(state_conv[sl].reshape(32, DFF))
        m["ck"] = np.ascontiguousarray(ckk[sl].reshape(16, 256, D))
        m["cv"] = np.ascontiguousarray(cvv[sl].reshape(16, 256, D))
        in_maps.append(m)
    if "nc" not in _NC_CACHE:
        _NC_CACHE["nc"] = build_nc()
    res = run_bass_kernel_spmd(_NC_CACHE["nc"], in_maps, core_ids=list(range(NCORES)))
    R = res.results
    cat = lambda k: np.stack([np.asarray(r[k], dtype=np.float32) for r in R], axis=0)
    y_prompt = cat("y_p")
    y_sample = cat("y_s").reshape(128, 4, D)
    sg_p = cat("sg_p")[None]
    sc_p = cat("sc_p")[None]
    mk_p = cat("mk_p").reshape(1, 8, 256, 4, 256)
    mv_p = cat("mv_p").reshape(1, 8, 256, 4, 256)
    sg_s = cat("sg_s").reshape(1, 128, 4, 64, 128)
    sc_s = cat("sc_s").reshape(1, 128, 2, DFF)
    sv_s = cat("sv_s").reshape(1, 128, 4, 4, 128)
    if DEBUG:
        _NC_CACHE["dbg"] = (R[0]["dbg1"], R[0]["dbg2"])
    return (y_prompt, y_sample, sg_p, sc_p, mk_p, mv_p, sg_s, sc_s, sv_s)
```

```python
import numpy as np
from contextlib import ExitStack
import concourse.bass as bass
import concourse.mybir as mybir
from concourse.bass_utils import run_bass_kernel_spmd

F32 = mybir.dt.float32
BF16 = mybir.dt.bfloat16
U8 = mybir.dt.uint8
AF = mybir.ActivationFunctionType
ALU = mybir.AluOpType
AX = mybir.AxisListType

NCORES = 8
TP, TS = 2048, 64
T = TP + TS
D, KC = 1024, 8
DIN = 2576
CQ, CK, CV, CR, CA, CU, CSV = 0, 256, 512, 1024, 1536, 1552, 2064
DFF, NF = 2816, 22
UNITS = [(0, 4), (4, 4), (8, 4), (12, 4), (16, 3), (19, 3)]
EPS = 1e-6
NB = 256
NB3 = 512
DEBUG = False


class Sched:
    ENGS = ["tensor", "vector", "scalar", "gpsimd", "sync"]

    def __init__(self):
        self.ops = {e: [] for e in self.ENGS}
        self.cnt = {e: 0 for e in self.ENGS}
        self.pending = {e: False for e in self.ENGS}
        self.writers, self.readers = {}, {}
        self.seen = {e: {} for e in self.ENGS}
        self.dma_cnt = {}
        self.sem_names = set()
        self.tags = {e: [] for e in self.ENGS}
        self.ctx = ""
        self.dma_hist = {}

    def _deps(self, eng, reads, writes):
        toks = {}

        def add(d):
            for s, v in d.items():
                if v > toks.get(s, 0):
                    toks[s] = v
        for k in reads:
            add(self.writers.get(k, {}))
        for k in writes:
            add(self.writers.get(k, {}))
            add(self.readers.get(k, {}))
        waits = []
        for s, v in toks.items():
            if s == "E_" + eng and eng != "gpsimd":
                continue
            if self.seen[eng].get(s, 0) >= v:
                continue
            self.seen[eng][s] = v
            waits.append((s, v))
        return waits

    def _commit(self, tok, reads, writes):
        s, v = tok
        for k in reads:
            r = self.readers.setdefault(k, {})
            r[s] = max(r.get(s, 0), v)
        for k in writes:
            self.writers[k] = {s: v}
            self.readers[k] = {}

    def op(self, eng, fn, reads=(), writes=(), sig=True):
        isps = lambda k: isinstance(k, str) and k.startswith("ps") and k[2:].isdigit()
        writes = list(writes) + [k for k in reads if isps(k)]
        reads = [k for k in reads if not isps(k)]
        waits = self._deps(eng, reads, writes)
        s = "E_" + eng
        self.sem_names.add(s)
        if sig:
            self.cnt[eng] += 1
            self.pending[eng] = False
            tok = (s, self.cnt[eng])
        else:
            self.pending[eng] = True
            tok = (s, self.cnt[eng] + 1)
        self.ops[eng].append((fn, waits, tok, 1 if sig else 0))
        self.tags[eng].append(self.ctx)
        self._commit(tok, reads, writes)

    def dma(self, eng, fn, key, reads=(), writes=()):
        waits = self._deps(eng, reads, writes)
        s = "D_" + key
        self.sem_names.add(s)
        self.dma_cnt[s] = self.dma_cnt.get(s, 0) + 16
        tok = (s, self.dma_cnt[s])
        hist = self.dma_hist.setdefault(eng, [])
        lim = 16 if eng == "gpsimd" else 24
        if len(hist) >= lim:
            os_, ov_ = hist[-lim]
            if self.seen[eng].get(os_, 0) < ov_:
                self.seen[eng][os_] = ov_
                waits = list(waits) + [(os_, ov_)]
        hist.append(tok)
        self.ops[eng].append((fn, waits, tok, 2))
        self.tags[eng].append(self.ctx)
        self._commit(tok, reads, writes)

    def fence(self, keys):
        for e in self.ENGS:
            assert not self.pending[e], e
        d = {"E_" + e: self.cnt[e] for e in self.ENGS if self.cnt[e] > 0}
        d.update(self.dma_cnt)
        for k in keys:
            self.writers[k] = dict(d)
            self.readers[k] = {}

    def emit(self, block, sems, final_waits):
        def body(eng_name):
            def f(eng):
                for fn, waits, tok, kind in self.ops[eng_name]:
                    for s, v in waits:
                        eng.wait_ge(sems[s], v)
                    r = fn(eng)
                    if kind == 2:
                        r.then_inc(sems[tok[0]], 16)
                    elif kind == 1:
                        r.then_inc(sems[tok[0]], 1)
                for s, v in final_waits.get(eng_name, []):
                    eng.wait_ge(sems[s], v)
            return f
        block.tensor(body("tensor"))
        block.vector(body("vector"))
        block.scalar(body("scalar"))
        block.gpsimd(body("gpsimd"))
        block.sync(body("sync"))


def host_consts():
    c = {}
    c["ident"] = np.eye(128, dtype=np.float32)
    r = np.arange(128)
    c["mask128"] = ((r[:, None] <= r[None, :]) & ((r[:, None] // 64) == (r[None, :] // 64))).astype(np.float32)
    r64 = np.arange(64)
    c["mask_s"] = ((r64[:, None] <= r64[None, :]) & ((r64[:, None] // 4) == (r64[None, :] // 4))).astype(np.float32)
    rm = np.ones((128, NB), np.float32)
    rm[:, 0::64] = 0.0
    c["rm"] = rm
    rms = np.ones((128, TS), np.float32)
    rms[:, 0::4] = 0.0
    c["rm_s"] = rms
    c["trilT"] = (r[:, None] <= r[None, :]).astype(np.float32)
    bm = np.zeros((128, 16), np.float32)
    bm[r64, r64 // 4] = 1.0
    c["bm"] = bm
    sel = np.zeros((4, 64), np.float32)
    sel[r64 % 4, r64] = 1.0
    c["sel"] = sel
    return c


def build_nc():
    nc = bass.Bass("TRN2", target_bir_lowering=False)

    def din(name, shape):
        return nc.dram_tensor(name, list(shape), F32, kind="ExternalInput").ap()

    def dout(name, shape):
        return nc.dram_tensor(name, list(shape), F32, kind="ExternalOutput").ap()

    x_p = din("x_p", [TP, D]); x_s = din("x_s", [TS, D]); mem = din("mem", [256, D])
    sgla = din("sgla", [16, 4, 64, 128]); sconv = din("sconv", [32, DFF])
    ck = din("ck", [16, 256, D]); cv = din("cv", [16, 256, D])
    w_in = din("w_in", [D, DIN]); w_alpha = din("w_alpha", [16, 256])
    w_s = din("w_s", [4, 128, 128]); b_s = din("b_s", [1, 512]); g_sgu = din("g_sgu", [1, 512])
    w_out = din("w_out", [D, D]); wq = din("wq", [D, D]); wk = din("wk", [D, D]); wv = din("wv", [D, D]); wo = din("wo", [D, D])
    w_gate = din("w_gate", [D, DFF]); w_up = din("w_up", [D, DFF]); w_down = din("w_down", [DFF, D])
    vec1 = din("vec1", [68, 128]); vec2 = din("vec2", [66, 128]); g_fin = din("g_fin", [1, D])
    cst = {k: din("c_" + k, v.shape) for k, v in host_consts().items()}

    y_p = dout("y_p", [TP, D]); y_s = dout("y_s", [TS, D]); sg_p = dout("sg_p", [4, 64, 128]); sc_p = dout("sc_p", [2, DFF])
    mk_p = dout("mk_p", [256, D]); mv_p = dout("mv_p", [256, D]); sg_s = dout("sg_s", [16, 4, 64, 128])
    sc_s = dout("sc_s", [32, DFF]); sv_s = dout("sv_s", [TS, 512])

    if DEBUG:
        dbg1 = dout("dbg1", [128, KC, T]); dbg2 = dout("dbg2", [128, KC, T])
    S = Sched()
    es = ExitStack()
    with es:
        def sb(name, shape, dt):
            return es.enter_context(nc.sbuf_tensor(name, list(shape), dt))
        xT = sb("xT", [128, KC, T], F32)
        R1 = sb("R1", [128, 57600], U8)
        R2 = sb("R2", [128, 32768], U8)
        R3 = sb("R3", [128, 33792], U8)
        PS = [es.enter_context(nc.psum_tensor("ps%d" % i, [128, 512], F32)) for i in range(8)]

        def carve(arena, off, shape, dt):
            esz = 4 if dt == F32 else 2
            n = int(np.prod(shape[1:]))
            v = arena[:, off:off + n * esz].bitcast(dt)
            if len(shape) == 3:
                v = v.rearrange("p (a b) -> p a b", a=shape[1])
            elif len(shape) == 4:
                v = v.rearrange("p (a b c) -> p a b c", a=shape[1], b=shape[2])
            return v

        class Lay:
            def __init__(self, arena, base=0):
                self.arena, self.off = arena, base

            def get(self, shape, dt):
                esz = 4 if dt == F32 else 2
                v = carve(self.arena, self.off, shape, dt)
                self.off += int(np.prod(shape[1:])) * esz
                self.off = (self.off + 63) // 64 * 64
                assert self.off <= self.arena.shape[1], (self.off, self.arena.shape)
                return v

        ident_f = sb("ident_f", [128, 128], F32); ident_b = sb("ident_b", [128, 128], BF16)
        ones_b = sb("ones_b", [128, 128], BF16)
        mask128 = sb("mask128", [128, 128], F32); mask_s = sb("mask_s", [64, 64], F32)
        rm = sb("rm", [128, NB], F32); rm_s = sb("rm_s", [128, TS], F32)
        bm = sb("bm", [128, 16], F32); sel = sb("sel", [4, 64], F32)
        colv1 = sb("colv1", [128, 68], F32); colv2 = sb("colv2", [128, 66], F32)
        nba = sb("nba", [128, 2], F32); gg = sb("gg", [128, 4], F32)
        gsgu_bc = sb("gsgu_bc", [128, 512], F32)
        Wm = sb("Wm", [128, 4, 128], BF16); Wm_s = sb("Wm_s", [64, 4, 64], BF16)
        bs_hi = sb("bs_hi", [1, 512], BF16); bs_lo = sb("bs_lo", [1, 512], BF16)
        walpha_b = sb("walpha_b", [16, 256], BF16)
        ghist = sb("ghist", [128, NF, 2], F32)
        KT = sb("KT", [128, 8, 256], BF16); vbf = sb("vbf", [128, 2, 1024], BF16)
        smx = sb("smx", [128, 4], F32)
        Sst = sb("Sst", [128, 2, 128], F32); S_bf = sb("S_bf", [128, 2, 128], BF16); S_bfB = sb("S_bfB", [128, 2, 128], BF16)
        S_bfs = [S_bf, S_bfB]
        gch = [0]

        G_MIX, G_X, G_MEM, G_FFN, G_FIN = 0, 8, 16, 24, 32

        def ld(eng, out, in_, key, reads=(), writes=(), **kw):
            S.dma(eng, lambda e: e.dma_start(out=out, in_=in_, **kw), key, reads=reads, writes=writes)

        def V(name, kw, reads=(), writes=()):
            S.op("vector", lambda e: getattr(e, name)(**kw), reads, writes)

        def A(fn, reads=(), writes=()):
            S.op("scalar", fn, reads, writes)

        def G(name, kw, reads=(), writes=()):
            S.op("gpsimd", lambda e: getattr(e, name)(**kw), reads, writes)

        def MM(out, lhsT, rhs, start, stop, reads, writes, sig=None):
            if sig is None:
                sig = stop
            S.op("tensor", lambda e: e.matmul(out, lhsT=lhsT, rhs=rhs, start=start, stop=stop), reads, writes, sig=sig)

        def TR(out, in_, ident, reads, writes, sig=True):
            S.op("tensor", lambda e: e.transpose(out=out, in_=in_, identity=ident), reads, writes, sig=sig)

        def act(out, in_, func, reads, writes, **kw):
            A(lambda e: e.activation(out=out, in_=in_, func=func, **kw), reads, writes)

        for nm, t in [("ident", ident_f), ("mask128", mask128), ("mask_s", mask_s), ("rm", rm), ("rm_s", rm_s),
                      ("bm", bm), ("sel", sel)]:
            ld("sync", t[:], cst[nm], "c_" + nm, writes=[nm])
        ld("sync", gsgu_bc[:], g_sgu.broadcast_to([128, 512]), "gsgu", writes=["gsgu_bc"])
        ld("gpsimd", walpha_b[:], w_alpha, "walpha", writes=["walpha_b"])
        V("tensor_copy", dict(out=ident_b[:], in_=ident_f[:]), ["ident"], ["ident_b"])
        G("memset", dict(ap=ones_b[:], constant=1.0), [], ["ones_b"])
        G("memset", dict(ap=ghist[:], constant=0.0), [], ["ghist"])
        G("memset", dict(ap=Sst[:], constant=0.0), [], ["Sst"])
        G("memset", dict(ap=S_bf[:], constant=0.0), [], [("S_bf", 0)])

        wk_b = carve(R2, 0, [128, 8, 1024], BF16); wv_b = carve(R2, 16384, [128, 8, 1024], BF16)
        w_in_b = carve(R1, 0, [128, 8, DIN], BF16); w_out_b = carve(R1, 41216, [128, 8, 1024], BF16)

        def load_w(dst, src, key, ncols, reads=(), writes=()):
            srcv = src.rearrange("(c p) n -> p c n", p=128)
            nck = srcv.shape[1]
            step = 8 if ncols <= 1024 else 4
            for c0 in range(0, nck, step):
                ld("gpsimd", dst[:, c0:c0 + step, :], srcv[:, c0:c0 + step, :], key, reads=reads, writes=[key] + list(writes))

        load_w(wk_b, wk, "wk", 1024)
        load_w(wv_b, wv, "wv", 1024)
        load_w(w_in_b, w_in, "w_in", DIN)
        load_w(w_out_b, w_out, "w_out", 1024)

        S.ctx = "p0"
        L = Lay(R3)
        stg = [L.get([128, 1024], F32) for _ in range(2)]
        memT = L.get([128, 8, 256], F32)
        mnT = L.get([128, 8, 256], BF16)
        sqb0 = L.get([128, 8, 256], BF16)
        rstd0 = L.get([128, 256], F32); tmpn0 = L.get([128, 256], F32)
        Wm32 = L.get([128, 4, 128], F32); wsl = L.get([128, 4, 128], F32)
        vst = L.get([128, 128], F32)
        bsf = L.get([128, 512], F32)
        trilT = L.get([128, 128], F32)
        ld("sync", trilT[:], cst["trilT"], "c_trilT", writes=["trilT"])
        ld("sync", bsf[0:1, :], b_s, "bsf", writes=["bsf"])
        act(bs_hi[:], bsf[0:1, :], AF.Copy, ["bsf"], ["bs_hi"])
        V("tensor_tensor", dict(out=bs_lo[:], in0=bsf[0:1, :], in1=bs_hi[:], op=ALU.subtract), ["bsf", "bs_hi"], ["bs_lo"])

        ld("sync", vst[0:68, :], vec1, "vst", writes=["vst"])
        TR(PS[0][:, 0:68], vst[0:68, :], ident_f[0:68, 0:68], ["vst", "ident"], ["ps0"])
        V("tensor_copy", dict(out=colv1[:], in_=PS[0][:, 0:68]), ["ps0"], ["colv1"])
        ld("sync", vst[0:66, :], vec2, "vst", writes=["vst"])
        TR(PS[0][:, 0:66], vst[0:66, :], ident_f[0:66, 0:66], ["vst", "ident"], ["ps0"])
        V("tensor_copy", dict(out=colv2[:], in_=PS[0][:, 0:66]), ["ps0"], ["colv2"])
        act(nba[:], colv1[:, 40:42], AF.Copy, ["colv1"], ["nba"], scale=-1.0)
        act(gg[:], colv1[:, 42:46], AF.Copy, ["colv1"], ["gg"], scale=0.5)

        ld("sync", wsl[:], w_s.rearrange("h i j -> i h j"), "wsl", writes=["wsl"])
        for h in range(4):
            TR(PS[1][:, h * 128:(h + 1) * 128], wsl[:, h, :], ident_f[:], ["wsl", "ident"], ["ps1"], sig=(h == 3))
        V("tensor_tensor", dict(out=Wm32[:], in0=PS[1][:].rearrange("p (h i) -> p h i", h=4),
                                    in1=trilT[:].unsqueeze(1).broadcast_to([128, 4, 128]), op=ALU.mult),
          ["ps1", "trilT"], ["Wm32"])
        V("tensor_copy", dict(out=Wm[:], in_=Wm32[:]), ["Wm32"], ["Wm"])
        for h in range(4):
            MM(PS[1][0:64, h * 64:(h + 1) * 64], sel[0:4, :], Wm32[0:4, h, 0:4].unsqueeze(1).broadcast_to([4, 16, 4]),
               True, True, ["sel", "Wm32"], ["ps1"], sig=(h == 3))
        V("tensor_tensor", dict(out=Wm_s[:], in0=PS[1][0:64, 0:256].rearrange("p (h i) -> p h i", h=4),
                                    in1=mask_s[:].unsqueeze(1).broadcast_to([64, 4, 64]), op=ALU.mult),
          ["ps1", "mask_s"], ["Wm_s"])

        def fm_norm(src_fn, n, gcol, out_fn, sqb, rstd, tmpn, psb, pskey, rkeys, wkeys, tag, do_stats=True, do_apply=True):
            if do_stats:
                fm_stats(src_fn, n, sqb, rstd, tmpn, psb, pskey, rkeys, tag)
            if do_apply:
                for c in range(KC):
                    V("scalar_tensor_tensor", dict(out=out_fn(c), in0=src_fn(c), scalar=colv1[:, gcol + c:gcol + c + 1],
                                                   in1=rstd[:, 0:n], op0=ALU.mult, op1=ALU.mult),
                      list(rkeys) + [tag + "rstd", "colv1"], [(wkeys[0], c)] if (wkeys[0] in ("xn", "hn") or (isinstance(wkeys[0], tuple) and wkeys[0][0] == "hnA")) else wkeys)

        def fm_squares(src_fn, n, sqb, rkeys, tag):
            for c in range(KC):
                act(sqb[:, c, 0:n], src_fn(c), AF.Square, rkeys, [tag + "sqb"])

        def fm_stats(src_fn, n, sqb, rstd, tmpn, psb, pskey, rkeys, tag, squares=True):
            if squares:
                fm_squares(src_fn, n, sqb, rkeys, tag)
            for c in range(KC):
                MM(psb[:, 0:n], ones_b[:], sqb[:, c, 0:n], c == 0, c == KC - 1, ["ones_b", tag + "sqb"], [pskey])
            act(tmpn[:, 0:n], psb[:, 0:n], AF.Ln, [pskey], [tag + "tmpn"], scale=1.0 / D, bias=EPS)
            V("tensor_scalar", dict(out=tmpn[:, 0:n], in0=tmpn[:, 0:n], scalar1=-0.5, scalar2=None, op0=ALU.mult),
              [tag + "tmpn"], [tag + "tmpn"])
            act(rstd[:, 0:n], tmpn[:, 0:n], AF.Exp, [tag + "tmpn"], [tag + "rstd"])

        ntile = TP // 128
        xs_slots = [carve(R2, 16384 + i_ * 4096, [128, 1024], F32) for i_ in range(2)]

        def xtile(t, buf, bkey, dkey, banks, extra_w=()):
            rows = 128 if t < ntile else TS
            src = x_p[t * 128:(t + 1) * 128, :] if t < ntile else x_s
            ld("sync", buf[0:rows, :], src, dkey, writes=[bkey] + list(extra_w))
            for half in range(2):
                pb, pk = PS[banks[half]], "ps%d" % banks[half]
                for c4 in range(4):
                    c = half * 4 + c4
                    TR(pb[:, c4 * 128:c4 * 128 + rows], buf[0:rows, c * 128:(c + 1) * 128], ident_f[0:rows, 0:rows],
                       [bkey, "ident"], [pk], sig=(c4 == 3))
                dst = xT[:, half * 4:half * 4 + 4, t * 128:t * 128 + rows]
                srcp = pb[:].rearrange("p (c n) -> p c n", c=4)[:, :, 0:rows]
                if half == 0:
                    V("tensor_copy", dict(out=dst, in_=srcp), [pk], [("xT", t // 2)])
                else:
                    act(dst, srcp, AF.Copy, [pk], [("xT", t // 2)])
        for t in range(2):
            xtile(t, stg[t % 2], ("stg", t % 2), "stg%d" % (t % 2), (2, 3))

        for t in range(2):
            ld("sync", stg[t][:], mem[t * 128:(t + 1) * 128, :], "stg%d" % t, writes=[("stg", t)])
            for half in range(2):
                pb = PS[2 + half]
                for c4 in range(4):
                    c = half * 4 + c4
                    TR(pb[:, c4 * 128:(c4 + 1) * 128], stg[t][:, c * 128:(c + 1) * 128], ident_f[:],
                       [("stg", t), "ident"], ["ps%d" % (2 + half)], sig=(c4 == 3))
                V("tensor_copy", dict(out=memT[:, half * 4:half * 4 + 4, t * 128:(t + 1) * 128],
                                                                  in_=pb[:].rearrange("p (c n) -> p c n", c=4)),
                  ["ps%d" % (2 + half)], ["memT"])
        fm_norm(lambda c: memT[:, c, :], 256, G_MEM, lambda c: mnT[:, c, :], sqb0, rstd0, tmpn0, PS[4], "ps4",
                ["memT"], ["mnT"], "p0")
        for t in range(2):
            for (wb, wkey, outd, isv) in [(wk_b, "wk", mk_p, False), (wv_b, "wv", mv_p, True)]:
                for half in range(2):
                    pb, pk = PS[5 + half], "ps%d" % (5 + half)
                    for c in range(KC):
                        MM(pb[:], mnT[:, c, t * 128:(t + 1) * 128], wb[:, c, half * 512:(half + 1) * 512], c == 0, c == KC - 1,
                           ["mnT", wkey], [pk])
                    act(stg[t][:, half * 512:(half + 1) * 512], pb[:], AF.Copy, [pk], [("stg", t)])
                    if isv:
                        V("tensor_copy", dict(out=vbf[:, t, half * 512:(half + 1) * 512], in_=pb[:]),
                          [pk], ["vbf"])
                ld("sync", outd[t * 128:(t + 1) * 128, :], stg[t][:], "o_mkv", reads=[("stg", t)])
        for oc in range(8):
            pb, pk = PS[5 + oc % 2], "ps%d" % (5 + oc % 2)
            for c in range(KC):
                MM(pb[:, 0:256], wk_b[:, c, oc * 128:(oc + 1) * 128], mnT[:, c, :], c == 0, c == KC - 1, ["mnT", "wk"], [pk])
            act(KT[:, oc, :], pb[:, 0:256], AF.Copy, [pk], ["KT"])

        wq_b = carve(R2, 16384, [128, 8, 1024], BF16); wo_b = carve(R2, 0, [128, 8, 1024], BF16)
        S.fence([("xs", 0), ("xs", 1)])
        L = Lay(R3)
        xn = L.get([128, 8, NB], BF16); sqb = L.get([128, 8, NB], BF16)
        rstd = L.get([128, NB], F32); tmpn = L.get([128, NB], F32)
        aT = L.get([128, NB], BF16)
        e1 = L.get([128, NB], F32); Bc = L.get([128, NB], F32)
        eb = L.get([128, 2, NB], F32); enb = L.get([128, 2, NB], F32)
        qm = L.get([128, 4, NB], BF16); ktT = L.get([128, 2, NB], BF16); khT = L.get([128, 2, NB], BF16)
        cat = L.get([128, 8, NB], BF16); th = L.get([128, NB], F32)
        v_tm = L.get([128, 2, 512], BF16)
        p1keys = [("xn", c_) for c_ in range(KC)] + ["m1sqb", "m1rstd", "m1tmpn", "aT", "e1", "Bc", ("eb", 0), ("eb", 1), ("enb", 0), ("enb", 1), "qm", ("ktT", 0), ("ktT", 1), ("khT", 0), ("khT", 1), "cat", "th", "v_tm"]
        L2 = Lay(R2)
        svg = L2.get([128, 512], F32); svsq = L2.get([128, 512], F32); svn = L2.get([128, 2, 512], BF16)
        svo = svsq
        sc_bd = L2.get([128, 4, 128], BF16); khm = L2.get([128, 2, 256], BF16)
        TR2_OFF = L2.off
        rstd_o = L2.get([128, NB], F32); tmp_o = L2.get([128, NB], F32); osq = L2.get([128, 4, NB], BF16)
        sst = L2.get([128, 8], F32); srs = L2.get([128, 8], F32)
        S0 = [L2.get([128, 2, 128], F32) for _ in range(2)]
        S0b = [L2.get([128, 2, 128], BF16) for _ in range(2)]
        S0 = S0 + [carve(R3, 4096 + i_ * 1024, [128, 2, 128], F32) for i_ in range(2)]
        S0b = S0b + [carve(R3, 4096 + 2048 + i_ * 512, [128, 2, 128], BF16) for i_ in range(2)]
        p1keys += ["svg", "svsq", "svn", "sc_bd", "khm", "rstd_o", "tmp_o", ("osq", 0), ("osq", 1), ("osq", 2), ("osq", 3), "sst", "srs",
                   ("S0", 0), ("S0", 1), ("S0b", 0), ("S0b", 1)]
        S.fence(p1keys)
        G("memset", dict(ap=qm[:], constant=0.0), [], ["qm"])
        G("memset", dict(ap=khm[:], constant=0.0), [], ["khm"])

        rr = [0]

        def rot():
            i = rr[0] % 2
            rr[0] += 1
            return PS[i], "ps%d" % i

        def mixer_block(tok0, n, xkeys, sample, pre=False, nxt=None, prev_wout=None, xfill=()):
            ntl = (n + 127) // 128
            blk = slice(tok0, tok0 + n)
            S.ctx = "p1.norm"
            fm_norm(lambda c: xT[:, c, blk], n, G_MIX, lambda c: xn[:, c, 0:n], sqb, rstd, tmpn, PS[0], "ps0",
                    xkeys, ["xn"], "m1", do_stats=not pre)
            xfill = list(xfill)
            if prev_wout is not None:
                prev_wout()
            if xfill:
                xfill.pop(0)()
            rr[0] = 1

            def proj(col0, m):
                pb, pk = rot()
                for c in range(KC):
                    MM(pb[0:m, 0:n], w_in_b[:, c, col0:col0 + m], xn[:, c, 0:n], c == 0, c == KC - 1, ["w_in", ("xn", c)], [pk])
                return pb, pk
            S.ctx = "p1.gates"
            pb, pk = proj(CA, 16)
            act(aT[0:16, 0:n], pb[0:16, 0:n], AF.Copy, [pk], ["aT"])
            rmask = rm_s if sample else rm
            for c in range(2):
                pb, pk = rot()
                MM(pb[:, 0:n], walpha_b[:, c * 128:(c + 1) * 128], aT[0:16, 0:n], True, True, ["walpha_b", "aT"], [pk])
                eX, eXk = (e1, "e1") if c == 0 else (th, "th")
                act(eX[:, 0:n], pb[:, 0:n], AF.Exp, [pk, "nba"], [eXk], scale=-1.0, bias=nba[:, c:c + 1])
                V("tensor_scalar", dict(out=eX[:, 0:n], in0=eX[:, 0:n], scalar1=1.0, scalar2=None, op0=ALU.add), [eXk], [eXk])
                act(eX[:, 0:n], eX[:, 0:n], AF.Ln, [eXk], [eXk])
                V("tensor_tensor_scan", dict(out=Bc[:, 0:n], data0=rmask[:, 0:n], data1=eX[:, 0:n], initial=0.0,
                                                 op0=ALU.mult, op1=ALU.add), [eXk, "rm", "rm_s"], ["Bc"])
                act(eb[:, c, 0:n], Bc[:, 0:n], AF.Exp, ["Bc"], [("eb", c)], scale=-1.0 / 16)
                act(enb[:, c, 0:n], Bc[:, 0:n], AF.Exp, ["Bc"], [("enb", c)], scale=1.0 / 16)
            S.ctx = "p1.qk"
            def proj4(col0, m, j):
                bi_ = (2, 3, 0, 1)[j % 4]
                pb, pk = PS[bi_], "ps%d" % bi_
                for c in range(KC):
                    MM(pb[0:m, 0:n], w_in_b[:, c, col0:col0 + m], xn[:, c, 0:n], c == 0, c == KC - 1, ["w_in", ("xn", c)], [pk])
                return pb, pk
            for c in range(2):
                pb, pk = proj4(CQ + c * 128, 128, c)
                for h2 in range(2):
                    rs_ = slice(h2 * 64, (h2 + 1) * 64)
                    V("scalar_tensor_tensor", dict(
                        out=qm[rs_, 2 * c + h2, 0:n], in0=pb[rs_, 0:n], scalar=0.125, in1=eb[rs_, c, 0:n],
                        op0=ALU.mult, op1=ALU.mult), [pk, ("eb", c)], ["qm"])
            if xfill:
                S.ctx = "p1.x"
                xfill.pop(0)()
                S.ctx = "p1.qk"
            cl = 4 if sample else 64
            for c in range(2):
                pb, pk = proj4(CK + c * 128, 128, 2 + c)
                V("tensor_tensor", dict(out=ktT[:, c, 0:n], in0=pb[:, 0:n], in1=enb[:, c, 0:n], op=ALU.mult),
                  [pk, ("enb", c)], [("ktT", c)])
                G("tensor_tensor", dict(
                    out=khT[:, c, 0:n].rearrange("p (a b) -> p a b", b=cl),
                    in0=ktT[:, c, 0:n].rearrange("p (a b) -> p a b", b=cl),
                    in1=eb[:, c, cl - 1:n:cl].unsqueeze(2).broadcast_to([128, n // cl, cl]), op=ALU.mult),
                  [("ktT", c), ("eb", c)], [("khT", c)])
            S.ctx = "p1.ru"
            def ru_piece(j):
                S.ctx = "p1.ru"
                if j < 4:
                    pb, pk = proj(CR + j * 128, 128)
                    act(th[:, 0:n], pb[:, 0:n], AF.Tanh, [pk], ["th"], scale=0.5)
                    V("scalar_tensor_tensor", dict(out=cat[:, j, 0:n], in0=th[:, 0:n], scalar=1.0, in1=pb[:, 0:n],
                                                   op0=ALU.add, op1=ALU.mult), [pk, "th"], [("cat", j)])
                else:
                    pb, pk = proj(CU + (j - 4) * 128, 128)
                    act(cat[:, j, 0:n], pb[:, 0:n], AF.Gelu_apprx_tanh, [pk], [("cat", j)])
                S.ctx = "p1.gla"
            ru_q = list(range(8))
            S.ctx = "p1.tm"
            if sample:
                G("memset", dict(ap=v_tm[:], constant=0.0), [], ["v_tm"])
                G("memset", dict(ap=svn[:], constant=0.0), [], ["svn"])
                G("memset", dict(ap=sc_bd[:], constant=0.0), [], ["sc_bd"])
            for tl in range(ntl):
                rows = min(128, n - tl * 128)
                tsl = slice(tl * 128, tl * 128 + rows)
                for c in range(KC):
                    MM(PS[2][0:rows, :], xn[:, c, tsl], w_in_b[:, c, CV:CV + 512], c == 0, c == KC - 1, [("xn", c), "w_in"], ["ps2"])
                act(v_tm[0:rows, tl, :], PS[2][0:rows, :], AF.Copy, ["ps2"], ["v_tm"])
                for c in range(KC):
                    MM(PS[3][0:rows, :], xn[:, c, tsl], w_in_b[:, c, CSV:CSV + 512], c == 0, c == KC - 1, [("xn", c), "w_in"], ["ps3"])
                act(svg[0:rows, :], PS[3][0:rows, :], AF.Gelu_apprx_tanh, ["ps3"], ["svg"])
                V("tensor_tensor", dict(out=svsq[0:rows, :], in0=svg[0:rows, :], in1=svg[0:rows, :], op=ALU.mult),
                  ["svg"], ["svsq"])
                V("tensor_reduce", dict(out=sst[0:rows, 0:4], in_=svsq[0:rows, :].rearrange("p (h d) -> p h d", h=4),
                                                       axis=AX.X, op=ALU.add), ["svsq"], ["sst"])
                act(sst[0:rows, 4:8], sst[0:rows, 0:4], AF.Ln, ["sst"], ["sst"], scale=1.0 / 128, bias=EPS)
                V("tensor_scalar", dict(out=sst[0:rows, 4:8], in0=sst[0:rows, 4:8], scalar1=-0.5, scalar2=None, op0=ALU.mult),
                  ["sst"], ["sst"])
                act(srs[0:rows, 0:4], sst[0:rows, 4:8], AF.Exp, ["sst"], ["srs"])
                for h in range(4):
                    hs = slice(h * 128, (h + 1) * 128)
                    dst = svo[0:rows, hs] if sample else svn[0:rows, tl, hs]
                    V("scalar_tensor_tensor", dict(
                        out=dst, in0=svg[0:rows, hs], scalar=srs[0:rows, h:h + 1], in1=gsgu_bc[0:rows, hs],
                        op0=ALU.mult, op1=ALU.mult), ["svg", "srs", "gsgu_bc"], ["svsq" if sample else "svn"])
                if sample:
                    V("tensor_copy", dict(out=svn[0:rows, 0, :], in_=svo[0:rows, :]), ["svsq", "svn"], ["svn"])
                    ld("sync", sv_s, svo[0:rows, :], "o_svs", reads=["svsq"])
            if nxt is not None:
                S.ctx = "p1.norm"
                ntok0, nn_, nxk = nxt
                nblk = slice(ntok0, ntok0 + nn_)
                fm_squares(lambda c: xT[:, c, nblk], nn_, sqb, nxk, "m1")
            S.ctx = "p1.gla"
            for tl in range(ntl):
                rows = min(128, n - tl * 128)
                tsl = slice(tl * 128, tl * 128 + rows)
                for h in range(4):
                    MM(PS[4][0:rows, h * 128:h * 128 + rows], ktT[:, h // 2, tsl], qm[:, h, tsl], True, True,
                       [("ktT", h // 2), "qm"], ["ps4"], sig=(h == 3))
                msk = mask_s[:] if sample else mask128[:]
                V("tensor_tensor", dict(
                    out=sc_bd[0:rows, :, 0:rows], in0=PS[4][0:rows, :].rearrange("p (h i) -> p h i", h=4)[:, :, 0:rows],
                    in1=msk.unsqueeze(1).broadcast_to([rows, 4, rows]), op=ALU.mult), ["ps4", "mask128", "mask_s"], ["sc_bd"])
                p5b = PS[5][:].bitcast(BF16)
                for c in range(2):
                    TR(p5b[0:rows, c * 128:(c + 1) * 128], khT[:, c, tsl], ident_b[:], [("khT", c), "ident_b"], ["ps5"], sig=(c == 1))
                if not sample:
                    for p in range(2):
                        prs = slice(p * 64, (p + 1) * 64)
                        act(khm[prs, p, :], p5b[prs, 0:256], AF.Copy, ["ps5"], ["khm"])
                else:
                    act(tmp_o[0:64, 0:128].bitcast(BF16), p5b[0:64, 0:256], AF.Copy, ["ps5"], ["tmp_o"])
                groups = [(p, slice(p * 64, (p + 1) * 64), None) for p in range(rows // 64)] if not sample else \
                    [(b, slice(b * 4, (b + 1) * 4), b) for b in range(16)]
                for gi, (p, csl, bidx) in enumerate(groups):
                    if sample:
                        s = bidx % 4
                        ld("sync", S0[s][:], sgla[bidx].rearrange("(c h2) d v -> (h2 d) c v", c=2), "S0_%d" % s,
                           writes=[("S0", s)] + (["m1sqb"] if s >= 2 else []))
                        act(S0b[s][:], S0[s][:], AF.Copy, [("S0", s)], [("S0b", s)] + (["m1sqb"] if s >= 2 else []))
                        V("tensor_scalar", dict(out=khm[0:64, bidx % 2, :], in0=tmp_o[0:64, 0:128].bitcast(BF16),
                                                                    scalar1=bm[0:64, bidx:bidx + 1], scalar2=None, op0=ALU.mult),
                          ["tmp_o", "bm"], ["khm"])
                        Sb_cur, Sb_key = S0b[s], ("S0b", s)
                        khm_cur = khm[:, bidx % 2, :]
                    else:
                        rslot, wslot = gch[0] % 2, (gch[0] + 1) % 2
                        gch[0] += 1
                        Sb_cur, Sb_key = S_bfs[rslot], ("S_bf", rslot)
                        khm_cur = khm[:, p, :]
                    ncol = csl.stop - csl.start
                    gsl = slice(tl * 128 + csl.start, tl * 128 + csl.stop)
                    ubank = (2 + rslot) if not sample else (4 + (gi % 2))
                    pu, puk = PS[ubank], "ps%d" % ubank

                    def u_mms():
                        for h in range(4):
                            c = h // 2
                            MM(pu[:, h * 128:(h + 1) * 128], khm_cur[:, c * 128:(c + 1) * 128], v_tm[:, tl, h * 128:(h + 1) * 128],
                               True, True, ["khm", "v_tm"], [puk], sig=(h == 3))

                    def o_mms():
                        for h in range(4):
                            c = h // 2
                            pob = PS[6 + h // 2]
                            ocol = (h % 2) * NB + tl * 128
                            outap = pob[:, ocol + csl.start:ocol + csl.stop]
                            MM(outap, Sb_cur[:, c, :], qm[:, h, gsl], True, False, [Sb_key, "qm"], ["ps%d" % (6 + h // 2)], sig=False)
                            MM(outap, v_tm[:, tl, h * 128:(h + 1) * 128], sc_bd[:, h, csl], False, True, ["v_tm", "sc_bd"],
                               ["ps%d" % (6 + h // 2)], sig=(h % 2 == 1))
                    if sample or gi > 0:
                        u_mms()
                        o_mms()
                    else:
                        o_mms()
                        u_mms()
                    ecol = tl * 128 + csl.stop - 1
                    for h in range(4):
                        c, h2 = h // 2, h % 2
                        rs_ = slice(h2 * 64, (h2 + 1) * 64)
                        if sample:
                            V("scalar_tensor_tensor", dict(
                                out=S0[s][rs_, c, :], in0=S0[s][rs_, c, :], scalar=eb[rs_, c, ecol:ecol + 1],
                                in1=pu[rs_, h * 128:(h + 1) * 128], op0=ALU.mult, op1=ALU.add), [puk, ("eb", c), ("S0", s)], [("S0", s)])
                        else:
                            V("scalar_tensor_tensor", dict(
                                out=S_bfs[wslot][rs_, c, :], in0=Sst[rs_, c, :], scalar=eb[rs_, c, ecol:ecol + 1],
                                in1=pu[rs_, h * 128:(h + 1) * 128], op0=ALU.mult, op1=ALU.add), [puk, ("eb", c), "Sst"], [("S_bf", wslot)])
                    if not sample:
                        for h in range(4):
                            c, h2 = h // 2, h % 2
                            rs_ = slice(h2 * 64, (h2 + 1) * 64)
                            V("scalar_tensor_tensor", dict(
                                out=Sst[rs_, c, :], in0=Sst[rs_, c, :], scalar=eb[rs_, c, ecol:ecol + 1],
                                in1=pu[rs_, h * 128:(h + 1) * 128], op0=ALU.mult, op1=ALU.add), [puk, ("eb", c), "Sst"], ["Sst"])
                    if sample:
                        ld("sync", sg_s[bidx].rearrange("(c h2) d v -> (h2 d) c v", c=2), S0[s][:], "o_sgs", reads=[("S0", s)])
                        if ru_q:
                            ru_piece(ru_q.pop(0))
                    else:
                        for _ in range(2):
                            if ru_q:
                                ru_piece(ru_q.pop(0))
            S.ctx = "p1.onorm"
            while ru_q:
                ru_piece(ru_q.pop(0))
            if nxt is not None:
                S.ctx = "p1.norm"
                fm_stats(lambda c: xT[:, c, nblk], nn_, sqb, rstd, tmpn, PS[0], "ps0", nxk, "m1", squares=False)
            S.ctx = "p1.onorm"
            for h in range(4):
                pob, pok = PS[6 + h // 2], "ps%d" % (6 + h // 2)
                osl = slice((h % 2) * NB, (h % 2) * NB + n)
                act(osq[:, h, 0:n], pob[:, osl], AF.Square, [pok], [("osq", h)])
            for h in range(4):
                pb, pk = rot()
                for tl in range(ntl):
                    rows = min(128, n - tl * 128)
                    zo = pb[:, tl * 128:tl * 128 + rows]
                    if sample:
                        MM(zo, svn[0:64, 0, h * 128:(h + 1) * 128], Wm_s[:, h, :], True, False, ["svn", "Wm_s"], [pk], sig=False)
                        bh = bs_hi[0:1, h * 128:h * 128 + 4].unsqueeze(1).broadcast_to([1, 16, 4])
                        bl = bs_lo[0:1, h * 128:h * 128 + 4].unsqueeze(1).broadcast_to([1, 16, 4])
                    else:
                        MM(zo, svn[:, tl, h * 128:(h + 1) * 128], Wm[:, h, :], True, False, ["svn", "Wm"], [pk], sig=False)
                        bh = bs_hi[0:1, h * 128:(h + 1) * 128]
                        bl = bs_lo[0:1, h * 128:(h + 1) * 128]
                    MM(zo, ones_b[0:1, :], bh, False, False, ["ones_b", "bs_hi"], [pk], sig=False)
                    MM(zo, ones_b[0:1, :], bl, False, True, ["ones_b", "bs_lo"], [pk], sig=True)
                V("tensor_tensor", dict(out=cat[:, 4 + h, 0:n], in0=pb[:, 0:n], in1=cat[:, 4 + h, 0:n], op=ALU.mult),
                  [pk, ("cat", 4 + h)], [("cat", 4 + h)])
            S.ctx = "p1.wout"
            S.ctx = "p1.onorm"
            tr2 = carve(R2, TR2_OFF, [128, 2, NB], F32)
            for hp in range(2):
                pob, pok = PS[6 + hp], "ps%d" % (6 + hp)
                pb, pk = rot()
                for h2_ in range(2):
                    h = 2 * hp + h2_
                    MM(pb[:, h2_ * NB:h2_ * NB + n], ones_b[:], osq[:, h, 0:n], True, True, ["ones_b", ("osq", h)], [pk], sig=(h2_ == 1))
                pv_ = pb[:].rearrange("p (a b) -> p a b", a=2)[:, :, 0:n]
                act(tr2[:, :, 0:n], pv_, AF.Ln, [pk], ["tmp_o", "rstd_o"], scale=1.0 / 128, bias=EPS)
                V("tensor_scalar", dict(out=tr2[:, :, 0:n], in0=tr2[:, :, 0:n], scalar1=-0.5, scalar2=None, op0=ALU.mult),
                  ["tmp_o", "rstd_o"], ["tmp_o", "rstd_o"])
                act(tr2[:, :, 0:n], tr2[:, :, 0:n], AF.Exp, ["tmp_o", "rstd_o"], ["tmp_o", "rstd_o"])
                for h2_ in range(2):
                    h = 2 * hp + h2_
                    osl = slice(h2_ * NB, h2_ * NB + n)
                    V("scalar_tensor_tensor", dict(out=tr2[:, h2_, 0:n], in0=pob[:, osl], scalar=gg[:, h:h + 1],
                                                   in1=tr2[:, h2_, 0:n], op0=ALU.mult, op1=ALU.mult),
                      [pok, "gg", "tmp_o", "rstd_o"], ["tmp_o", "rstd_o"])
                for h2_ in range(2):
                    h = 2 * hp + h2_
                    V("tensor_tensor", dict(out=cat[:, h, 0:n], in0=tr2[:, h2_, 0:n], in1=cat[:, h, 0:n], op=ALU.mult),
                      ["tmp_o", "rstd_o", ("cat", h)], [("cat", h)])
            catk = [("cat", j) for j in range(8)]
            def wout_part():
                S.ctx = "p1.wout"
                for oc in range(8):
                    bi_ = (2, 3, 0, 1)[oc % 4]
                    pb, pk = PS[bi_], "ps%d" % bi_
                    for c in range(KC):
                        MM(pb[:, 0:n], w_out_b[:, c, oc * 128:(oc + 1) * 128], cat[:, c, 0:n], c == 0, c == KC - 1, ["w_out"] + catk, [pk])
                    V("tensor_tensor", dict(out=xT[:, oc, blk], in0=pb[:, 0:n], in1=xT[:, oc, blk], op=ALU.add),
                      [pk] + list(xkeys), list(xkeys))
            return wout_part

        mblocks = [(b * NB, NB, [("xT", b)]) for b in range(TP // NB)] + [(TP, TS, [("xT", 8)])]
        for i_, (tk, nn, xk) in enumerate(mblocks):
            smp = (i_ == len(mblocks) - 1)
            if smp:
                ld("sync", sg_p.rearrange("(c h2) d v -> (h2 d) c v", c=2), Sst[:], "o_sgp", reads=["Sst"])
            if i_ <= 6:
                tl_ = [2 * i_ + 2, 2 * i_ + 3]
            elif i_ == 7:
                tl_ = [16]
            else:
                tl_ = []
            xf_ = [(lambda t=t: xtile(t, xs_slots[t % 2], ("xs", t % 2), "xs%d" % (t % 2), (0, 1), extra_w=["wv"])) for t in tl_]
            pw_ = mixer_block(tk, nn, xk, smp, pre=(i_ > 0), nxt=(mblocks[i_ + 1] if i_ + 1 < len(mblocks) else None),
                              prev_wout=(pw_ if i_ > 0 else None), xfill=xf_)
            if i_ == 7:
                load_w(wq_b, wq, "wq", 1024, writes=["wv", ("xs", 0), ("xs", 1)])
        pw_()

        if DEBUG:
            ld("sync", dbg1, xT[:], "o_dbg1", reads=[("xT", i) for i in range(9)])
        S.fence(["wo"])
        load_w(wo_b, wo, "wo", 1024)
        L = Lay(R3)
        hn = L.get([128, 8, NB], BF16); sqb2 = L.get([128, 8, NB], BF16)
        rstd2 = L.get([128, NB], F32); tmpn2 = L.get([128, NB], F32)
        qT = L.get([128, 8, NB], BF16)
        rinv = L.get([128, 256], F32); pn = L.get([128, 4, 256], BF16)
        pT = L.get([128, 4, 2, NB], BF16); oT = L.get([128, 8, NB], BF16)
        qTm = [L.get([128, 8, TS], BF16) for _ in range(2)]
        KTs = L.get([128, 8, 256], BF16)
        qT_s = L.get([128, 8, TS], BF16); pT_s = L.get([128, 4, 2, TS], BF16)
        kslot = [carve(R1, 49152 + i * 4096, [128, 2, 1024], BF16) for i in range(2)]
        p2keys = [("hn", c_) for c_ in range(KC)] + ["m2sqb", "m2rstd", "m2tmpn"] + [("qT", o_) for o_ in range(8)] + [("pn", 0), ("pn", 1), ("pn", 2), ("pn", 3), "pT", "oT", "qT_s", "pT_s", "rinv", ("KTs", 0), ("KTs", 1)] + [("sm8", h) for h in range(4)] + [("sm12", h) for h in range(4)] + [("smx", 0), ("smx", 1), ("qTm", 0), ("qTm", 1), "KTs",
                  ("kslot", 0), ("kslot", 1)]
        S.fence(p2keys)
        def slot_views(si):
            base = si * 24576
            return (carve(R1, base, [128, 8, 512], BF16), carve(R1, base + 8192, [128, 8, 512], BF16),
                    carve(R1, base + 16384, [128, 4, 1024], BF16))
        S.fence([("fslot", 0), ("fslot", 1)])

        def unit_pieces(u):
            f0, nf = UNITS[u]
            si = u % 2
            wg_v, wu_v, wd_v = slot_views(si)
            key = "fs%d" % si
            gsrc = w_gate.rearrange("(c p) n -> p c n", p=128)
            usrc = w_up.rearrange("(c p) n -> p c n", p=128)
            pcs = []
            pcs.append(lambda: ld("gpsimd", wg_v[:, :, 0:nf * 128], gsrc[:, :, f0 * 128:(f0 + nf) * 128], key, writes=[("fslot", si)]))
            pcs.append(lambda: ld("gpsimd", wu_v[:, :, 0:nf * 128], usrc[:, :, f0 * 128:(f0 + nf) * 128], key, writes=[("fslot", si)]))
            pcs.append(lambda: ld("gpsimd", wd_v[:, 0:nf, :],
                                  w_down[f0 * 128:(f0 + nf) * 128, :].rearrange("(l p) n -> p l n", p=128), key, writes=[("fslot", si)]))
            return pcs

        def load_unit(u):
            for p_ in unit_pieces(u):
                p_()

        def softmax_tile(rows, psA, psB, pkeys):
            for hh, pb in enumerate([psA, psB]):
                V("tensor_reduce", dict(out=smx[0:rows, 2 * hh:2 * hh + 2], in_=pb[0:rows, :].rearrange("p (h m) -> p h m", h=2),
                                        axis=AX.X, op=ALU.max, negate=True), [pkeys[hh]], [("smx", hh)])
            for h in range(4):
                pb = [psA, psB][h // 2]
                act(pn[0:rows, h, :], pb[0:rows, (h % 2) * 256:(h % 2 + 1) * 256], AF.Exp, [pkeys[h // 2], ("smx", h // 2)],
                    [("pn", h)], bias=smx[0:rows, h:h + 1])

        def attn_block(tok0, n, xkeys, prev_wo, sfill, pre=False, nxt=None):
            blk = slice(tok0, tok0 + n)
            sfill = list(sfill)

            def fill():
                if sfill:
                    sfill.pop(0)()
                if wpieces:
                    wpieces.pop(0)()
            S.ctx = "p2.norm"
            fm_norm(lambda c: xT[:, c, blk], n, G_X, lambda c: hn[:, c, 0:n], sqb2, rstd2, tmpn2, PS[0], "ps0",
                    xkeys, ["hn"], "m2", do_stats=not pre)
            rr[0] = 1
            S.ctx = "p2.q"
            for oc in range(8):
                bi_ = (0, 1, 2, 3)[oc % 4]
                pb, pk = PS[bi_], "ps%d" % bi_
                for c in range(KC):
                    MM(pb[:, 0:n], wq_b[:, c, oc * 128:(oc + 1) * 128], hn[:, c, 0:n], c == 0, c == KC - 1, ["wq", ("hn", c)], [pk])
                act(qT[:, oc, 0:n], pb[:, 0:n], AF.Copy, [pk], [("qT", oc)], scale=1.0 / 16)
            fill()
            ntl = n // 128

            def scores(tl):
                S.ctx = "p2.sc"
                tsl = slice(tl * 128, (tl + 1) * 128)
                for h in range(4):
                    pb, pk = PS[2 + h // 2], "ps%d" % (2 + h // 2)
                    for dc in range(2):
                        MM(pb[:, (h % 2) * 256:(h % 2 + 1) * 256], qT[:, 2 * h + dc, tsl], KT[:, 2 * h + dc, :], dc == 0, dc == 1,
                           [("qT", 2 * h + dc), "KT"], [pk])

            def transposes(tl):
                S.ctx = "p2.pT"
                tsl = slice(tl * 128, (tl + 1) * 128)
                p4b = PS[4][:].bitcast(BF16)
                for h in range(4):
                    for mc in range(2):
                        TR(p4b[:, (h * 2 + mc) * 128:(h * 2 + mc + 1) * 128], pn[:, h, mc * 128:(mc + 1) * 128], ident_b[:],
                           [("pn", h), "ident_b"], ["ps4"], sig=(mc == 1))
                V("tensor_copy", dict(out=pT[:, :, :, tsl], in_=p4b.rearrange("p (h m t) -> p h m t", h=4, m=2)), ["ps4"], ["pT"])
            scores(0)
            for tl in range(ntl):
                S.ctx = "p2.smax"
                softmax_tile(128, PS[2], PS[3], ["ps2", "ps3"])
                if tl + 1 < ntl:
                    scores(tl + 1)
                if tl == 0 and prev_wo is not None:
                    prev_wo()
                transposes(tl)
                fill()
            if nxt is not None:
                S.ctx = "p2.norm"
                ntok0, nn_, nxk = nxt
                nblk = slice(ntok0, ntok0 + nn_)
                fm_squares(lambda c: xT[:, c, nblk], nn_, sqb2, nxk, "m2")
            S.ctx = "p2.pv"
            pvb = [0, 1, 4]
            pvi = [0]

            def rot3():
                i_ = pvb[pvi[0] % 3]
                pvi[0] += 1
                return PS[i_], "ps%d" % i_
            for h in range(4):
                pbs, pks = rot3()
                for mc in range(2):
                    MM(pbs[:, 0:n], ones_b[:], pT[:, h, mc, 0:n], mc == 0, mc == 1, ["ones_b", "pT"], [pks])
                V("reciprocal", dict(out=rinv[:, 0:n], in_=pbs[:, 0:n]), [pks], ["rinv"])
                for dc in range(2):
                    oc = 2 * h + dc
                    pb, pk = rot3()
                    for mc in range(2):
                        MM(pb[:, 0:n], vbf[:, mc, oc * 128:(oc + 1) * 128], pT[:, h, mc, 0:n], mc == 0, mc == 1,
                           ["vbf", "pT"], [pk], sig=(mc == 1))
                    V("tensor_tensor", dict(out=oT[:, oc, 0:n], in0=pb[:, 0:n], in1=rinv[:, 0:n], op=ALU.mult),
                      [pk, "rinv"], ["oT"])
            if nxt is not None:
                S.ctx = "p2.norm"
                fm_stats(lambda c: xT[:, c, nblk], nn_, sqb2, rstd2, tmpn2, PS[0], "ps0", nxk, "m2", squares=False)
            fill()
            for fl in sfill:
                fl()

            def wo_part():
                S.ctx = "p2.wo"
                for oc in range(8):
                    pb, pk = rot()
                    for c in range(KC):
                        MM(pb[:, 0:n], wo_b[:, c, oc * 128:(oc + 1) * 128], oT[:, c, 0:n], c == 0, c == KC - 1, ["wo", "oT"], [pk])
                    V("tensor_tensor", dict(out=xT[:, oc, blk], in0=pb[:, 0:n], in1=xT[:, oc, blk], op=ALU.add),
                      [pk] + list(xkeys), list(xkeys))
            return wo_part

        sblk = slice(TP, TP + TS)
        skeys = [("xT", 8)]

        def s_init():
            S.ctx = "p2.s_init"
            fm_norm(lambda c: xT[:, c, sblk], TS, G_X, lambda c: hn[:, c, 0:TS], sqb2, rstd2, tmpn2, PS[0], "ps0",
                    skeys, ["hn"], "m2")
            rr[0] = 1
            for oc in range(8):
                pb, pk = rot()
                for c in range(KC):
                    MM(pb[:, 0:TS], wq_b[:, c, oc * 128:(oc + 1) * 128], hn[:, c, 0:TS], c == 0, c == KC - 1, ["wq", ("hn", c)], [pk])
                act(qT_s[:, oc, :], pb[:, 0:TS], AF.Copy, [pk], ["qT_s"], scale=1.0 / 16)
            G("memset", dict(ap=qTm[0][:], constant=0.0), [], [("qTm", 0)])
            G("memset", dict(ap=qTm[1][:], constant=0.0), [], [("qTm", 1)])

        def s_kpass(b):
            S.ctx = "p2.s_k"
            s = b % 2
            ld("gpsimd", kslot[s][:], ck[b].rearrange("(mt p) d -> p mt d", p=128), "ks%d" % s, writes=[("kslot", s)])
            if b >= 2:
                V("memset", dict(ap=qTm[s][:, :, (b - 2) * 4:(b - 2) * 4 + 4], constant=0.0), [], [("qTm", s)])
            V("tensor_copy", dict(out=qTm[s][:, :, b * 4:b * 4 + 4], in_=qT_s[:, :, b * 4:b * 4 + 4]), ["qT_s"], [("qTm", s)])
            for half in range(2):
                p45 = PS[4 + half][:].bitcast(BF16)
                for oc4 in range(4):
                    oc = half * 4 + oc4
                    for mt in range(2):
                        TR(p45[:, oc4 * 256 + mt * 128:oc4 * 256 + (mt + 1) * 128], kslot[s][:, mt, oc * 128:(oc + 1) * 128],
                           ident_b[:], [("kslot", s), "ident_b"], ["ps%d" % (4 + half)], sig=(oc4 == 3 and mt == 1))
                src = p45.rearrange("p (o m) -> p o m", o=4)
                if half == 0:
                    V("tensor_copy", dict(out=KTs[:, 0:4, :], in_=src), ["ps4"], [("KTs", 0)])
                else:
                    act(KTs[:, 4:8, :], src, AF.Copy, ["ps5"], [("KTs", 1)])
            for h in range(4):
                pb, pk = PS[6 + h // 2], "ps%d" % (6 + h // 2)
                for dc in range(2):
                    first = (b == 0 and h % 2 == 0 and dc == 0)
                    last = (b == 15 and dc == 1)
                    MM(pb[0:64, (h % 2) * 256:(h % 2 + 1) * 256], qTm[s][:, 2 * h + dc, :], KTs[:, 2 * h + dc, :], first, last,
                       [("qTm", s), ("KTs", h // 2)], [pk], sig=(dc == 1))

        def s_mid():
            S.ctx = "p2.s_mid"
            softmax_tile(64, PS[6], PS[7], ["ps6", "ps7"])
            p4b = PS[4][:].bitcast(BF16)
            for h in range(4):
                for mc in range(2):
                    TR(p4b[:, (h * 2 + mc) * 64:(h * 2 + mc + 1) * 64], pn[0:64, h, mc * 128:(mc + 1) * 128], ident_b[0:64, 0:64],
                       [("pn", h), "ident_b"], ["ps4"], sig=(mc == 1))
            V("tensor_copy", dict(out=pT_s[:], in_=p4b[:, 0:512].rearrange("p (h m t) -> p h m t", h=4, m=2)), ["ps4"], ["pT_s"])

        def s_vpass(b):
            S.ctx = "p2.s_v"
            s = b % 2
            ld("gpsimd", kslot[s][:], cv[b].rearrange("(mt p) d -> p mt d", p=128), "ks%d" % s, writes=[("kslot", s)])
            for oc in range(8):
                for mc in range(2):
                    MM(PS[5][:, oc * 64 + b * 4:oc * 64 + b * 4 + 4], kslot[s][:, mc, oc * 128:(oc + 1) * 128],
                       pT_s[:, oc // 2, mc, b * 4:b * 4 + 4], mc == 0, mc == 1, [("kslot", s), "pT_s"], ["ps5"],
                       sig=(oc == 7 and mc == 1))

        def s_fin():
            S.ctx = "p2.s_fin"
            pbs, pks = rot()
            for h in range(4):
                for mc in range(2):
                    MM(pbs[:, h * TS:(h + 1) * TS], ones_b[:], pT_s[:, h, mc, :], mc == 0, mc == 1, ["ones_b", "pT_s"], [pks], sig=(mc == 1))
            V("reciprocal", dict(out=rinv[:, 0:4 * TS], in_=pbs[:, 0:4 * TS]), [pks], ["rinv"])
            V("tensor_tensor", dict(out=oT[:, :, 0:TS].rearrange("p (h d) t -> p h d t", h=4),
                                    in0=PS[5][:].rearrange("p (h d t) -> p h d t", h=4, d=2),
                                    in1=rinv[:, 0:4 * TS].rearrange("p (h t) -> p h t", h=4).unsqueeze(2).broadcast_to([128, 4, 2, TS]),
                                    op=ALU.mult), ["ps5", "rinv"], ["oT"])
            for oc in range(8):
                pb, pk = rot()
                for c in range(KC):
                    MM(pb[:, 0:TS], wo_b[:, c, oc * 128:(oc + 1) * 128], oT[:, c, 0:TS], c == 0, c == KC - 1, ["wo", "oT"], [pk])
                V("tensor_tensor", dict(out=xT[:, oc, sblk], in0=pb[:, 0:TS], in1=xT[:, oc, sblk], op=ALU.add),
                  [pk] + skeys, skeys)

        s_init()
        prev_wo = None
        wpieces = unit_pieces(0) + unit_pieces(1)
        for b in range(TP // NB):
            if b < 4:
                sf = [(lambda bb=bb: s_kpass(bb)) for bb in range(4 * b, 4 * b + 4)]
            else:
                sf = [(lambda bb=bb: s_vpass(bb)) for bb in range(4 * (b - 4), 4 * (b - 4) + 4)]
            nxt_ = ((b + 1) * NB, NB, [("xT", b + 1)]) if b + 1 < TP // NB else None
            prev_wo = attn_block(b * NB, NB, [("xT", b)], prev_wo, sf, pre=(b > 0), nxt=nxt_)
            if b == 3:
                s_mid()
        prev_wo()
        for p_ in wpieces:
            p_()
        s_fin()

        if DEBUG:
            ld("sync", dbg2, xT[:], "o_dbg2", reads=[("xT", i) for i in range(9)])
        hnA = carve(R3, 0, [128, 8, T], BF16)
        L = Lay(R2)
        sqb3 = L.get([128, 8, 256], BF16)
        rstd3 = L.get([128, 256], F32); tmpn3 = L.get([128, 256], F32)
        gsbs = [L.get([128, NB3 + 2], F32) for _ in range(2)]
        cbs = [L.get([128, NB3], F32) for _ in range(2)]
        actbs = [L.get([128, 4, NB3], BF16) for _ in range(2)]
        sconvT = L.get([128, NF, 32], F32)
        gkeep = sconvT
        gsss = [L.get([128, 16, 6], F32) for _ in range(2)]
        ubs = [L.get([128, NB3], BF16) for _ in range(2)]
        stgs = [carve(R1, 49152 + i_ * 4096, [128, 1024], F32) for i_ in range(2)]
        stg3 = stgs[0]
        gfin_bc = L.get([128, 1024], F32)
        fst = L.get([128, 16], F32)
        p3keys = [(("hnA", i_), c_) for i_ in range(5) for c_ in range(KC)] + [("gsbh", 0), ("gsbh", 1), ("gssh", 0), ("gssh", 1), "m3sqb", "m3rstd", "m3tmpn", ("gsb", 0), ("gsb", 1), ("cb", 0), ("cb", 1), "sconvT", ("gss", 0), ("gss", 1), ("ub", 0), ("ub", 1), "stg3", ("stg", 1), "gfin_bc", ("fst", 0), ("fst", 1)] + \
                 [("actb", a_, l_) for a_ in range(2) for l_ in range(4)]
        S.fence(p3keys)
        ld("sync", gfin_bc[:], g_fin.broadcast_to([128, D]), "gfin", writes=["gfin_bc"])
        for half in range(3):
            c0, c1 = half * 1024, min(DFF, (half + 1) * 1024)
            ld("sync", stg3[0:32, 0:c1 - c0], sconv[:, c0:c1], "stg3", writes=["stg3"])
            nfh = (c1 - c0) // 128
            for j in range(nfh):
                TR(PS[7][:, j * 32:(j + 1) * 32], stg3[0:32, j * 128:(j + 1) * 128], ident_f[0:32, 0:32], ["stg3", "ident"], ["ps7"],
                   sig=(j == nfh - 1))
            V("tensor_copy", dict(out=sconvT[:, half * 8:half * 8 + nfh, :],
                                  in_=PS[7][:, 0:nfh * 32].rearrange("p (f t) -> p f t", t=32)), ["ps7"], ["sconvT"])
        blocks3 = [(b * NB3, NB3, [("xT", 2 * b), ("xT", 2 * b + 1)], False) for b in range(TP // NB3)] + [(TP, TS, [("xT", 8)], True)]
        gcnt = [0]

        def ffn_front(u, bi, ab, fillers=()):
            S.ctx = "p3.front"
            tails = {}
            f0, nf = UNITS[u]
            si = u % 2
            wg_v, wu_v, wd_v = slot_views(si)
            fk = ("fslot", si)
            tok0, n, xkeys, sample = blocks3[bi]
            blk = slice(tok0, tok0 + n)
            if u == 0:
                for s0 in range(0, n, 256):
                    nn = min(256, n - s0)
                    sub = slice(tok0 + s0, tok0 + s0 + nn)
                    fm_norm(lambda c: xT[:, c, sub], nn, G_FFN, lambda c: hnA[:, c, sub], sqb3, rstd3, tmpn3, PS[0], "ps0",
                            xkeys, [("hnA", bi)], "m3")
            hk = ("hnA", bi)
            for lf in range(nf):
                f = f0 + lf
                gi = gcnt[0] % 2
                gcnt[0] += 1
                gsb, cb = gsbs[gi], cbs[gi]
                gk, ck_ = ("gsb", gi), ("cb", gi)
                pg, pgk = PS[1 + (lf % 2)], "ps%d" % (1 + lf % 2)
                pub_ = 3 if u == len(UNITS) - 1 else 3 + (lf % 2)
                pu, puk = PS[pub_], "ps%d" % pub_
                for c in range(KC):
                    MM(pg[:, 0:n], wg_v[:, c, lf * 128:(lf + 1) * 128], hnA[:, c, blk], c == 0, c == KC - 1, [fk, (hk, c)], [pgk])
                for c in range(KC):
                    MM(pu[:, 0:n], wu_v[:, c, lf * 128:(lf + 1) * 128], hnA[:, c, blk], c == 0, c == KC - 1, [fk, (hk, c)], [puk])
                cw = lambda j, f=f: colv2[:, j * NF + f:j * NF + f + 1]
                cbias = colv1[:, 46 + f:47 + f]
                ghk = ("gsbh", gi)
                if not sample:
                    V("tensor_copy", dict(out=gsb[:, 0:2], in_=ghist[:, f, :]), ["ghist"], [ghk])
                    act(gsb[:, 2:2 + n], pg[:, 0:n], AF.Copy, [pgk], [gk])
                    V("tensor_copy", dict(out=ghist[:, f, :], in_=gsb[:, n:n + 2]), [gk], ["ghist"])
                    g0, g1, g2 = gsb[:, 0:n], gsb[:, 1:1 + n], gsb[:, 2:2 + n]
                    cbo = cb[:, 0:n]
                else:
                    gk = ("gss", gi)
                    ghk = ("gssh", gi)
                    gss = gsss[gi]
                    V("tensor_copy", dict(out=gss[:, :, 0:2], in_=sconvT[:, f, :].rearrange("p (b j) -> p b j", j=2)),
                      ["sconvT"], [ghk])
                    act(gss[:, :, 2:6], pg[:, 0:n].rearrange("p (b t) -> p b t", t=4), AF.Copy, [pgk], [gk])
                    V("tensor_copy", dict(out=gkeep[:, f, :].rearrange("p (b j) -> p b j", j=2), in_=gss[:, :, 4:6]),
                      [gk], ["sconvT"])
                    g0, g1, g2 = gss[:, :, 0:4], gss[:, :, 1:5], gss[:, :, 2:6]
                    cbo = cb[:, 0:n].rearrange("p (b t) -> p b t", t=4)
                ub, ubk = ubs[gi], ("ub", gi)
                act(ub[:, 0:n], pu[:, 0:n], AF.Copy, [puk], [ubk])

                def ident(cbo=cbo, g2=g2, gk=gk, ck_=ck_, cw=cw, cbias=cbias):
                    act(cbo, g2, AF.Identity, [gk, "colv1", "colv2"], [ck_], scale=cw(2), bias=cbias)

                S.ctx = "p3.front"

                def tail(cbo=cbo, g1=g1, g0=g0, cw=cw, gk=gk, ghk=ghk, ck_=ck_, cb=cb, ub=ub, ubk=ubk, lf=lf):
                    S.ctx = "p3.chain"
                    V("scalar_tensor_tensor", dict(out=cbo, in0=g1, scalar=cw(1), in1=cbo, op0=ALU.mult, op1=ALU.add), [gk, ghk, ck_], [ck_])
                    V("scalar_tensor_tensor", dict(out=cbo, in0=g0, scalar=cw(0), in1=cbo, op0=ALU.mult, op1=ALU.add), [gk, ghk, ck_], [ck_])
                    act(cb[:, 0:n], cb[:, 0:n], AF.Gelu_apprx_tanh, [ck_], [ck_])
                    V("tensor_tensor", dict(out=actbs[ab][:, lf, 0:n], in0=cb[:, 0:n], in1=ub[:, 0:n], op=ALU.mult),
                      [ck_, ubk], [("actb", ab, lf)])
                tails[lf] = tail
                if lf >= 1:
                    tails[lf - 1]()
                ident()
                if fillers:
                    fillers.pop(0)()
                S.ctx = "p3.front"
            return tails[nf - 1]

        fcnt = [0]

        def final_out(bi):
            tok0, n, xkeys, sample = blocks3[bi]
            tiles = list(range(0, n, 128))
            As, Bs = [], []
            for s0 in tiles:
                par = fcnt[0] % 2
                fcnt[0] += 1
                As.append(lambda s0=s0, par=par: final_A(tok0, n, xkeys, s0, par))
                Bs.append([(lambda s0=s0, par=par, half=half: final_B(tok0, n, xkeys, sample, s0, par, half)) for half in range(2)])
            seq = []
            nt = len(tiles)
            seq.append(As[0])
            if nt > 1:
                seq.append(As[1])
            for i_ in range(nt):
                seq += Bs[i_]
                if i_ + 2 < nt:
                    seq.append(As[i_ + 2])
            return seq

        def final_A(tok0, n, xkeys, s0, par):
            S.ctx = "p3.final"
            rows = min(128, n - s0)
            tsl = slice(tok0 + s0, tok0 + s0 + rows)
            fo = par * 8
            act(sqb3[:, :, 0:rows], xT[:, :, tsl], AF.Square, list(xkeys), ["m3sqb"])
            for c in range(KC):
                MM(PS[4][0:rows, 0:1], sqb3[:, c, 0:rows], ones_b[:, 0:1], c == 0, c == KC - 1, ["m3sqb", "ones_b"], ["ps4"])
            act(fst[0:rows, fo + 3:fo + 4], PS[4][0:rows, 0:1], AF.Ln, ["ps4"], [("fst", par)], scale=1.0 / D, bias=EPS)
            V("tensor_scalar", dict(out=fst[0:rows, fo + 4:fo + 5], in0=fst[0:rows, fo + 3:fo + 4], scalar1=-0.5, scalar2=None, op0=ALU.mult),
              [("fst", par)], [("fst", par)])
            act(fst[0:rows, fo + 5:fo + 6], fst[0:rows, fo + 4:fo + 5], AF.Exp, [("fst", par)], [("fst", par)])

        def final_B(tok0, n, xkeys, sample, s0, par, half):
            S.ctx = "p3.final"
            rows = min(128, n - s0)
            tsl = slice(tok0 + s0, tok0 + s0 + rows)
            fo = par * 8
            stg_, sk_ = stgs[par], ("stg3" if par == 0 else ("stg", 1))
            bank_ = 7 if half == 0 else 0
            pb, pk = PS[bank_], "ps%d" % bank_
            for c4 in range(4):
                c = half * 4 + c4
                TR(pb[0:rows, c4 * 128:(c4 + 1) * 128], xT[:, c, tsl], ident_f[:], list(xkeys) + ["ident"], [pk], sig=(c4 == 3))
            V("scalar_tensor_tensor", dict(out=stg_[0:rows, half * 512:(half + 1) * 512], in0=pb[0:rows, :], scalar=fst[0:rows, fo + 5:fo + 6],
                                           in1=gfin_bc[0:rows, half * 512:(half + 1) * 512], op0=ALU.mult, op1=ALU.mult),
              [pk, ("fst", par), "gfin_bc"], [sk_])
            if half == 1:
                dst = y_s if sample else y_p[tok0 + s0:tok0 + s0 + rows, :]
                ld("sync", dst, stg_[0:rows, :], "o_y%d" % par, reads=[sk_])

        def ffn_back(u, bi, ab, fillers=None):
            S.ctx = "p3.back"
            f0, nf = UNITS[u]
            si = u % 2
            wg_v, wu_v, wd_v = slot_views(si)
            fk = ("fslot", si)
            tok0, n, xkeys, sample = blocks3[bi]
            blk = slice(tok0, tok0 + n)
            dbanks = [5, 6]
            if u != len(UNITS) - 1:
                dbanks.append(7)
                if u > 0:
                    dbanks.append(0)
            for oc in range(8):
                bi_ = dbanks[oc % len(dbanks)]
                pb, pk = PS[bi_], "ps%d" % bi_
                for lf in range(nf):
                    MM(pb[:, 0:n], wd_v[:, lf, oc * 128:(oc + 1) * 128], actbs[ab][:, lf, 0:n], lf == 0, lf == nf - 1,
                       [fk] + [("actb", ab, l) for l in range(nf)], [pk])
                V("tensor_tensor", dict(out=xT[:, oc, blk], in0=pb[:, 0:n], in1=xT[:, oc, blk], op=ALU.add),
                  [pk] + list(xkeys), list(xkeys))
                if fillers:
                    fillers.pop(0)()
                    S.ctx = "p3.back"

        def conv_state_out(f0, nf):
            S.ctx = "p3.cso"
            stgp = stgs[1]
            for j in range(nf):
                TR(PS[0][0:2, j * 128:(j + 1) * 128], ghist[:, f0 + j, :], ident_f[:], ["ghist", "ident"], ["ps0"], sig=(j == nf - 1))
            V("tensor_copy", dict(out=stgp[0:2, 0:nf * 128], in_=PS[0][0:2, 0:nf * 128]), ["ps0"], [("stg", 1)])
            ld("sync", sc_p[:, f0 * 128:(f0 + nf) * 128], stgp[0:2, 0:nf * 128], "o_scp", reads=[("stg", 1)])
            for j in range(nf):
                TR(PS[7][0:32, j * 128:(j + 1) * 128], gkeep[:, f0 + j, :], ident_f[:], ["sconvT", "ident"], ["ps7"], sig=(j == nf - 1))
            V("tensor_copy", dict(out=stg3[0:32, 0:nf * 128], in_=PS[7][0:32, 0:nf * 128]), ["ps7"], ["stg3"])
            ld("sync", sc_s[:, f0 * 128:(f0 + nf) * 128], stg3[0:32, 0:nf * 128], "o_scs", reads=["stg3"])

        abc = 0
        for u in range(len(UNITS)):
            prev = None
            lastu = (u == len(UNITS) - 1)
            pend = []
            for bi in range(len(blocks3)):
                ab = abc % 2
                abc += 1
                deferred = ffn_front(u, bi, ab, fillers=pend)
                if prev is not None:
                    ffn_back(u, *prev, fillers=pend)
                    if lastu:
                        pend += final_out(prev[0])
                deferred()
                prev = (bi, ab)
            ffn_back(u, *prev, fillers=pend)
            if lastu:
                pend += final_out(prev[0])
            for fl in pend:
                fl()
            conv_state_out(*UNITS[u])
            if u + 2 < len(UNITS):
                load_unit(u + 2)
        _NC_CACHE["sched"] = S
        sems = {n: es.enter_context(nc.semaphore(n)) for n in sorted(S.sem_names)}
        final = {"sync": [(s, v) for s, v in S.dma_cnt.items() if s.startswith("D_o_")]}
        with nc.Block() as block:
            S.emit(block, sems, final)
    return nc


_NC_CACHE = {}


def kernel(**inp):
    f = lambda a: np.ascontiguousarray(np.asarray(a, dtype=np.float32))
    x_prompt, x_sample, mem_prompt = f(inp["x_prompt"]), f(inp["x_sample"]), f(inp["mem_prompt"])
    state_gla, state_conv = f(inp["state_gla"])[0], f(inp["state_conv"])[0]
    ckk, cvv = f(inp["cache_mem_k"])[0], f(inp["cache_mem_v"])[0]
    vec1 = np.concatenate([f(inp["g_mix"]).reshape(8, 128), f(inp["g_x"]).reshape(8, 128), f(inp["g_mem"]).reshape(8, 128),
                           f(inp["g_ffn"]).reshape(8, 128), f(inp["g_final"]).reshape(8, 128), f(inp["b_alpha"]).reshape(2, 128),
                           f(inp["g_gla_out"]).reshape(4, 128), f(inp["conv_b"]).reshape(22, 128)], axis=0)
    vec2 = f(inp["conv_w"]).reshape(66, 128)
    shared = {
        "w_in": f(inp["w_in"])[0], "w_alpha": f(inp["w_alpha"])[0], "w_s": f(inp["w_s"])[0],
        "b_s": f(inp["b_s"]).reshape(1, 512), "g_sgu": f(inp["g_sgu"]).reshape(1, 512),
        "w_out": f(inp["w_out"])[0], "wq": f(inp["wq_x"])[0], "wk": f(inp["wk_x"])[0], "wv": f(inp["wv_x"])[0], "wo": f(inp["wo_x"])[0],
        "w_gate": f(inp["w_gate"])[0], "w_up": f(inp["w_up"])[0], "w_down": f(inp["w_down"])[0],
        "vec1": np.ascontiguousarray(vec1), "vec2": np.ascontiguousarray(vec2), "g_fin": f(inp["g_final"]).reshape(1, D),
    }
    for k, v in host_consts().items():
        shared["c_" + k] = v
    in_maps = []
    for c in range(NCORES):
        m = dict(shared)
        sl = slice(c * 16, (c + 1) * 16)
        m["x_p"] = x_prompt[c]
        m["x_s"] = np.ascontiguousarray(x_sample[sl].reshape(TS, D))
        m["mem"] = mem_prompt[c]
        m["sgla"] = np.ascontiguousarray(state_gla[sl])
        m["sconv"] = np.ascontiguousarray(state_conv[sl].reshape(32, DFF))
        m["ck"] = np.ascontiguousarray(ckk[sl].reshape(16, 256, D))
        m["cv"] = np.ascontiguousarray(cvv[sl].reshape(16, 256, D))
        in_maps.append(m)
    if "nc" not in _NC_CACHE:
        _NC_CACHE["nc"] = build_nc()
    res = run_bass_kernel_spmd(_NC_CACHE["nc"], in_maps, core_ids=list(range(NCORES)))
    R = res.results
    cat = lambda k: np.stack([np.asarray(r[k], dtype=np.float32) for r in R], axis=0)
    y_prompt = cat("y_p")
    y_sample = cat("y_s").reshape(128, 4, D)
    sg_p = cat("sg_p")[None]
    sc_p = cat("sc_p")[None]
    mk_p = cat("mk_p").reshape(1, 8, 256, 4, 256)
    mv_p = cat("mv_p").reshape(1, 8, 256, 4, 256)
    sg_s = cat("sg_s").reshape(1, 128, 4, 64, 128)
    sc_s = cat("sc_s").reshape(1, 128, 2, DFF)
    sv_s = cat("sv_s").reshape(1, 128, 4, 4, 128)
    if DEBUG:
        _NC_CACHE["dbg"] = (R[0]["dbg1"], R[0]["dbg2"])
    return (y_prompt, y_sample, sg_p, sc_p, mk_p, mv_p, sg_s, sc_s, sv_s)
```

```python
import numpy as np
from contextlib import ExitStack
import concourse.bass as bass
import concourse.mybir as mybir
from concourse.bass_utils import run_bass_kernel_spmd

F32 = mybir.dt.float32
BF16 = mybir.dt.bfloat16
U8 = mybir.dt.uint8
AF = mybir.ActivationFunctionType
ALU = mybir.AluOpType
AX = mybir.AxisListType

NCORES = 8
TP, TS = 2048, 64
T = TP + TS
D, KC = 1024, 8
DIN = 2576
CQ, CK, CV, CR, CA, CU, CSV = 0, 256, 512, 1024, 1536, 1552, 2064
DFF, NF = 2816, 22
UNITS = [(0, 4), (4, 4), (8, 4), (12, 4), (16, 3), (19, 3)]
EPS = 1e-6
NB = 256
NB3 = 512
DEBUG = False


class Sched:
    ENGS = ["tensor", "vector", "scalar", "gpsimd", "sync"]

    def __init__(self):
        self.ops = {e: [] for e in self.ENGS}
        self.cnt = {e: 0 for e in self.ENGS}
        self.pending = {e: False for e in self.ENGS}
        self.writers, self.readers = {}, {}
        self.seen = {e: {} for e in self.ENGS}
        self.dma_cnt = {}
        self.sem_names = set()
        self.tags = {e: [] for e in self.ENGS}
        self.ctx = ""
        self.dma_hist = {}

    def _deps(self, eng, reads, writes):
        toks = {}

        def add(d):
            for s, v in d.items():
                if v > toks.get(s, 0):
                    toks[s] = v
        for k in reads:
            add(self.writers.get(k, {}))
        for k in writes:
            add(self.writers.get(k, {}))
            add(self.readers.get(k, {}))
        waits = []
        for s, v in toks.items():
            if s == "E_" + eng and eng != "gpsimd":
                continue
            if self.seen[eng].get(s, 0) >= v:
                continue
            self.seen[eng][s] = v
            waits.append((s, v))
        return waits

    def _commit(self, tok, reads, writes):
        s, v = tok
        for k in reads:
            r = self.readers.setdefault(k, {})
            r[s] = max(r.get(s, 0), v)
        for k in writes:
            self.writers[k] = {s: v}
            self.readers[k] = {}

    def op(self, eng, fn, reads=(), writes=(), sig=True):
        isps = lambda k: isinstance(k, str) and k.startswith("ps") and k[2:].isdigit()
        writes = list(writes) + [k for k in reads if isps(k)]
        reads = [k for k in reads if not isps(k)]
        waits = self._deps(eng, reads, writes)
        s = "E_" + eng
        self.sem_names.add(s)
        if sig:
            self.cnt[eng] += 1
            self.pending[eng] = False
            tok = (s, self.cnt[eng])
        else:
            self.pending[eng] = True
            tok = (s, self.cnt[eng] + 1)
        self.ops[eng].append((fn, waits, tok, 1 if sig else 0))
        self.tags[eng].append(self.ctx)
        self._commit(tok, reads, writes)

    def dma(self, eng, fn, key, reads=(), writes=()):
        waits = self._deps(eng, reads, writes)
        s = "D_" + key
        self.sem_names.add(s)
        self.dma_cnt[s] = self.dma_cnt.get(s, 0) + 16
        tok = (s, self.dma_cnt[s])
        hist = self.dma_hist.setdefault(eng, [])
        lim = 16 if eng == "gpsimd" else 24
        if len(hist) >= lim:
            os_, ov_ = hist[-lim]
            if self.seen[eng].get(os_, 0) < ov_:
                self.seen[eng][os_] = ov_
                waits = list(waits) + [(os_, ov_)]
        hist.append(tok)
        self.ops[eng].append((fn, waits, tok, 2))
        self.tags[eng].append(self.ctx)
        self._commit(tok, reads, writes)

    def fence(self, keys):
        for e in self.ENGS:
            assert not self.pending[e], e
        d = {"E_" + e: self.cnt[e] for e in self.ENGS if self.cnt[e] > 0}
        d.update(self.dma_cnt)
        for k in keys:
            self.writers[k] = dict(d)
            self.readers[k] = {}

    def emit(self, block, sems, final_waits):
        def body(eng_name):
            def f(eng):
                for fn, waits, tok, kind in self.ops[eng_name]:
                    for s, v in waits:
                        eng.wait_ge(sems[s], v)
                    r = fn(eng)
                    if kind == 2:
                        r.then_inc(sems[tok[0]], 16)
                    elif kind == 1:
                        r.then_inc(sems[tok[0]], 1)
                for s, v in final_waits.get(eng_name, []):
                    eng.wait_ge(sems[s], v)
            return f
        block.tensor(body("tensor"))
        block.vector(body("vector"))
        block.scalar(body("scalar"))
        block.gpsimd(body("gpsimd"))
        block.sync(body("sync"))


def host_consts():
    c = {}
    c["ident"] = np.eye(128, dtype=np.float32)
    r = np.arange(128)
    c["mask128"] = ((r[:, None] <= r[None, :]) & ((r[:, None] // 64) == (r[None, :] // 64))).astype(np.float32)
    r64 = np.arange(64)
    c["mask_s"] = ((r64[:, None] <= r64[None, :]) & ((r64[:, None] // 4) == (r64[None, :] // 4))).astype(np.float32)
    rm = np.ones((128, NB), np.float32)
    rm[:, 0::64] = 0.0
    c["rm"] = rm
    rms = np.ones((128, TS), np.float32)
    rms[:, 0::4] = 0.0
    c["rm_s"] = rms
    c["trilT"] = (r[:, None] <= r[None, :]).astype(np.float32)
    bm = np.zeros((128, 16), np.float32)
    bm[r64, r64 // 4] = 1.0
    c["bm"] = bm
    sel = np.zeros((4, 64), np.float32)
    sel[r64 % 4, r64] = 1.0
    c["sel"] = sel
    return c


def build_nc():
    nc = bass.Bass("TRN2", target_bir_lowering=False)

    def din(name, shape):
        return nc.dram_tensor(name, list(shape), F32, kind="ExternalInput").ap()

    def dout(name, shape):
        return nc.dram_tensor(name, list(shape), F32, kind="ExternalOutput").ap()

    x_p = din("x_p", [TP, D]); x_s = din("x_s", [TS, D]); mem = din("mem", [256, D])
    sgla = din("sgla", [16, 4, 64, 128]); sconv = din("sconv", [32, DFF])
    ck = din("ck", [16, 256, D]); cv = din("cv", [16, 256, D])
    w_in = din("w_in", [D, DIN]); w_alpha = din("w_alpha", [16, 256])
    w_s = din("w_s", [4, 128, 128]); b_s = din("b_s", [1, 512]); g_sgu = din("g_sgu", [1, 512])
    w_out = din("w_out", [D, D]); wq = din("wq", [D, D]); wk = din("wk", [D, D]); wv = din("wv", [D, D]); wo = din("wo", [D, D])
    w_gate = din("w_gate", [D, DFF]); w_up = din("w_up", [D, DFF]); w_down = din("w_down", [DFF, D])
    vec1 = din("vec1", [68, 128]); vec2 = din("vec2", [66, 128]); g_fin = din("g_fin", [1, D])
    cst = {k: din("c_" + k, v.shape) for k, v in host_consts().items()}

    y_p = dout("y_p", [TP, D]); y_s = dout("y_s", [TS, D]); sg_p = dout("sg_p", [4, 64, 128]); sc_p = dout("sc_p", [2, DFF])
    mk_p = dout("mk_p", [256, D]); mv_p = dout("mv_p", [256, D]); sg_s = dout("sg_s", [16, 4, 64, 128])
    sc_s = dout("sc_s", [32, DFF]); sv_s = dout("sv_s", [TS, 512])

    if DEBUG:
        dbg1 = dout("dbg1", [128, KC, T]); dbg2 = dout("dbg2", [128, KC, T])
    S = Sched()
    es = ExitStack()
    with es:
        def sb(name, shape, dt):
            return es.enter_context(nc.sbuf_tensor(name, list(shape), dt))
        xT = sb("xT", [128, KC, T], F32)
        R1 = sb("R1", [128, 57600], U8)
        R2 = sb("R2", [128, 32768], U8)
        R3 = sb("R3", [128, 33792], U8)
        PS = [es.enter_context(nc.psum_tensor("ps%d" % i, [128, 512], F32)) for i in range(8)]

        def carve(arena, off, shape, dt):
            esz = 4 if dt == F32 else 2
            n = int(np.prod(shape[1:]))
            v = arena[:, off:off + n * esz].bitcast(dt)
            if len(shape) == 3:
                v = v.rearrange("p (a b) -> p a b", a=shape[1])
            elif len(shape) == 4:
                v = v.rearrange("p (a b c) -> p a b c", a=shape[1], b=shape[2])
            return v

        class Lay:
            def __init__(self, arena, base=0):
                self.arena, self.off = arena, base

            def get(self, shape, dt):
                esz = 4 if dt == F32 else 2
                v = carve(self.arena, self.off, shape, dt)
                self.off += int(np.prod(shape[1:])) * esz
                self.off = (self.off + 63) // 64 * 64
                assert self.off <= self.arena.shape[1], (self.off, self.arena.shape)
                return v

        ident_f = sb("ident_f", [128, 128], F32); ident_b = sb("ident_b", [128, 128], BF16)
        ones_b = sb("ones_b", [128, 128], BF16)
        mask128 = sb("mask128", [128, 128], F32); mask_s = sb("mask_s", [64, 64], F32)
        rm = sb("rm", [128, NB], F32); rm_s = sb("rm_s", [128, TS], F32)
        bm = sb("bm", [128, 16], F32); sel = sb("sel", [4, 64], F32)
        colv1 = sb("colv1", [128, 68], F32); colv2 = sb("colv2", [128, 66], F32)
        nba = sb("nba", [128, 2], F32); gg = sb("gg", [128, 4], F32)
        gsgu_bc = sb("gsgu_bc", [128, 512], F32)
        Wm = sb("Wm", [128, 4, 128], BF16); Wm_s = sb("Wm_s", [64, 4, 64], BF16)
        bs_hi = sb("bs_hi", [1, 512], BF16); bs_lo = sb("bs_lo", [1, 512], BF16)
        walpha_b = sb("walpha_b", [16, 256], BF16)
        ghist = sb("ghist", [128, NF, 2], F32)
        KT = sb("KT", [128, 8, 256], BF16); vbf = sb("vbf", [128, 2, 1024], BF16)
        smx = sb("smx", [128, 4], F32)
        Sst = sb("Sst", [128, 2, 128], F32); S_bf = sb("S_bf", [128, 2, 128], BF16); S_bfB = sb("S_bfB", [128, 2, 128], BF16)
        S_bfs = [S_bf, S_bfB]
        gch = [0]

        G_MIX, G_X, G_MEM, G_FFN, G_FIN = 0, 8, 16, 24, 32

        def ld(eng, out, in_, key, reads=(), writes=(), **kw):
            S.dma(eng, lambda e: e.dma_start(out=out, in_=in_, **kw), key, reads=reads, writes=writes)

        def V(name, kw, reads=(), writes=()):
            S.op("vector", lambda e: getattr(e, name)(**kw), reads, writes)

        def A(fn, reads=(), writes=()):
            S.op("scalar", fn, reads, writes)

        def G(name, kw, reads=(), writes=()):
            S.op("gpsimd", lambda e: getattr(e, name)(**kw), reads, writes)

        def MM(out, lhsT, rhs, start, stop, reads, writes, sig=None):
            if sig is None:
                sig = stop
            S.op("tensor", lambda e: e.matmul(out, lhsT=lhsT, rhs=rhs, start=start, stop=stop), reads, writes, sig=sig)

        def TR(out, in_, ident, reads, writes, sig=True):
            S.op("tensor", lambda e: e.transpose(out=out, in_=in_, identity=ident), reads, writes, sig=sig)

        def act(out, in_, func, reads, writes, **kw):
            A(lambda e: e.activation(out=out, in_=in_, func=func, **kw), reads, writes)

        for nm, t in [("ident", ident_f), ("mask128", mask128), ("mask_s", mask_s), ("rm", rm), ("rm_s", rm_s),
                      ("bm", bm), ("sel", sel)]:
            ld("sync", t[:], cst[nm], "c_" + nm, writes=[nm])
        ld("sync", gsgu_bc[:], g_sgu.broadcast_to([128, 512]), "gsgu", writes=["gsgu_bc"])
        ld("gpsimd", walpha_b[:], w_alpha, "walpha", writes=["walpha_b"])
        V("tensor_copy", dict(out=ident_b[:], in_=ident_f[:]), ["ident"], ["ident_b"])
        G("memset", dict(ap=ones_b[:], constant=1.0), [], ["ones_b"])
        G("memset", dict(ap=ghist[:], constant=0.0), [], ["ghist"])
        G("memset", dict(ap=Sst[:], constant=0.0), [], ["Sst"])
        G("memset", dict(ap=S_bf[:], constant=0.0), [], [("S_bf", 0)])

        wk_b = carve(R2, 0, [128, 8, 1024], BF16); wv_b = carve(R2, 16384, [128, 8, 1024], BF16)
        w_in_b = carve(R1, 0, [128, 8, DIN], BF16); w_out_b = carve(R1, 41216, [128, 8, 1024], BF16)

        def load_w(dst, src, key, ncols, reads=(), writes=()):
            srcv = src.rearrange("(c p) n -> p c n", p=128)
            nck = srcv.shape[1]
            step = 8 if ncols <= 1024 else 4
            for c0 in range(0, nck, step):
                ld("gpsimd", dst[:, c0:c0 + step, :], srcv[:, c0:c0 + step, :], key, reads=reads, writes=[key] + list(writes))

        load_w(wk_b, wk, "wk", 1024)
        load_w(wv_b, wv, "wv", 1024)
        load_w(w_in_b, w_in, "w_in", DIN)
        load_w(w_out_b, w_out, "w_out", 1024)

        S.ctx = "p0"
        L = Lay(R3)
        stg = [L.get([128, 1024], F32) for _ in range(2)]
        memT = L.get([128, 8, 256], F32)
        mnT = L.get([128, 8, 256], BF16)
        sqb0 = L.get([128, 8, 256], BF16)
        rstd0 = L.get([128, 256], F32); tmpn0 = L.get([128, 256], F32)
        Wm32 = L.get([128, 4, 128], F32); wsl = L.get([128, 4, 128], F32)
        vst = L.get([128, 128], F32)
        bsf = L.get([128, 512], F32)
        trilT = L.get([128, 128], F32)
        ld("sync", trilT[:], cst["trilT"], "c_trilT", writes=["trilT"])
        ld("sync", bsf[0:1, :], b_s, "bsf", writes=["bsf"])
        act(bs_hi[:], bsf[0:1, :], AF.Copy, ["bsf"], ["bs_hi"])
        V("tensor_tensor", dict(out=bs_lo[:], in0=bsf[0:1, :], in1=bs_hi[:], op=ALU.subtract), ["bsf", "bs_hi"], ["bs_lo"])

        ld("sync", vst[0:68, :], vec1, "vst", writes=["vst"])
        TR(PS[0][:, 0:68], vst[0:68, :], ident_f[0:68, 0:68], ["vst", "ident"], ["ps0"])
        V("tensor_copy", dict(out=colv1[:], in_=PS[0][:, 0:68]), ["ps0"], ["colv1"])
        ld("sync", vst[0:66, :], vec2, "vst", writes=["vst"])
        TR(PS[0][:, 0:66], vst[0:66, :], ident_f[0:66, 0:66], ["vst", "ident"], ["ps0"])
        V("tensor_copy", dict(out=colv2[:], in_=PS[0][:, 0:66]), ["ps0"], ["colv2"])
        act(nba[:], colv1[:, 40:42], AF.Copy, ["colv1"], ["nba"], scale=-1.0)
        act(gg[:], colv1[:, 42:46], AF.Copy, ["colv1"], ["gg"], scale=0.5)

        ld("sync", wsl[:], w_s.rearrange("h i j -> i h j"), "wsl", writes=["wsl"])
        for h in range(4):
            TR(PS[1][:, h * 128:(h + 1) * 128], wsl[:, h, :], ident_f[:], ["wsl", "ident"], ["ps1"], sig=(h == 3))
        V("tensor_tensor", dict(out=Wm32[:], in0=PS[1][:].rearrange("p (h i) -> p h i", h=4),
                                    in1=trilT[:].unsqueeze(1).broadcast_to([128, 4, 128]), op=ALU.mult),
          ["ps1", "trilT"], ["Wm32"])
        V("tensor_copy", dict(out=Wm[:], in_=Wm32[:]), ["Wm32"], ["Wm"])
        for h in range(4):
            MM(PS[1][0:64, h * 64:(h + 1) * 64], sel[0:4, :], Wm32[0:4, h, 0:4].unsqueeze(1).broadcast_to([4, 16, 4]),
               True, True, ["sel", "Wm32"], ["ps1"], sig=(h == 3))
        V("tensor_tensor", dict(out=Wm_s[:], in0=PS[1][0:64, 0:256].rearrange("p (h i) -> p h i", h=4),
                                    in1=mask_s[:].unsqueeze(1).broadcast_to([64, 4, 64]), op=ALU.mult),
          ["ps1", "mask_s"], ["Wm_s"])

        def fm_norm(src_fn, n, gcol, out_fn, sqb, rstd, tmpn, psb, pskey, rkeys, wkeys, tag, do_stats=True, do_apply=True):
            if do_stats:
                fm_stats(src_fn, n, sqb, rstd, tmpn, psb, pskey, rkeys, tag)
            if do_apply:
                for c in range(KC):
                    V("scalar_tensor_tensor", dict(out=out_fn(c), in0=src_fn(c), scalar=colv1[:, gcol + c:gcol + c + 1],
                                                   in1=rstd[:, 0:n], op0=ALU.mult, op1=ALU.mult),
                      list(rkeys) + [tag + "rstd", "colv1"], [(wkeys[0], c)] if (wkeys[0] in ("xn", "hn") or (isinstance(wkeys[0], tuple) and wkeys[0][0] == "hnA")) else wkeys)

        def fm_squares(src_fn, n, sqb, rkeys, tag):
            for c in range(KC):
                act(sqb[:, c, 0:n], src_fn(c), AF.Square, rkeys, [tag + "sqb"])

        def fm_stats(src_fn, n, sqb, rstd, tmpn, psb, pskey, rkeys, tag, squares=True):
            if squares:
                fm_squares(src_fn, n, sqb, rkeys, tag)
            for c in range(KC):
                MM(psb[:, 0:n], ones_b[:], sqb[:, c, 0:n], c == 0, c == KC - 1, ["ones_b", tag + "sqb"], [pskey])
            act(tmpn[:, 0:n], psb[:, 0:n], AF.Ln, [pskey], [tag + "tmpn"], scale=1.0 / D, bias=EPS)
            V("tensor_scalar", dict(out=tmpn[:, 0:n], in0=tmpn[:, 0:n], scalar1=-0.5, scalar2=None, op0=ALU.mult),
              [tag + "tmpn"], [tag + "tmpn"])
            act(rstd[:, 0:n], tmpn[:, 0:n], AF.Exp, [tag + "tmpn"], [tag + "rstd"])

        ntile = TP // 128
        xs_slots = [carve(R2, 16384 + i_ * 4096, [128, 1024], F32) for i_ in range(2)]

        def xtile(t, buf, bkey, dkey, banks, extra_w=()):
            rows = 128 if t < ntile else TS
            src = x_p[t * 128:(t + 1) * 128, :] if t < ntile else x_s
            ld("sync", buf[0:rows, :], src, dkey, writes=[bkey] + list(extra_w))
            for half in range(2):
                pb, pk = PS[banks[half]], "ps%d" % banks[half]
                for c4 in range(4):
                    c = half * 4 + c4
                    TR(pb[:, c4 * 128:c4 * 128 + rows], buf[0:rows, c * 128:(c + 1) * 128], ident_f[0:rows, 0:rows],
                       [bkey, "ident"], [pk], sig=(c4 == 3))
                dst = xT[:, half * 4:half * 4 + 4, t * 128:t * 128 + rows]
                srcp = pb[:].rearrange("p (c n) -> p c n", c=4)[:, :, 0:rows]
                if half == 0:
                    V("tensor_copy", dict(out=dst, in_=srcp), [pk], [("xT", t // 2)])
                else:
                    act(dst, srcp, AF.Copy, [pk], [("xT", t // 2)])
        for t in range(2):
            xtile(t, stg[t % 2], ("stg", t % 2), "stg%d" % (t % 2), (2, 3))

        for t in range(2):
            ld("sync", stg[t][:], mem[t * 128:(t + 1) * 128, :], "stg%d" % t, writes=[("stg", t)])
            for half in range(2):
                pb = PS[2 + half]
                for c4 in range(4):
                    c = half * 4 + c4
                    TR(pb[:, c4 * 128:(c4 + 1) * 128], stg[t][:, c * 128:(c + 1) * 128], ident_f[:],
                       [("stg", t), "ident"], ["ps%d" % (2 + half)], sig=(c4 == 3))
                V("tensor_copy", dict(out=memT[:, half * 4:half * 4 + 4, t * 128:(t + 1) * 128],
                                                                  in_=pb[:].rearrange("p (c n) -> p c n", c=4)),
                  ["ps%d" % (2 + half)], ["memT"])
        fm_norm(lambda c: memT[:, c, :], 256, G_MEM, lambda c: mnT[:, c, :], sqb0, rstd0, tmpn0, PS[4], "ps4",
                ["memT"], ["mnT"], "p0")
        for t in range(2):
            for (wb, wkey, outd, isv) in [(wk_b, "wk", mk_p, False), (wv_b, "wv", mv_p, True)]:
                for half in range(2):
                    pb, pk = PS[5 + half], "ps%d" % (5 + half)
                    for c in range(KC):
                        MM(pb[:], mnT[:, c, t * 128:(t + 1) * 128], wb[:, c, half * 512:(half + 1) * 512], c == 0, c == KC - 1,
                           ["mnT", wkey], [pk])
                    act(stg[t][:, half * 512:(half + 1) * 512], pb[:], AF.Copy, [pk], [("stg", t)])
                    if isv:
                        V("tensor_copy", dict(out=vbf[:, t, half * 512:(half + 1) * 512], in_=pb[:]),
                          [pk], ["vbf"])
                ld("sync", outd[t * 128:(t + 1) * 128, :], stg[t][:], "o_mkv", reads=[("stg", t)])
        for oc in range(8):
            pb, pk = PS[5 + oc % 2], "ps%d" % (5 + oc % 2)
            for c in range(KC):
                MM(pb[:, 0:256], wk_b[:, c, oc * 128:(oc + 1) * 128], mnT[:, c, :], c == 0, c == KC - 1, ["mnT", "wk"], [pk])
            act(KT[:, oc, :], pb[:, 0:256], AF.Copy, [pk], ["KT"])

        wq_b = carve(R2, 16384, [128, 8, 1024], BF16); wo_b = carve(R2, 0, [128, 8, 1024], BF16)
        S.fence([("xs", 0), ("xs", 1)])
        L = Lay(R3)
        xn = L.get([128, 8, NB], BF16); sqb = L.get([128, 8, NB], BF16)
        rstd = L.get([128, NB], F32); tmpn = L.get([128, NB], F32)
        aT = L.get([128, NB], BF16)
        e1 = L.get([128, NB], F32); Bc = L.get([128, NB], F32)
        eb = L.get([128, 2, NB], F32); enb = L.get([128, 2, NB], F32)
        qm = L.get([128, 4, NB], BF16); ktT = L.get([128, 2, NB], BF16); khT = L.get([128, 2, NB], BF16)
        cat = L.get([128, 8, NB], BF16); th = L.get([128, NB], F32)
        v_tm = L.get([128, 2, 512], BF16)
        p1keys = [("xn", c_) for c_ in range(KC)] + ["m1sqb", "m1rstd", "m1tmpn", "aT", "e1", "Bc", ("eb", 0), ("eb", 1), ("enb", 0), ("enb", 1), "qm", ("ktT", 0), ("ktT", 1), ("khT", 0), ("khT", 1), "cat", "th", "v_tm"]
        L2 = Lay(R2)
        svg = L2.get([128, 512], F32); svsq = L2.get([128, 512], F32); svn = L2.get([128, 2, 512], BF16)
        svo = svsq
        sc_bd = L2.get([128, 4, 128], BF16); khm = L2.get([128, 2, 256], BF16)
        TR2_OFF = L2.off
        rstd_o = L2.get([128, NB], F32); tmp_o = L2.get([128, NB], F32); osq = L2.get([128, 4, NB], BF16)
        sst = L2.get([128, 8], F32); srs = L2.get([128, 8], F32)
        S0 = [L2.get([128, 2, 128], F32) for _ in range(2)]
        S0b = [L2.get([128, 2, 128], BF16) for _ in range(2)]
        S0 = S0 + [carve(R3, 4096 + i_ * 1024, [128, 2, 128], F32) for i_ in range(2)]
        S0b = S0b + [carve(R3, 4096 + 2048 + i_ * 512, [128, 2, 128], BF16) for i_ in range(2)]
        p1keys += ["svg", "svsq", "svn", "sc_bd", "khm", "rstd_o", "tmp_o", ("osq", 0), ("osq", 1), ("osq", 2), ("osq", 3), "sst", "srs",
                   ("S0", 0), ("S0", 1), ("S0b", 0), ("S0b", 1)]
        S.fence(p1keys)
        G("memset", dict(ap=qm[:], constant=0.0), [], ["qm"])
        G("memset", dict(ap=khm[:], constant=0.0), [], ["khm"])

        rr = [0]

        def rot():
            i = rr[0] % 2
            rr[0] += 1
            return PS[i], "ps%d" % i

        def mixer_block(tok0, n, xkeys, sample, pre=False, nxt=None, prev_wout=None, xfill=()):
            ntl = (n + 127) // 128
            blk = slice(tok0, tok0 + n)
            S.ctx = "p1.norm"
            fm_norm(lambda c: xT[:, c, blk], n, G_MIX, lambda c: xn[:, c, 0:n], sqb, rstd, tmpn, PS[0], "ps0",
                    xkeys, ["xn"], "m1", do_stats=not pre)
            xfill = list(xfill)
            if prev_wout is not None:
                prev_wout()
            if xfill:
                xfill.pop(0)()
            rr[0] = 1

            def proj(col0, m):
                pb, pk = rot()
                for c in range(KC):
                    MM(pb[0:m, 0:n], w_in_b[:, c, col0:col0 + m], xn[:, c, 0:n], c == 0, c == KC - 1, ["w_in", ("xn", c)], [pk])
                return pb, pk
            S.ctx = "p1.gates"
            pb, pk = proj(CA, 16)
            act(aT[0:16, 0:n], pb[0:16, 0:n], AF.Copy, [pk], ["aT"])
            rmask = rm_s if sample else rm
            for c in range(2):
                pb, pk = rot()
                MM(pb[:, 0:n], walpha_b[:, c * 128:(c + 1) * 128], aT[0:16, 0:n], True, True, ["walpha_b", "aT"], [pk])
                eX, eXk = (e1, "e1") if c == 0 else (th, "th")
                act(eX[:, 0:n], pb[:, 0:n], AF.Exp, [pk, "nba"], [eXk], scale=-1.0, bias=nba[:, c:c + 1])
                V("tensor_scalar", dict(out=eX[:, 0:n], in0=eX[:, 0:n], scalar1=1.0, scalar2=None, op0=ALU.add), [eXk], [eXk])
                act(eX[:, 0:n], eX[:, 0:n], AF.Ln, [eXk], [eXk])
                V("tensor_tensor_scan", dict(out=Bc[:, 0:n], data0=rmask[:, 0:n], data1=eX[:, 0:n], initial=0.0,
                                                 op0=ALU.mult, op1=ALU.add), [eXk, "rm", "rm_s"], ["Bc"])
                act(eb[:, c, 0:n], Bc[:, 0:n], AF.Exp, ["Bc"], [("eb", c)], scale=-1.0 / 16)
                act(enb[:, c, 0:n], Bc[:, 0:n], AF.Exp, ["Bc"], [("enb", c)], scale=1.0 / 16)
            S.ctx = "p1.qk"
            def proj4(col0, m, j):
                bi_ = (2, 3, 0, 1)[j % 4]
                pb, pk = PS[bi_], "ps%d" % bi_
                for c in range(KC):
                    MM(pb[0:m, 0:n], w_in_b[:, c, col0:col0 + m], xn[:, c, 0:n], c == 0, c == KC - 1, ["w_in", ("xn", c)], [pk])
                return pb, pk
            for c in range(2):
                pb, pk = proj4(CQ + c * 128, 128, c)
                for h2 in range(2):
                    rs_ = slice(h2 * 64, (h2 + 1) * 64)
                    V("scalar_tensor_tensor", dict(
                        out=qm[rs_, 2 * c + h2, 0:n], in0=pb[rs_, 0:n], scalar=0.125, in1=eb[rs_, c, 0:n],
                        op0=ALU.mult, op1=ALU.mult), [pk, ("eb", c)], ["qm"])
            if xfill:
                S.ctx = "p1.x"
                xfill.pop(0)()
                S.ctx = "p1.qk"
            cl = 4 if sample else 64
            for c in range(2):
                pb, pk = proj4(CK + c * 128, 128, 2 + c)
                V("tensor_tensor", dict(out=ktT[:, c, 0:n], in0=pb[:, 0:n], in1=enb[:, c, 0:n], op=ALU.mult),
                  [pk, ("enb", c)], [("ktT", c)])
                G("tensor_tensor", dict(
                    out=khT[:, c, 0:n].rearrange("p (a b) -> p a b", b=cl),
                    in0=ktT[:, c, 0:n].rearrange("p (a b) -> p a b", b=cl),
                    in1=eb[:, c, cl - 1:n:cl].unsqueeze(2).broadcast_to([128, n // cl, cl]), op=ALU.mult),
                  [("ktT", c), ("eb", c)], [("khT", c)])
            S.ctx = "p1.ru"
            def ru_piece(j):
                S.ctx = "p1.ru"
                if j < 4:
                    pb, pk = proj(CR + j * 128, 128)
                    act(th[:, 0:n], pb[:, 0:n], AF.Tanh, [pk], ["th"], scale=0.5)
                    V("scalar_tensor_tensor", dict(out=cat[:, j, 0:n], in0=th[:, 0:n], scalar=1.0, in1=pb[:, 0:n],
                                                   op0=ALU.add, op1=ALU.mult), [pk, "th"], [("cat", j)])
                else:
                    pb, pk = proj(CU + (j - 4) * 128, 128)
                    act(cat[:, j, 0:n], pb[:, 0:n], AF.Gelu_apprx_tanh, [pk], [("cat", j)])
                S.ctx = "p1.gla"
            ru_q = list(range(8))
            S.ctx = "p1.tm"
            if sample:
                G("memset", dict(ap=v_tm[:], constant=0.0), [], ["v_tm"])
                G("memset", dict(ap=svn[:], constant=0.0), [], ["svn"])
                G("memset", dict(ap=sc_bd[:], constant=0.0), [], ["sc_bd"])
            for tl in range(ntl):
                rows = min(128, n - tl * 128)
                tsl = slice(tl * 128, tl * 128 + rows)
                for c in range(KC):
                    MM(PS[2][0:rows, :], xn[:, c, tsl], w_in_b[:, c, CV:CV + 512], c == 0, c == KC - 1, [("xn", c), "w_in"], ["ps2"])
                act(v_tm[0:rows, tl, :], PS[2][0:rows, :], AF.Copy, ["ps2"], ["v_tm"])
                for c in range(KC):
                    MM(PS[3][0:rows, :], xn[:, c, tsl], w_in_b[:, c, CSV:CSV + 512], c == 0, c == KC - 1, [("xn", c), "w_in"], ["ps3"])
                act(svg[0:rows, :], PS[3][0:rows, :], AF.Gelu_apprx_tanh, ["ps3"], ["svg"])
                V("tensor_tensor", dict(out=svsq[0:rows, :], in0=svg[0:rows, :], in1=svg[0:rows, :], op=ALU.mult),
                  ["svg"], ["svsq"])
                V("tensor_reduce", dict(out=sst[0:rows, 0:4], in_=svsq[0:rows, :].rearrange("p (h d) -> p h d", h=4),
                                                       axis=AX.X, op=ALU.add), ["svsq"], ["sst"])
                act(sst[0:rows, 4:8], sst[0:rows, 0:4], AF.Ln, ["sst"], ["sst"], scale=1.0 / 128, bias=EPS)
                V("tensor_scalar", dict(out=sst[0:rows, 4:8], in0=sst[0:rows, 4:8], scalar1=-0.5, scalar2=None, op0=ALU.mult),
                  ["sst"], ["sst"])
                act(srs[0:rows, 0:4], sst[0:rows, 4:8], AF.Exp, ["sst"], ["srs"])
                for h in range(4):
                    hs = slice(h * 128, (h + 1) * 128)
                    dst = svo[0:rows, hs] if sample else svn[0:rows, tl, hs]
                    V("scalar_tensor_tensor", dict(
                        out=dst, in0=svg[0:rows, hs], scalar=srs[0:rows, h:h + 1], in1=gsgu_bc[0:rows, hs],
                        op0=ALU.mult, op1=ALU.mult), ["svg", "srs", "gsgu_bc"], ["svsq" if sample else "svn"])
                if sample:
                    V("tensor_copy", dict(out=svn[0:rows, 0, :], in_=svo[0:rows, :]), ["svsq", "svn"], ["svn"])
                    ld("sync", sv_s, svo[0:rows, :], "o_svs", reads=["svsq"])
            if nxt is not None:
                S.ctx = "p1.norm"
                ntok0, nn_, nxk = nxt
                nblk = slice(ntok0, ntok0 + nn_)
                fm_squares(lambda c: xT[:, c, nblk], nn_, sqb, nxk, "m1")
            S.ctx = "p1.gla"
            for tl in range(ntl):
                rows = min(128, n - tl * 128)
                tsl = slice(tl * 128, tl * 128 + rows)
                for h in range(4):
                    MM(PS[4][0:rows, h * 128:h * 128 + rows], ktT[:, h // 2, tsl], qm[:, h, tsl], True, True,
                       [("ktT", h // 2), "qm"], ["ps4"], sig=(h == 3))
                msk = mask_s[:] if sample else mask128[:]
                V("tensor_tensor", dict(
                    out=sc_bd[0:rows, :, 0:rows], in0=PS[4][0:rows, :].rearrange("p (h i) -> p h i", h=4)[:, :, 0:rows],
                    in1=msk.unsqueeze(1).broadcast_to([rows, 4, rows]), op=ALU.mult), ["ps4", "mask128", "mask_s"], ["sc_bd"])
                p5b = PS[5][:].bitcast(BF16)
                for c in range(2):
                    TR(p5b[0:rows, c * 128:(c + 1) * 128], khT[:, c, tsl], ident_b[:], [("khT", c), "ident_b"], ["ps5"], sig=(c == 1))
                if not sample:
                    for p in range(2):
                        prs = slice(p * 64, (p + 1) * 64)
                        act(khm[prs, p, :], p5b[prs, 0:256], AF.Copy, ["ps5"], ["khm"])
                else:
                    act(tmp_o[0:64, 0:128].bitcast(BF16), p5b[0:64, 0:256], AF.Copy, ["ps5"], ["tmp_o"])
                groups = [(p, slice(p * 64, (p + 1) * 64), None) for p in range(rows // 64)] if not sample else \
                    [(b, slice(b * 4, (b + 1) * 4), b) for b in range(16)]
                for gi, (p, csl, bidx) in enumerate(groups):
                    if sample:
                        s = bidx % 4
                        ld("sync", S0[s][:], sgla[bidx].rearrange("(c h2) d v -> (h2 d) c v", c=2), "S0_%d" % s,
                           writes=[("S0", s)] + (["m1sqb"] if s >= 2 else []))
                        act(S0b[s][:], S0[s][:], AF.Copy, [("S0", s)], [("S0b", s)] + (["m1sqb"] if s >= 2 else []))
                        V("tensor_scalar", dict(out=khm[0:64, bidx % 2, :], in0=tmp_o[0:64, 0:128].bitcast(BF16),
                                                                    scalar1=bm[0:64, bidx:bidx + 1], scalar2=None, op0=ALU.mult),
                          ["tmp_o", "bm"], ["khm"])
                        Sb_cur, Sb_key = S0b[s], ("S0b", s)
                        khm_cur = khm[:, bidx % 2, :]
                    else:
                        rslot, wslot = gch[0] % 2, (gch[0] + 1) % 2
                        gch[0] += 1
                        Sb_cur, Sb_key = S_bfs[rslot], ("S_bf", rslot)
                        khm_cur = khm[:, p, :]
                    ncol = csl.stop - csl.start
                    gsl = slice(tl * 128 + csl.start, tl * 128 + csl.stop)
                    ubank = (2 + rslot) if not sample else (4 + (gi % 2))
                    pu, puk = PS[ubank], "ps%d" % ubank

                    def u_mms():
                        for h in range(4):
                            c = h // 2
                            MM(pu[:, h * 128:(h + 1) * 128], khm_cur[:, c * 128:(c + 1) * 128], v_tm[:, tl, h * 128:(h + 1) * 128],
                               True, True, ["khm", "v_tm"], [puk], sig=(h == 3))

                    def o_mms():
                        for h in range(4):
                            c = h // 2
                            pob = PS[6 + h // 2]
                            ocol = (h % 2) * NB + tl * 128
                            outap = pob[:, ocol + csl.start:ocol + csl.stop]
                            MM(outap, Sb_cur[:, c, :], qm[:, h, gsl], True, False, [Sb_key, "qm"], ["ps%d" % (6 + h // 2)], sig=False)
                            MM(outap, v_tm[:, tl, h * 128:(h + 1) * 128], sc_bd[:, h, csl], False, True, ["v_tm", "sc_bd"],
                               ["ps%d" % (6 + h // 2)], sig=(h % 2 == 1))
                    if sample or gi > 0:
                        u_mms()
                        o_mms()
                    else:
                        o_mms()
                        u_mms()
                    ecol = tl * 128 + csl.stop - 1
                    for h in range(4):
                        c, h2 = h // 2, h % 2
                        rs_ = slice(h2 * 64, (h2 + 1) * 64)
                        if sample:
                            V("scalar_tensor_tensor", dict(
                                out=S0[s][rs_, c, :], in0=S0[s][rs_, c, :], scalar=eb[rs_, c, ecol:ecol + 1],
                                in1=pu[rs_, h * 128:(h + 1) * 128], op0=ALU.mult, op1=ALU.add), [puk, ("eb", c), ("S0", s)], [("S0", s)])
                        else:
                            V("scalar_tensor_tensor", dict(
                                out=S_bfs[wslot][rs_, c, :], in0=Sst[rs_, c, :], scalar=eb[rs_, c, ecol:ecol + 1],
                                in1=pu[rs_, h * 128:(h + 1) * 128], op0=ALU.mult, op1=ALU.add), [puk, ("eb", c), "Sst"], [("S_bf", wslot)])
                    if not sample:
                        for h in range(4):
                            c, h2 = h // 2, h % 2
                            rs_ = slice(h2 * 64, (h2 + 1) * 64)
                            V("scalar_tensor_tensor", dict(
                                out=Sst[rs_, c, :], in0=Sst[rs_, c, :], scalar=eb[rs_, c, ecol:ecol + 1],
                                in1=pu[rs_, h * 128:(h + 1) * 128], op0=ALU.mult, op1=ALU.add), [puk, ("eb", c), "Sst"], ["Sst"])
                    if sample:
                        ld("sync", sg_s[bidx].rearrange("(c h2) d v -> (h2 d) c v", c=2), S0[s][:], "o_sgs", reads=[("S0", s)])
                        if ru_q:
                            ru_piece(ru_q.pop(0))
                    else:
                        for _ in range(2):
                            if ru_q:
                                ru_piece(ru_q.pop(0))
            S.ctx = "p1.onorm"
            while ru_q:
                ru_piece(ru_q.pop(0))
            if nxt is not None:
                S.ctx = "p1.norm"
                fm_stats(lambda c: xT[:, c, nblk], nn_, sqb, rstd, tmpn, PS[0], "ps0", nxk, "m1", squares=False)
            S.ctx = "p1.onorm"
            for h in range(4):
                pob, pok = PS[6 + h // 2], "ps%d" % (6 + h // 2)
                osl = slice((h % 2) * NB, (h % 2) * NB + n)
                act(osq[:, h, 0:n], pob[:, osl], AF.Square, [pok], [("osq", h)])
            for h in range(4):
                pb, pk = rot()
                for tl in range(ntl):
                    rows = min(128, n - tl * 128)
                    zo = pb[:, tl * 128:tl * 128 + rows]
                    if sample:
                        MM(zo, svn[0:64, 0, h * 128:(h + 1) * 128], Wm_s[:, h, :], True, False, ["svn", "Wm_s"], [pk], sig=False)
                        bh = bs_hi[0:1, h * 128:h * 128 + 4].unsqueeze(1).broadcast_to([1, 16, 4])
                        bl = bs_lo[0:1, h * 128:h * 128 + 4].unsqueeze(1).broadcast_to([1, 16, 4])
                    else:
                        MM(zo, svn[:, tl, h * 128:(h + 1) * 128], Wm[:, h, :], True, False, ["svn", "Wm"], [pk], sig=False)
                        bh = bs_hi[0:1, h * 128:(h + 1) * 128]
                        bl = bs_lo[0:1, h * 128:(h + 1) * 128]
                    MM(zo, ones_b[0:1, :], bh, False, False, ["ones_b", "bs_hi"], [pk], sig=False)
                    MM(zo, ones_b[0:1, :], bl, False, True, ["ones_b", "bs_lo"], [pk], sig=True)
                V("tensor_tensor", dict(out=cat[:, 4 + h, 0:n], in0=pb[:, 0:n], in1=cat[:, 4 + h, 0:n], op=ALU.mult),
                  [pk, ("cat", 4 + h)], [("cat", 4 + h)])
            S.ctx = "p1.wout"
            S.ctx = "p1.onorm"
            tr2 = carve(R2, TR2_OFF, [128, 2, NB], F32)
            for hp in range(2):
                pob, pok = PS[6 + hp], "ps%d" % (6 + hp)
                pb, pk = rot()
                for h2_ in range(2):
                    h = 2 * hp + h2_
                    MM(pb[:, h2_ * NB:h2_ * NB + n], ones_b[:], osq[:, h, 0:n], True, True, ["ones_b", ("osq", h)], [pk], sig=(h2_ == 1))
                pv_ = pb[:].rearrange("p (a b) -> p a b", a=2)[:, :, 0:n]
                act(tr2[:, :, 0:n], pv_, AF.Ln, [pk], ["tmp_o", "rstd_o"], scale=1.0 / 128, bias=EPS)
                V("tensor_scalar", dict(out=tr2[:, :, 0:n], in0=tr2[:, :, 0:n], scalar1=-0.5, scalar2=None, op0=ALU.mult),
                  ["tmp_o", "rstd_o"], ["tmp_o", "rstd_o"])
                act(tr2[:, :, 0:n], tr2[:, :, 0:n], AF.Exp, ["tmp_o", "rstd_o"], ["tmp_o", "rstd_o"])
                for h2_ in range(2):
                    h = 2 * hp + h2_
                    osl = slice(h2_ * NB, h2_ * NB + n)
                    V("scalar_tensor_tensor", dict(out=tr2[:, h2_, 0:n], in0=pob[:, osl], scalar=gg[:, h:h + 1],
                                                   in1=tr2[:, h2_, 0:n], op0=ALU.mult, op1=ALU.mult),
                      [pok, "gg", "tmp_o", "rstd_o"], ["tmp_o", "rstd_o"])
                for h2_ in range(2):
                    h = 2 * hp + h2_
                    V("tensor_tensor", dict(out=cat[:, h, 0:n], in0=tr2[:, h2_, 0:n], in1=cat[:, h, 0:n], op=ALU.mult),
                      ["tmp_o", "rstd_o", ("cat", h)], [("cat", h)])
            catk = [("cat", j) for j in range(8)]
            def wout_part():
                S.ctx = "p1.wout"
                for oc in range(8):
                    bi_ = (2, 3, 0, 1)[oc % 4]
                    pb, pk = PS[bi_], "ps%d" % bi_
                    for c in range(KC):
                        MM(pb[:, 0:n], w_out_b[:, c, oc * 128:(oc + 1) * 128], cat[:, c, 0:n], c == 0, c == KC - 1, ["w_out", ("cat", c)], [pk])
                    V("tensor_tensor", dict(out=xT[:, oc, blk], in0=pb[:, 0:n], in1=xT[:, oc, blk], op=ALU.add),
                      [pk] + list(xkeys), list(xkeys))
            return wout_part

        mblocks = [(b * NB, NB, [("xT", b)]) for b in range(TP // NB)] + [(TP, TS, [("xT", 8)])]
        for i_, (tk, nn, xk) in enumerate(mblocks):
            smp = (i_ == len(mblocks) - 1)
            if smp:
                ld("sync", sg_p.rearrange("(c h2) d v -> (h2 d) c v", c=2), Sst[:], "o_sgp", reads=["Sst"])
            if i_ <= 6:
                tl_ = [2 * i_ + 2, 2 * i_ + 3]
            elif i_ == 7:
                tl_ = [16]
            else:
                tl_ = []
            xf_ = [(lambda t=t: xtile(t, xs_slots[t % 2], ("xs", t % 2), "xs%d" % (t % 2), (0, 1), extra_w=["wv"])) for t in tl_]
            pw_ = mixer_block(tk, nn, xk, smp, pre=(i_ > 0), nxt=(mblocks[i_ + 1] if i_ + 1 < len(mblocks) else None),
                              prev_wout=(pw_ if i_ > 0 else None), xfill=xf_)
            if i_ == 7:
                load_w(wq_b, wq, "wq", 1024, writes=["wv", ("xs", 0), ("xs", 1)])
        pw_()

        if DEBUG:
            ld("sync", dbg1, xT[:], "o_dbg1", reads=[("xT", i) for i in range(9)])
        S.fence(["wo"])
        load_w(wo_b, wo, "wo", 1024)
        L = Lay(R3)
        hn = L.get([128, 8, NB], BF16); sqb2 = L.get([128, 8, NB], BF16)
        rstd2 = L.get([128, NB], F32); tmpn2 = L.get([128, NB], F32)
        qT = L.get([128, 8, NB], BF16)
        rinv = L.get([128, 256], F32); pn = L.get([128, 4, 256], BF16)
        pT = L.get([128, 4, 2, NB], BF16); oT = L.get([128, 8, NB], BF16)
        qTm = [L.get([128, 8, TS], BF16) for _ in range(2)]
        KTs = L.get([128, 8, 256], BF16)
        qT_s = L.get([128, 8, TS], BF16); pT_s = L.get([128, 4, 2, TS], BF16)
        kslot = [carve(R1, 49152 + i * 4096, [128, 2, 1024], BF16) for i in range(2)]
        p2keys = [("hn", c_) for c_ in range(KC)] + ["m2sqb", "m2rstd", "m2tmpn"] + [("qT", o_) for o_ in range(8)] + [("pn", 0), ("pn", 1), ("pn", 2), ("pn", 3), "pT", "oT", "qT_s", "pT_s", "rinv", ("KTs", 0), ("KTs", 1)] + [("sm8", h) for h in range(4)] + [("sm12", h) for h in range(4)] + [("smx", 0), ("smx", 1), ("qTm", 0), ("qTm", 1), "KTs",
                  ("kslot", 0), ("kslot", 1)]
        S.fence(p2keys)
        def slot_views(si):
            base = si * 24576
            return (carve(R1, base, [128, 8, 512], BF16), carve(R1, base + 8192, [128, 8, 512], BF16),
                    carve(R1, base + 16384, [128, 4, 1024], BF16))
        S.fence([("fslot", 0), ("fslot", 1)])

        def unit_pieces(u):
            f0, nf = UNITS[u]
            si = u % 2
            wg_v, wu_v, wd_v = slot_views(si)
            key = "fs%d" % si
            gsrc = w_gate.rearrange("(c p) n -> p c n", p=128)
            usrc = w_up.rearrange("(c p) n -> p c n", p=128)
            pcs = []
            pcs.append(lambda: ld("gpsimd", wg_v[:, :, 0:nf * 128], gsrc[:, :, f0 * 128:(f0 + nf) * 128], key, writes=[("fslot", si)]))
            pcs.append(lambda: ld("gpsimd", wu_v[:, :, 0:nf * 128], usrc[:, :, f0 * 128:(f0 + nf) * 128], key, writes=[("fslot", si)]))
            pcs.append(lambda: ld("gpsimd", wd_v[:, 0:nf, :],
                                  w_down[f0 * 128:(f0 + nf) * 128, :].rearrange("(l p) n -> p l n", p=128), key, writes=[("fslot", si)]))
            return pcs

        def load_unit(u):
            for p_ in unit_pieces(u):
                p_()

        def softmax_tile(rows, psA, psB, pkeys):
            for hh, pb in enumerate([psA, psB]):
                V("tensor_reduce", dict(out=smx[0:rows, 2 * hh:2 * hh + 2], in_=pb[0:rows, :].rearrange("p (h m) -> p h m", h=2),
                                        axis=AX.X, op=ALU.max, negate=True), [pkeys[hh]], [("smx", hh)])
            for h in range(4):
                pb = [psA, psB][h // 2]
                act(pn[0:rows, h, :], pb[0:rows, (h % 2) * 256:(h % 2 + 1) * 256], AF.Exp, [pkeys[h // 2], ("smx", h // 2)],
                    [("pn", h)], bias=smx[0:rows, h:h + 1])

        def attn_block(tok0, n, xkeys, prev_wo, sfill, pre=False, nxt=None):
            blk = slice(tok0, tok0 + n)
            sfill = list(sfill)

            def fill():
                if sfill:
                    sfill.pop(0)()
                if wpieces:
                    wpieces.pop(0)()
            S.ctx = "p2.norm"
            fm_norm(lambda c: xT[:, c, blk], n, G_X, lambda c: hn[:, c, 0:n], sqb2, rstd2, tmpn2, PS[0], "ps0",
                    xkeys, ["hn"], "m2", do_stats=not pre)
            rr[0] = 1
            S.ctx = "p2.q"
            for oc in range(8):
                bi_ = (0, 1, 2, 3)[oc % 4]
                pb, pk = PS[bi_], "ps%d" % bi_
                for c in range(KC):
                    MM(pb[:, 0:n], wq_b[:, c, oc * 128:(oc + 1) * 128], hn[:, c, 0:n], c == 0, c == KC - 1, ["wq", ("hn", c)], [pk])
                act(qT[:, oc, 0:n], pb[:, 0:n], AF.Copy, [pk], [("qT", oc)], scale=1.0 / 16)
            fill()
            ntl = n // 128

            def scores(tl):
                S.ctx = "p2.sc"
                tsl = slice(tl * 128, (tl + 1) * 128)
                for h in range(4):
                    pb, pk = PS[2 + h // 2], "ps%d" % (2 + h // 2)
                    for dc in range(2):
                        MM(pb[:, (h % 2) * 256:(h % 2 + 1) * 256], qT[:, 2 * h + dc, tsl], KT[:, 2 * h + dc, :], dc == 0, dc == 1,
                           [("qT", 2 * h + dc), "KT"], [pk])

            def transposes(tl):
                S.ctx = "p2.pT"
                tsl = slice(tl * 128, (tl + 1) * 128)
                p4b = PS[4][:].bitcast(BF16)
                for h in range(4):
                    for mc in range(2):
                        TR(p4b[:, (h * 2 + mc) * 128:(h * 2 + mc + 1) * 128], pn[:, h, mc * 128:(mc + 1) * 128], ident_b[:],
                           [("pn", h), "ident_b"], ["ps4"], sig=(mc == 1))
                V("tensor_copy", dict(out=pT[:, :, :, tsl], in_=p4b.rearrange("p (h m t) -> p h m t", h=4, m=2)), ["ps4"], ["pT"])
            scores(0)
            for tl in range(ntl):
                S.ctx = "p2.smax"
                softmax_tile(128, PS[2], PS[3], ["ps2", "ps3"])
                if tl + 1 < ntl:
                    scores(tl + 1)
                if tl == 0 and prev_wo is not None:
                    prev_wo()
                transposes(tl)
                fill()
            if nxt is not None:
                S.ctx = "p2.norm"
                ntok0, nn_, nxk = nxt
                nblk = slice(ntok0, ntok0 + nn_)
                fm_squares(lambda c: xT[:, c, nblk], nn_, sqb2, nxk, "m2")
            S.ctx = "p2.pv"
            pvb = [0, 1, 4]
            pvi = [0]

            def rot3():
                i_ = pvb[pvi[0] % 3]
                pvi[0] += 1
                return PS[i_], "ps%d" % i_
            for h in range(4):
                pbs, pks = rot3()
                for mc in range(2):
                    MM(pbs[:, 0:n], ones_b[:], pT[:, h, mc, 0:n], mc == 0, mc == 1, ["ones_b", "pT"], [pks])
                V("reciprocal", dict(out=rinv[:, 0:n], in_=pbs[:, 0:n]), [pks], ["rinv"])
                for dc in range(2):
                    oc = 2 * h + dc
                    pb, pk = rot3()
                    for mc in range(2):
                        MM(pb[:, 0:n], vbf[:, mc, oc * 128:(oc + 1) * 128], pT[:, h, mc, 0:n], mc == 0, mc == 1,
                           ["vbf", "pT"], [pk], sig=(mc == 1))
                    V("tensor_tensor", dict(out=oT[:, oc, 0:n], in0=pb[:, 0:n], in1=rinv[:, 0:n], op=ALU.mult),
                      [pk, "rinv"], ["oT"])
            if nxt is not None:
                S.ctx = "p2.norm"
                fm_stats(lambda c: xT[:, c, nblk], nn_, sqb2, rstd2, tmpn2, PS[0], "ps0", nxk, "m2", squares=False)
            fill()
            for fl in sfill:
                fl()

            def wo_part():
                S.ctx = "p2.wo"
                for oc in range(8):
                    pb, pk = rot()
                    for c in range(KC):
                        MM(pb[:, 0:n], wo_b[:, c, oc * 128:(oc + 1) * 128], oT[:, c, 0:n], c == 0, c == KC - 1, ["wo", "oT"], [pk])
                    V("tensor_tensor", dict(out=xT[:, oc, blk], in0=pb[:, 0:n], in1=xT[:, oc, blk], op=ALU.add),
                      [pk] + list(xkeys), list(xkeys))
            return wo_part

        sblk = slice(TP, TP + TS)
        skeys = [("xT", 8)]

        def s_init():
            S.ctx = "p2.s_init"
            fm_norm(lambda c: xT[:, c, sblk], TS, G_X, lambda c: hn[:, c, 0:TS], sqb2, rstd2, tmpn2, PS[0], "ps0",
                    skeys, ["hn"], "m2")
            rr[0] = 1
            for oc in range(8):
                pb, pk = rot()
                for c in range(KC):
                    MM(pb[:, 0:TS], wq_b[:, c, oc * 128:(oc + 1) * 128], hn[:, c, 0:TS], c == 0, c == KC - 1, ["wq", ("hn", c)], [pk])
                act(qT_s[:, oc, :], pb[:, 0:TS], AF.Copy, [pk], ["qT_s"], scale=1.0 / 16)
            G("memset", dict(ap=qTm[0][:], constant=0.0), [], [("qTm", 0)])
            G("memset", dict(ap=qTm[1][:], constant=0.0), [], [("qTm", 1)])

        def s_kpass(b):
            S.ctx = "p2.s_k"
            s = b % 2
            ld("gpsimd", kslot[s][:], ck[b].rearrange("(mt p) d -> p mt d", p=128), "ks%d" % s, writes=[("kslot", s)])
            if b >= 2:
                V("memset", dict(ap=qTm[s][:, :, (b - 2) * 4:(b - 2) * 4 + 4], constant=0.0), [], [("qTm", s)])
            V("tensor_copy", dict(out=qTm[s][:, :, b * 4:b * 4 + 4], in_=qT_s[:, :, b * 4:b * 4 + 4]), ["qT_s"], [("qTm", s)])
            for half in range(2):
                p45 = PS[4 + half][:].bitcast(BF16)
                for oc4 in range(4):
                    oc = half * 4 + oc4
                    for mt in range(2):
                        TR(p45[:, oc4 * 256 + mt * 128:oc4 * 256 + (mt + 1) * 128], kslot[s][:, mt, oc * 128:(oc + 1) * 128],
                           ident_b[:], [("kslot", s), "ident_b"], ["ps%d" % (4 + half)], sig=(oc4 == 3 and mt == 1))
                src = p45.rearrange("p (o m) -> p o m", o=4)
                if half == 0:
                    V("tensor_copy", dict(out=KTs[:, 0:4, :], in_=src), ["ps4"], [("KTs", 0)])
                else:
                    act(KTs[:, 4:8, :], src, AF.Copy, ["ps5"], [("KTs", 1)])
            for h in range(4):
                pb, pk = PS[6 + h // 2], "ps%d" % (6 + h // 2)
                for dc in range(2):
                    first = (b == 0 and h % 2 == 0 and dc == 0)
                    last = (b == 15 and dc == 1)
                    MM(pb[0:64, (h % 2) * 256:(h % 2 + 1) * 256], qTm[s][:, 2 * h + dc, :], KTs[:, 2 * h + dc, :], first, last,
                       [("qTm", s), ("KTs", h // 2)], [pk], sig=(dc == 1))

        def s_mid():
            S.ctx = "p2.s_mid"
            softmax_tile(64, PS[6], PS[7], ["ps6", "ps7"])
            p4b = PS[4][:].bitcast(BF16)
            for h in range(4):
                for mc in range(2):
                    TR(p4b[:, (h * 2 + mc) * 64:(h * 2 + mc + 1) * 64], pn[0:64, h, mc * 128:(mc + 1) * 128], ident_b[0:64, 0:64],
                       [("pn", h), "ident_b"], ["ps4"], sig=(mc == 1))
            V("tensor_copy", dict(out=pT_s[:], in_=p4b[:, 0:512].rearrange("p (h m t) -> p h m t", h=4, m=2)), ["ps4"], ["pT_s"])

        def s_vpass(b):
            S.ctx = "p2.s_v"
            s = b % 2
            ld("gpsimd", kslot[s][:], cv[b].rearrange("(mt p) d -> p mt d", p=128), "ks%d" % s, writes=[("kslot", s)])
            for oc in range(8):
                for mc in range(2):
                    MM(PS[5][:, oc * 64 + b * 4:oc * 64 + b * 4 + 4], kslot[s][:, mc, oc * 128:(oc + 1) * 128],
                       pT_s[:, oc // 2, mc, b * 4:b * 4 + 4], mc == 0, mc == 1, [("kslot", s), "pT_s"], ["ps5"],
                       sig=(oc == 7 and mc == 1))

        def s_fin():
            S.ctx = "p2.s_fin"
            pbs, pks = rot()
            for h in range(4):
                for mc in range(2):
                    MM(pbs[:, h * TS:(h + 1) * TS], ones_b[:], pT_s[:, h, mc, :], mc == 0, mc == 1, ["ones_b", "pT_s"], [pks], sig=(mc == 1))
            V("reciprocal", dict(out=rinv[:, 0:4 * TS], in_=pbs[:, 0:4 * TS]), [pks], ["rinv"])
            V("tensor_tensor", dict(out=oT[:, :, 0:TS].rearrange("p (h d) t -> p h d t", h=4),
                                    in0=PS[5][:].rearrange("p (h d t) -> p h d t", h=4, d=2),
                                    in1=rinv[:, 0:4 * TS].rearrange("p (h t) -> p h t", h=4).unsqueeze(2).broadcast_to([128, 4, 2, TS]),
                                    op=ALU.mult), ["ps5", "rinv"], ["oT"])
            for oc in range(8):
                pb, pk = rot()
                for c in range(KC):
                    MM(pb[:, 0:TS], wo_b[:, c, oc * 128:(oc + 1) * 128], oT[:, c, 0:TS], c == 0, c == KC - 1, ["wo", "oT"], [pk])
                V("tensor_tensor", dict(out=xT[:, oc, sblk], in0=pb[:, 0:TS], in1=xT[:, oc, sblk], op=ALU.add),
                  [pk] + skeys, skeys)

        s_init()
        prev_wo = None
        wpieces = unit_pieces(0) + unit_pieces(1)
        for b in range(TP // NB):
            if b < 4:
                sf = [(lambda bb=bb: s_kpass(bb)) for bb in range(4 * b, 4 * b + 4)]
            else:
                sf = [(lambda bb=bb: s_vpass(bb)) for bb in range(4 * (b - 4), 4 * (b - 4) + 4)]
            nxt_ = ((b + 1) * NB, NB, [("xT", b + 1)]) if b + 1 < TP // NB else None
            prev_wo = attn_block(b * NB, NB, [("xT", b)], prev_wo, sf, pre=(b > 0), nxt=nxt_)
            if b == 3:
                s_mid()
        prev_wo()
        for p_ in wpieces:
            p_()
        s_fin()

        if DEBUG:
            ld("sync", dbg2, xT[:], "o_dbg2", reads=[("xT", i) for i in range(9)])
        hnA = carve(R3, 0, [128, 8, T], BF16)
        L = Lay(R2)
        sqb3 = L.get([128, 8, 256], BF16)
        rstd3 = L.get([128, 256], F32); tmpn3 = L.get([128, 256], F32)
        gsbs = [L.get([128, NB3 + 2], F32) for _ in range(2)]
        cbs = [L.get([128, NB3], F32) for _ in range(2)]
        actbs = [L.get([128, 4, NB3], BF16) for _ in range(2)]
        sconvT = L.get([128, NF, 32], F32)
        gkeep = sconvT
        gsss = [L.get([128, 16, 6], F32) for _ in range(2)]
        ubs = [L.get([128, NB3], BF16) for _ in range(2)]
        stgs = [carve(R1, 49152 + i_ * 4096, [128, 1024], F32) for i_ in range(2)]
        stg3 = stgs[0]
        gfin_bc = L.get([128, 1024], F32)
        fst = L.get([128, 16], F32)
        p3keys = [(("hnA", i_), c_) for i_ in range(5) for c_ in range(KC)] + [("gsbh", 0), ("gsbh", 1), ("gssh", 0), ("gssh", 1), "m3sqb", "m3rstd", "m3tmpn", ("gsb", 0), ("gsb", 1), ("cb", 0), ("cb", 1), "sconvT", ("gss", 0), ("gss", 1), ("ub", 0), ("ub", 1), "stg3", ("stg", 1), "gfin_bc", ("fst", 0), ("fst", 1)] + \
                 [("actb", a_, l_) for a_ in range(2) for l_ in range(4)]
        S.fence(p3keys)
        ld("sync", gfin_bc[:], g_fin.broadcast_to([128, D]), "gfin", writes=["gfin_bc"])
        for half in range(3):
            c0, c1 = half * 1024, min(DFF, (half + 1) * 1024)
            ld("sync", stg3[0:32, 0:c1 - c0], sconv[:, c0:c1], "stg3", writes=["stg3"])
            nfh = (c1 - c0) // 128
            for j in range(nfh):
                TR(PS[7][:, j * 32:(j + 1) * 32], stg3[0:32, j * 128:(j + 1) * 128], ident_f[0:32, 0:32], ["stg3", "ident"], ["ps7"],
                   sig=(j == nfh - 1))
            V("tensor_copy", dict(out=sconvT[:, half * 8:half * 8 + nfh, :],
                                  in_=PS[7][:, 0:nfh * 32].rearrange("p (f t) -> p f t", t=32)), ["ps7"], ["sconvT"])
        blocks3 = [(b * NB3, NB3, [("xT", 2 * b), ("xT", 2 * b + 1)], False) for b in range(TP // NB3)] + [(TP, TS, [("xT", 8)], True)]
        gcnt = [0]

        def ffn_front(u, bi, ab, fillers=()):
            S.ctx = "p3.front"
            tails = {}
            f0, nf = UNITS[u]
            si = u % 2
            wg_v, wu_v, wd_v = slot_views(si)
            fk = ("fslot", si)
            tok0, n, xkeys, sample = blocks3[bi]
            blk = slice(tok0, tok0 + n)
            if u == 0:
                for s0 in range(0, n, 256):
                    nn = min(256, n - s0)
                    sub = slice(tok0 + s0, tok0 + s0 + nn)
                    fm_norm(lambda c: xT[:, c, sub], nn, G_FFN, lambda c: hnA[:, c, sub], sqb3, rstd3, tmpn3, PS[0], "ps0",
                            xkeys, [("hnA", bi)], "m3")
            hk = ("hnA", bi)
            for lf in range(nf):
                f = f0 + lf
                gi = gcnt[0] % 2
                gcnt[0] += 1
                gsb, cb = gsbs[gi], cbs[gi]
                gk, ck_ = ("gsb", gi), ("cb", gi)
                pg, pgk = PS[1 + (lf % 2)], "ps%d" % (1 + lf % 2)
                pub_ = 3 if u == len(UNITS) - 1 else 3 + (lf % 2)
                pu, puk = PS[pub_], "ps%d" % pub_
                for c in range(KC):
                    MM(pg[:, 0:n], wg_v[:, c, lf * 128:(lf + 1) * 128], hnA[:, c, blk], c == 0, c == KC - 1, [fk, (hk, c)], [pgk])
                for c in range(KC):
                    MM(pu[:, 0:n], wu_v[:, c, lf * 128:(lf + 1) * 128], hnA[:, c, blk], c == 0, c == KC - 1, [fk, (hk, c)], [puk])
                cw = lambda j, f=f: colv2[:, j * NF + f:j * NF + f + 1]
                cbias = colv1[:, 46 + f:47 + f]
                ghk = ("gsbh", gi)
                if not sample:
                    V("tensor_copy", dict(out=gsb[:, 0:2], in_=ghist[:, f, :]), ["ghist"], [ghk])
                    act(gsb[:, 2:2 + n], pg[:, 0:n], AF.Copy, [pgk], [gk])
                    V("tensor_copy", dict(out=ghist[:, f, :], in_=gsb[:, n:n + 2]), [gk], ["ghist"])
                    g0, g1, g2 = gsb[:, 0:n], gsb[:, 1:1 + n], gsb[:, 2:2 + n]
                    cbo = cb[:, 0:n]
                else:
                    gk = ("gss", gi)
                    ghk = ("gssh", gi)
                    gss = gsss[gi]
                    V("tensor_copy", dict(out=gss[:, :, 0:2], in_=sconvT[:, f, :].rearrange("p (b j) -> p b j", j=2)),
                      ["sconvT"], [ghk])
                    act(gss[:, :, 2:6], pg[:, 0:n].rearrange("p (b t) -> p b t", t=4), AF.Copy, [pgk], [gk])
                    V("tensor_copy", dict(out=gkeep[:, f, :].rearrange("p (b j) -> p b j", j=2), in_=gss[:, :, 4:6]),
                      [gk], ["sconvT"])
                    g0, g1, g2 = gss[:, :, 0:4], gss[:, :, 1:5], gss[:, :, 2:6]
                    cbo = cb[:, 0:n].rearrange("p (b t) -> p b t", t=4)
                ub, ubk = ubs[gi], ("ub", gi)
                act(ub[:, 0:n], pu[:, 0:n], AF.Copy, [puk], [ubk])

                def ident(cbo=cbo, g2=g2, gk=gk, ck_=ck_, cw=cw, cbias=cbias):
                    act(cbo, g2, AF.Identity, [gk, "colv1", "colv2"], [ck_], scale=cw(2), bias=cbias)

                S.ctx = "p3.front"

                def tail(cbo=cbo, g1=g1, g0=g0, cw=cw, gk=gk, ghk=ghk, ck_=ck_, cb=cb, ub=ub, ubk=ubk, lf=lf):
                    S.ctx = "p3.chain"
                    V("scalar_tensor_tensor", dict(out=cbo, in0=g1, scalar=cw(1), in1=cbo, op0=ALU.mult, op1=ALU.add), [gk, ghk, ck_], [ck_])
                    V("scalar_tensor_tensor", dict(out=cbo, in0=g0, scalar=cw(0), in1=cbo, op0=ALU.mult, op1=ALU.add), [gk, ghk, ck_], [ck_])
                    act(cb[:, 0:n], cb[:, 0:n], AF.Gelu_apprx_tanh, [ck_], [ck_])
                    V("tensor_tensor", dict(out=actbs[ab][:, lf, 0:n], in0=cb[:, 0:n], in1=ub[:, 0:n], op=ALU.mult),
                      [ck_, ubk], [("actb", ab, lf)])
                tails[lf] = tail
                if lf >= 1:
                    tails[lf - 1]()
                ident()
                if fillers:
                    fillers.pop(0)()
                S.ctx = "p3.front"
            return tails[nf - 1]

        fcnt = [0]

        def final_out(bi):
            tok0, n, xkeys, sample = blocks3[bi]
            tiles = list(range(0, n, 128))
            As, Bs = [], []
            for s0 in tiles:
                par = fcnt[0] % 2
                fcnt[0] += 1
                As.append(lambda s0=s0, par=par: final_A(tok0, n, xkeys, s0, par))
                Bs.append([(lambda s0=s0, par=par, half=half: final_B(tok0, n, xkeys, sample, s0, par, half)) for half in range(2)])
            seq = []
            nt = len(tiles)
            seq.append(As[0])
            if nt > 1:
                seq.append(As[1])
            for i_ in range(nt):
                seq += Bs[i_]
                if i_ + 2 < nt:
                    seq.append(As[i_ + 2])
            return seq

        def final_A(tok0, n, xkeys, s0, par):
            S.ctx = "p3.final"
            rows = min(128, n - s0)
            tsl = slice(tok0 + s0, tok0 + s0 + rows)
            fo = par * 8
            act(sqb3[:, :, 0:rows], xT[:, :, tsl], AF.Square, list(xkeys), ["m3sqb"])
            for c in range(KC):
                MM(PS[4][0:rows, 0:1], sqb3[:, c, 0:rows], ones_b[:, 0:1], c == 0, c == KC - 1, ["m3sqb", "ones_b"], ["ps4"])
            act(fst[0:rows, fo + 3:fo + 4], PS[4][0:rows, 0:1], AF.Ln, ["ps4"], [("fst", par)], scale=1.0 / D, bias=EPS)
            V("tensor_scalar", dict(out=fst[0:rows, fo + 4:fo + 5], in0=fst[0:rows, fo + 3:fo + 4], scalar1=-0.5, scalar2=None, op0=ALU.mult),
              [("fst", par)], [("fst", par)])
            act(fst[0:rows, fo + 5:fo + 6], fst[0:rows, fo + 4:fo + 5], AF.Exp, [("fst", par)], [("fst", par)])

        def final_B(tok0, n, xkeys, sample, s0, par, half):
            S.ctx = "p3.final"
            rows = min(128, n - s0)
            tsl = slice(tok0 + s0, tok0 + s0 + rows)
            fo = par * 8
            stg_, sk_ = stgs[par], ("stg3" if par == 0 else ("stg", 1))
            bank_ = 7 if half == 0 else 0
            pb, pk = PS[bank_], "ps%d" % bank_
            for c4 in range(4):
                c = half * 4 + c4
                TR(pb[0:rows, c4 * 128:(c4 + 1) * 128], xT[:, c, tsl], ident_f[:], list(xkeys) + ["ident"], [pk], sig=(c4 == 3))
            V("scalar_tensor_tensor", dict(out=stg_[0:rows, half * 512:(half + 1) * 512], in0=pb[0:rows, :], scalar=fst[0:rows, fo + 5:fo + 6],
                                           in1=gfin_bc[0:rows, half * 512:(half + 1) * 512], op0=ALU.mult, op1=ALU.mult),
              [pk, ("fst", par), "gfin_bc"], [sk_])
            if half == 1:
                dst = y_s if sample else y_p[tok0 + s0:tok0 + s0 + rows, :]
                ld("sync", dst, stg_[0:rows, :], "o_y%d" % par, reads=[sk_])

        def ffn_back(u, bi, ab, fillers=None):
            S.ctx = "p3.back"
            f0, nf = UNITS[u]
            si = u % 2
            wg_v, wu_v, wd_v = slot_views(si)
            fk = ("fslot", si)
            tok0, n, xkeys, sample = blocks3[bi]
            blk = slice(tok0, tok0 + n)
            dbanks = [5, 6]
            if u != len(UNITS) - 1:
                dbanks.append(7)
                if u > 0:
                    dbanks.append(0)
            for oc in range(8):
                bi_ = dbanks[oc % len(dbanks)]
                pb, pk = PS[bi_], "ps%d" % bi_
                for lf in range(nf):
                    MM(pb[:, 0:n], wd_v[:, lf, oc * 128:(oc + 1) * 128], actbs[ab][:, lf, 0:n], lf == 0, lf == nf - 1,
                       [fk] + [("actb", ab, l) for l in range(nf)], [pk])
                V("tensor_tensor", dict(out=xT[:, oc, blk], in0=pb[:, 0:n], in1=xT[:, oc, blk], op=ALU.add),
                  [pk] + list(xkeys), list(xkeys))
                if fillers:
                    fillers.pop(0)()
                    S.ctx = "p3.back"

        def conv_state_out(f0, nf):
            S.ctx = "p3.cso"
            stgp = stgs[1]
            for j in range(nf):
                TR(PS[0][0:2, j * 128:(j + 1) * 128], ghist[:, f0 + j, :], ident_f[:], ["ghist", "ident"], ["ps0"], sig=(j == nf - 1))
            V("tensor_copy", dict(out=stgp[0:2, 0:nf * 128], in_=PS[0][0:2, 0:nf * 128]), ["ps0"], [("stg", 1)])
            ld("sync", sc_p[:, f0 * 128:(f0 + nf) * 128], stgp[0:2, 0:nf * 128], "o_scp", reads=[("stg", 1)])
            for j in range(nf):
                TR(PS[7][0:32, j * 128:(j + 1) * 128], gkeep[:, f0 + j, :], ident_f[:], ["sconvT", "ident"], ["ps7"], sig=(j == nf - 1))
            V("tensor_copy", dict(out=stg3[0:32, 0:nf * 128], in_=PS[7][0:32, 0:nf * 128]), ["ps7"], ["stg3"])
            ld("sync", sc_s[:, f0 * 128:(f0 + nf) * 128], stg3[0:32, 0:nf * 128], "o_scs", reads=["stg3"])

        abc = 0
        for u in range(len(UNITS)):
            prev = None
            lastu = (u == len(UNITS) - 1)
            pend = []
            for bi in range(len(blocks3)):
                ab = abc % 2
                abc += 1
                deferred = ffn_front(u, bi, ab, fillers=pend)
                if prev is not None:
                    ffn_back(u, *prev, fillers=pend)
                    if lastu:
                        pend += final_out(prev[0])
                deferred()
                prev = (bi, ab)
            ffn_back(u, *prev, fillers=pend)
            if lastu:
                pend += final_out(prev[0])
            for fl in pend:
                fl()
            conv_state_out(*UNITS[u])
            if u + 2 < len(UNITS):
                load_unit(u + 2)
        _NC_CACHE["sched"] = S
        sems = {n: es.enter_context(nc.semaphore(n)) for n in sorted(S.sem_names)}
        final = {"sync": [(s, v) for s, v in S.dma_cnt.items() if s.startswith("D_o_")]}
        with nc.Block() as block:
            S.emit(block, sems, final)
    return nc


_NC_CACHE = {}


def kernel(**inp):
    f = lambda a: np.ascontiguousarray(np.asarray(a, dtype=np.float32))
    x_prompt, x_sample, mem_prompt = f(inp["x_prompt"]), f(inp["x_sample"]), f(inp["mem_prompt"])
    state_gla, state_conv = f(inp["state_gla"])[0], f(inp["state_conv"])[0]
    ckk, cvv = f(inp["cache_mem_k"])[0], f(inp["cache_mem_v"])[0]
    vec1 = np.concatenate([f(inp["g_mix"]).reshape(8, 128), f(inp["g_x"]).reshape(8, 128), f(inp["g_mem"]).reshape(8, 128),
                           f(inp["g_ffn"]).reshape(8, 128), f(inp["g_final"]).reshape(8, 128), f(inp["b_alpha"]).reshape(2, 128),
                           f(inp["g_gla_out"]).reshape(4, 128), f(inp["conv_b"]).reshape(22, 128)], axis=0)
    vec2 = f(inp["conv_w"]).reshape(66, 128)
    shared = {
        "w_in": f(inp["w_in"])[0], "w_alpha": f(inp["w_alpha"])[0], "w_s": f(inp["w_s"])[0],
        "b_s": f(inp["b_s"]).reshape(1, 512), "g_sgu": f(inp["g_sgu"]).reshape(1, 512),
        "w_out": f(inp["w_out"])[0], "wq": f(inp["wq_x"])[0], "wk": f(inp["wk_x"])[0], "wv": f(inp["wv_x"])[0], "wo": f(inp["wo_x"])[0],
        "w_gate": f(inp["w_gate"])[0], "w_up": f(inp["w_up"])[0], "w_down": f(inp["w_down"])[0],
        "vec1": np.ascontiguousarray(vec1), "vec2": np.ascontiguousarray(vec2), "g_fin": f(inp["g_final"]).reshape(1, D),
    }
    for k, v in host_consts().items():
        shared["c_" + k] = v
    in_maps = []
    for c in range(NCORES):
        m = dict(shared)
        sl = slice(c * 16, (c + 1) * 16)
        m["x_p"] = x_prompt[c]
        m["x_s"] = np.ascontiguousarray(x_sample[sl].reshape(TS, D))
        m["mem"] = mem_prompt[c]
        m["sgla"] = np.ascontiguousarray(state_gla[sl])
        m["sconv"] = np.ascontiguousarray(state_conv[sl].reshape(32, DFF))
        m["ck"] = np.ascontiguousarray(ckk[sl].reshape(16, 256, D))
        m["cv"] = np.ascontiguousarray(cvv[sl].reshape(16, 256, D))
        in_maps.append(m)
    if "nc" not in _NC_CACHE:
        _NC_CACHE["nc"] = build_nc()
    res = run_bass_kernel_spmd(_NC_CACHE["nc"], in_maps, core_ids=list(range(NCORES)))
    R = res.results
    cat = lambda k: np.stack([np.asarray(r[k], dtype=np.float32) for r in R], axis=0)
    y_prompt = cat("y_p")
    y_sample = cat("y_s").reshape(128, 4, D)
    sg_p = cat("sg_p")[None]
    sc_p = cat("sc_p")[None]
    mk_p = cat("mk_p").reshape(1, 8, 256, 4, 256)
    mv_p = cat("mv_p").reshape(1, 8, 256, 4, 256)
    sg_s = cat("sg_s").reshape(1, 128, 4, 64, 128)
    sc_s = cat("sc_s").reshape(1, 128, 2, DFF)
    sv_s = cat("sv_s").reshape(1, 128, 4, 4, 128)
    if DEBUG:
        _NC_CACHE["dbg"] = (R[0]["dbg1"], R[0]["dbg2"])
    return (y_prompt, y_sample, sg_p, sc_p, mk_p, mv_p, sg_s, sc_s, sv_s)
```

```python
import numpy as np
from contextlib import ExitStack
import concourse.bass as bass
import concourse.mybir as mybir
from concourse.bass_utils import run_bass_kernel_spmd

F32 = mybir.dt.float32
BF16 = mybir.dt.bfloat16
U8 = mybir.dt.uint8
AF = mybir.ActivationFunctionType
ALU = mybir.AluOpType
AX = mybir.AxisListType

NCORES = 8
TP, TS = 2048, 64
T = TP + TS
D, KC = 1024, 8
DIN = 2576
CQ, CK, CV, CR, CA, CU, CSV = 0, 256, 512, 1024, 1536, 1552, 2064
DFF, NF = 2816, 22
UNITS = [(0, 4), (4, 4), (8, 4), (12, 4), (16, 3), (19, 3)]
EPS = 1e-6
NB = 256
NB3 = 512
DEBUG = False


class Sched:
    ENGS = ["tensor", "vector", "scalar", "gpsimd", "sync"]

    def __init__(self):
        self.ops = {e: [] for e in self.ENGS}
        self.cnt = {e: 0 for e in self.ENGS}
        self.pending = {e: False for e in self.ENGS}
        self.writers, self.readers = {}, {}
        self.seen = {e: {} for e in self.ENGS}
        self.dma_cnt = {}
        self.sem_names = set()
        self.tags = {e: [] for e in self.ENGS}
        self.ctx = ""
        self.dma_hist = {}

    def _deps(self, eng, reads, writes):
        toks = {}

        def add(d):
            for s, v in d.items():
                if v > toks.get(s, 0):
                    toks[s] = v
        for k in reads:
            add(self.writers.get(k, {}))
        for k in writes:
            add(self.writers.get(k, {}))
            add(self.readers.get(k, {}))
        waits = []
        for s, v in toks.items():
            if s == "E_" + eng and eng != "gpsimd":
                continue
            if self.seen[eng].get(s, 0) >= v:
                continue
            self.seen[eng][s] = v
            waits.append((s, v))
        return waits

    def _commit(self, tok, reads, writes):
        s, v = tok
        for k in reads:
            r = self.readers.setdefault(k, {})
            r[s] = max(r.get(s, 0), v)
        for k in writes:
            self.writers[k] = {s: v}
            self.readers[k] = {}

    def op(self, eng, fn, reads=(), writes=(), sig=True):
        isps = lambda k: isinstance(k, str) and k.startswith("ps") and k[2:].isdigit()
        writes = list(writes) + [k for k in reads if isps(k)]
        reads = [k for k in reads if not isps(k)]
        waits = self._deps(eng, reads, writes)
        s = "E_" + eng
        self.sem_names.add(s)
        if sig:
            self.cnt[eng] += 1
            self.pending[eng] = False
            tok = (s, self.cnt[eng])
        else:
            self.pending[eng] = True
            tok = (s, self.cnt[eng] + 1)
        self.ops[eng].append((fn, waits, tok, 1 if sig else 0))
        self.tags[eng].append(self.ctx)
        self._commit(tok, reads, writes)

    def dma(self, eng, fn, key, reads=(), writes=()):
        waits = self._deps(eng, reads, writes)
        s = "D_" + key
        self.sem_names.add(s)
        self.dma_cnt[s] = self.dma_cnt.get(s, 0) + 16
        tok = (s, self.dma_cnt[s])
        hist = self.dma_hist.setdefault(eng, [])
        lim = 16 if eng == "gpsimd" else 24
        if len(hist) >= lim:
            os_, ov_ = hist[-lim]
            if self.seen[eng].get(os_, 0) < ov_:
                self.seen[eng][os_] = ov_
                waits = list(waits) + [(os_, ov_)]
        hist.append(tok)
        self.ops[eng].append((fn, waits, tok, 2))
        self.tags[eng].append(self.ctx)
        self._commit(tok, reads, writes)

    def fence(self, keys):
        for e in self.ENGS:
            assert not self.pending[e], e
        d = {"E_" + e: self.cnt[e] for e in self.ENGS if self.cnt[e] > 0}
        d.update(self.dma_cnt)
        for k in keys:
            self.writers[k] = dict(d)
            self.readers[k] = {}

    def emit(self, block, sems, final_waits):
        def body(eng_name):
            def f(eng):
                for fn, waits, tok, kind in self.ops[eng_name]:
                    for s, v in waits:
                        eng.wait_ge(sems[s], v)
                    r = fn(eng)
                    if kind == 2:
                        r.then_inc(sems[tok[0]], 16)
                    elif kind == 1:
                        r.then_inc(sems[tok[0]], 1)
                for s, v in final_waits.get(eng_name, []):
                    eng.wait_ge(sems[s], v)
            return f
        block.tensor(body("tensor"))
        block.vector(body("vector"))
        block.scalar(body("scalar"))
        block.gpsimd(body("gpsimd"))
        block.sync(body("sync"))


def host_consts():
    c = {}
    c["ident"] = np.eye(128, dtype=np.float32)
    r = np.arange(128)
    c["mask128"] = ((r[:, None] <= r[None, :]) & ((r[:, None] // 64) == (r[None, :] // 64))).astype(np.float32)
    r64 = np.arange(64)
    c["mask_s"] = ((r64[:, None] <= r64[None, :]) & ((r64[:, None] // 4) == (r64[None, :] // 4))).astype(np.float32)
    rm = np.ones((128, NB), np.float32)
    rm[:, 0::64] = 0.0
    c["rm"] = rm
    rms = np.ones((128, TS), np.float32)
    rms[:, 0::4] = 0.0
    c["rm_s"] = rms
    c["trilT"] = (r[:, None] <= r[None, :]).astype(np.float32)
    bm = np.zeros((128, 16), np.float32)
    bm[r64, r64 // 4] = 1.0
    c["bm"] = bm
    sel = np.zeros((4, 64), np.float32)
    sel[r64 % 4, r64] = 1.0
    c["sel"] = sel
    return c


def build_nc():
    nc = bass.Bass("TRN2", target_bir_lowering=False)

    def din(name, shape):
        return nc.dram_tensor(name, list(shape), F32, kind="ExternalInput").ap()

    def dout(name, shape):
        return nc.dram_tensor(name, list(shape), F32, kind="ExternalOutput").ap()

    x_p = din("x_p", [TP, D]); x_s = din("x_s", [TS, D]); mem = din("mem", [256, D])
    sgla = din("sgla", [16, 4, 64, 128]); sconv = din("sconv", [32, DFF])
    ck = din("ck", [16, 256, D]); cv = din("cv", [16, 256, D])
    w_in = din("w_in", [D, DIN]); w_alpha = din("w_alpha", [16, 256])
    w_s = din("w_s", [4, 128, 128]); b_s = din("b_s", [1, 512]); g_sgu = din("g_sgu", [1, 512])
    w_out = din("w_out", [D, D]); wq = din("wq", [D, D]); wk = din("wk", [D, D]); wv = din("wv", [D, D]); wo = din("wo", [D, D])
    w_gate = din("w_gate", [D, DFF]); w_up = din("w_up", [D, DFF]); w_down = din("w_down", [DFF, D])
    vec1 = din("vec1", [68, 128]); vec2 = din("vec2", [66, 128]); g_fin = din("g_fin", [1, D])
    cst = {k: din("c_" + k, v.shape) for k, v in host_consts().items()}

    y_p = dout("y_p", [TP, D]); y_s = dout("y_s", [TS, D]); sg_p = dout("sg_p", [4, 64, 128]); sc_p = dout("sc_p", [2, DFF])
    mk_p = dout("mk_p", [256, D]); mv_p = dout("mv_p", [256, D]); sg_s = dout("sg_s", [16, 4, 64, 128])
    sc_s = dout("sc_s", [32, DFF]); sv_s = dout("sv_s", [TS, 512])

    if DEBUG:
        dbg1 = dout("dbg1", [128, KC, T]); dbg2 = dout("dbg2", [128, KC, T])
    S = Sched()
    es = ExitStack()
    with es:
        def sb(name, shape, dt):
            return es.enter_context(nc.sbuf_tensor(name, list(shape), dt))
        xT = sb("xT", [128, KC, T], F32)
        R1 = sb("R1", [128, 57600], U8)
        R2 = sb("R2", [128, 32768], U8)
        R3 = sb("R3", [128, 33792], U8)
        PS = [es.enter_context(nc.psum_tensor("ps%d" % i, [128, 512], F32)) for i in range(8)]

        def carve(arena, off, shape, dt):
            esz = 4 if dt == F32 else 2
            n = int(np.prod(shape[1:]))
            v = arena[:, off:off + n * esz].bitcast(dt)
            if len(shape) == 3:
                v = v.rearrange("p (a b) -> p a b", a=shape[1])
            elif len(shape) == 4:
                v = v.rearrange("p (a b c) -> p a b c", a=shape[1], b=shape[2])
            return v

        class Lay:
            def __init__(self, arena, base=0):
                self.arena, self.off = arena, base

            def get(self, shape, dt):
                esz = 4 if dt == F32 else 2
                v = carve(self.arena, self.off, shape, dt)
                self.off += int(np.prod(shape[1:])) * esz
                self.off = (self.off + 63) // 64 * 64
                assert self.off <= self.arena.shape[1], (self.off, self.arena.shape)
                return v

        ident_f = sb("ident_f", [128, 128], F32); ident_b = sb("ident_b", [128, 128], BF16)
        ones_b = sb("ones_b", [128, 128], BF16)
        mask128 = sb("mask128", [128, 128], F32); mask_s = sb("mask_s", [64, 64], F32)
        rm = sb("rm", [128, NB], F32); rm_s = sb("rm_s", [128, TS], F32)
        bm = sb("bm", [128, 16], F32); sel = sb("sel", [4, 64], F32)
        colv1 = sb("colv1", [128, 68], F32); colv2 = sb("colv2", [128, 66], F32)
        nba = sb("nba", [128, 2], F32); gg = sb("gg", [128, 4], F32)
        gsgu_bc = sb("gsgu_bc", [128, 512], F32)
        Wm = sb("Wm", [128, 4, 128], BF16); Wm_s = sb("Wm_s", [64, 4, 64], BF16)
        bs_hi = sb("bs_hi", [1, 512], BF16); bs_lo = sb("bs_lo", [1, 512], BF16)
        walpha_b = sb("walpha_b", [16, 256], BF16)
        ghist = sb("ghist", [128, NF, 2], F32)
        KT = sb("KT", [128, 8, 256], BF16); vbf = sb("vbf", [128, 2, 1024], BF16)
        smx = sb("smx", [128, 4], F32)
        Sst = sb("Sst", [128, 2, 128], F32); S_bf = sb("S_bf", [128, 2, 128], BF16); S_bfB = sb("S_bfB", [128, 2, 128], BF16)
        S_bfs = [S_bf, S_bfB]
        gch = [0]

        G_MIX, G_X, G_MEM, G_FFN, G_FIN = 0, 8, 16, 24, 32

        def ld(eng, out, in_, key, reads=(), writes=(), **kw):
            S.dma(eng, lambda e: e.dma_start(out=out, in_=in_, **kw), key, reads=reads, writes=writes)

        def V(name, kw, reads=(), writes=()):
            S.op("vector", lambda e: getattr(e, name)(**kw), reads, writes)

        def A(fn, reads=(), writes=()):
            S.op("scalar", fn, reads, writes)

        def G(name, kw, reads=(), writes=()):
            S.op("gpsimd", lambda e: getattr(e, name)(**kw), reads, writes)

        def MM(out, lhsT, rhs, start, stop, reads, writes, sig=None):
            if sig is None:
                sig = stop
            S.op("tensor", lambda e: e.matmul(out, lhsT=lhsT, rhs=rhs, start=start, stop=stop), reads, writes, sig=sig)

        def TR(out, in_, ident, reads, writes, sig=True):
            S.op("tensor", lambda e: e.transpose(out=out, in_=in_, identity=ident), reads, writes, sig=sig)

        def act(out, in_, func, reads, writes, **kw):
            A(lambda e: e.activation(out=out, in_=in_, func=func, **kw), reads, writes)

        for nm, t in [("ident", ident_f), ("mask128", mask128), ("mask_s", mask_s), ("rm", rm), ("rm_s", rm_s),
                      ("bm", bm), ("sel", sel)]:
            ld("sync", t[:], cst[nm], "c_" + nm, writes=[nm])
        ld("sync", gsgu_bc[:], g_sgu.broadcast_to([128, 512]), "gsgu", writes=["gsgu_bc"])
        ld("gpsimd", walpha_b[:], w_alpha, "walpha", writes=["walpha_b"])
        V("tensor_copy", dict(out=ident_b[:], in_=ident_f[:]), ["ident"], ["ident_b"])
        G("memset", dict(ap=ones_b[:], constant=1.0), [], ["ones_b"])
        G("memset", dict(ap=ghist[:], constant=0.0), [], ["ghist"])
        G("memset", dict(ap=Sst[:], constant=0.0), [], ["Sst"])
        G("memset", dict(ap=S_bf[:], constant=0.0), [], [("S_bf", 0)])

        wk_b = carve(R2, 0, [128, 8, 1024], BF16); wv_b = carve(R2, 16384, [128, 8, 1024], BF16)
        w_in_b = carve(R1, 0, [128, 8, DIN], BF16); w_out_b = carve(R1, 41216, [128, 8, 1024], BF16)

        def load_w(dst, src, key, ncols, reads=(), writes=()):
            srcv = src.rearrange("(c p) n -> p c n", p=128)
            nck = srcv.shape[1]
            step = 8 if ncols <= 1024 else 4
            for c0 in range(0, nck, step):
                ld("gpsimd", dst[:, c0:c0 + step, :], srcv[:, c0:c0 + step, :], key, reads=reads, writes=[key] + list(writes))

        load_w(wk_b, wk, "wk", 1024)
        load_w(wv_b, wv, "wv", 1024)
        load_w(w_in_b, w_in, "w_in", DIN)
        load_w(w_out_b, w_out, "w_out", 1024)

        S.ctx = "p0"
        L = Lay(R3)
        stg = [L.get([128, 1024], F32) for _ in range(2)]
        memT = L.get([128, 8, 256], F32)
        mnT = L.get([128, 8, 256], BF16)
        sqb0 = L.get([128, 8, 256], BF16)
        rstd0 = L.get([128, 256], F32); tmpn0 = L.get([128, 256], F32)
        Wm32 = L.get([128, 4, 128], F32); wsl = L.get([128, 4, 128], F32)
        vst = L.get([128, 128], F32)
        bsf = L.get([128, 512], F32)
        trilT = L.get([128, 128], F32)
        ld("sync", trilT[:], cst["trilT"], "c_trilT", writes=["trilT"])
        ld("sync", bsf[0:1, :], b_s, "bsf", writes=["bsf"])
        act(bs_hi[:], bsf[0:1, :], AF.Copy, ["bsf"], ["bs_hi"])
        V("tensor_tensor", dict(out=bs_lo[:], in0=bsf[0:1, :], in1=bs_hi[:], op=ALU.subtract), ["bsf", "bs_hi"], ["bs_lo"])

        ld("sync", vst[0:68, :], vec1, "vst", writes=["vst"])
        TR(PS[0][:, 0:68], vst[0:68, :], ident_f[0:68, 0:68], ["vst", "ident"], ["ps0"])
        V("tensor_copy", dict(out=colv1[:], in_=PS[0][:, 0:68]), ["ps0"], ["colv1"])
        ld("sync", vst[0:66, :], vec2, "vst", writes=["vst"])
        TR(PS[0][:, 0:66], vst[0:66, :], ident_f[0:66, 0:66], ["vst", "ident"], ["ps0"])
        V("tensor_copy", dict(out=colv2[:], in_=PS[0][:, 0:66]), ["ps0"], ["colv2"])
        act(nba[:], colv1[:, 40:42], AF.Copy, ["colv1"], ["nba"], scale=-1.0)
        act(gg[:], colv1[:, 42:46], AF.Copy, ["colv1"], ["gg"], scale=0.5)

        ld("sync", wsl[:], w_s.rearrange("h i j -> i h j"), "wsl", writes=["wsl"])
        for h in range(4):
            TR(PS[1][:, h * 128:(h + 1) * 128], wsl[:, h, :], ident_f[:], ["wsl", "ident"], ["ps1"], sig=(h == 3))
        V("tensor_tensor", dict(out=Wm32[:], in0=PS[1][:].rearrange("p (h i) -> p h i", h=4),
                                    in1=trilT[:].unsqueeze(1).broadcast_to([128, 4, 128]), op=ALU.mult),
          ["ps1", "trilT"], ["Wm32"])
        V("tensor_copy", dict(out=Wm[:], in_=Wm32[:]), ["Wm32"], ["Wm"])
        for h in range(4):
            MM(PS[1][0:64, h * 64:(h + 1) * 64], sel[0:4, :], Wm32[0:4, h, 0:4].unsqueeze(1).broadcast_to([4, 16, 4]),
               True, True, ["sel", "Wm32"], ["ps1"], sig=(h == 3))
        V("tensor_tensor", dict(out=Wm_s[:], in0=PS[1][0:64, 0:256].rearrange("p (h i) -> p h i", h=4),
                                    in1=mask_s[:].unsqueeze(1).broadcast_to([64, 4, 64]), op=ALU.mult),
          ["ps1", "mask_s"], ["Wm_s"])

        def fm_norm(src_fn, n, gcol, out_fn, sqb, rstd, tmpn, psb, pskey, rkeys, wkeys, tag, do_stats=True, do_apply=True):
            if do_stats:
                fm_stats(src_fn, n, sqb, rstd, tmpn, psb, pskey, rkeys, tag)
            if do_apply:
                for c in range(KC):
                    V("scalar_tensor_tensor", dict(out=out_fn(c), in0=src_fn(c), scalar=colv1[:, gcol + c:gcol + c + 1],
                                                   in1=rstd[:, 0:n], op0=ALU.mult, op1=ALU.mult),
                      list(rkeys) + [tag + "rstd", "colv1"], [(wkeys[0], c)] if (wkeys[0] in ("xn", "hn") or (isinstance(wkeys[0], tuple) and wkeys[0][0] == "hnA")) else wkeys)

        def fm_squares(src_fn, n, sqb, rkeys, tag):
            for c in range(KC):
                act(sqb[:, c, 0:n], src_fn(c), AF.Square, rkeys, [tag + "sqb"])

        def fm_stats(src_fn, n, sqb, rstd, tmpn, psb, pskey, rkeys, tag, squares=True):
            if squares:
                fm_squares(src_fn, n, sqb, rkeys, tag)
            for c in range(KC):
                MM(psb[:, 0:n], ones_b[:], sqb[:, c, 0:n], c == 0, c == KC - 1, ["ones_b", tag + "sqb"], [pskey])
            act(tmpn[:, 0:n], psb[:, 0:n], AF.Ln, [pskey], [tag + "tmpn"], scale=1.0 / D, bias=EPS)
            V("tensor_scalar", dict(out=tmpn[:, 0:n], in0=tmpn[:, 0:n], scalar1=-0.5, scalar2=None, op0=ALU.mult),
              [tag + "tmpn"], [tag + "tmpn"])
            act(rstd[:, 0:n], tmpn[:, 0:n], AF.Exp, [tag + "tmpn"], [tag + "rstd"])

        ntile = TP // 128
        xs_slots = [carve(R2, 16384 + i_ * 4096, [128, 1024], F32) for i_ in range(2)]

        def xtile(t, buf, bkey, dkey, banks, extra_w=()):
            rows = 128 if t < ntile else TS
            src = x_p[t * 128:(t + 1) * 128, :] if t < ntile else x_s
            ld("sync", buf[0:rows, :], src, dkey, writes=[bkey] + list(extra_w))
            for half in range(2):
                pb, pk = PS[banks[half]], "ps%d" % banks[half]
                for c4 in range(4):
                    c = half * 4 + c4
                    TR(pb[:, c4 * 128:c4 * 128 + rows], buf[0:rows, c * 128:(c + 1) * 128], ident_f[0:rows, 0:rows],
                       [bkey, "ident"], [pk], sig=(c4 == 3))
                dst = xT[:, half * 4:half * 4 + 4, t * 128:t * 128 + rows]
                srcp = pb[:].rearrange("p (c n) -> p c n", c=4)[:, :, 0:rows]
                if half == 0:
                    V("tensor_copy", dict(out=dst, in_=srcp), [pk], [("xT", t // 2)])
                else:
                    act(dst, srcp, AF.Copy, [pk], [("xT", t // 2)])
        for t in range(2):
            xtile(t, stg[t % 2], ("stg", t % 2), "stg%d" % (t % 2), (2, 3))

        for t in range(2):
            ld("sync", stg[t][:], mem[t * 128:(t + 1) * 128, :], "stg%d" % t, writes=[("stg", t)])
            for half in range(2):
                pb = PS[2 + half]
                for c4 in range(4):
                    c = half * 4 + c4
                    TR(pb[:, c4 * 128:(c4 + 1) * 128], stg[t][:, c * 128:(c + 1) * 128], ident_f[:],
                       [("stg", t), "ident"], ["ps%d" % (2 + half)], sig=(c4 == 3))
                V("tensor_copy", dict(out=memT[:, half * 4:half * 4 + 4, t * 128:(t + 1) * 128],
                                                                  in_=pb[:].rearrange("p (c n) -> p c n", c=4)),
                  ["ps%d" % (2 + half)], ["memT"])
        fm_norm(lambda c: memT[:, c, :], 256, G_MEM, lambda c: mnT[:, c, :], sqb0, rstd0, tmpn0, PS[4], "ps4",
                ["memT"], ["mnT"], "p0")
        for t in range(2):
            for (wb, wkey, outd, isv) in [(wk_b, "wk", mk_p, False), (wv_b, "wv", mv_p, True)]:
                for half in range(2):
                    pb, pk = PS[5 + half], "ps%d" % (5 + half)
                    for c in range(KC):
                        MM(pb[:], mnT[:, c, t * 128:(t + 1) * 128], wb[:, c, half * 512:(half + 1) * 512], c == 0, c == KC - 1,
                           ["mnT", wkey], [pk])
                    act(stg[t][:, half * 512:(half + 1) * 512], pb[:], AF.Copy, [pk], [("stg", t)])
                    if isv:
                        V("tensor_copy", dict(out=vbf[:, t, half * 512:(half + 1) * 512], in_=pb[:]),
                          [pk], ["vbf"])
                ld("sync", outd[t * 128:(t + 1) * 128, :], stg[t][:], "o_mkv", reads=[("stg", t)])
        for oc in range(8):
            pb, pk = PS[5 + oc % 2], "ps%d" % (5 + oc % 2)
            for c in range(KC):
                MM(pb[:, 0:256], wk_b[:, c, oc * 128:(oc + 1) * 128], mnT[:, c, :], c == 0, c == KC - 1, ["mnT", "wk"], [pk])
            act(KT[:, oc, :], pb[:, 0:256], AF.Copy, [pk], ["KT"])

        wq_b = carve(R2, 16384, [128, 8, 1024], BF16); wo_b = carve(R2, 0, [128, 8, 1024], BF16)
        S.fence([("xs", 0), ("xs", 1)])
        L = Lay(R3)
        xn = L.get([128, 8, NB], BF16); sqb = L.get([128, 8, NB], BF16)
        rstd = L.get([128, NB], F32); tmpn = L.get([128, NB], F32)
        aT = L.get([128, NB], BF16)
        e1 = L.get([128, NB], F32); Bc = L.get([128, NB], F32)
        eb = L.get([128, 2, NB], F32); enb = L.get([128, 2, NB], F32)
        qm = L.get([128, 4, NB], BF16); ktT = L.get([128, 2, NB], BF16); khT = L.get([128, 2, NB], BF16)
        cat = L.get([128, 8, NB], BF16); th = L.get([128, NB], F32)
        v_tm = L.get([128, 2, 512], BF16)
        p1keys = [("xn", c_) for c_ in range(KC)] + ["m1sqb", "m1rstd", "m1tmpn", "aT", "e1", "Bc", ("eb", 0), ("eb", 1), ("enb", 0), ("enb", 1), "qm", ("ktT", 0), ("ktT", 1), ("khT", 0), ("khT", 1), "cat", "th", "v_tm"]
        L2 = Lay(R2)
        svg = L2.get([128, 512], F32); svsq = L2.get([128, 512], F32); svn = L2.get([128, 2, 512], BF16)
        svo = svsq
        sc_bd = L2.get([128, 4, 128], BF16); khm = L2.get([128, 2, 256], BF16)
        TR2_OFF = L2.off
        rstd_o = L2.get([128, NB], F32); tmp_o = L2.get([128, NB], F32); osq = L2.get([128, 4, NB], BF16)
        sst = L2.get([128, 8], F32); srs = L2.get([128, 8], F32)
        S0 = [L2.get([128, 2, 128], F32) for _ in range(2)]
        S0b = [L2.get([128, 2, 128], BF16) for _ in range(2)]
        S0 = S0 + [carve(R3, 4096 + i_ * 1024, [128, 2, 128], F32) for i_ in range(2)]
        S0b = S0b + [carve(R3, 4096 + 2048 + i_ * 512, [128, 2, 128], BF16) for i_ in range(2)]
        p1keys += ["svg", "svsq", "svn", "sc_bd", ("khm", 0), ("khm", 1), "rstd_o", "tmp_o", ("osq", 0), ("osq", 1), ("osq", 2), ("osq", 3), "sst", "srs",
                   ("S0", 0), ("S0", 1), ("S0b", 0), ("S0b", 1)]
        S.fence(p1keys)
        G("memset", dict(ap=qm[:], constant=0.0), [], ["qm"])
        G("memset", dict(ap=khm[:], constant=0.0), [], [("khm", 0), ("khm", 1)])

        rr = [0]

        def rot():
            i = rr[0] % 2
            rr[0] += 1
            return PS[i], "ps%d" % i

        def mixer_block(tok0, n, xkeys, sample, pre=False, nxt=None, prev_wout=None, xfill=()):
            ntl = (n + 127) // 128
            blk = slice(tok0, tok0 + n)
            S.ctx = "p1.norm"
            fm_norm(lambda c: xT[:, c, blk], n, G_MIX, lambda c: xn[:, c, 0:n], sqb, rstd, tmpn, PS[0], "ps0",
                    xkeys, ["xn"], "m1", do_stats=not pre)
            xfill = list(xfill)
            if prev_wout is not None:
                prev_wout()
            if xfill:
                xfill.pop(0)()
            rr[0] = 1

            def proj(col0, m):
                pb, pk = rot()
                for c in range(KC):
                    MM(pb[0:m, 0:n], w_in_b[:, c, col0:col0 + m], xn[:, c, 0:n], c == 0, c == KC - 1, ["w_in", ("xn", c)], [pk])
                return pb, pk
            S.ctx = "p1.gates"
            pb, pk = proj(CA, 16)
            act(aT[0:16, 0:n], pb[0:16, 0:n], AF.Copy, [pk], ["aT"])
            rmask = rm_s if sample else rm
            for c in range(2):
                pb, pk = rot()
                MM(pb[:, 0:n], walpha_b[:, c * 128:(c + 1) * 128], aT[0:16, 0:n], True, True, ["walpha_b", "aT"], [pk])
                eX, eXk = (e1, "e1") if c == 0 else (th, "th")
                act(eX[:, 0:n], pb[:, 0:n], AF.Exp, [pk, "nba"], [eXk], scale=-1.0, bias=nba[:, c:c + 1])
                V("tensor_scalar", dict(out=eX[:, 0:n], in0=eX[:, 0:n], scalar1=1.0, scalar2=None, op0=ALU.add), [eXk], [eXk])
                act(eX[:, 0:n], eX[:, 0:n], AF.Ln, [eXk], [eXk])
                V("tensor_tensor_scan", dict(out=Bc[:, 0:n], data0=rmask[:, 0:n], data1=eX[:, 0:n], initial=0.0,
                                                 op0=ALU.mult, op1=ALU.add), [eXk, "rm", "rm_s"], ["Bc"])
                act(eb[:, c, 0:n], Bc[:, 0:n], AF.Exp, ["Bc"], [("eb", c)], scale=-1.0 / 16)
                act(enb[:, c, 0:n], Bc[:, 0:n], AF.Exp, ["Bc"], [("enb", c)], scale=1.0 / 16)
            S.ctx = "p1.qk"
            def proj4(col0, m, j):
                bi_ = (2, 3, 0, 1)[j % 4]
                pb, pk = PS[bi_], "ps%d" % bi_
                for c in range(KC):
                    MM(pb[0:m, 0:n], w_in_b[:, c, col0:col0 + m], xn[:, c, 0:n], c == 0, c == KC - 1, ["w_in", ("xn", c)], [pk])
                return pb, pk
            for c in range(2):
                pb, pk = proj4(CQ + c * 128, 128, c)
                for h2 in range(2):
                    rs_ = slice(h2 * 64, (h2 + 1) * 64)
                    V("scalar_tensor_tensor", dict(
                        out=qm[rs_, 2 * c + h2, 0:n], in0=pb[rs_, 0:n], scalar=0.125, in1=eb[rs_, c, 0:n],
                        op0=ALU.mult, op1=ALU.mult), [pk, ("eb", c)], ["qm"])
            if xfill:
                S.ctx = "p1.x"
                xfill.pop(0)()
                S.ctx = "p1.qk"
            cl = 4 if sample else 64
            for c in range(2):
                pb, pk = proj4(CK + c * 128, 128, 2 + c)
                V("tensor_tensor", dict(out=ktT[:, c, 0:n], in0=pb[:, 0:n], in1=enb[:, c, 0:n], op=ALU.mult),
                  [pk, ("enb", c)], [("ktT", c)])
                G("tensor_tensor", dict(
                    out=khT[:, c, 0:n].rearrange("p (a b) -> p a b", b=cl),
                    in0=ktT[:, c, 0:n].rearrange("p (a b) -> p a b", b=cl),
                    in1=eb[:, c, cl - 1:n:cl].unsqueeze(2).broadcast_to([128, n // cl, cl]), op=ALU.mult),
                  [("ktT", c), ("eb", c)], [("khT", c)])
            S.ctx = "p1.ru"
            def ru_piece(j):
                S.ctx = "p1.ru"
                if j < 4:
                    pb, pk = proj(CR + j * 128, 128)
                    act(th[:, 0:n], pb[:, 0:n], AF.Tanh, [pk], ["th"], scale=0.5)
                    V("scalar_tensor_tensor", dict(out=cat[:, j, 0:n], in0=th[:, 0:n], scalar=1.0, in1=pb[:, 0:n],
                                                   op0=ALU.add, op1=ALU.mult), [pk, "th"], [("cat", j)])
                else:
                    pb, pk = proj(CU + (j - 4) * 128, 128)
                    act(cat[:, j, 0:n], pb[:, 0:n], AF.Gelu_apprx_tanh, [pk], [("cat", j)])
                S.ctx = "p1.gla"
            ru_q = list(range(8))
            S.ctx = "p1.tm"
            if sample:
                G("memset", dict(ap=v_tm[:], constant=0.0), [], ["v_tm"])
                G("memset", dict(ap=svn[:], constant=0.0), [], ["svn"])
                G("memset", dict(ap=sc_bd[:], constant=0.0), [], ["sc_bd"])
            for tl in range(ntl):
                rows = min(128, n - tl * 128)
                tsl = slice(tl * 128, tl * 128 + rows)
                for c in range(KC):
                    MM(PS[2][0:rows, :], xn[:, c, tsl], w_in_b[:, c, CV:CV + 512], c == 0, c == KC - 1, [("xn", c), "w_in"], ["ps2"])
                act(v_tm[0:rows, tl, :], PS[2][0:rows, :], AF.Copy, ["ps2"], ["v_tm"])
                for c in range(KC):
                    MM(PS[3][0:rows, :], xn[:, c, tsl], w_in_b[:, c, CSV:CSV + 512], c == 0, c == KC - 1, [("xn", c), "w_in"], ["ps3"])
                act(svg[0:rows, :], PS[3][0:rows, :], AF.Gelu_apprx_tanh, ["ps3"], ["svg"])
                V("tensor_tensor", dict(out=svsq[0:rows, :], in0=svg[0:rows, :], in1=svg[0:rows, :], op=ALU.mult),
                  ["svg"], ["svsq"])
                V("tensor_reduce", dict(out=sst[0:rows, 0:4], in_=svsq[0:rows, :].rearrange("p (h d) -> p h d", h=4),
                                                       axis=AX.X, op=ALU.add), ["svsq"], ["sst"])
                act(sst[0:rows, 4:8], sst[0:rows, 0:4], AF.Ln, ["sst"], ["sst"], scale=1.0 / 128, bias=EPS)
                V("tensor_scalar", dict(out=sst[0:rows, 4:8], in0=sst[0:rows, 4:8], scalar1=-0.5, scalar2=None, op0=ALU.mult),
                  ["sst"], ["sst"])
                act(srs[0:rows, 0:4], sst[0:rows, 4:8], AF.Exp, ["sst"], ["srs"])
                for h in range(4):
                    hs = slice(h * 128, (h + 1) * 128)
                    dst = svo[0:rows, hs] if sample else svn[0:rows, tl, hs]
                    V("scalar_tensor_tensor", dict(
                        out=dst, in0=svg[0:rows, hs], scalar=srs[0:rows, h:h + 1], in1=gsgu_bc[0:rows, hs],
                        op0=ALU.mult, op1=ALU.mult), ["svg", "srs", "gsgu_bc"], ["svsq" if sample else "svn"])
                if sample:
                    V("tensor_copy", dict(out=svn[0:rows, 0, :], in_=svo[0:rows, :]), ["svsq", "svn"], ["svn"])
                    ld("sync", sv_s, svo[0:rows, :], "o_svs", reads=["svsq"])
            if nxt is not None:
                S.ctx = "p1.norm"
                ntok0, nn_, nxk = nxt
                nblk = slice(ntok0, ntok0 + nn_)
                fm_squares(lambda c: xT[:, c, nblk], nn_, sqb, nxk, "m1")
            S.ctx = "p1.gla"
            for tl in range(ntl):
                rows = min(128, n - tl * 128)
                tsl = slice(tl * 128, tl * 128 + rows)
                for h in range(4):
                    MM(PS[4][0:rows, h * 128:h * 128 + rows], ktT[:, h // 2, tsl], qm[:, h, tsl], True, True,
                       [("ktT", h // 2), "qm"], ["ps4"], sig=(h == 3))
                msk = mask_s[:] if sample else mask128[:]
                V("tensor_tensor", dict(
                    out=sc_bd[0:rows, :, 0:rows], in0=PS[4][0:rows, :].rearrange("p (h i) -> p h i", h=4)[:, :, 0:rows],
                    in1=msk.unsqueeze(1).broadcast_to([rows, 4, rows]), op=ALU.mult), ["ps4", "mask128", "mask_s"], ["sc_bd"])
                p5b = PS[5][:].bitcast(BF16)
                for c in range(2):
                    TR(p5b[0:rows, c * 128:(c + 1) * 128], khT[:, c, tsl], ident_b[:], [("khT", c), "ident_b"], ["ps5"], sig=(c == 1))
                if not sample:
                    for p in range(2):
                        prs = slice(p * 64, (p + 1) * 64)
                        act(khm[prs, p, :], p5b[prs, 0:256], AF.Copy, ["ps5"], [("khm", p)])
                else:
                    act(tmp_o[0:64, 0:128].bitcast(BF16), p5b[0:64, 0:256], AF.Copy, ["ps5"], ["tmp_o"])
                groups = [(p, slice(p * 64, (p + 1) * 64), None) for p in range(rows // 64)] if not sample else \
                    [(b, slice(b * 4, (b + 1) * 4), b) for b in range(16)]
                for gi, (p, csl, bidx) in enumerate(groups):
                    if sample:
                        s = bidx % 4
                        ld("sync", S0[s][:], sgla[bidx].rearrange("(c h2) d v -> (h2 d) c v", c=2), "S0_%d" % s,
                           writes=[("S0", s)] + (["m1sqb"] if s >= 2 else []))
                        act(S0b[s][:], S0[s][:], AF.Copy, [("S0", s)], [("S0b", s)] + (["m1sqb"] if s >= 2 else []))
                        V("tensor_scalar", dict(out=khm[0:64, bidx % 2, :], in0=tmp_o[0:64, 0:128].bitcast(BF16),
                                                                    scalar1=bm[0:64, bidx:bidx + 1], scalar2=None, op0=ALU.mult),
                          ["tmp_o", "bm"], [("khm", bidx % 2)])
                        Sb_cur, Sb_key = S0b[s], ("S0b", s)
                        khm_cur = khm[:, bidx % 2, :]
                        khk = ("khm", bidx % 2)
                    else:
                        rslot, wslot = gch[0] % 2, (gch[0] + 1) % 2
                        gch[0] += 1
                        Sb_cur, Sb_key = S_bfs[rslot], ("S_bf", rslot)
                        khm_cur = khm[:, p, :]
                        khk = ("khm", p)
                    ncol = csl.stop - csl.start
                    gsl = slice(tl * 128 + csl.start, tl * 128 + csl.stop)
                    ubank = (2 + rslot) if not sample else (4 + (gi % 2))
                    pu, puk = PS[ubank], "ps%d" % ubank

                    def u_mms():
                        for h in range(4):
                            c = h // 2
                            MM(pu[:, h * 128:(h + 1) * 128], khm_cur[:, c * 128:(c + 1) * 128], v_tm[:, tl, h * 128:(h + 1) * 128],
                               True, True, [khk, "v_tm"], [puk], sig=(h == 3))

                    def o_mms():
                        for h in range(4):
                            c = h // 2
                            pob = PS[6 + h // 2]
                            ocol = (h % 2) * NB + tl * 128
                            outap = pob[:, ocol + csl.start:ocol + csl.stop]
                            MM(outap, Sb_cur[:, c, :], qm[:, h, gsl], True, False, [Sb_key, "qm"], ["ps%d" % (6 + h // 2)], sig=False)
                            MM(outap, v_tm[:, tl, h * 128:(h + 1) * 128], sc_bd[:, h, csl], False, True, ["v_tm", "sc_bd"],
                               ["ps%d" % (6 + h // 2)], sig=(h % 2 == 1))
                    if sample or gi > 0:
                        u_mms()
                        o_mms()
                    else:
                        o_mms()
                        u_mms()
                    ecol = tl * 128 + csl.stop - 1
                    for h in range(4):
                        c, h2 = h // 2, h % 2
                        rs_ = slice(h2 * 64, (h2 + 1) * 64)
                        if sample:
                            V("scalar_tensor_tensor", dict(
                                out=S0[s][rs_, c, :], in0=S0[s][rs_, c, :], scalar=eb[rs_, c, ecol:ecol + 1],
                                in1=pu[rs_, h * 128:(h + 1) * 128], op0=ALU.mult, op1=ALU.add), [puk, ("eb", c), ("S0", s)], [("S0", s)])
                        else:
                            V("scalar_tensor_tensor", dict(
                                out=S_bfs[wslot][rs_, c, :], in0=Sst[rs_, c, :], scalar=eb[rs_, c, ecol:ecol + 1],
                                in1=pu[rs_, h * 128:(h + 1) * 128], op0=ALU.mult, op1=ALU.add), [puk, ("eb", c), "Sst"], [("S_bf", wslot)])
                    if not sample:
                        for h in range(4):
                            c, h2 = h // 2, h % 2
                            rs_ = slice(h2 * 64, (h2 + 1) * 64)
                            V("scalar_tensor_tensor", dict(
                                out=Sst[rs_, c, :], in0=Sst[rs_, c, :], scalar=eb[rs_, c, ecol:ecol + 1],
                                in1=pu[rs_, h * 128:(h + 1) * 128], op0=ALU.mult, op1=ALU.add), [puk, ("eb", c), "Sst"], ["Sst"])
                    if sample:
                        ld("sync", sg_s[bidx].rearrange("(c h2) d v -> (h2 d) c v", c=2), S0[s][:], "o_sgs", reads=[("S0", s)])
                        if ru_q:
                            ru_piece(ru_q.pop(0))
                    else:
                        for _ in range(2):
                            if ru_q:
                                ru_piece(ru_q.pop(0))
            S.ctx = "p1.onorm"
            while ru_q:
                ru_piece(ru_q.pop(0))
            if nxt is not None:
                S.ctx = "p1.norm"
                fm_stats(lambda c: xT[:, c, nblk], nn_, sqb, rstd, tmpn, PS[0], "ps0", nxk, "m1", squares=False)
            S.ctx = "p1.onorm"
            for h in range(4):
                pob, pok = PS[6 + h // 2], "ps%d" % (6 + h // 2)
                osl = slice((h % 2) * NB, (h % 2) * NB + n)
                act(osq[:, h, 0:n], pob[:, osl], AF.Square, [pok], [("osq", h)])
            for h in range(4):
                pb, pk = rot()
                for tl in range(ntl):
                    rows = min(128, n - tl * 128)
                    zo = pb[:, tl * 128:tl * 128 + rows]
                    if sample:
                        MM(zo, svn[0:64, 0, h * 128:(h + 1) * 128], Wm_s[:, h, :], True, False, ["svn", "Wm_s"], [pk], sig=False)
                        bh = bs_hi[0:1, h * 128:h * 128 + 4].unsqueeze(1).broadcast_to([1, 16, 4])
                        bl = bs_lo[0:1, h * 128:h * 128 + 4].unsqueeze(1).broadcast_to([1, 16, 4])
                    else:
                        MM(zo, svn[:, tl, h * 128:(h + 1) * 128], Wm[:, h, :], True, False, ["svn", "Wm"], [pk], sig=False)
                        bh = bs_hi[0:1, h * 128:(h + 1) * 128]
                        bl = bs_lo[0:1, h * 128:(h + 1) * 128]
                    MM(zo, ones_b[0:1, :], bh, False, False, ["ones_b", "bs_hi"], [pk], sig=False)
                    MM(zo, ones_b[0:1, :], bl, False, True, ["ones_b", "bs_lo"], [pk], sig=True)
                V("tensor_tensor", dict(out=cat[:, 4 + h, 0:n], in0=pb[:, 0:n], in1=cat[:, 4 + h, 0:n], op=ALU.mult),
                  [pk, ("cat", 4 + h)], [("cat", 4 + h)])
            S.ctx = "p1.wout"
            S.ctx = "p1.onorm"
            tr2 = carve(R2, TR2_OFF, [128, 2, NB], F32)
            for hp in range(2):
                pob, pok = PS[6 + hp], "ps%d" % (6 + hp)
                pb, pk = rot()
                for h2_ in range(2):
                    h = 2 * hp + h2_
                    MM(pb[:, h2_ * NB:h2_ * NB + n], ones_b[:], osq[:, h, 0:n], True, True, ["ones_b", ("osq", h)], [pk], sig=(h2_ == 1))
                pv_ = pb[:].rearrange("p (a b) -> p a b", a=2)[:, :, 0:n]
                act(tr2[:, :, 0:n], pv_, AF.Ln, [pk], ["tmp_o", "rstd_o"], scale=1.0 / 128, bias=EPS)
                V("tensor_scalar", dict(out=tr2[:, :, 0:n], in0=tr2[:, :, 0:n], scalar1=-0.5, scalar2=None, op0=ALU.mult),
                  ["tmp_o", "rstd_o"], ["tmp_o", "rstd_o"])
                act(tr2[:, :, 0:n], tr2[:, :, 0:n], AF.Exp, ["tmp_o", "rstd_o"], ["tmp_o", "rstd_o"])
                for h2_ in range(2):
                    h = 2 * hp + h2_
                    osl = slice(h2_ * NB, h2_ * NB + n)
                    V("scalar_tensor_tensor", dict(out=tr2[:, h2_, 0:n], in0=pob[:, osl], scalar=gg[:, h:h + 1],
                                                   in1=tr2[:, h2_, 0:n], op0=ALU.mult, op1=ALU.mult),
                      [pok, "gg", "tmp_o", "rstd_o"], ["tmp_o", "rstd_o"])
                for h2_ in range(2):
                    h = 2 * hp + h2_
                    V("tensor_tensor", dict(out=cat[:, h, 0:n], in0=tr2[:, h2_, 0:n], in1=cat[:, h, 0:n], op=ALU.mult),
                      ["tmp_o", "rstd_o", ("cat", h)], [("cat", h)])
            catk = [("cat", j) for j in range(8)]
            def wout_part():
                S.ctx = "p1.wout"
                for oc in range(8):
                    bi_ = (2, 3, 0, 1)[oc % 4]
                    pb, pk = PS[bi_], "ps%d" % bi_
                    for c in range(KC):
                        MM(pb[:, 0:n], w_out_b[:, c, oc * 128:(oc + 1) * 128], cat[:, c, 0:n], c == 0, c == KC - 1, ["w_out", ("cat", c)], [pk])
                    V("tensor_tensor", dict(out=xT[:, oc, blk], in0=pb[:, 0:n], in1=xT[:, oc, blk], op=ALU.add),
                      [pk] + list(xkeys), list(xkeys))
            return wout_part

        mblocks = [(b * NB, NB, [("xT", b)]) for b in range(TP // NB)] + [(TP, TS, [("xT", 8)])]
        for i_, (tk, nn, xk) in enumerate(mblocks):
            smp = (i_ == len(mblocks) - 1)
            if smp:
                ld("sync", sg_p.rearrange("(c h2) d v -> (h2 d) c v", c=2), Sst[:], "o_sgp", reads=["Sst"])
            if i_ <= 6:
                tl_ = [2 * i_ + 2, 2 * i_ + 3]
            elif i_ == 7:
                tl_ = [16]
            else:
                tl_ = []
            xf_ = [(lambda t=t: xtile(t, xs_slots[t % 2], ("xs", t % 2), "xs%d" % (t % 2), (0, 1), extra_w=["wv"])) for t in tl_]
            pw_ = mixer_block(tk, nn, xk, smp, pre=(i_ > 0), nxt=(mblocks[i_ + 1] if i_ + 1 < len(mblocks) else None),
                              prev_wout=(pw_ if i_ > 0 else None), xfill=xf_)
            if i_ == 7:
                load_w(wq_b, wq, "wq", 1024, writes=["wv", ("xs", 0), ("xs", 1)])
        pw_()

        if DEBUG:
            ld("sync", dbg1, xT[:], "o_dbg1", reads=[("xT", i) for i in range(9)])
        S.fence(["wo"])
        load_w(wo_b, wo, "wo", 1024)
        L = Lay(R3)
        hn = L.get([128, 8, NB], BF16); sqb2 = L.get([128, 8, NB], BF16)
        rstd2 = L.get([128, NB], F32); tmpn2 = L.get([128, NB], F32)
        qT = L.get([128, 8, NB], BF16)
        rinv = L.get([128, 256], F32); pn = L.get([128, 4, 256], BF16)
        pT = L.get([128, 4, 2, NB], BF16); oT = L.get([128, 8, NB], BF16)
        qTm = [L.get([128, 8, TS], BF16) for _ in range(2)]
        KTs = L.get([128, 8, 256], BF16)
        qT_s = L.get([128, 8, TS], BF16); pT_s = L.get([128, 4, 2, TS], BF16)
        kslot = [carve(R1, 49152 + i * 4096, [128, 2, 1024], BF16) for i in range(2)]
        p2keys = [("hn", c_) for c_ in range(KC)] + ["m2sqb", "m2rstd", "m2tmpn"] + [("qT", o_) for o_ in range(8)] + [("pn", 0), ("pn", 1), ("pn", 2), ("pn", 3), "pT", "oT", "qT_s", "pT_s", "rinv", ("KTs", 0), ("KTs", 1)] + [("sm8", h) for h in range(4)] + [("sm12", h) for h in range(4)] + [("smx", 0), ("smx", 1), ("qTm", 0), ("qTm", 1), "KTs",
                  ("kslot", 0), ("kslot", 1)]
        S.fence(p2keys)
        def slot_views(si):
            base = si * 24576
            return (carve(R1, base, [128, 8, 512], BF16), carve(R1, base + 8192, [128, 8, 512], BF16),
                    carve(R1, base + 16384, [128, 4, 1024], BF16))
        S.fence([("fslot", 0), ("fslot", 1)])

        def unit_pieces(u):
            f0, nf = UNITS[u]
            si = u % 2
            wg_v, wu_v, wd_v = slot_views(si)
            key = "fs%d" % si
            gsrc = w_gate.rearrange("(c p) n -> p c n", p=128)
            usrc = w_up.rearrange("(c p) n -> p c n", p=128)
            pcs = []
            pcs.append(lambda: ld("gpsimd", wg_v[:, :, 0:nf * 128], gsrc[:, :, f0 * 128:(f0 + nf) * 128], key, writes=[("fslot", si)]))
            pcs.append(lambda: ld("gpsimd", wu_v[:, :, 0:nf * 128], usrc[:, :, f0 * 128:(f0 + nf) * 128], key, writes=[("fslot", si)]))
            pcs.append(lambda: ld("gpsimd", wd_v[:, 0:nf, :],
                                  w_down[f0 * 128:(f0 + nf) * 128, :].rearrange("(l p) n -> p l n", p=128), key, writes=[("fslot", si)]))
            return pcs

        def load_unit(u):
            for p_ in unit_pieces(u):
                p_()

        def softmax_tile(rows, psA, psB, pkeys):
            for hh, pb in enumerate([psA, psB]):
                V("tensor_reduce", dict(out=smx[0:rows, 2 * hh:2 * hh + 2], in_=pb[0:rows, :].rearrange("p (h m) -> p h m", h=2),
                                        axis=AX.X, op=ALU.max, negate=True), [pkeys[hh]], [("smx", hh)])
            for h in range(4):
                pb = [psA, psB][h // 2]
                act(pn[0:rows, h, :], pb[0:rows, (h % 2) * 256:(h % 2 + 1) * 256], AF.Exp, [pkeys[h // 2], ("smx", h // 2)],
                    [("pn", h)], bias=smx[0:rows, h:h + 1])

        def attn_block(tok0, n, xkeys, prev_wo, sfill, pre=False, nxt=None):
            blk = slice(tok0, tok0 + n)
            sfill = list(sfill)

            def fill():
                if sfill:
                    sfill.pop(0)()
                if wpieces:
                    wpieces.pop(0)()
            S.ctx = "p2.norm"
            fm_norm(lambda c: xT[:, c, blk], n, G_X, lambda c: hn[:, c, 0:n], sqb2, rstd2, tmpn2, PS[0], "ps0",
                    xkeys, ["hn"], "m2", do_stats=not pre)
            rr[0] = 1
            S.ctx = "p2.q"
            for oc in range(8):
                bi_ = (0, 1, 2, 3)[oc % 4]
                pb, pk = PS[bi_], "ps%d" % bi_
                for c in range(KC):
                    MM(pb[:, 0:n], wq_b[:, c, oc * 128:(oc + 1) * 128], hn[:, c, 0:n], c == 0, c == KC - 1, ["wq", ("hn", c)], [pk])
                act(qT[:, oc, 0:n], pb[:, 0:n], AF.Copy, [pk], [("qT", oc)], scale=1.0 / 16)
            fill()
            ntl = n // 128

            def scores(tl):
                S.ctx = "p2.sc"
                tsl = slice(tl * 128, (tl + 1) * 128)
                for h in range(4):
                    pb, pk = PS[2 + h // 2], "ps%d" % (2 + h // 2)
                    for dc in range(2):
                        MM(pb[:, (h % 2) * 256:(h % 2 + 1) * 256], qT[:, 2 * h + dc, tsl], KT[:, 2 * h + dc, :], dc == 0, dc == 1,
                           [("qT", 2 * h + dc), "KT"], [pk])

            def transposes(tl):
                S.ctx = "p2.pT"
                tsl = slice(tl * 128, (tl + 1) * 128)
                p4b = PS[4][:].bitcast(BF16)
                for h in range(4):
                    for mc in range(2):
                        TR(p4b[:, (h * 2 + mc) * 128:(h * 2 + mc + 1) * 128], pn[:, h, mc * 128:(mc + 1) * 128], ident_b[:],
                           [("pn", h), "ident_b"], ["ps4"], sig=(mc == 1))
                V("tensor_copy", dict(out=pT[:, :, :, tsl], in_=p4b.rearrange("p (h m t) -> p h m t", h=4, m=2)), ["ps4"], ["pT"])
            scores(0)
            for tl in range(ntl):
                S.ctx = "p2.smax"
                softmax_tile(128, PS[2], PS[3], ["ps2", "ps3"])
                if tl + 1 < ntl:
                    scores(tl + 1)
                if tl == 0 and prev_wo is not None:
                    prev_wo()
                transposes(tl)
                fill()
            if nxt is not None:
                S.ctx = "p2.norm"
                ntok0, nn_, nxk = nxt
                nblk = slice(ntok0, ntok0 + nn_)
                fm_squares(lambda c: xT[:, c, nblk], nn_, sqb2, nxk, "m2")
            S.ctx = "p2.pv"
            pvb = [0, 1, 4]
            pvi = [0]

            def rot3():
                i_ = pvb[pvi[0] % 3]
                pvi[0] += 1
                return PS[i_], "ps%d" % i_
            for h in range(4):
                pbs, pks = rot3()
                for mc in range(2):
                    MM(pbs[:, 0:n], ones_b[:], pT[:, h, mc, 0:n], mc == 0, mc == 1, ["ones_b", "pT"], [pks])
                V("reciprocal", dict(out=rinv[:, 0:n], in_=pbs[:, 0:n]), [pks], ["rinv"])
                for dc in range(2):
                    oc = 2 * h + dc
                    pb, pk = rot3()
                    for mc in range(2):
                        MM(pb[:, 0:n], vbf[:, mc, oc * 128:(oc + 1) * 128], pT[:, h, mc, 0:n], mc == 0, mc == 1,
                           ["vbf", "pT"], [pk], sig=(mc == 1))
                    V("tensor_tensor", dict(out=oT[:, oc, 0:n], in0=pb[:, 0:n], in1=rinv[:, 0:n], op=ALU.mult),
                      [pk, "rinv"], ["oT"])
            if nxt is not None:
                S.ctx = "p2.norm"
                fm_stats(lambda c: xT[:, c, nblk], nn_, sqb2, rstd2, tmpn2, PS[0], "ps0", nxk, "m2", squares=False)
            fill()
            for fl in sfill:
                fl()

            def wo_part():
                S.ctx = "p2.wo"
                for oc in range(8):
                    pb, pk = rot()
                    for c in range(KC):
                        MM(pb[:, 0:n], wo_b[:, c, oc * 128:(oc + 1) * 128], oT[:, c, 0:n], c == 0, c == KC - 1, ["wo", "oT"], [pk])
                    V("tensor_tensor", dict(out=xT[:, oc, blk], in0=pb[:, 0:n], in1=xT[:, oc, blk], op=ALU.add),
                      [pk] + list(xkeys), list(xkeys))
            return wo_part

        sblk = slice(TP, TP + TS)
        skeys = [("xT", 8)]

        def s_init():
            S.ctx = "p2.s_init"
            fm_norm(lambda c: xT[:, c, sblk], TS, G_X, lambda c: hn[:, c, 0:TS], sqb2, rstd2, tmpn2, PS[0], "ps0",
                    skeys, ["hn"], "m2")
            rr[0] = 1
            for oc in range(8):
                pb, pk = rot()
                for c in range(KC):
                    MM(pb[:, 0:TS], wq_b[:, c, oc * 128:(oc + 1) * 128], hn[:, c, 0:TS], c == 0, c == KC - 1, ["wq", ("hn", c)], [pk])
                act(qT_s[:, oc, :], pb[:, 0:TS], AF.Copy, [pk], ["qT_s"], scale=1.0 / 16)
            G("memset", dict(ap=qTm[0][:], constant=0.0), [], [("qTm", 0)])
            G("memset", dict(ap=qTm[1][:], constant=0.0), [], [("qTm", 1)])

        def s_kpass(b):
            S.ctx = "p2.s_k"
            s = b % 2
            ld("gpsimd", kslot[s][:], ck[b].rearrange("(mt p) d -> p mt d", p=128), "ks%d" % s, writes=[("kslot", s)])
            if b >= 2:
                V("memset", dict(ap=qTm[s][:, :, (b - 2) * 4:(b - 2) * 4 + 4], constant=0.0), [], [("qTm", s)])
            V("tensor_copy", dict(out=qTm[s][:, :, b * 4:b * 4 + 4], in_=qT_s[:, :, b * 4:b * 4 + 4]), ["qT_s"], [("qTm", s)])
            for half in range(2):
                p45 = PS[4 + half][:].bitcast(BF16)
                for oc4 in range(4):
                    oc = half * 4 + oc4
                    for mt in range(2):
                        TR(p45[:, oc4 * 256 + mt * 128:oc4 * 256 + (mt + 1) * 128], kslot[s][:, mt, oc * 128:(oc + 1) * 128],
                           ident_b[:], [("kslot", s), "ident_b"], ["ps%d" % (4 + half)], sig=(oc4 == 3 and mt == 1))
                src = p45.rearrange("p (o m) -> p o m", o=4)
                if half == 0:
                    V("tensor_copy", dict(out=KTs[:, 0:4, :], in_=src), ["ps4"], [("KTs", 0)])
                else:
                    act(KTs[:, 4:8, :], src, AF.Copy, ["ps5"], [("KTs", 1)])
            for h in range(4):
                pb, pk = PS[6 + h // 2], "ps%d" % (6 + h // 2)
                for dc in range(2):
                    first = (b == 0 and h % 2 == 0 and dc == 0)
                    last = (b == 15 and dc == 1)
                    MM(pb[0:64, (h % 2) * 256:(h % 2 + 1) * 256], qTm[s][:, 2 * h + dc, :], KTs[:, 2 * h + dc, :], first, last,
                       [("qTm", s), ("KTs", h // 2)], [pk], sig=(dc == 1))

        def s_mid():
            S.ctx = "p2.s_mid"
            softmax_tile(64, PS[6], PS[7], ["ps6", "ps7"])
            p4b = PS[4][:].bitcast(BF16)
            for h in range(4):
                for mc in range(2):
                    TR(p4b[:, (h * 2 + mc) * 64:(h * 2 + mc + 1) * 64], pn[0:64, h, mc * 128:(mc + 1) * 128], ident_b[0:64, 0:64],
                       [("pn", h), "ident_b"], ["ps4"], sig=(mc == 1))
            V("tensor_copy", dict(out=pT_s[:], in_=p4b[:, 0:512].rearrange("p (h m t) -> p h m t", h=4, m=2)), ["ps4"], ["pT_s"])

        def s_vpass(b):
            S.ctx = "p2.s_v"
            s = b % 2
            ld("gpsimd", kslot[s][:], cv[b].rearrange("(mt p) d -> p mt d", p=128), "ks%d" % s, writes=[("kslot", s)])
            for oc in range(8):
                for mc in range(2):
                    MM(PS[5][:, oc * 64 + b * 4:oc * 64 + b * 4 + 4], kslot[s][:, mc, oc * 128:(oc + 1) * 128],
                       pT_s[:, oc // 2, mc, b * 4:b * 4 + 4], mc == 0, mc == 1, [("kslot", s), "pT_s"], ["ps5"],
                       sig=(oc == 7 and mc == 1))

        def s_fin():
            S.ctx = "p2.s_fin"
            pbs, pks = rot()
            for h in range(4):
                for mc in range(2):
                    MM(pbs[:, h * TS:(h + 1) * TS], ones_b[:], pT_s[:, h, mc, :], mc == 0, mc == 1, ["ones_b", "pT_s"], [pks], sig=(mc == 1))
            V("reciprocal", dict(out=rinv[:, 0:4 * TS], in_=pbs[:, 0:4 * TS]), [pks], ["rinv"])
            V("tensor_tensor", dict(out=oT[:, :, 0:TS].rearrange("p (h d) t -> p h d t", h=4),
                                    in0=PS[5][:].rearrange("p (h d t) -> p h d t", h=4, d=2),
                                    in1=rinv[:, 0:4 * TS].rearrange("p (h t) -> p h t", h=4).unsqueeze(2).broadcast_to([128, 4, 2, TS]),
                                    op=ALU.mult), ["ps5", "rinv"], ["oT"])
            for oc in range(8):
                pb, pk = rot()
                for c in range(KC):
                    MM(pb[:, 0:TS], wo_b[:, c, oc * 128:(oc + 1) * 128], oT[:, c, 0:TS], c == 0, c == KC - 1, ["wo", "oT"], [pk])
                V("tensor_tensor", dict(out=xT[:, oc, sblk], in0=pb[:, 0:TS], in1=xT[:, oc, sblk], op=ALU.add),
                  [pk] + skeys, skeys)

        s_init()
        prev_wo = None
        wpieces = unit_pieces(0) + unit_pieces(1)
        for b in range(TP // NB):
            if b < 4:
                sf = [(lambda bb=bb: s_kpass(bb)) for bb in range(4 * b, 4 * b + 4)]
            else:
                sf = [(lambda bb=bb: s_vpass(bb)) for bb in range(4 * (b - 4), 4 * (b - 4) + 4)]
            nxt_ = ((b + 1) * NB, NB, [("xT", b + 1)]) if b + 1 < TP // NB else None
            prev_wo = attn_block(b * NB, NB, [("xT", b)], prev_wo, sf, pre=(b > 0), nxt=nxt_)
            if b == 3:
                s_mid()
        prev_wo()
        for p_ in wpieces:
            p_()
        s_fin()

        if DEBUG:
            ld("sync", dbg2, xT[:], "o_dbg2", reads=[("xT", i) for i in range(9)])
        hnA = carve(R3, 0, [128, 8, T], BF16)
        L = Lay(R2)
        sqb3 = L.get([128, 8, 256], BF16)
        rstd3 = L.get([128, 256], F32); tmpn3 = L.get([128, 256], F32)
        gsbs = [L.get([128, NB3 + 2], F32) for _ in range(2)]
        cbs = [L.get([128, NB3], F32) for _ in range(2)]
        actbs = [L.get([128, 4, NB3], BF16) for _ in range(2)]
        sconvT = L.get([128, NF, 32], F32)
        gkeep = sconvT
        gsss = [L.get([128, 16, 6], F32) for _ in range(2)]
        ubs = [L.get([128, NB3], BF16) for _ in range(2)]
        stgs = [carve(R1, 49152 + i_ * 4096, [128, 1024], F32) for i_ in range(2)]
        stg3 = stgs[0]
        gfin_bc = L.get([128, 1024], F32)
        fst = L.get([128, 16], F32)
        p3keys = [(("hnA", i_), c_) for i_ in range(5) for c_ in range(KC)] + [("gsbh", 0), ("gsbh", 1), ("gssh", 0), ("gssh", 1), "m3sqb", "m3rstd", "m3tmpn", ("gsb", 0), ("gsb", 1), ("cb", 0), ("cb", 1), "sconvT", ("gss", 0), ("gss", 1), ("ub", 0), ("ub", 1), "stg3", ("stg", 1), "gfin_bc", ("fst", 0), ("fst", 1)] + \
                 [("actb", a_, l_) for a_ in range(2) for l_ in range(4)]
        S.fence(p3keys)
        ld("sync", gfin_bc[:], g_fin.broadcast_to([128, D]), "gfin", writes=["gfin_bc"])
        for half in range(3):
            c0, c1 = half * 1024, min(DFF, (half + 1) * 1024)
            ld("sync", stg3[0:32, 0:c1 - c0], sconv[:, c0:c1], "stg3", writes=["stg3"])
            nfh = (c1 - c0) // 128
            for j in range(nfh):
                TR(PS[7][:, j * 32:(j + 1) * 32], stg3[0:32, j * 128:(j + 1) * 128], ident_f[0:32, 0:32], ["stg3", "ident"], ["ps7"],
                   sig=(j == nfh - 1))
            V("tensor_copy", dict(out=sconvT[:, half * 8:half * 8 + nfh, :],
                                  in_=PS[7][:, 0:nfh * 32].rearrange("p (f t) -> p f t", t=32)), ["ps7"], ["sconvT"])
        blocks3 = [(b * NB3, NB3, [("xT", 2 * b), ("xT", 2 * b + 1)], False) for b in range(TP // NB3)] + [(TP, TS, [("xT", 8)], True)]
        gcnt = [0]

        def ffn_front(u, bi, ab, fillers=()):
            S.ctx = "p3.front"
            tails = {}
            f0, nf = UNITS[u]
            si = u % 2
            wg_v, wu_v, wd_v = slot_views(si)
            fk = ("fslot", si)
            tok0, n, xkeys, sample = blocks3[bi]
            blk = slice(tok0, tok0 + n)
            if u == 0:
                for s0 in range(0, n, 256):
                    nn = min(256, n - s0)
                    sub = slice(tok0 + s0, tok0 + s0 + nn)
                    fm_norm(lambda c: xT[:, c, sub], nn, G_FFN, lambda c: hnA[:, c, sub], sqb3, rstd3, tmpn3, PS[0], "ps0",
                            xkeys, [("hnA", bi)], "m3")
            hk = ("hnA", bi)
            for lf in range(nf):
                f = f0 + lf
                gi = gcnt[0] % 2
                gcnt[0] += 1
                gsb, cb = gsbs[gi], cbs[gi]
                gk, ck_ = ("gsb", gi), ("cb", gi)
                pg, pgk = PS[1 + (lf % 2)], "ps%d" % (1 + lf % 2)
                pub_ = 3 if u == len(UNITS) - 1 else 3 + (lf % 2)
                pu, puk = PS[pub_], "ps%d" % pub_
                for c in range(KC):
                    MM(pg[:, 0:n], wg_v[:, c, lf * 128:(lf + 1) * 128], hnA[:, c, blk], c == 0, c == KC - 1, [fk, (hk, c)], [pgk])
                for c in range(KC):
                    MM(pu[:, 0:n], wu_v[:, c, lf * 128:(lf + 1) * 128], hnA[:, c, blk], c == 0, c == KC - 1, [fk, (hk, c)], [puk])
                cw = lambda j, f=f: colv2[:, j * NF + f:j * NF + f + 1]
                cbias = colv1[:, 46 + f:47 + f]
                ghk = ("gsbh", gi)
                if not sample:
                    V("tensor_copy", dict(out=gsb[:, 0:2], in_=ghist[:, f, :]), ["ghist"], [ghk])
                    act(gsb[:, 2:2 + n], pg[:, 0:n], AF.Copy, [pgk], [gk])
                    V("tensor_copy", dict(out=ghist[:, f, :], in_=gsb[:, n:n + 2]), [gk], ["ghist"])
                    g0, g1, g2 = gsb[:, 0:n], gsb[:, 1:1 + n], gsb[:, 2:2 + n]
                    cbo = cb[:, 0:n]
                else:
                    gk = ("gss", gi)
                    ghk = ("gssh", gi)
                    gss = gsss[gi]
                    V("tensor_copy", dict(out=gss[:, :, 0:2], in_=sconvT[:, f, :].rearrange("p (b j) -> p b j", j=2)),
                      ["sconvT"], [ghk])
                    act(gss[:, :, 2:6], pg[:, 0:n].rearrange("p (b t) -> p b t", t=4), AF.Copy, [pgk], [gk])
                    V("tensor_copy", dict(out=gkeep[:, f, :].rearrange("p (b j) -> p b j", j=2), in_=gss[:, :, 4:6]),
                      [gk], ["sconvT"])
                    g0, g1, g2 = gss[:, :, 0:4], gss[:, :, 1:5], gss[:, :, 2:6]
                    cbo = cb[:, 0:n].rearrange("p (b t) -> p b t", t=4)
                ub, ubk = ubs[gi], ("ub", gi)
                act(ub[:, 0:n], pu[:, 0:n], AF.Copy, [puk], [ubk])

                def ident(cbo=cbo, g2=g2, gk=gk, ck_=ck_, cw=cw, cbias=cbias):
                    act(cbo, g2, AF.Identity, [gk, "colv1", "colv2"], [ck_], scale=cw(2), bias=cbias)

                S.ctx = "p3.front"

                def tail(cbo=cbo, g1=g1, g0=g0, cw=cw, gk=gk, ghk=ghk, ck_=ck_, cb=cb, ub=ub, ubk=ubk, lf=lf):
                    S.ctx = "p3.chain"
                    V("scalar_tensor_tensor", dict(out=cbo, in0=g1, scalar=cw(1), in1=cbo, op0=ALU.mult, op1=ALU.add), [gk, ghk, ck_], [ck_])
                    V("scalar_tensor_tensor", dict(out=cbo, in0=g0, scalar=cw(0), in1=cbo, op0=ALU.mult, op1=ALU.add), [gk, ghk, ck_], [ck_])
                    act(cb[:, 0:n], cb[:, 0:n], AF.Gelu_apprx_tanh, [ck_], [ck_])
                    V("tensor_tensor", dict(out=actbs[ab][:, lf, 0:n], in0=cb[:, 0:n], in1=ub[:, 0:n], op=ALU.mult),
                      [ck_, ubk], [("actb", ab, lf)])
                tails[lf] = tail
                if lf >= 1:
                    tails[lf - 1]()
                ident()
                if fillers:
                    fillers.pop(0)()
                S.ctx = "p3.front"
            return tails[nf - 1]

        fcnt = [0]

        def final_out(bi):
            tok0, n, xkeys, sample = blocks3[bi]
            tiles = list(range(0, n, 128))
            As, Bs = [], []
            for s0 in tiles:
                par = fcnt[0] % 2
                fcnt[0] += 1
                As.append(lambda s0=s0, par=par: final_A(tok0, n, xkeys, s0, par))
                Bs.append([(lambda s0=s0, par=par, half=half: final_B(tok0, n, xkeys, sample, s0, par, half)) for half in range(2)])
            seq = []
            nt = len(tiles)
            seq.append(As[0])
            if nt > 1:
                seq.append(As[1])
            for i_ in range(nt):
                seq += Bs[i_]
                if i_ + 2 < nt:
                    seq.append(As[i_ + 2])
            return seq

        def final_A(tok0, n, xkeys, s0, par):
            S.ctx = "p3.final"
            rows = min(128, n - s0)
            tsl = slice(tok0 + s0, tok0 + s0 + rows)
            fo = par * 8
            act(sqb3[:, :, 0:rows], xT[:, :, tsl], AF.Square, list(xkeys), ["m3sqb"])
            for c in range(KC):
                MM(PS[4][0:rows, 0:1], sqb3[:, c, 0:rows], ones_b[:, 0:1], c == 0, c == KC - 1, ["m3sqb", "ones_b"], ["ps4"])
            act(fst[0:rows, fo + 3:fo + 4], PS[4][0:rows, 0:1], AF.Ln, ["ps4"], [("fst", par)], scale=1.0 / D, bias=EPS)
            V("tensor_scalar", dict(out=fst[0:rows, fo + 4:fo + 5], in0=fst[0:rows, fo + 3:fo + 4], scalar1=-0.5, scalar2=None, op0=ALU.mult),
              [("fst", par)], [("fst", par)])
            act(fst[0:rows, fo + 5:fo + 6], fst[0:rows, fo + 4:fo + 5], AF.Exp, [("fst", par)], [("fst", par)])

        def final_B(tok0, n, xkeys, sample, s0, par, half):
            S.ctx = "p3.final"
            rows = min(128, n - s0)
            tsl = slice(tok0 + s0, tok0 + s0 + rows)
            fo = par * 8
            stg_, sk_ = stgs[par], ("stg3" if par == 0 else ("stg", 1))
            bank_ = 7 if half == 0 else 0
            pb, pk = PS[bank_], "ps%d" % bank_
            for c4 in range(4):
                c = half * 4 + c4
                TR(pb[0:rows, c4 * 128:(c4 + 1) * 128], xT[:, c, tsl], ident_f[:], list(xkeys) + ["ident"], [pk], sig=(c4 == 3))
            V("scalar_tensor_tensor", dict(out=stg_[0:rows, half * 512:(half + 1) * 512], in0=pb[0:rows, :], scalar=fst[0:rows, fo + 5:fo + 6],
                                           in1=gfin_bc[0:rows, half * 512:(half + 1) * 512], op0=ALU.mult, op1=ALU.mult),
              [pk, ("fst", par), "gfin_bc"], [sk_])
            if half == 1:
                dst = y_s if sample else y_p[tok0 + s0:tok0 + s0 + rows, :]
                ld("sync", dst, stg_[0:rows, :], "o_y%d" % par, reads=[sk_])

        def ffn_back(u, bi, ab, fillers=None):
            S.ctx = "p3.back"
            f0, nf = UNITS[u]
            si = u % 2
            wg_v, wu_v, wd_v = slot_views(si)
            fk = ("fslot", si)
            tok0, n, xkeys, sample = blocks3[bi]
            blk = slice(tok0, tok0 + n)
            dbanks = [5, 6]
            if u != len(UNITS) - 1:
                dbanks.append(7)
                if u > 0:
                    dbanks.append(0)
            for oc in range(8):
                bi_ = dbanks[oc % len(dbanks)]
                pb, pk = PS[bi_], "ps%d" % bi_
                for lf in range(nf):
                    MM(pb[:, 0:n], wd_v[:, lf, oc * 128:(oc + 1) * 128], actbs[ab][:, lf, 0:n], lf == 0, lf == nf - 1,
                       [fk] + [("actb", ab, l) for l in range(nf)], [pk])
                V("tensor_tensor", dict(out=xT[:, oc, blk], in0=pb[:, 0:n], in1=xT[:, oc, blk], op=ALU.add),
                  [pk] + list(xkeys), list(xkeys))
                if fillers:
                    fillers.pop(0)()
                    S.ctx = "p3.back"

        def conv_state_out(f0, nf):
            S.ctx = "p3.cso"
            stgp = stgs[1]
            for j in range(nf):
                TR(PS[0][0:2, j * 128:(j + 1) * 128], ghist[:, f0 + j, :], ident_f[:], ["ghist", "ident"], ["ps0"], sig=(j == nf - 1))
            V("tensor_copy", dict(out=stgp[0:2, 0:nf * 128], in_=PS[0][0:2, 0:nf * 128]), ["ps0"], [("stg", 1)])
            ld("sync", sc_p[:, f0 * 128:(f0 + nf) * 128], stgp[0:2, 0:nf * 128], "o_scp", reads=[("stg", 1)])
            for j in range(nf):
                TR(PS[7][0:32, j * 128:(j + 1) * 128], gkeep[:, f0 + j, :], ident_f[:], ["sconvT", "ident"], ["ps7"], sig=(j == nf - 1))
            V("tensor_copy", dict(out=stg3[0:32, 0:nf * 128], in_=PS[7][0:32, 0:nf * 128]), ["ps7"], ["stg3"])
            ld("sync", sc_s[:, f0 * 128:(f0 + nf) * 128], stg3[0:32, 0:nf * 128], "o_scs", reads=["stg3"])

        abc = 0
        for u in range(len(UNITS)):
            prev = None
            lastu = (u == len(UNITS) - 1)
            pend = []
            for bi in range(len(blocks3)):
                ab = abc % 2
                abc += 1
                deferred = ffn_front(u, bi, ab, fillers=pend)
                if prev is not None:
                    ffn_back(u, *prev, fillers=pend)
                    if lastu:
                        pend += final_out(prev[0])
                deferred()
                prev = (bi, ab)
            ffn_back(u, *prev, fillers=pend)
            if lastu:
                pend += final_out(prev[0])
            for fl in pend:
                fl()
            conv_state_out(*UNITS[u])
            if u + 2 < len(UNITS):
                load_unit(u + 2)
        _NC_CACHE["sched"] = S
        sems = {n: es.enter_context(nc.semaphore(n)) for n in sorted(S.sem_names)}
        final = {"sync": [(s, v) for s, v in S.dma_cnt.items() if s.startswith("D_o_")]}
        with nc.Block() as block:
            S.emit(block, sems, final)
    return nc


_NC_CACHE = {}


def kernel(**inp):
    f = lambda a: np.ascontiguousarray(np.asarray(a, dtype=np.float32))
    x_prompt, x_sample, mem_prompt = f(inp["x_prompt"]), f(inp["x_sample"]), f(inp["mem_prompt"])
    state_gla, state_conv = f(inp["state_gla"])[0], f(inp["state_conv"])[0]
    ckk, cvv = f(inp["cache_mem_k"])[0], f(inp["cache_mem_v"])[0]
    vec1 = np.concatenate([f(inp["g_mix"]).reshape(8, 128), f(inp["g_x"]).reshape(8, 128), f(inp["g_mem"]).reshape(8, 128),
                           f(inp["g_ffn"]).reshape(8, 128), f(inp["g_final"]).reshape(8, 128), f(inp["b_alpha"]).reshape(2, 128),
                           f(inp["g_gla_out"]).reshape(4, 128), f(inp["conv_b"]).reshape(22, 128)], axis=0)
    vec2 = f(inp["conv_w"]).reshape(66, 128)
    shared = {
        "w_in": f(inp["w_in"])[0], "w_alpha": f(inp["w_alpha"])[0], "w_s": f(inp["w_s"])[0],
        "b_s": f(inp["b_s"]).reshape(1, 512), "g_sgu": f(inp["g_sgu"]).reshape(1, 512),
        "w_out": f(inp["w_out"])[0], "wq": f(inp["wq_x"])[0], "wk": f(inp["wk_x"])[0], "wv": f(inp["wv_x"])[0], "wo": f(inp["wo_x"])[0],
        "w_gate": f(inp["w_gate"])[0], "w_up": f(inp["w_up"])[0], "w_down": f(inp["w_down"])[0],
        "vec1": np.ascontiguousarray(vec1), "vec2": np.ascontiguousarray(vec2), "g_fin": f(inp["g_final"]).reshape(1, D),
    }
    for k, v in host_consts().items():
        shared["c_" + k] = v
    in_maps = []
    for c in range(NCORES):
        m = dict(shared)
        sl = slice(c * 16, (c + 1) * 16)
        m["x_p"] = x_prompt[c]
        m["x_s"] = np.ascontiguousarray(x_sample[sl].reshape(TS, D))
        m["mem"] = mem_prompt[c]
        m["sgla"] = np.ascontiguousarray(state_gla[sl])
        m["sconv"] = np.ascontiguousarray(state_conv[sl].reshape(32, DFF))
        m["ck"] = np.ascontiguousarray(ckk[sl].reshape(16, 256, D))
        m["cv"] = np.ascontiguousarray(cvv[sl].reshape(16, 256, D))
        in_maps.append(m)
    if "nc" not in _NC_CACHE:
        _NC_CACHE["nc"] = build_nc()
    res = run_bass_kernel_spmd(_NC_CACHE["nc"], in_maps, core_ids=list(range(NCORES)))
    R = res.results
    cat = lambda k: np.stack([np.asarray(r[k], dtype=np.float32) for r in R], axis=0)
    y_prompt = cat("y_p")
    y_sample = cat("y_s").reshape(128, 4, D)
    sg_p = cat("sg_p")[None]
    sc_p = cat("sc_p")[None]
    mk_p = cat("mk_p").reshape(1, 8, 256, 4, 256)
    mv_p = cat("mv_p").reshape(1, 8, 256, 4, 256)
    sg_s = cat("sg_s").reshape(1, 128, 4, 64, 128)
    sc_s = cat("sc_s").reshape(1, 128, 2, DFF)
    sv_s = cat("sv_s").reshape(1, 128, 4, 4, 128)
    if DEBUG:
        _NC_CACHE["dbg"] = (R[0]["dbg1"], R[0]["dbg2"])
    return (y_prompt, y_sample, sg_p, sc_p, mk_p, mv_p, sg_s, sc_s, sv_s)
```
